# Optimizing a Trainium2 kernel written in Bass

```python
import math
import jax, jax.numpy as jnp
from jax import lax
import numpy as np

D_MODEL = 2048
BATCH = 2
SEQ = 4096
DEPTH = 1

PLE_DIM = 256
N_HEADS = 16
HEAD_DIM = 128
D_ATTN = N_HEADS * HEAD_DIM
D_RNN = ((4 * D_MODEL // 3 + 255) // 256) * 256
N_RNN_BLOCKS = 16
RNN_BLOCK = D_RNN // N_RNN_BLOCKS
CONV_WIDTH = 4
LRU_C = 8.0
D_FF = ((8 * D_MODEL + 3 * 256 - 1) // (3 * 256)) * 256
Q_BLOCK = 128
EPS = 1e-6

IN_WIDTHS = (D_RNN, D_RNN, D_ATTN, D_ATTN, D_ATTN, D_MODEL, D_MODEL)
D_IN = sum(IN_WIDTHS)
IN_SPLIT_POINTS = tuple(int(v) for v in np.cumsum(IN_WIDTHS)[:-1])

kernel_name = "hybrid_rglru_stickbreaking_block"


def rmsnorm(x, g):
    x32 = x.astype(jnp.float32)
    r = x32 * lax.rsqrt(jnp.mean(x32 * x32, axis=-1, keepdims=True) + EPS)
    return (r * g.astype(jnp.float32)).astype(x.dtype)


def causal_depthwise_conv(x, w, b):
    S = x.shape[1]
    xp = jnp.pad(x, ((0, 0), (CONV_WIDTH - 1, 0), (0, 0)))
    y = sum(xp[:, k:k + S, :] * w[k] for k in range(CONV_WIDTH))
    return y + b


def rg_lru(x, w_a, b_a, w_x, b_x, lam):
    B, S, _ = x.shape
    xb = x.reshape(B, S, N_RNN_BLOCKS, RNN_BLOCK)
    r = jax.nn.sigmoid(jnp.einsum('bsnc,ncd->bsnd', xb, w_a).reshape(B, S, D_RNN) + b_a)
    i = jax.nn.sigmoid(jnp.einsum('bsnc,ncd->bsnd', xb, w_x).reshape(B, S, D_RNN) + b_x)
    log_a = (-LRU_C * r.astype(jnp.float32) * jax.nn.softplus(-lam.astype(jnp.float32)))
    a = jnp.exp(log_a)
    u = jnp.sqrt(-jnp.expm1(2.0 * log_a)) * (i * x).astype(jnp.float32)

    def combine(c1, c2):
        a1, b1 = c1
        a2, b2 = c2
        return a1 * a2, a2 * b1 + b2

    _, h = lax.associative_scan(combine, (a, u), axis=1)
    return h.astype(x.dtype)


def head_rmsnorm(x, g):
    x32 = x.astype(jnp.float32)
    r = x32 * lax.rsqrt(jnp.mean(x32 * x32, axis=-1, keepdims=True) + EPS)
    return (r * g.astype(jnp.float32)).astype(x.dtype)


def stick_breaking_attention(q, k, v):
    S = q.shape[2]
    scale = 1.0 / math.sqrt(HEAD_DIM)
    outs = []
    for blk in range(S // Q_BLOCK):
        t0 = blk * Q_BLOCK
        tk = t0 + Q_BLOCK
        qb = q[:, :, t0:tk]
        kb = k[:, :, :tk]
        vb = v[:, :, :tk]
        z = jnp.einsum('bhqd,bhkd->bhqk', qb, kb).astype(jnp.float32) * scale
        q_pos = t0 + jnp.arange(Q_BLOCK)[:, None]
        k_pos = jnp.arange(tk)[None, :]
        causal = k_pos < q_pos
        log_keep = jnp.where(causal, jax.nn.log_sigmoid(-z), 0.0)
        incl = lax.cumsum(log_keep, axis=3, reverse=True)
        excl = jnp.concatenate([incl[..., 1:], jnp.zeros_like(incl[..., :1])], axis=-1)
        w = jnp.where(causal, jnp.exp(jax.nn.log_sigmoid(z) + excl), 0.0)
        outs.append(jnp.einsum('bhqk,bhkd->bhqd', w.astype(vb.dtype), vb))
    return jnp.concatenate(outs, axis=2)


def setup_inputs(seed: int = 0) -> dict:
    key = jax.random.key(seed)
    ks = jax.random.split(key, 24)
    f32 = jnp.float32

    def nrm(k, shape, fan_in):
        return jax.random.normal(k, shape, f32) * (fan_in ** -0.5)

    def gain(k, shape):
        return 1.0 + 0.02 * jax.random.normal(k, shape, f32)

    u = jax.random.uniform(ks[9], (DEPTH, D_RNN), f32, 0.9, 0.999)
    s = u ** (1.0 / LRU_C)
    lru_lambda = jnp.log(s) - jnp.log1p(-s)

    return {
        "x": jax.random.normal(ks[0], (BATCH, SEQ, D_MODEL), f32),
        "p": jax.random.normal(ks[1], (DEPTH, BATCH, SEQ, PLE_DIM), f32),
        "g_mix": gain(ks[2], (DEPTH, D_MODEL)),
        "w_in": nrm(ks[3], (DEPTH, D_MODEL, D_IN), D_MODEL),
        "conv_w": nrm(ks[4], (DEPTH, CONV_WIDTH, D_RNN), CONV_WIDTH),
        "conv_b": 0.02 * jax.random.normal(ks[5], (DEPTH, D_RNN), f32),
        "w_rg_a": nrm(ks[6], (DEPTH, N_RNN_BLOCKS, RNN_BLOCK, RNN_BLOCK), RNN_BLOCK),
        "b_rg_a": 0.02 * jax.random.normal(ks[7], (DEPTH, D_RNN), f32),
        "w_rg_x": nrm(ks[8], (DEPTH, N_RNN_BLOCKS, RNN_BLOCK, RNN_BLOCK), RNN_BLOCK),
        "b_rg_x": 0.02 * jax.random.normal(ks[10], (DEPTH, D_RNN), f32),
        "lru_lambda": lru_lambda,
        "q_gain": gain(ks[11], (DEPTH, HEAD_DIM)),
        "k_gain": gain(ks[12], (DEPTH, HEAD_DIM)),
        "w_rnn_out": nrm(ks[13], (DEPTH, D_RNN, D_MODEL), D_RNN),
        "w_attn_out": nrm(ks[14], (DEPTH, D_ATTN, D_MODEL), D_ATTN),
        "w_o": nrm(ks[15], (DEPTH, D_MODEL, D_MODEL), D_MODEL),
        "g_ffn": gain(ks[16], (DEPTH, D_MODEL)),
        "w_ffn_gu": nrm(ks[17], (DEPTH, D_MODEL, 2 * D_FF), D_MODEL),
        "w_ffn_down": nrm(ks[18], (DEPTH, D_FF, D_MODEL), D_FF),
        "g_ple": gain(ks[19], (DEPTH, D_MODEL)),
        "w_ple_gate": nrm(ks[20], (DEPTH, D_MODEL, D_MODEL), D_MODEL),
        "w_ple_proj": nrm(ks[21], (DEPTH, PLE_DIM, D_MODEL), PLE_DIM),
    }


def reference(x, p, g_mix, w_in, conv_w, conv_b, w_rg_a, b_rg_a, w_rg_x, b_rg_x,
              lru_lambda, q_gain, k_gain, w_rnn_out, w_attn_out, w_o, g_ffn,
              w_ffn_gu, w_ffn_down, g_ple, w_ple_gate, w_ple_proj):
    B, S, _ = x.shape
    for i in range(DEPTH):
        h = rmsnorm(x, g_mix[i])
        proj = h @ w_in[i]
        x_r, g_r, q, k, v, gate_r, gate_a = jnp.split(proj, IN_SPLIT_POINTS, axis=-1)

        xc = causal_depthwise_conv(x_r, conv_w[i], conv_b[i])
        y_r = jax.nn.gelu(g_r) * rg_lru(xc, w_rg_a[i], b_rg_a[i], w_rg_x[i], b_rg_x[i], lru_lambda[i])
        y_r = y_r @ w_rnn_out[i]

        q = head_rmsnorm(q.reshape(B, S, N_HEADS, HEAD_DIM).transpose(0, 2, 1, 3), q_gain[i])
        k = head_rmsnorm(k.reshape(B, S, N_HEADS, HEAD_DIM).transpose(0, 2, 1, 3), k_gain[i])
        v = v.reshape(B, S, N_HEADS, HEAD_DIM).transpose(0, 2, 1, 3)
        o = stick_breaking_attention(q, k, v)
        y_a = o.transpose(0, 2, 1, 3).reshape(B, S, D_ATTN) @ w_attn_out[i]

        mix = jax.nn.sigmoid(gate_r) * y_r + jax.nn.sigmoid(gate_a) * y_a
        x = x + mix @ w_o[i]

        h = rmsnorm(x, g_ffn[i])
        gu = h @ w_ffn_gu[i]
        g, u = jnp.split(gu, 2, axis=-1)
        x = x + (jax.nn.silu(g) * u) @ w_ffn_down[i]

        ple_gate = jax.nn.sigmoid(rmsnorm(x, g_ple[i]) @ w_ple_gate[i])
        x = x + ple_gate * (p[i] @ w_ple_proj[i])
    return x
```

```python
import numpy as np
from contextlib import ExitStack
import concourse.bass as bass
import concourse.mybir as mybir
from concourse.bass_utils import run_bass_kernel_spmd

F32 = mybir.dt.float32
BF16 = mybir.dt.bfloat16
U8 = mybir.dt.uint8
AF = mybir.ActivationFunctionType
ALU = mybir.AluOpType

S = 4096
D = 2048
NKC = 16
TC = 512
NCH = S // TC
SUB = 128
EPS = 1e-6
QSCALE = 1.0 / np.sqrt(128.0)
NEG = -30000.0
ARENA = 198 * 1024
GELU_C = 0.7978845608028654

ENGS = ["pe", "act", "dve", "pool", "sp"]


class Slot:
    __slots__ = ("name", "w", "r", "excl")

    def __init__(self, name="", excl=False):
        self.name = name
        self.w = None
        self.r = []
        self.excl = excl


class Op:
    __slots__ = ("eng", "fn", "deps", "needs_inc", "seq", "kind", "sem", "semv", "pos", "raw", "cost", "idx",
                 "nd", "rt", "fin", "succ")


class FW:
    NDS = 8

    def __init__(self):
        self.ops = {e: [] for e in ENGS}
        self.cur_barrier = {e: None for e in ENGS}
        self.since_barrier = []
        self.cc_count = 0
        self.stopped = False
        self.all = []

    def stop_if(self, tag):
        import os
        if os.environ.get("KSTOP", "") == tag:
            self.stopped = True

    DEFCOST = {"pe": 4.6, "act": 0.6, "dve": 0.64, "pool": 5.0, "sp": 3.0}

    def op(self, eng, fn, reads=(), writes=(), kind="c", cost=None):
        o = Op()
        o.cost = cost if cost is not None else (30.0 if kind == "cc" else self.DEFCOST[eng])
        if self.stopped:
            o.eng, o.fn, o.kind, o.deps, o.raw, o.pos = eng, fn, "skip", set(), set(), -1
            return o
        o.eng, o.fn, o.kind = eng, fn, kind
        o.needs_inc = False
        o.seq = 0
        o.sem = None
        o.semv = 0
        deps = set()
        raw = set()
        for s in reads:
            if s.w is not None:
                deps.add(s.w)
                raw.add(s.w)
            if s.excl:
                deps.update(s.r)
        for s in writes:
            if s.w is not None:
                deps.add(s.w)
            deps.update(s.r)
        for s in reads:
            s.r.append(o)
        for s in writes:
            s.w = o
            s.r = []
        if self.cur_barrier[eng] is not None:
            deps.add(self.cur_barrier[eng])
        if kind == "cc":
            o.sem = ("cc", eng, self.cc_count)
            self.cc_count += 1
            o.semv = 1
        deps.discard(o)
        o.deps = deps
        o.raw = raw
        o.pos = len(self.ops[eng])
        self.ops[eng].append(o)
        o.idx = len(self.all)
        self.all.append(o)
        self.since_barrier.append(o)
        return o

    def schedule(self, reorder=("pe", "act", "dve", "sp"), W=200):
        import bisect
        for o in self.all:
            o.succ = []
        for o in self.all:
            o.nd = len(o.deps)
            o.rt = 0.0
            for d in o.deps:
                d.succ.append(o)
        blev = {}
        for o in reversed(self.all):
            m = 0.0
            for c in o.succ:
                v = blev[id(c)]
                if v > m:
                    m = v
            blev[id(o)] = m + o.cost
        import os as _os3
        use_bl = _os3.environ.get("KPRIO", "bl") == "bl"
        for o in self.all:
            o.seq = (-blev[id(o)] if use_bl else 0.0, o.idx)
        avail = {e: [] for e in ENGS}
        ptr = {e: 0 for e in ENGS}
        free_at = {e: 0.0 for e in ENGS}
        order = {e: [] for e in ENGS}
        for o in self.all:
            if o.nd == 0 and o.eng in reorder:
                bisect.insort(avail[o.eng], (o.seq, o))
        remaining = len(self.all)
        while remaining:
            best = None
            for e in ENGS:
                if e in reorder:
                    lst = avail[e]
                    if not lst:
                        continue
                    fa = free_at[e]
                    cb = None
                    for (ix, o) in lst[:W]:
                        st = o.rt if o.rt > fa else fa
                        if cb is None or st < cb[0] - 1e-9:
                            cb = (st, ix, o)
                        if st <= fa:
                            break
                else:
                    if ptr[e] >= len(self.ops[e]):
                        continue
                    o = self.ops[e][ptr[e]]
                    if o.nd > 0:
                        continue
                    st = max(o.rt, free_at[e])
                    cb = (st, o.seq, o)
                if best is None or (cb[0], cb[1]) < (best[0], best[1]):
                    best = cb
            assert best is not None, "scheduler stuck"
            st, ix, o = best
            e = o.eng
            if e in reorder:
                i = bisect.bisect_left(avail[e], (ix, ))
                assert avail[e][i][1] is o
                avail[e].pop(i)
            else:
                ptr[e] += 1
            if o.kind in ("dma", "cc"):
                free_at[e] = st + (1.0 if e == "pool" else 0.15)
            else:
                free_at[e] = st + o.cost
            o.fin = st + o.cost
            order[e].append(o)
            remaining -= 1
            for c in o.succ:
                c.nd -= 1
                if o.fin > c.rt:
                    c.rt = o.fin
                if c.nd == 0 and c.eng in reorder:
                    bisect.insort(avail[c.eng], (c.seq, c))
        for e in ENGS:
            self.ops[e] = order[e]
            for i, o in enumerate(order[e]):
                o.pos = i
        self.makespan = max(o.fin for o in self.all)

    def barrier(self, engines):
        if self.stopped:
            return
        prior = [o for o in self.since_barrier if o.kind != "cc"]
        b = self.op("dve", lambda e: e.nop(), cost=0.1)
        b.deps.update(prior)
        b.deps.discard(b)
        self.since_barrier = [b]
        for e in engines:
            self.cur_barrier[e] = b

    def finalize(self):
        for e in ENGS:
            n = 0
            hist = []
            for o in self.ops[e]:
                if o.kind != "dma":
                    continue
                o.sem = ("dma", e, n % self.NDS)
                o.semv = 16 * (n // self.NDS + 1)
                if n >= self.NDS:
                    o.deps.add(hist[n - self.NDS])
                hist.append(o)
                n += 1
        for e in ENGS:
            for o in self.ops[e]:
                for d in o.deps:
                    if d.kind != "c":
                        continue
                    if d.eng != o.eng:
                        d.needs_inc = True
                    elif d in o.raw and o.pos - d.pos <= 1:
                        d.needs_inc = True
        for e in ENGS:
            n = 0
            for o in self.ops[e]:
                if o.kind == "c" and o.needs_inc:
                    n += 1
                    o.seq = n
                    o.sem = ("eng", e)
                    o.semv = n

    def emit(self, eng, handle, sems):
        waited = {}
        for o in self.ops[eng]:
            need = {}
            for d in o.deps:
                if d.kind == "c":
                    if d.eng == eng and not (d in o.raw and o.pos - d.pos <= 1):
                        continue
                if need.get(d.sem, 0) < d.semv:
                    need[d.sem] = d.semv
            for k, v in need.items():
                if waited.get(k, 0) >= v:
                    continue
                waited[k] = v
                handle.wait_ge(sems[k], v)
            inst = o.fn(handle)
            if o.kind == "dma":
                inst.then_inc(sems[o.sem], 16)
            elif o.kind == "cc":
                inst.then_inc(sems[o.sem], 1)
            elif o.needs_inc:
                inst.then_inc(sems[o.sem], 1)


class Arena:
    def __init__(self, ap):
        self.ap = ap
        self.top = 0

    def alloc(self, parts, free, dtype):
        esz = 4 if dtype == F32 else 2
        n = int(np.prod(free)) * esz
        off = (self.top + 63) // 64 * 64
        self.top = off + n
        assert self.top <= ARENA, f"arena overflow {self.top}"
        v = self.ap[0:parts, off:off + n].bitcast(dtype)
        if len(free) == 2:
            v = v.rearrange("p (a b) -> p a b", a=free[0])
        elif len(free) == 3:
            v = v.rearrange("p (a b c) -> p a b c", a=free[0], b=free[1])
        return v


class Ring:
    def __init__(self, items):
        self.items = items
        self.i = 0

    def next(self):
        it = self.items[self.i % len(self.items)]
        self.i += 1
        return it


def build_program():
    nc = bass.Bass("TRN2", target_bir_lowering=False)

    def din(name, shape):
        return nc.dram_tensor(name, shape, F32, kind="ExternalInput").ap()

    xTt = din("xTt", [S // SUB, 128, NKC * SUB])
    xoTt = din("xoTt", [1024 // SUB, 128, NKC * SUB])
    xo2 = din("xo2", [16, 2, 128, 512])
    pTt = din("pTt", [128, 2 * 1024])
    w1a = din("w1a", [128, NKC * 1408])
    w1b = din("w1b", [128, NKC * 1536])
    vec88 = din("vec88", [88, 64])
    vec128 = din("vec128", [128, 50])
    wax = din("wax", [88, 2 * 4 * 2 * 176])
    cmat = din("cmat", [128, 384 + 896])
    wt_g = din("wt_g", [32, 128, 2048])
    wt_rnn = din("wt_rnn", [16, 2, 128, 11 * 128])
    wt_att = din("wt_att", [16, 128, 2048])
    wt_o = din("wt_o", [16, 128, 2048])
    wt_gu = din("wt_gu", [88, 128, 2048])
    wt_dn = din("wt_dn", [2, 16, 2, 128, 11 * 128])
    wt_pg = din("wt_pg", [16, 128, 2048])
    wt_pp = din("wt_pp", [16, 128, 2 * 128])
    outT = nc.dram_tensor("outT", [16, 128, 1024], F32, kind="ExternalOutput").ap()
    cin_yr = nc.dram_tensor("cin_yr", [NCH, 704, TC], BF16)
    cout_yr = nc.dram_tensor("cout_yr", [NCH, 2816, TC], BF16)
    cin_o = nc.dram_tensor("cin_o", [NCH, 512, TC], BF16)
    cout_o = nc.dram_tensor("cout_o", [NCH, 2048, TC], BF16)
    RG = [[0, 1, 2, 3], [4, 5, 6, 7]]
    s_cyr = [Slot() for _ in range(NCH)]
    s_co = [Slot() for _ in range(NCH)]

    fw = FW()
    es = ExitStack()
    with es:
        arena_t = es.enter_context(nc.sbuf_tensor("arena", [128, ARENA], U8))
        cst_t = es.enter_context(nc.sbuf_tensor("cst", [128, 2432], F32))
        psb2 = [es.enter_context(nc.psum_tensor(f"ps{i}", [128, 1024], F32)) for i in range(4)]
        psb = [psb2[i // 2][:, (i % 2) * 512:(i % 2 + 1) * 512] for i in range(8)]
        ps_slots = [Slot(f"ps{i}", excl=True) for i in range(8)]

        cst = cst_t
        c_vec128 = cst[:, 0:50]
        c_vec88 = cst[0:88, 64:128]
        c_der88 = cst[0:88, 128:160]
        c_num = cst[:, 160:168]
        cb = cst[:, 192:192 + 768].bitcast(BF16)
        c_ident = cb[:, 0:128]
        c_L = cb[:, 128:256]
        c_SU = cb[:, 256:384]
        c_mask = cb[:, 384:384 + 896]
        c_onesm = cb[:, 1280:1408]
        c_onesh = cb[:, 1408:1536]
        c_halo = cst[0:88, 968:968 + 24].rearrange("p (a b) -> p a b", a=8)
        c_hlast = cst[0:88, 1000:1008]
        c_wax = cst[0:88, 1016:1016 + 1408].bitcast(BF16).rearrange("p (w n i c) -> p w n i c", w=2, n=4, i=2)
        s_const = Slot("const")
        s_halo = [Slot() for _ in range(4)]
        s_hlast = [Slot() for _ in range(8)]
        EPS_AP = c_num[:, 0:1]
        ONE_AP = c_num[:, 1:2]

        def g128(col):
            return c_vec128[:, col:col + 1]

        fw.op("sp", lambda e: e.dma_start(out=c_vec128, in_=vec128[:, :]), writes=[s_const], kind="dma")
        fw.op("sp", lambda e: e.dma_start(out=c_vec88, in_=vec88[:, :]), writes=[s_const], kind="dma")
        fw.op("pool", lambda e: e.dma_start(out=cb[:, 0:1280], in_=cmat[:, :]), writes=[s_const], kind="dma")
        fw.op("pool", lambda e: e.dma_start(out=c_wax.rearrange("p w n i c -> p (w n i c)"), in_=wax[:, :]),
              writes=[s_const], kind="dma")

        def setup_consts(e):
            e.memset(c_num[:, 0:1], EPS)
            e.memset(c_num[:, 1:2], 1.0)
            e.memset(c_num[:, 2:3], 0.0)
            e.memset(c_onesm, 1.0 / 2048.0)
            e.memset(c_onesh, 1.0 / 128.0)
            e.memset(c_halo, 0.0)
            return e.memset(c_hlast, 0.0)
        fw.op("dve", setup_consts, writes=[s_const] + s_halo + s_hlast)

        v88 = c_vec88.rearrange("p (c f) -> p c f", c=8)
        hba = c_der88[:, 0:8]
        hbx = c_der88[:, 8:16]
        scl = c_der88[:, 16:24]
        tmp8 = c_der88[:, 24:32]
        fw.op("act", lambda e: e.activation(out=tmp8, in_=v88[:, :, 7], func=AF.Exp, scale=-1.0),
              reads=[s_const], writes=[s_const])
        fw.op("dve", lambda e: e.tensor_scalar(out=hba, in0=v88[:, :, 5], scalar1=0.5, scalar2=None, op0=ALU.mult),
              reads=[s_const], writes=[s_const])
        fw.op("act", lambda e: e.activation(out=scl, in_=tmp8, func=AF.Ln, bias=ONE_AP[0:88, :], scale=1.0),
              reads=[s_const], writes=[s_const])
        fw.op("dve", lambda e: e.tensor_scalar(out=hbx, in0=v88[:, :, 6], scalar1=0.5, scalar2=None, op0=ALU.mult),
              reads=[s_const], writes=[s_const])
        fw.op("dve", lambda e: e.tensor_scalar(out=scl, in0=scl, scalar1=-4.0, scalar2=None, op0=ALU.mult),
              reads=[s_const], writes=[s_const])

        arena_ap = arena_t[:, :]
        fw.stop_if("setup")

        def rmsnorm_sub(load_fn, src_ap, src_slots, gcol0, dst_ap, dst_slot, ntok, scratch, psring, sq_eng="act"):
            sq, s_sq, lnm, s_ln = scratch
            if load_fn is not None:
                load_fn()
            if sq_eng == "act":
                fw.op("act", lambda e: e.activation(out=sq, in_=src_ap, func=AF.Square),
                      reads=src_slots, writes=[s_sq], cost=2.0)
            else:
                fw.op("dve", lambda e: e.tensor_tensor(out=sq, in0=src_ap, in1=src_ap, op=ALU.mult),
                      reads=src_slots, writes=[s_sq], cost=2.4)
            ps, s_ps = psring.next()

            def f_mm(e):
                for kc in range(NKC):
                    last = e.matmul(ps[:, 0:ntok], lhsT=c_onesm, rhs=sq[:, kc, :], start=(kc == 0), stop=(kc == NKC - 1))
                return last
            fw.op("pe", f_mm, reads=[s_sq, s_const], writes=[s_ps], cost=16 * 0.12)
            fw.op("act", lambda e: e.activation(out=lnm[:, 0:ntok], in_=ps[:, 0:ntok], func=AF.Ln, bias=EPS_AP, scale=1.0),
                  reads=[s_ps], writes=[s_ln])
            fw.op("act", lambda e: e.activation(out=lnm[:, ntok:2 * ntok], in_=lnm[:, 0:ntok], func=AF.Exp, scale=-0.5),
                  reads=[s_ln], writes=[s_ln])

            def f_h(e):
                for kc in range(NKC):
                    last = e.scalar_tensor_tensor(out=dst_ap[:, kc, :], in0=src_ap[:, kc, :], scalar=g128(gcol0 + kc),
                                                  in1=lnm[:, ntok:2 * ntok], op0=ALU.mult, op1=ALU.mult)
                return last
            fw.op("dve", f_h, reads=list(src_slots) + [s_ln, s_const], writes=[dst_slot], cost=16 * 0.22)

        A = Arena(arena_ap)
        W1a = A.alloc(128, (NKC, 1408), BF16)
        s_W1a = Slot("W1a")
        xs_r = [(A.alloc(128, (NKC, SUB), F32), Slot()) for _ in range(2)]
        xsR = Ring(xs_r)
        sq_a = A.alloc(128, (NKC, SUB), BF16)
        lnm_a = A.alloc(128, (2 * SUB,), F32)
        scratchA = (sq_a, Slot(), lnm_a, Slot())
        hT_ring = Ring([(A.alloc(128, (NKC, TC), BF16), [Slot() for _ in range(TC // SUB)]) for _ in range(2)])
        yrb = [(A.alloc(88, (8, TC), BF16), Slot()) for _ in range(2)]
        NSET = 2
        sets = []
        for _ in range(NSET):
            d = {}
            d["xr"] = A.alloc(88, (2, TC + 3), F32)
            d["gr"] = A.alloc(88, (2, TC), F32)
            d["xc"] = A.alloc(88, (2, TC), F32)
            d["xcb"] = A.alloc(88, (2, TC), BF16)
            for nm in ("rp", "ip", "a", "t", "s", "h"):
                d[nm] = A.alloc(88, (2, TC), F32)
            d["slots"] = {nm: [Slot(), Slot()] for nm in ("xr", "gr", "xc", "xcb", "rp", "ip", "a", "t", "s", "h")}
            sets.append(d)
        psA = Ring(list(zip([p[:, :] for p in psb[0:6]], ps_slots[0:6])))
        psA_ms = Ring(list(zip([p[:, :] for p in psb[6:8]], ps_slots[6:8])))

        w1a_v = w1a.rearrange("p (k n) -> p k n", k=NKC)
        s_W1a_n = [Slot() for _ in range(4)]
        for n4 in range(4):
            for off in (0, 704):
                c0 = off + 176 * n4
                fw.op("pool", lambda e, c0=c0: e.dma_start(out=W1a[:, :, c0:c0 + 176], in_=w1a_v[:, :, c0:c0 + 176]),
                      writes=[s_W1a_n[n4]], kind="dma")

        def proj_fm(Wt, s_W, col0, M, ps, s_ps, h_ap, h_slots, ntok=TC):
            def f(e):
                for kc in range(NKC):
                    last = e.matmul(ps[0:M, 0:ntok], lhsT=Wt[:, kc, col0:col0 + M], rhs=h_ap[:, kc, :],
                                    start=(kc == 0), stop=(kc == NKC - 1))
                return last
            fw.op("pe", f, reads=[s_W] + list(h_slots), writes=[s_ps])

        def hT_chunk(G, xsrc, scratch, psring, sq_eng="act"):
            hT, s_hT = hT_ring.next()
            for sb in range(TC // SUB):
                sc = G * (TC // SUB) + sb
                xs, s_xs = xsR.next()
                fw.op("sp", lambda e, xs=xs, sc=sc: e.dma_start(out=xs.rearrange("p k t -> p (k t)"), in_=xsrc[sc]),
                      writes=[s_xs], kind="dma")
                rmsnorm_sub(None, xs, [s_xs], 0, hT[:, :, sb * SUB:(sb + 1) * SUB], s_hT[sb], SUB, scratch, psring, sq_eng)
            return hT, s_hT

        for G in range(NCH):
            hT, s_hT = hT_chunk(G, xTt, scratchA, psA_ms)
            yb, s_yb = yrb[G % 2]
            for n in range(4):
                st = sets[(G * 4 + n) % NSET]
                sl = st["slots"]
                for j in range(2):
                    c8 = 2 * n + j
                    ps, s_ps = psA.next()
                    proj_fm(W1a, s_W1a_n[n], c8 * 88, 88, ps, s_ps, hT, s_hT)
                    fw.op("act", lambda e, ps=ps, st=st, j=j: e.activation(out=st["xr"][:, j, 3:TC + 3], in_=ps[0:88, :], func=AF.Copy),
                          reads=[s_ps], writes=[sl["xr"][j]])
                    ps, s_ps = psA.next()
                    proj_fm(W1a, s_W1a_n[n], 704 + c8 * 88, 88, ps, s_ps, hT, s_hT)
                    fw.op("act", lambda e, ps=ps, st=st, j=j: e.activation(out=st["gr"][:, j, :], in_=ps[0:88, :], func=AF.Copy),
                          reads=[s_ps], writes=[sl["gr"][j]])
                fw.op("dve", lambda e, st=st, n=n: e.tensor_copy(out=st["xr"][:, :, 0:3], in_=c_halo[:, 2 * n:2 * n + 2, :]),
                      reads=[s_halo[n]], writes=sl["xr"])
                fw.op("dve", lambda e, st=st, n=n: e.tensor_copy(out=c_halo[:, 2 * n:2 * n + 2, :], in_=st["xr"][:, :, TC:TC + 3]),
                      reads=sl["xr"], writes=[s_halo[n]])
                for k in range(4):
                    for j in range(2):
                        c8 = 2 * n + j
                        if k == 0:
                            fw.op("dve", lambda e, st=st, j=j, c8=c8: e.tensor_scalar(
                                out=st["xc"][:, j, :], in0=st["xr"][:, j, 0:TC], scalar1=v88[:, c8, 0:1], scalar2=v88[:, c8, 4:5],
                                op0=ALU.mult, op1=ALU.add), reads=[sl["xr"][j], s_const], writes=[sl["xc"][j]])
                        else:
                            fw.op("dve", lambda e, st=st, j=j, c8=c8, k=k: e.scalar_tensor_tensor(
                                out=st["xc"][:, j, :], in0=st["xr"][:, j, k:k + TC], scalar=v88[:, c8, k:k + 1], in1=st["xc"][:, j, :],
                                op0=ALU.mult, op1=ALU.add), reads=[sl["xr"][j], sl["xc"][j], s_const], writes=[sl["xc"][j]])
                for j in range(2):
                    fw.op("act", lambda e, st=st, j=j: e.activation(out=st["xcb"][:, j, :], in_=st["xc"][:, j, :], func=AF.Copy),
                          reads=[sl["xc"][j]], writes=[sl["xcb"][j]])
                gps = []
                for j in range(2):
                    for w in range(2):
                        ps, s_ps = psA.next()

                        def f(e, ps=ps, st=st, j=j, w=w, n=n):
                            for i in range(2):
                                last = e.matmul(ps[0:88, :], lhsT=c_wax[:, w, n, i, 88 * j:88 * j + 88], rhs=st["xcb"][:, i, :],
                                                start=(i == 0), stop=(i == 1))
                            return last
                        fw.op("pe", f, reads=sl["xcb"] + [s_const], writes=[s_ps], cost=0.9)
                        gps.append((ps, s_ps))
                for j in range(2):
                    c8 = 2 * n + j
                    ps, s_ps = gps[2 * j]
                    fw.op("act", lambda e, ps=ps, st=st, j=j, c8=c8: e.activation(
                        out=st["rp"][:, j, :], in_=ps[0:88, :], func=AF.Tanh, bias=hba[:, c8:c8 + 1], scale=0.5),
                        reads=[s_ps, s_const], writes=[sl["rp"][j]])
                    ps, s_ps = gps[2 * j + 1]
                    fw.op("act", lambda e, ps=ps, st=st, j=j, c8=c8: e.activation(
                        out=st["ip"][:, j, :], in_=ps[0:88, :], func=AF.Tanh, bias=hbx[:, c8:c8 + 1], scale=0.5),
                        reads=[s_ps, s_const], writes=[sl["ip"][j]])
                for j in range(2):
                    c8 = 2 * n + j
                    fw.op("act", lambda e, st=st, j=j, c8=c8: e.activation(
                        out=st["a"][:, j, :], in_=st["rp"][:, j, :], func=AF.Exp, bias=scl[:, c8:c8 + 1], scale=scl[:, c8:c8 + 1]),
                        reads=[sl["rp"][j], s_const], writes=[sl["a"][j]])
                for j in range(2):
                    fw.op("dve", lambda e, st=st, j=j: e.scalar_tensor_tensor(
                        out=st["t"][:, j, :], in0=st["ip"][:, j, :], scalar=1.0, in1=st["xc"][:, j, :], op0=ALU.add, op1=ALU.mult),
                        reads=[sl["ip"][j], sl["xc"][j]], writes=[sl["t"][j]])
                for j in range(2):
                    fw.op("act", lambda e, st=st, j=j: e.activation(out=st["s"][:, j, :], in_=st["a"][:, j, :], func=AF.Square),
                          reads=[sl["a"][j]], writes=[sl["s"][j]])
                for j in range(2):
                    fw.op("act", lambda e, st=st, j=j: e.activation(out=st["s"][:, j, :], in_=st["s"][:, j, :], func=AF.Sqrt,
                                                                    bias=ONE_AP[0:88, :], scale=-1.0),
                          reads=[sl["s"][j], s_const], writes=[sl["s"][j]])
                for j in range(2):
                    fw.op("dve", lambda e, st=st, j=j: e.scalar_tensor_tensor(
                        out=st["t"][:, j, :], in0=st["s"][:, j, :], scalar=0.5, in1=st["t"][:, j, :], op0=ALU.mult, op1=ALU.mult),
                        reads=[sl["s"][j], sl["t"][j]], writes=[sl["t"][j]])
                for j in range(2):
                    c8 = 2 * n + j
                    fw.op("dve", lambda e, st=st, j=j, c8=c8: e.tensor_tensor_scan(
                        out=st["h"][:, j, :], data0=st["a"][:, j, :], data1=st["t"][:, j, :], initial=c_hlast[:, c8:c8 + 1],
                        op0=ALU.mult, op1=ALU.add), reads=[sl["a"][j], sl["t"][j], s_hlast[c8]], writes=[sl["h"][j]])
                for j in range(2):
                    c8 = 2 * n + j
                    fw.op("act", lambda e, st=st, j=j, c8=c8: e.activation(out=c_hlast[:, c8:c8 + 1], in_=st["h"][:, j, TC - 1:TC], func=AF.Copy),
                          reads=[sl["h"][j]], writes=[s_hlast[c8]])
                for j in range(2):
                    fw.op("dve", lambda e, st=st, j=j: e.tensor_tensor(out=st["ip"][:, j, :], in0=st["gr"][:, j, :], in1=st["gr"][:, j, :], op=ALU.mult),
                          reads=[sl["gr"][j]], writes=[sl["ip"][j]])
                for j in range(2):
                    fw.op("dve", lambda e, st=st, j=j: e.tensor_scalar(out=st["ip"][:, j, :], in0=st["ip"][:, j, :], scalar1=0.044715, scalar2=1.0,
                                                                      op0=ALU.mult, op1=ALU.add), reads=[sl["ip"][j]], writes=[sl["ip"][j]])
                for j in range(2):
                    fw.op("dve", lambda e, st=st, j=j: e.tensor_tensor(out=st["ip"][:, j, :], in0=st["ip"][:, j, :], in1=st["gr"][:, j, :], op=ALU.mult),
                          reads=[sl["ip"][j], sl["gr"][j]], writes=[sl["ip"][j]])
                for j in range(2):
                    fw.op("act", lambda e, st=st, j=j: e.activation(out=st["rp"][:, j, :], in_=st["ip"][:, j, :], func=AF.Tanh, scale=GELU_C),
                          reads=[sl["ip"][j]], writes=[sl["rp"][j]])
                for j in range(2):
                    fw.op("dve", lambda e, st=st, j=j: e.scalar_tensor_tensor(
                        out=st["ip"][:, j, :], in0=st["rp"][:, j, :], scalar=1.0, in1=st["gr"][:, j, :], op0=ALU.add, op1=ALU.mult),
                        reads=[sl["rp"][j], sl["gr"][j]], writes=[sl["ip"][j]])
                for j in range(2):
                    c8 = 2 * n + j
                    fw.op("dve", lambda e, st=st, j=j, c8=c8, yb=yb: e.scalar_tensor_tensor(
                        out=yb[:, c8, :], in0=st["ip"][:, j, :], scalar=0.5, in1=st["h"][:, j, :], op0=ALU.mult, op1=ALU.mult),
                        reads=[sl["ip"][j], sl["h"][j]], writes=[s_yb])
            s_ci = Slot()
            fw.op("sp", lambda e, yb=yb, G=G: e.dma_start(
                out=cin_yr.ap()[G].rearrange("(c p) t -> p c t", p=88), in_=yb),
                reads=[s_yb], writes=[s_ci], kind="dma")
            fw.op("pool", lambda e, G=G: e.collective_compute("AllGather", ALU.bypass, replica_groups=RG,
                                                              ins=[cin_yr.ap()[G].opt()], outs=[cout_yr.ap()[G].opt()]),
                  reads=[s_ci], writes=[s_cyr[G]], kind="cc")
            if G == 0:
                fw.stop_if("passA1")
        fw.stop_if("passA")


        fw.stop_if("ag1")
        B = Arena(arena_ap)
        W1b = B.alloc(128, (NKC, 1536), BF16)
        s_W1b = Slot("W1b")
        KT = B.alloc(128, (4, S), BF16)
        s_KT = [[Slot() for _ in range(NCH)] for _ in range(4)]
        Vt = B.alloc(128, (S // 128, 512), BF16)
        s_V = [Slot() for _ in range(S // 128)]
        xs_rb = [(B.alloc(128, (NKC, SUB), F32), Slot()) for _ in range(1)]
        xsR.items = xs_rb
        sq_b = B.alloc(128, (NKC, SUB), BF16)
        lnm_b = B.alloc(128, (2 * SUB,), F32)
        scratchB = (sq_b, Slot(), lnm_b, Slot())
        hT_ring.items = [(B.alloc(128, (NKC, TC), BF16), [Slot() for _ in range(TC // SUB)])]
        QT_ring = Ring([(B.alloc(128, (4, TC), BF16), [Slot() for _ in range(4)]) for _ in range(2)])
        nrm = [dict(raw=B.alloc(128, (TC,), F32), sq=B.alloc(128, (TC,), BF16), ln=B.alloc(128, (TC,), F32),
                    s_raw=Slot(), s_sq=Slot(), s_ln=Slot()) for _ in range(2)]
        nrmR = Ring(nrm)
        e_r = [(B.alloc(128, (2, TC), F32), Slot()) for _ in range(3)]
        sp_r = [(B.alloc(128, (2, TC), BF16), Slot()) for _ in range(3)]
        x_r = [(B.alloc(128, (2, TC), F32), Slot()) for _ in range(2)]
        w_r = [(B.alloc(128, (2, TC), BF16), Slot()) for _ in range(3)]
        psz2 = psb2[0][:, :].rearrange("p (a b) -> p a b", a=2)
        psc2 = psb2[1][:, :].rearrange("p (a b) -> p a b", a=2)
        obf = [(B.alloc(128, (TC,), BF16), Slot()) for _ in range(2)]
        obfR = Ring(obf)
        psz = [(psb[0][:, :], ps_slots[0]), (psb[1][:, :], ps_slots[1])]
        psc = [(psb[2][:, :], ps_slots[2]), (psb[3][:, :], ps_slots[3])]
        pso = [(psb[4][:, :], ps_slots[4]), (psb[5][:, :], ps_slots[5])]
        psM = Ring([(psb[6][:, :], ps_slots[6]), (psb[7][:, :], ps_slots[7])])

        fw.barrier(ENGS)
        w1b_v = w1b.rearrange("p (k n) -> p k n", k=NKC)
        s_W1b_c = [Slot() for _ in range(3)]
        for ci in range(3):
            fw.op("pool", lambda e, ci=ci: e.dma_start(out=W1b[:, :, 512 * ci:512 * ci + 512], in_=w1b_v[:, :, 512 * ci:512 * ci + 512]),
                  writes=[s_W1b_c[ci]], kind="dma")

        def qk_norm(col0, gain_col, dst_ap, dst_slot, hT, s_hT):
            nb = nrmR.next()
            ps, s_ps = psM.next()
            proj_fm(W1b, s_W1b_c[col0 // 512], col0, 128, ps, s_ps, hT, s_hT)
            fw.op("dve", lambda e: e.tensor_copy(out=nb["raw"], in_=ps), reads=[s_ps], writes=[nb["s_raw"]])
            fw.op("dve", lambda e: e.tensor_tensor(out=nb["sq"], in0=nb["raw"], in1=nb["raw"], op=ALU.mult),
                  reads=[nb["s_raw"]], writes=[nb["s_sq"]])
            ps2, s_ps2 = psM.next()
            fw.op("pe", lambda e: e.matmul(ps2, lhsT=c_onesh, rhs=nb["sq"], start=True, stop=True),
                  reads=[nb["s_sq"], s_const], writes=[s_ps2], cost=0.3)
            fw.op("act", lambda e: e.activation(out=nb["ln"], in_=ps2, func=AF.Ln, bias=EPS_AP, scale=1.0),
                  reads=[s_ps2], writes=[nb["s_ln"]])
            fw.op("act", lambda e: e.activation(out=nb["ln"], in_=nb["ln"], func=AF.Exp, scale=-0.5),
                  reads=[nb["s_ln"]], writes=[nb["s_ln"]])
            fw.op("dve", lambda e: e.scalar_tensor_tensor(out=dst_ap, in0=nb["raw"], scalar=g128(gain_col), in1=nb["ln"],
                                                          op0=ALU.mult, op1=ALU.mult),
                  reads=[nb["s_raw"], nb["s_ln"], s_const], writes=[dst_slot])

        import os as _os
        for G in range(int(_os.environ.get("KNB", NCH))):
            fw.stop_if("b_w")
            hT, s_hT = hT_chunk(G, xTt, scratchB, psM, "dve")
            QT, s_QT = QT_ring.next()
            fw.stop_if("b_h")
            for hh in range(4):
                qk_norm(hh * 128, 48, QT[:, hh, :], s_QT[hh], hT, s_hT)
                qk_norm(512 + hh * 128, 49, KT[:, hh, G * TC:(G + 1) * TC], s_KT[hh][G], hT, s_hT)
            fw.stop_if("b_qk")
            for tb in range(4):
                ps, s_ps = psM.next()

                def f(e, ps=ps, tb=tb, hT=hT):
                    for kc in range(NKC):
                        last = e.matmul(ps, lhsT=hT[:, kc, tb * 128:(tb + 1) * 128], rhs=W1b[:, kc, 1024:1536],
                                        start=(kc == 0), stop=(kc == NKC - 1))
                    return last
                fw.op("pe", f, reads=[s_W1b_c[2]] + s_hT, writes=[s_ps])
                fw.op("dve", lambda e, ps=ps, tb=tb, G=G: e.tensor_copy(out=Vt[:, 4 * G + tb, :], in_=ps),
                      reads=[s_ps], writes=[s_V[4 * G + tb]])
            fw.stop_if("b_v")
            NT = 4 * G + 4
            s_cio = [Slot() for _ in range(4)]
            for hp in range(2):
                heads = [2 * hp, 2 * hp + 1]
                etile = {}
                sptile = {}
                wtile = {}

                def stageZ(s):
                    kb = NT - 1 - s
                    diag = kb - 4 * G
                    for li, hh in enumerate(heads):
                        ps, s_ps = psz[li]

                        def f(e, ps=ps, hh=hh, kb=kb, diag=diag, QT=QT):
                            last = e.matmul(ps, lhsT=KT[:, hh, kb * 128:(kb + 1) * 128], rhs=QT[:, hh, :], start=True, stop=(diag < 0))
                            if diag >= 0:
                                o0 = 384 - 128 * diag
                                last = e.matmul(ps, lhsT=c_ident, rhs=c_mask[:, o0:o0 + TC], start=False, stop=True)
                            return last
                        fw.op("pe", f, reads=[s_KT[hh][kb // 4], s_QT[hh], s_const], writes=[s_ps], cost=0.6 if diag >= 0 else 0.3)
                    eb, s_eb = e_r[s % 3]
                    etile[s] = (eb, s_eb)
                    fw.op("act", lambda e, eb=eb: e.activation(out=eb, in_=psz2, func=AF.Exp, scale=float(QSCALE)),
                          reads=[psz[0][1], psz[1][1]], writes=[s_eb], cost=0.97)
                    sb_, s_sb = sp_r[s % 3]
                    sptile[s] = (sb_, s_sb)
                    fw.op("act", lambda e, eb=eb, sb_=sb_: e.activation(out=sb_, in_=eb, func=AF.Ln, bias=ONE_AP, scale=1.0),
                          reads=[s_eb, s_const], writes=[s_sb], cost=0.97)

                def stageC(s):
                    sb_, s_sb = sptile[s]
                    for li, hh in enumerate(heads):
                        ps, s_ps = psc[li]
                        rd = [s_sb, s_const]
                        if s > 0:
                            sprev, s_sprev = sptile[s - 1]
                            rd += [s_sprev, s_ps]
                        else:
                            sprev = None

                        def f(e, ps=ps, sb_=sb_, sprev=sprev, s=s, li=li):
                            if s > 0:
                                e.matmul(ps, lhsT=c_SU, rhs=sprev[:, li, :], start=False, stop=False, skip_group_check=True)
                            return e.matmul(ps, lhsT=c_L, rhs=sb_[:, li, :], start=(s == 0), stop=True, skip_group_check=True)
                        fw.op("pe", f, reads=rd, writes=[s_ps], cost=0.6 if s > 0 else 0.3)
                    xb, s_xb = x_r[s % 2]
                    fw.op("act", lambda e, xb=xb: e.activation(out=xb, in_=psc2, func=AF.Exp, scale=-1.0),
                          reads=[psc[0][1], psc[1][1]], writes=[s_xb], cost=0.97)
                    eb, s_eb = etile[s]
                    wb, s_wb = w_r[s % 3]
                    wtile[s] = (wb, s_wb)
                    fw.op("dve", lambda e, xb=xb, eb=eb, wb=wb: e.tensor_tensor(out=wb, in0=eb, in1=xb, op=ALU.mult),
                          reads=[s_xb, s_eb], writes=[s_wb], cost=1.2)

                def stageO(s):
                    kb = NT - 1 - s
                    for li, hh in enumerate(heads):
                        ps, s_ps = pso[li]
                        wb, s_wb = wtile[s]
                        fw.op("pe", lambda e, ps=ps, wb=wb, hh=hh, kb=kb, s=s, li=li: e.matmul(
                            ps, lhsT=Vt[:, kb, hh * 128:(hh + 1) * 128], rhs=wb[:, li, :], start=(s == 0), stop=(s == NT - 1)),
                            reads=[s_wb, s_V[kb], s_ps] if s > 0 else [s_wb, s_V[kb]], writes=[s_ps], cost=0.3)

                for step in range(NT + 2):
                    if step < NT:
                        stageZ(step)
                    if 0 <= step - 1 < NT:
                        stageC(step - 1)
                    if 0 <= step - 2 < NT:
                        stageO(step - 2)
                for li, hh in enumerate(heads):
                    ps, s_ps = pso[li]
                    ob, s_ob = obfR.next()
                    fw.op("dve", lambda e, ps=ps, ob=ob: e.tensor_copy(out=ob, in_=ps), reads=[s_ps], writes=[s_ob])
                    fw.op("sp", lambda e, ob=ob, hh=hh, G=G: e.dma_start(
                        out=cin_o.ap()[G, hh * 128:(hh + 1) * 128, :], in_=ob), reads=[s_ob], writes=[s_cio[hh]], kind="dma")
            fw.op("pool", lambda e, G=G: e.collective_compute("AllGather", ALU.bypass, replica_groups=RG,
                                                              ins=[cin_o.ap()[G].opt()], outs=[cout_o.ap()[G].opt()]),
                  reads=s_cio, writes=[s_co[G]], kind="cc")

        fw.stop_if("passB")

        P = Arena(arena_ap)
        NU = 10
        wunits = [(P.alloc(128, (NKC, 128), BF16), Slot()) for _ in range(NU)]
        wu_i = [0]
        tmpf = [(P.alloc(128, (TC,), F32), Slot()) for _ in range(4)]
        tmpR = Ring(tmpf)
        R1 = P.alloc(128, (NKC, 1024), F32)
        R1b = R1.rearrange("p k t -> p (k t)").bitcast(BF16)
        h_a = R1b[:, 0:NKC * 1024].rearrange("p (k t) -> p k t", k=NKC)
        oT_a = R1b[:, NKC * 1024:2 * NKC * 1024].rearrange("p (k t) -> p k t", k=NKC)
        R2 = P.alloc(128, (22, 1024), BF16)
        R3 = P.alloc(128, (NKC, 1024), BF16)
        pTb = P.alloc(128, (2, 1024), BF16)
        s_xo = [[Slot() for _ in range(2)] for _ in range(NKC)]
        s_R1h = [Slot() for _ in range(8)]
        s_oT = Slot()
        s_R2 = Slot()
        s_R3 = [[Slot() for _ in range(2)] for _ in range(NKC)]
        s_h2 = [Slot() for _ in range(8)]
        s_pT = Slot()
        psP = Ring(list(zip([p[:, :] for p in psb], ps_slots)))
        sq_p = P.alloc(128, (NKC, SUB), BF16)
        lnm_p = P.alloc(128, (2 * SUB,), F32)
        scratchP = (sq_p, Slot(), lnm_p, Slot())
        xs_rp = [(R3.rearrange("p k t -> p (k t)")[:, (8 + 4 * i) * 1024:(12 + 4 * i) * 1024].bitcast(F32).rearrange("p (k t) -> p k t", k=NKC), Slot())
                 for i in range(2)]

        fw.barrier(ENGS)

        def wload(src_ap, nk=NKC):
            u, s_u = wunits[wu_i[0] % NU]
            wu_i[0] += 1
            fw.op("pool", lambda e: e.dma_start(out=u[:, 0:nk, :].rearrange("p k n -> p (k n)"), in_=src_ap),
                  writes=[s_u], kind="dma")
            return u, s_u

        def acc_group(ps, s_ps, parts, tb, extra_reads):
            def f(e):
                tot = sum(p[2] for p in parts)
                i = 0
                for (u, s_u, nk, act, k0, rows) in parts:
                    for kc in range(nk):
                        last = e.matmul(ps, lhsT=u[0:rows, kc, :], rhs=act[0:rows, k0 + kc, tb * TC:(tb + 1) * TC],
                                        start=(i == 0), stop=(i == tot - 1))
                        i += 1
                return last
            fw.op("pe", f, reads=[p[1] for p in parts] + list(extra_reads), writes=[s_ps], cost=0.29 * sum(p[2] for p in parts))

        for sc in range(8):
            xs, s_xs = xs_rp[sc % 2]
            fw.op("sp", lambda e, xs=xs, sc=sc: e.dma_start(out=xs.rearrange("p k t -> p (k t)"), in_=xoTt[sc]),
                  writes=[s_xs], kind="dma")
            rmsnorm_sub(None, xs, [s_xs], 0, h_a[:, :, sc * SUB:(sc + 1) * SUB], s_R1h[sc], SUB, scratchP, psP)

        for tb in range(2):
            def f_yr(e, tb=tb):
                rank = e.partition_id() % 4
                return e.dma_start(out=R2[:, :, tb * TC:(tb + 1) * TC],
                                   in_=cout_yr.ap()[bass.ds(2 * rank + tb, 1)].rearrange("o (k p) t -> p (o k) t", p=128))
            fw.op("pool", f_yr, reads=s_cyr, writes=[s_R2], kind="dma")

            def f_o(e, tb=tb):
                rank = e.partition_id() % 4
                return e.dma_start(out=oT_a[:, :, tb * TC:(tb + 1) * TC],
                                   in_=cout_o.ap()[bass.ds(2 * rank + tb, 1)].rearrange("o (k p) t -> p (o k) t", p=128))
            fw.op("pool", f_o, reads=s_co, writes=[s_oT], kind="dma")
        fw.op("pool", lambda e: e.dma_start(out=pTb.rearrange("p k t -> p (k t)"), in_=pTt[:, :]), writes=[s_pT], kind="dma")

        fw.stop_if("s0")
        for cc in range(NKC):
            ugr = wload(wt_g[cc])
            uga = wload(wt_g[16 + cc])
            ur0 = wload(wt_rnn[cc, 0], 11)
            ur1 = wload(wt_rnn[cc, 1], 11)
            uat = wload(wt_att[cc])
            for tb in range(2):
                hs = s_R1h[4 * tb:4 * tb + 4]
                p_gr = psP.next()
                acc_group(p_gr[0], p_gr[1], [(ugr[0], ugr[1], NKC, h_a, 0, 128)], tb, hs)
                p_ga = psP.next()
                acc_group(p_ga[0], p_ga[1], [(uga[0], uga[1], NKC, h_a, 0, 128)], tb, hs)
                p_yr = psP.next()
                acc_group(p_yr[0], p_yr[1], [(ur0[0], ur0[1], 11, R2, 0, 128), (ur1[0], ur1[1], 11, R2, 11, 128)], tb, [s_R2])
                p_ya = psP.next()
                acc_group(p_ya[0], p_ya[1], [(uat[0], uat[1], NKC, oT_a, 0, 128)], tb, [s_oT])
                t_sr = tmpR.next()
                t_sa = tmpR.next()
                fw.op("act", lambda e, p=p_gr, t=t_sr: e.activation(out=t[0], in_=p[0], func=AF.Sigmoid), reads=[p_gr[1]], writes=[t_sr[1]])
                fw.op("act", lambda e, p=p_ga, t=t_sa: e.activation(out=t[0], in_=p[0], func=AF.Sigmoid), reads=[p_ga[1]], writes=[t_sa[1]])
                fw.op("dve", lambda e, p=p_yr, t=t_sr: e.tensor_tensor(out=t[0], in0=t[0], in1=p[0], op=ALU.mult),
                      reads=[p_yr[1], t_sr[1]], writes=[t_sr[1]])
                fw.op("dve", lambda e, p=p_ya, t=t_sa: e.tensor_tensor(out=t[0], in0=t[0], in1=p[0], op=ALU.mult),
                      reads=[p_ya[1], t_sa[1]], writes=[t_sa[1]])
                fw.op("dve", lambda e, a=t_sr, b=t_sa, cc=cc, tb=tb: e.tensor_tensor(
                    out=R3[:, cc, tb * TC:(tb + 1) * TC], in0=a[0], in1=b[0], op=ALU.add),
                    reads=[t_sr[1], t_sa[1]], writes=[s_R3[cc][tb]])

        fw.barrier(["pe", "act", "dve", "sp"])
        fw.stop_if("sa")
        mix_slots = [s_R3[c][t] for c in range(NKC) for t in range(2)]
        for cc in range(NKC):
            uo = wload(wt_o[cc])
            for tb in range(2):
                fw.op("sp", lambda e, cc=cc, tb=tb: e.dma_start(out=R1[:, cc, tb * TC:(tb + 1) * TC], in_=xo2[cc, tb]),
                      writes=[s_xo[cc][tb]], kind="dma")
                p = psP.next()
                acc_group(p[0], p[1], [(uo[0], uo[1], NKC, R3, 0, 128)], tb, [s_R3[c][tb] for c in range(NKC)])
                fw.op("dve", lambda e, p=p, cc=cc, tb=tb: e.tensor_tensor(
                    out=R1[:, cc, tb * TC:(tb + 1) * TC], in0=R1[:, cc, tb * TC:(tb + 1) * TC], in1=p[0], op=ALU.add),
                    reads=[p[1], s_xo[cc][tb]], writes=[s_xo[cc][tb]])

        fw.barrier(["pe", "act", "dve", "sp"])

        def rms_resident(gcol0, dst, dst_slots):
            for sc in range(8):
                tb = sc // 4
                rmsnorm_sub(None, R1[:, :, sc * SUB:(sc + 1) * SUB], [s_xo[c][tb] for c in range(NKC)], gcol0,
                            dst[:, :, sc * SUB:(sc + 1) * SUB], dst_slots[sc], SUB, scratchP, psP)

        fw.stop_if("sb")
        rms_resident(16, R3, s_h2)
        for half in range(2):
            for m in range(22):
                mm = half * 22 + m
                ug = wload(wt_gu[mm])
                uu = wload(wt_gu[44 + mm])
                for tb in range(2):
                    hs = s_h2[4 * tb:4 * tb + 4]
                    p_g = psP.next()
                    acc_group(p_g[0], p_g[1], [(ug[0], ug[1], NKC, R3, 0, 128)], tb, hs)
                    p_u = psP.next()
                    acc_group(p_u[0], p_u[1], [(uu[0], uu[1], NKC, R3, 0, 128)], tb, hs)
                    t_s = tmpR.next()
                    fw.op("act", lambda e, p=p_g, t=t_s: e.activation(out=t[0], in_=p[0], func=AF.Silu), reads=[p_g[1]], writes=[t_s[1]])
                    fw.op("dve", lambda e, p=p_u, t=t_s, m=m, tb=tb: e.tensor_tensor(
                        out=R2[:, m, tb * TC:(tb + 1) * TC], in0=t[0], in1=p[0], op=ALU.mult),
                        reads=[p_u[1], t_s[1]], writes=[s_R2])
            for cc in range(NKC):
                ud0 = wload(wt_dn[half, cc, 0], 11)
                ud1 = wload(wt_dn[half, cc, 1], 11)
                for tb in range(2):
                    p = psP.next()
                    acc_group(p[0], p[1], [(ud0[0], ud0[1], 11, R2, 0, 128), (ud1[0], ud1[1], 11, R2, 11, 128)], tb, [s_R2])
                    fw.op("dve", lambda e, p=p, cc=cc, tb=tb: e.tensor_tensor(
                        out=R1[:, cc, tb * TC:(tb + 1) * TC], in0=R1[:, cc, tb * TC:(tb + 1) * TC], in1=p[0], op=ALU.add),
                        reads=[p[1], s_xo[cc][tb]], writes=[s_xo[cc][tb]])
            fw.barrier(["pe", "act", "dve", "sp"])

        fw.stop_if("sc")
        h3 = R2[:, 0:NKC, :]
        s_h3 = [Slot() for _ in range(8)]
        rms_resident(32, h3, s_h3)
        out_ops = []
        for cc in range(NKC):
            upg = wload(wt_pg[cc])
            upp = wload(wt_pp[cc], 2)
            for tb in range(2):
                hs = s_h3[4 * tb:4 * tb + 4]
                p_g = psP.next()
                acc_group(p_g[0], p_g[1], [(upg[0], upg[1], NKC, h3, 0, 128)], tb, hs)
                p_p = psP.next()

                acc_group(p_p[0], p_p[1], [(upp[0], upp[1], 2, pTb, 0, 128)], tb, [s_pT])
                t_s = tmpR.next()
                fw.op("act", lambda e, p=p_g, t=t_s: e.activation(out=t[0], in_=p[0], func=AF.Sigmoid), reads=[p_g[1]], writes=[t_s[1]])
                fw.op("dve", lambda e, p=p_p, t=t_s: e.tensor_tensor(out=t[0], in0=t[0], in1=p[0], op=ALU.mult),
                      reads=[p_p[1], t_s[1]], writes=[t_s[1]])
                fw.op("dve", lambda e, t=t_s, cc=cc, tb=tb: e.tensor_tensor(
                    out=R1[:, cc, tb * TC:(tb + 1) * TC], in0=R1[:, cc, tb * TC:(tb + 1) * TC], in1=t[0], op=ALU.add),
                    reads=[t_s[1], s_xo[cc][tb]], writes=[s_xo[cc][tb]])
            out_ops.append(fw.op("sp", lambda e, cc=cc: e.dma_start(out=outT[cc], in_=R1[:, cc, :]),
                                 reads=[s_xo[cc][0], s_xo[cc][1]], kind="dma"))
        s_fin = Slot("fin")
        fin = fw.op("sp", lambda e: e.nop(), kind="c")
        fin.deps.update(o_ for o_ in out_ops if o_.kind != "skip")

        import os as _os2
        if _os2.environ.get("KNOSCHED", "") != "1":
            fw.schedule()
        fw.finalize()
        sems = {}
        for e in ENGS:
            sems[("eng", e)] = es.enter_context(nc.semaphore(f"s_{e}"))
            for i in range(FW.NDS):
                sems[("dma", e, i)] = es.enter_context(nc.semaphore(f"d_{e}{i}"))
        for i in range(fw.cc_count):
            sems[("cc", "pool", i)] = es.enter_context(nc.semaphore(f"cc{i}"))
        block = es.enter_context(nc.Block())

        @block.tensor
        def _(t):
            fw.emit("pe", t, sems)

        @block.scalar
        def _(a):
            fw.emit("act", a, sems)

        @block.vector
        def _(v):
            fw.emit("dve", v, sems)

        @block.gpsimd
        def _(g):
            fw.emit("pool", g, sems)

        @block.sync
        def _(sy):
            fw.emit("sp", sy, sems)
    return nc


def _tile_w(w, nk):
    K, N = w.shape
    assert K == nk * 128
    t = w.reshape(nk, 128, N // 128, 128).transpose(2, 1, 0, 3)
    return np.ascontiguousarray(t).reshape(N // 128, 128, nk * 128)


_NC_CACHE = {}


def kernel(x, p, g_mix, w_in, conv_w, conv_b, w_rg_a, b_rg_a, w_rg_x, b_rg_x, lru_lambda, q_gain, k_gain,
           w_rnn_out, w_attn_out, w_o, g_ffn, w_ffn_gu, w_ffn_down, g_ple, w_ple_gate, w_ple_proj):
    f = np.float32
    x = np.asarray(x, f)
    p = np.asarray(p, f)
    w_in0 = np.asarray(w_in, f)[0]
    if "nc" not in _NC_CACHE:
        _NC_CACHE["nc"] = build_program()
    nc = _NC_CACHE["nc"]

    vec128 = np.zeros((128, 50), f)
    vec128[:, 0:16] = np.asarray(g_mix, f)[0].reshape(16, 128).T
    vec128[:, 16:32] = np.asarray(g_ffn, f)[0].reshape(16, 128).T
    vec128[:, 32:48] = np.asarray(g_ple, f)[0].reshape(16, 128).T
    vec128[:, 48] = np.asarray(q_gain, f)[0]
    vec128[:, 49] = np.asarray(k_gain, f)[0]
    cmat = np.zeros((128, 384 + 896), f)
    cmat[:, 0:128] = np.eye(128, dtype=f)
    jj = np.arange(128)[:, None]
    ss = np.arange(128)[None, :]
    cmat[:, 128:256] = (jj >= ss).astype(f)
    cmat[:, 256:384] = (jj < ss).astype(f)
    xx = np.arange(896)[None, :]
    cmat[:, 384:] = np.where(xx <= jj + 384, NEG, 0.0).astype(f)
    wt_g = _tile_w(w_in0[:, 11776:15872], 16)
    wr = np.asarray(w_rnn_out, f)[0]
    wt_rnn = _tile_w(wr, 22).reshape(16, 128, 2, 11 * 128).transpose(0, 2, 1, 3).copy()
    wt_att = _tile_w(np.asarray(w_attn_out, f)[0], 16)
    wt_o = _tile_w(np.asarray(w_o, f)[0], 16)
    wt_gu = _tile_w(np.asarray(w_ffn_gu, f)[0], 16)
    wd = np.asarray(w_ffn_down, f)[0]
    wt_dn = np.stack([_tile_w(wd[h * 2816:(h + 1) * 2816], 22).reshape(16, 128, 2, 11 * 128).transpose(0, 2, 1, 3)
                      for h in range(2)]).copy()
    wt_pg = _tile_w(np.asarray(w_ple_gate, f)[0], 16)
    wpp = np.asarray(w_ple_proj, f)[0]
    wt_pp = _tile_w(wpp, 2)
    cw = np.asarray(conv_w, f)[0]
    cbv = np.asarray(conv_b, f)[0]
    ba = np.asarray(b_rg_a, f)[0]
    bx = np.asarray(b_rg_x, f)[0]
    lam = np.asarray(lru_lambda, f)[0]
    wa_all = np.asarray(w_rg_a, f)[0]
    wx_all = np.asarray(w_rg_x, f)[0]

    in_maps = []
    for c in range(8):
        b, g = c // 4, c % 4
        xb = x[b]
        xTt = np.ascontiguousarray(xb.reshape(32, 128, 16, 128).transpose(0, 3, 2, 1)).reshape(32, 128, 16 * 128)
        xo = xb[1024 * g:1024 * (g + 1)]
        xoTt = xTt[8 * g:8 * g + 8]
        xo2 = np.ascontiguousarray(xo.reshape(2, 512, 16, 128).transpose(2, 0, 3, 1))
        pb = p[0, b, 1024 * g:1024 * (g + 1)]
        pTt = np.ascontiguousarray(pb.reshape(1024, 2, 128).transpose(2, 1, 0)).reshape(128, 2 * 1024)
        w1a = np.concatenate([w_in0[:, 704 * g:704 * (g + 1)], w_in0[:, 2816 + 704 * g:2816 + 704 * (g + 1)]], axis=1)
        w1a = np.ascontiguousarray(w1a.reshape(16, 128, 1408).transpose(1, 0, 2)).reshape(128, 16 * 1408)
        w1b = np.concatenate([w_in0[:, 5632 + 512 * g:5632 + 512 * (g + 1)], w_in0[:, 7680 + 512 * g:7680 + 512 * (g + 1)],
                              w_in0[:, 9728 + 512 * g:9728 + 512 * (g + 1)]], axis=1)
        w1b = np.ascontiguousarray(w1b.reshape(16, 128, 1536).transpose(1, 0, 2)).reshape(128, 16 * 1536)
        ch = slice(704 * g, 704 * (g + 1))
        v88 = np.zeros((88, 8, 8), f)
        for k in range(4):
            v88[:, :, k] = cw[k, ch].reshape(8, 88).T
        v88[:, :, 4] = cbv[ch].reshape(8, 88).T
        v88[:, :, 5] = ba[ch].reshape(8, 88).T
        v88[:, :, 6] = bx[ch].reshape(8, 88).T
        v88[:, :, 7] = lam[ch].reshape(8, 88).T
        wax = np.stack([wa_all[4 * g:4 * g + 4], wx_all[4 * g:4 * g + 4]])
        wax = np.ascontiguousarray(wax.reshape(2, 4, 2, 88, 176).transpose(3, 0, 1, 2, 4)).reshape(88, 2 * 4 * 2 * 176)
        in_maps.append(dict(xTt=xTt, xoTt=np.ascontiguousarray(xoTt), xo2=xo2, pTt=pTt, w1a=w1a, w1b=w1b,
                            vec88=v88.reshape(88, 64), vec128=vec128, wax=wax, cmat=cmat, wt_g=wt_g, wt_rnn=wt_rnn,
                            wt_att=wt_att, wt_o=wt_o, wt_gu=wt_gu, wt_dn=wt_dn, wt_pg=wt_pg, wt_pp=wt_pp))
    res = run_bass_kernel_spmd(nc, in_maps, core_ids=list(range(8)))
    out = np.empty((2, S, D), f)
    for c in range(8):
        b, g = c // 4, c % 4
        oT = np.asarray(res.results[c]["outT"], f).reshape(2048, 1024)
        out[b, 1024 * g:1024 * (g + 1), :] = oT.T
    return out
```

```python
import numpy as np
from contextlib import ExitStack
import concourse.bass as bass
import concourse.mybir as mybir
from concourse.bass_utils import run_bass_kernel_spmd

F32 = mybir.dt.float32
BF16 = mybir.dt.bfloat16
U8 = mybir.dt.uint8
AF = mybir.ActivationFunctionType
ALU = mybir.AluOpType

S = 4096
D = 2048
NKC = 16
TC = 512
NCH = S // TC
SUB = 128
EPS = 1e-6
QSCALE = 1.0 / np.sqrt(128.0)
NEG = -30000.0
ARENA = 198 * 1024
GELU_C = 0.7978845608028654

ENGS = ["pe", "act", "dve", "pool", "sp"]


class Slot:
    __slots__ = ("name", "w", "r", "excl")

    def __init__(self, name="", excl=False):
        self.name = name
        self.w = None
        self.r = []
        self.excl = excl


class Op:
    __slots__ = ("eng", "fn", "deps", "needs_inc", "seq", "kind", "sem", "semv", "pos", "raw", "cost", "idx",
                 "nd", "rt", "fin", "succ")


class FW:
    NDS = 8

    def __init__(self):
        self.ops = {e: [] for e in ENGS}
        self.cur_barrier = {e: None for e in ENGS}
        self.since_barrier = []
        self.cc_count = 0
        self.stopped = False
        self.all = []

    def stop_if(self, tag):
        import os
        if os.environ.get("KSTOP", "") == tag:
            self.stopped = True

    DEFCOST = {"pe": 4.6, "act": 0.65, "dve": 0.65, "pool": 5.0, "sp": 3.0}

    def op(self, eng, fn, reads=(), writes=(), kind="c", cost=None):
        o = Op()
        o.cost = cost if cost is not None else (30.0 if kind == "cc" else self.DEFCOST[eng])
        if self.stopped:
            o.eng, o.fn, o.kind, o.deps, o.raw, o.pos = eng, fn, "skip", set(), set(), -1
            return o
        o.eng, o.fn, o.kind = eng, fn, kind
        o.needs_inc = False
        o.seq = 0
        o.sem = None
        o.semv = 0
        deps = set()
        raw = set()
        for s in reads:
            if s.w is not None:
                deps.add(s.w)
                raw.add(s.w)
            if s.excl:
                deps.update(s.r)
        for s in writes:
            if s.w is not None:
                deps.add(s.w)
            deps.update(s.r)
        for s in reads:
            s.r.append(o)
        for s in writes:
            s.w = o
            s.r = []
        if self.cur_barrier[eng] is not None:
            deps.add(self.cur_barrier[eng])
        if kind == "cc":
            o.sem = ("cc", eng, self.cc_count)
            self.cc_count += 1
            o.semv = 1
        deps.discard(o)
        o.deps = deps
        o.raw = raw
        o.pos = len(self.ops[eng])
        self.ops[eng].append(o)
        o.idx = len(self.all)
        self.all.append(o)
        self.since_barrier.append(o)
        return o

    def schedule(self, reorder=("pe", "act", "dve", "sp"), W=200):
        import bisect
        for o in self.all:
            o.succ = []
        for o in self.all:
            o.nd = len(o.deps)
            o.rt = 0.0
            for d in o.deps:
                d.succ.append(o)
        blev = {}
        for o in reversed(self.all):
            m = 0.0
            for c in o.succ:
                v = blev[id(c)]
                if v > m:
                    m = v
            blev[id(o)] = m + o.cost
        import os as _os3
        use_bl = _os3.environ.get("KPRIO", "bl") == "bl"
        for o in self.all:
            o.seq = (-blev[id(o)] if use_bl else 0.0, o.idx)
        avail = {e: [] for e in ENGS}
        ptr = {e: 0 for e in ENGS}
        free_at = {e: 0.0 for e in ENGS}
        order = {e: [] for e in ENGS}
        for o in self.all:
            if o.nd == 0 and o.eng in reorder:
                bisect.insort(avail[o.eng], (o.seq, o))
        remaining = len(self.all)
        while remaining:
            best = None
            for e in ENGS:
                if e in reorder:
                    lst = avail[e]
                    if not lst:
                        continue
                    fa = free_at[e]
                    cb = None
                    for (ix, o) in lst[:W]:
                        st = o.rt if o.rt > fa else fa
                        if cb is None or st < cb[0] - 1e-9:
                            cb = (st, ix, o)
                        if st <= fa:
                            break
                else:
                    if ptr[e] >= len(self.ops[e]):
                        continue
                    o = self.ops[e][ptr[e]]
                    if o.nd > 0:
                        continue
                    st = max(o.rt, free_at[e])
                    cb = (st, o.seq, o)
                if best is None or (cb[0], cb[1]) < (best[0], best[1]):
                    best = cb
            assert best is not None, "scheduler stuck"
            st, ix, o = best
            e = o.eng
            if e in reorder:
                i = bisect.bisect_left(avail[e], (ix, ))
                assert avail[e][i][1] is o
                avail[e].pop(i)
            else:
                ptr[e] += 1
            if o.kind in ("dma", "cc"):
                free_at[e] = st + (1.0 if e == "pool" else 0.15)
            else:
                free_at[e] = st + o.cost
            o.fin = st + o.cost
            order[e].append(o)
            remaining -= 1
            for c in o.succ:
                c.nd -= 1
                if o.fin > c.rt:
                    c.rt = o.fin
                if c.nd == 0 and c.eng in reorder:
                    bisect.insort(avail[c.eng], (c.seq, c))
        for e in ENGS:
            self.ops[e] = order[e]
            for i, o in enumerate(order[e]):
                o.pos = i
        self.makespan = max(o.fin for o in self.all)

    def barrier(self, engines):
        if self.stopped:
            return
        prior = list(self.since_barrier)
        b = self.op("dve", lambda e: e.nop(), cost=0.1)
        b.deps.update(prior)
        b.deps.discard(b)
        self.since_barrier = [b]
        for e in engines:
            self.cur_barrier[e] = b

    def finalize(self):
        for e in ENGS:
            n = 0
            hist = []
            for o in self.ops[e]:
                if o.kind != "dma":
                    continue
                o.sem = ("dma", e, n % self.NDS)
                o.semv = 16 * (n // self.NDS + 1)
                if n >= self.NDS:
                    o.deps.add(hist[n - self.NDS])
                hist.append(o)
                n += 1
        for e in ENGS:
            for o in self.ops[e]:
                for d in o.deps:
                    if d.kind != "c":
                        continue
                    if d.eng != o.eng:
                        d.needs_inc = True
                    elif d in o.raw and o.pos - d.pos <= 1:
                        d.needs_inc = True
        for e in ENGS:
            n = 0
            for o in self.ops[e]:
                if o.kind == "c" and o.needs_inc:
                    n += 1
                    o.seq = n
                    o.sem = ("eng", e)
                    o.semv = n

    def emit(self, eng, handle, sems):
        waited = {}
        for o in self.ops[eng]:
            need = {}
            for d in o.deps:
                if d.kind == "c":
                    if d.eng == eng and not (d in o.raw and o.pos - d.pos <= 1):
                        continue
                if need.get(d.sem, 0) < d.semv:
                    need[d.sem] = d.semv
            for k, v in need.items():
                if waited.get(k, 0) >= v:
                    continue
                waited[k] = v
                handle.wait_ge(sems[k], v)
            inst = o.fn(handle)
            if o.kind == "dma":
                inst.then_inc(sems[o.sem], 16)
            elif o.kind == "cc":
                inst.then_inc(sems[o.sem], 1)
            elif o.needs_inc:
                inst.then_inc(sems[o.sem], 1)


class Arena:
    def __init__(self, ap):
        self.ap = ap
        self.top = 0

    def alloc(self, parts, free, dtype):
        esz = 4 if dtype == F32 else 2
        n = int(np.prod(free)) * esz
        off = (self.top + 63) // 64 * 64
        self.top = off + n
        assert self.top <= ARENA, f"arena overflow {self.top}"
        v = self.ap[0:parts, off:off + n].bitcast(dtype)
        if len(free) == 2:
            v = v.rearrange("p (a b) -> p a b", a=free[0])
        elif len(free) == 3:
            v = v.rearrange("p (a b c) -> p a b c", a=free[0], b=free[1])
        return v


class Ring:
    def __init__(self, items):
        self.items = items
        self.i = 0

    def next(self):
        it = self.items[self.i % len(self.items)]
        self.i += 1
        return it


def build_program():
    nc = bass.Bass("TRN2", target_bir_lowering=False)

    def din(name, shape):
        return nc.dram_tensor(name, shape, F32, kind="ExternalInput").ap()

    xTt = din("xTt", [S // SUB, 128, NKC * SUB])
    xo2 = din("xo2", [16, 2, 128, 512])
    pTt = din("pTt", [128, 2 * 1024])
    w1a = din("w1a", [128, NKC * 1408])
    w1b = din("w1b", [128, NKC * 1536])
    vec88 = din("vec88", [88, 64])
    vec128 = din("vec128", [128, 50])
    wax = din("wax", [88, 2 * 4 * 2 * 176])
    cmat = din("cmat", [128, 384 + 896])
    wt_g = din("wt_g", [32, 128, 2048])
    wt_rnn = din("wt_rnn", [16, 2, 128, 11 * 128])
    wt_att = din("wt_att", [16, 128, 2048])
    wt_o = din("wt_o", [16, 128, 2048])
    wt_gu = din("wt_gu", [88, 128, 2048])
    wt_dn = din("wt_dn", [2, 16, 2, 128, 11 * 128])
    wt_pg = din("wt_pg", [16, 128, 2048])
    wt_pp = din("wt_pp", [16, 128, 2 * 128])
    outT = nc.dram_tensor("outT", [16, 128, 1024], F32, kind="ExternalOutput").ap()
    cin_yr = nc.dram_tensor("cin_yr", [NCH, 704, TC], BF16)
    cout_yr = nc.dram_tensor("cout_yr", [NCH, 2816, TC], BF16)
    cin_o = nc.dram_tensor("cin_o", [NCH, 512, TC], BF16)
    cout_o = nc.dram_tensor("cout_o", [NCH, 2048, TC], BF16)
    RG = [[0, 1, 2, 3], [4, 5, 6, 7]]
    hbuf = nc.dram_tensor("hbuf", [NCH, 128, NKC * TC], BF16)
    s_hbuf = [Slot() for _ in range(NCH)]
    s_cyr = [Slot() for _ in range(NCH)]
    s_co = [Slot() for _ in range(NCH)]

    fw = FW()
    es = ExitStack()
    with es:
        arena_t = es.enter_context(nc.sbuf_tensor("arena", [128, ARENA], U8))
        cst_t = es.enter_context(nc.sbuf_tensor("cst", [128, 2432], F32))
        psb2 = [es.enter_context(nc.psum_tensor(f"ps{i}", [128, 1024], F32)) for i in range(4)]
        psb = [psb2[i // 2][:, (i % 2) * 512:(i % 2 + 1) * 512] for i in range(8)]
        ps_slots = [Slot(f"ps{i}", excl=True) for i in range(8)]

        cst = cst_t
        c_vec128 = cst[:, 0:50]
        c_vec88 = cst[0:88, 64:128]
        c_der88 = cst[0:88, 128:160]
        c_num = cst[:, 160:168]
        cb = cst[:, 192:192 + 768].bitcast(BF16)
        c_ident = cb[:, 0:128]
        c_L = cb[:, 128:256]
        c_SU = cb[:, 256:384]
        c_mask = cb[:, 384:384 + 896]
        c_onesm = cb[:, 1280:1408]
        c_onesh = cb[:, 1408:1536]
        c_halo = cst[0:88, 968:968 + 24].rearrange("p (a b) -> p a b", a=8)
        c_hlast = cst[0:88, 1000:1008]
        c_wax = cst[0:88, 1016:1016 + 1408].bitcast(BF16).rearrange("p (w n i c) -> p w n i c", w=2, n=4, i=2)
        s_const = Slot("const")
        s_halo = [Slot() for _ in range(4)]
        s_hlast = [Slot() for _ in range(8)]
        EPS_AP = c_num[:, 0:1]
        ONE_AP = c_num[:, 1:2]

        def g128(col):
            return c_vec128[:, col:col + 1]

        fw.op("sp", lambda e: e.dma_start(out=c_vec128, in_=vec128[:, :]), writes=[s_const], kind="dma")
        fw.op("sp", lambda e: e.dma_start(out=c_vec88, in_=vec88[:, :]), writes=[s_const], kind="dma")
        fw.op("pool", lambda e: e.dma_start(out=cb[:, 0:1280], in_=cmat[:, :]), writes=[s_const], kind="dma")
        fw.op("pool", lambda e: e.dma_start(out=c_wax.rearrange("p w n i c -> p (w n i c)"), in_=wax[:, :]),
              writes=[s_const], kind="dma")

        def setup_consts(e):
            e.memset(c_num[:, 0:1], EPS)
            e.memset(c_num[:, 1:2], 1.0)
            e.memset(c_num[:, 2:3], 0.0)
            e.memset(c_onesm, 1.0 / 2048.0)
            e.memset(c_onesh, 1.0 / 128.0)
            e.memset(c_halo, 0.0)
            return e.memset(c_hlast, 0.0)
        fw.op("dve", setup_consts, writes=[s_const] + s_halo + s_hlast)

        v88 = c_vec88.rearrange("p (c f) -> p c f", c=8)
        hba = c_der88[:, 0:8]
        hbx = c_der88[:, 8:16]
        scl = c_der88[:, 16:24]
        tmp8 = c_der88[:, 24:32]
        fw.op("act", lambda e: e.activation(out=tmp8, in_=v88[:, :, 7], func=AF.Exp, scale=-1.0),
              reads=[s_const], writes=[s_const])
        fw.op("dve", lambda e: e.tensor_scalar(out=hba, in0=v88[:, :, 5], scalar1=0.5, scalar2=None, op0=ALU.mult),
              reads=[s_const], writes=[s_const])
        fw.op("act", lambda e: e.activation(out=scl, in_=tmp8, func=AF.Ln, bias=ONE_AP[0:88, :], scale=1.0),
              reads=[s_const], writes=[s_const])
        fw.op("dve", lambda e: e.tensor_scalar(out=hbx, in0=v88[:, :, 6], scalar1=0.5, scalar2=None, op0=ALU.mult),
              reads=[s_const], writes=[s_const])
        fw.op("dve", lambda e: e.tensor_scalar(out=scl, in0=scl, scalar1=-4.0, scalar2=None, op0=ALU.mult),
              reads=[s_const], writes=[s_const])

        arena_ap = arena_t[:, :]
        fw.stop_if("setup")

        def rmsnorm_sub(load_fn, src_ap, src_slots, gcol0, dst_ap, dst_slot, ntok, scratch, psring, sq_eng="act"):
            sq, s_sq, lnm, s_ln = scratch
            if load_fn is not None:
                load_fn()
            if sq_eng == "act":
                fw.op("act", lambda e: e.activation(out=sq, in_=src_ap, func=AF.Square),
                      reads=src_slots, writes=[s_sq], cost=2.0)
            else:
                fw.op("dve", lambda e: e.tensor_tensor(out=sq, in0=src_ap, in1=src_ap, op=ALU.mult),
                      reads=src_slots, writes=[s_sq], cost=2.4)
            ps, s_ps = psring.next()

            def f_mm(e):
                for kc in range(NKC):
                    last = e.matmul(ps[:, 0:ntok], lhsT=c_onesm, rhs=sq[:, kc, :], start=(kc == 0), stop=(kc == NKC - 1))
                return last
            fw.op("pe", f_mm, reads=[s_sq, s_const], writes=[s_ps], cost=16 * 0.12)
            fw.op("act", lambda e: e.activation(out=lnm[:, 0:ntok], in_=ps[:, 0:ntok], func=AF.Ln, bias=EPS_AP, scale=1.0),
                  reads=[s_ps], writes=[s_ln])
            fw.op("act", lambda e: e.activation(out=lnm[:, ntok:2 * ntok], in_=lnm[:, 0:ntok], func=AF.Exp, scale=-0.5),
                  reads=[s_ln], writes=[s_ln])

            def f_h(e):
                for kc in range(NKC):
                    last = e.scalar_tensor_tensor(out=dst_ap[:, kc, :], in0=src_ap[:, kc, :], scalar=g128(gcol0 + kc),
                                                  in1=lnm[:, ntok:2 * ntok], op0=ALU.mult, op1=ALU.mult)
                return last
            fw.op("dve", f_h, reads=list(src_slots) + [s_ln, s_const], writes=[dst_slot], cost=16 * 0.22)

        A = Arena(arena_ap)
        W1a = A.alloc(128, (NKC, 1408), BF16)
        s_W1a = Slot("W1a")
        xs_r = [(A.alloc(128, (NKC, SUB), F32), Slot()) for _ in range(2)]
        xsR = Ring(xs_r)
        sq_a = A.alloc(128, (NKC, SUB), BF16)
        lnm_a = A.alloc(128, (2 * SUB,), F32)
        scratchA = (sq_a, Slot(), lnm_a, Slot())
        hT_ring = Ring([(A.alloc(128, (NKC, TC), BF16), [Slot() for _ in range(TC // SUB)]) for _ in range(2)])
        yrb = [(A.alloc(88, (8, TC), BF16), Slot()) for _ in range(2)]
        NSET = 2
        sets = []
        for _ in range(NSET):
            d = {}
            d["xr"] = A.alloc(88, (2, TC + 3), F32)
            d["gr"] = A.alloc(88, (2, TC), F32)
            d["xc"] = A.alloc(88, (2, TC), F32)
            d["xcb"] = A.alloc(88, (2, TC), BF16)
            for nm in ("rp", "ip", "a", "t", "s", "h"):
                d[nm] = A.alloc(88, (2, TC), F32)
            d["slots"] = {nm: [Slot(), Slot()] for nm in ("xr", "gr", "xc", "xcb", "rp", "ip", "a", "t", "s", "h")}
            sets.append(d)
        psA = Ring(list(zip([p[:, :] for p in psb[0:6]], ps_slots[0:6])))
        psA_ms = Ring(list(zip([p[:, :] for p in psb[6:8]], ps_slots[6:8])))

        for q4 in range(4):
            fw.op("pool", lambda e, q4=q4: e.dma_start(
                out=W1a[:, 4 * q4:4 * q4 + 4, :],
                in_=w1a[:, 4 * q4 * 1408:(4 * q4 + 4) * 1408].rearrange("p (k n) -> p k n", k=4)),
                writes=[s_W1a], kind="dma")

        def proj_fm(Wt, s_W, col0, M, ps, s_ps, h_ap, h_slots, ntok=TC):
            def f(e):
                for kc in range(NKC):
                    last = e.matmul(ps[0:M, 0:ntok], lhsT=Wt[:, kc, col0:col0 + M], rhs=h_ap[:, kc, :],
                                    start=(kc == 0), stop=(kc == NKC - 1))
                return last
            fw.op("pe", f, reads=[s_W] + list(h_slots), writes=[s_ps])

        def hT_chunk(G, xsrc, scratch, psring, sq_eng="act"):
            hT, s_hT = hT_ring.next()
            for sb in range(TC // SUB):
                sc = G * (TC // SUB) + sb
                xs, s_xs = xsR.next()
                fw.op("sp", lambda e, xs=xs, sc=sc: e.dma_start(out=xs.rearrange("p k t -> p (k t)"), in_=xsrc[sc]),
                      writes=[s_xs], kind="dma")
                rmsnorm_sub(None, xs, [s_xs], 0, hT[:, :, sb * SUB:(sb + 1) * SUB], s_hT[sb], SUB, scratch, psring, sq_eng)
            return hT, s_hT

        for G in range(NCH):
            hT, s_hT = hT_chunk(G, xTt, scratchA, psA_ms)
            fw.op("sp", lambda e, hT=hT, G=G: e.dma_start(out=hbuf.ap()[G], in_=hT.rearrange("p k t -> p (k t)")),
                  reads=s_hT, writes=[s_hbuf[G]], kind="dma")
            yb, s_yb = yrb[G % 2]
            for n in range(4):
                st = sets[(G * 4 + n) % NSET]
                sl = st["slots"]
                for j in range(2):
                    c8 = 2 * n + j
                    ps, s_ps = psA.next()
                    proj_fm(W1a, s_W1a, c8 * 88, 88, ps, s_ps, hT, s_hT)
                    fw.op("act", lambda e, ps=ps, st=st, j=j: e.activation(out=st["xr"][:, j, 3:TC + 3], in_=ps[0:88, :], func=AF.Copy),
                          reads=[s_ps], writes=[sl["xr"][j]])
                    ps, s_ps = psA.next()
                    proj_fm(W1a, s_W1a, 704 + c8 * 88, 88, ps, s_ps, hT, s_hT)
                    fw.op("act", lambda e, ps=ps, st=st, j=j: e.activation(out=st["gr"][:, j, :], in_=ps[0:88, :], func=AF.Copy),
                          reads=[s_ps], writes=[sl["gr"][j]])
                fw.op("dve", lambda e, st=st, n=n: e.tensor_copy(out=st["xr"][:, :, 0:3], in_=c_halo[:, 2 * n:2 * n + 2, :]),
                      reads=[s_halo[n]], writes=sl["xr"])
                fw.op("dve", lambda e, st=st, n=n: e.tensor_copy(out=c_halo[:, 2 * n:2 * n + 2, :], in_=st["xr"][:, :, TC:TC + 3]),
                      reads=sl["xr"], writes=[s_halo[n]])
                for k in range(4):
                    for j in range(2):
                        c8 = 2 * n + j
                        if k == 0:
                            fw.op("dve", lambda e, st=st, j=j, c8=c8: e.tensor_scalar(
                                out=st["xc"][:, j, :], in0=st["xr"][:, j, 0:TC], scalar1=v88[:, c8, 0:1], scalar2=v88[:, c8, 4:5],
                                op0=ALU.mult, op1=ALU.add), reads=[sl["xr"][j], s_const], writes=[sl["xc"][j]])
                        else:
                            fw.op("dve", lambda e, st=st, j=j, c8=c8, k=k: e.scalar_tensor_tensor(
                                out=st["xc"][:, j, :], in0=st["xr"][:, j, k:k + TC], scalar=v88[:, c8, k:k + 1], in1=st["xc"][:, j, :],
                                op0=ALU.mult, op1=ALU.add), reads=[sl["xr"][j], sl["xc"][j], s_const], writes=[sl["xc"][j]])
                for j in range(2):
                    fw.op("act", lambda e, st=st, j=j: e.activation(out=st["xcb"][:, j, :], in_=st["xc"][:, j, :], func=AF.Copy),
                          reads=[sl["xc"][j]], writes=[sl["xcb"][j]])
                gps = []
                for j in range(2):
                    for w in range(2):
                        ps, s_ps = psA.next()

                        def f(e, ps=ps, st=st, j=j, w=w, n=n):
                            for i in range(2):
                                last = e.matmul(ps[0:88, :], lhsT=c_wax[:, w, n, i, 88 * j:88 * j + 88], rhs=st["xcb"][:, i, :],
                                                start=(i == 0), stop=(i == 1))
                            return last
                        fw.op("pe", f, reads=sl["xcb"] + [s_const], writes=[s_ps], cost=0.9)
                        gps.append((ps, s_ps))
                for j in range(2):
                    c8 = 2 * n + j
                    ps, s_ps = gps[2 * j]
                    fw.op("act", lambda e, ps=ps, st=st, j=j, c8=c8: e.activation(
                        out=st["rp"][:, j, :], in_=ps[0:88, :], func=AF.Tanh, bias=hba[:, c8:c8 + 1], scale=0.5),
                        reads=[s_ps, s_const], writes=[sl["rp"][j]])
                    ps, s_ps = gps[2 * j + 1]
                    fw.op("act", lambda e, ps=ps, st=st, j=j, c8=c8: e.activation(
                        out=st["ip"][:, j, :], in_=ps[0:88, :], func=AF.Tanh, bias=hbx[:, c8:c8 + 1], scale=0.5),
                        reads=[s_ps, s_const], writes=[sl["ip"][j]])
                for j in range(2):
                    c8 = 2 * n + j
                    fw.op("act", lambda e, st=st, j=j, c8=c8: e.activation(
                        out=st["a"][:, j, :], in_=st["rp"][:, j, :], func=AF.Exp, bias=scl[:, c8:c8 + 1], scale=scl[:, c8:c8 + 1]),
                        reads=[sl["rp"][j], s_const], writes=[sl["a"][j]])
                for j in range(2):
                    fw.op("dve", lambda e, st=st, j=j: e.scalar_tensor_tensor(
                        out=st["t"][:, j, :], in0=st["ip"][:, j, :], scalar=1.0, in1=st["xc"][:, j, :], op0=ALU.add, op1=ALU.mult),
                        reads=[sl["ip"][j], sl["xc"][j]], writes=[sl["t"][j]])
                for j in range(2):
                    fw.op("act", lambda e, st=st, j=j: e.activation(out=st["s"][:, j, :], in_=st["a"][:, j, :], func=AF.Square),
                          reads=[sl["a"][j]], writes=[sl["s"][j]])
                for j in range(2):
                    fw.op("act", lambda e, st=st, j=j: e.activation(out=st["s"][:, j, :], in_=st["s"][:, j, :], func=AF.Sqrt,
                                                                    bias=ONE_AP[0:88, :], scale=-1.0),
                          reads=[sl["s"][j], s_const], writes=[sl["s"][j]])
                for j in range(2):
                    fw.op("dve", lambda e, st=st, j=j: e.scalar_tensor_tensor(
                        out=st["t"][:, j, :], in0=st["s"][:, j, :], scalar=0.5, in1=st["t"][:, j, :], op0=ALU.mult, op1=ALU.mult),
                        reads=[sl["s"][j], sl["t"][j]], writes=[sl["t"][j]])
                for j in range(2):
                    c8 = 2 * n + j
                    fw.op("dve", lambda e, st=st, j=j, c8=c8: e.tensor_tensor_scan(
                        out=st["h"][:, j, :], data0=st["a"][:, j, :], data1=st["t"][:, j, :], initial=c_hlast[:, c8:c8 + 1],
                        op0=ALU.mult, op1=ALU.add), reads=[sl["a"][j], sl["t"][j], s_hlast[c8]], writes=[sl["h"][j]])
                for j in range(2):
                    c8 = 2 * n + j
                    fw.op("act", lambda e, st=st, j=j, c8=c8: e.activation(out=c_hlast[:, c8:c8 + 1], in_=st["h"][:, j, TC - 1:TC], func=AF.Copy),
                          reads=[sl["h"][j]], writes=[s_hlast[c8]])
                for j in range(2):
                    fw.op("dve", lambda e, st=st, j=j: e.tensor_tensor(out=st["ip"][:, j, :], in0=st["gr"][:, j, :], in1=st["gr"][:, j, :], op=ALU.mult),
                          reads=[sl["gr"][j]], writes=[sl["ip"][j]])
                for j in range(2):
                    fw.op("dve", lambda e, st=st, j=j: e.tensor_scalar(out=st["ip"][:, j, :], in0=st["ip"][:, j, :], scalar1=0.044715, scalar2=1.0,
                                                                      op0=ALU.mult, op1=ALU.add), reads=[sl["ip"][j]], writes=[sl["ip"][j]])
                for j in range(2):
                    fw.op("dve", lambda e, st=st, j=j: e.tensor_tensor(out=st["ip"][:, j, :], in0=st["ip"][:, j, :], in1=st["gr"][:, j, :], op=ALU.mult),
                          reads=[sl["ip"][j], sl["gr"][j]], writes=[sl["ip"][j]])
                for j in range(2):
                    fw.op("act", lambda e, st=st, j=j: e.activation(out=st["rp"][:, j, :], in_=st["ip"][:, j, :], func=AF.Tanh, scale=GELU_C),
                          reads=[sl["ip"][j]], writes=[sl["rp"][j]])
                for j in range(2):
                    fw.op("dve", lambda e, st=st, j=j: e.scalar_tensor_tensor(
                        out=st["ip"][:, j, :], in0=st["rp"][:, j, :], scalar=1.0, in1=st["gr"][:, j, :], op0=ALU.add, op1=ALU.mult),
                        reads=[sl["rp"][j], sl["gr"][j]], writes=[sl["ip"][j]])
                for j in range(2):
                    c8 = 2 * n + j
                    fw.op("dve", lambda e, st=st, j=j, c8=c8, yb=yb: e.scalar_tensor_tensor(
                        out=yb[:, c8, :], in0=st["ip"][:, j, :], scalar=0.5, in1=st["h"][:, j, :], op0=ALU.mult, op1=ALU.mult),
                        reads=[sl["ip"][j], sl["h"][j]], writes=[s_yb])
            s_ci = Slot()
            fw.op("sp", lambda e, yb=yb, G=G: e.dma_start(
                out=cin_yr.ap()[G].rearrange("(c p) t -> p c t", p=88), in_=yb),
                reads=[s_yb], writes=[s_ci], kind="dma")
            fw.op("pool", lambda e, G=G: e.collective_compute("AllGather", ALU.bypass, replica_groups=RG,
                                                              ins=[cin_yr.ap()[G].opt()], outs=[cout_yr.ap()[G].opt()]),
                  reads=[s_ci], writes=[s_cyr[G]], kind="cc")
            if G == 0:
                fw.stop_if("passA1")
        fw.stop_if("passA")


        fw.stop_if("ag1")
        B = Arena(arena_ap)
        W1b = B.alloc(128, (NKC, 1536), BF16)
        s_W1b = Slot("W1b")
        KT = B.alloc(128, (4, S), BF16)
        s_KT = [[Slot() for _ in range(NCH)] for _ in range(4)]
        Vt = B.alloc(128, (S // 128, 512), BF16)
        s_V = [Slot() for _ in range(S // 128)]
        hT_ring.items = [(B.alloc(128, (NKC, TC), BF16), [Slot() for _ in range(TC // SUB)]) for _ in range(2)]
        QT_ring = Ring([(B.alloc(128, (4, TC), BF16), [Slot() for _ in range(4)]) for _ in range(2)])
        nrm = [dict(raw=B.alloc(128, (TC,), F32), sq=B.alloc(128, (TC,), BF16), ln=B.alloc(128, (TC,), F32),
                    s_raw=Slot(), s_sq=Slot(), s_ln=Slot()) for _ in range(2)]
        nrmR = Ring(nrm)
        e_r = [(B.alloc(128, (2, TC), F32), Slot()) for _ in range(3)]
        sp_r = [(B.alloc(128, (2, TC), BF16), Slot()) for _ in range(3)]
        x_r = [(B.alloc(128, (2, TC), F32), Slot()) for _ in range(2)]
        w_r = [(B.alloc(128, (2, TC), BF16), Slot()) for _ in range(3)]
        psz2 = psb2[0][:, :].rearrange("p (a b) -> p a b", a=2)
        psc2 = psb2[1][:, :].rearrange("p (a b) -> p a b", a=2)
        obf = [(B.alloc(128, (TC,), BF16), Slot()) for _ in range(2)]
        obfR = Ring(obf)
        psz = [(psb[0][:, :], ps_slots[0]), (psb[1][:, :], ps_slots[1])]
        psc = [(psb[2][:, :], ps_slots[2]), (psb[3][:, :], ps_slots[3])]
        pso = [(psb[4][:, :], ps_slots[4]), (psb[5][:, :], ps_slots[5])]
        psM = Ring([(psb[6][:, :], ps_slots[6]), (psb[7][:, :], ps_slots[7])])

        fw.barrier(ENGS)
        for q4 in range(4):
            fw.op("pool", lambda e, q4=q4: e.dma_start(
                out=W1b[:, 4 * q4:4 * q4 + 4, :],
                in_=w1b[:, 4 * q4 * 1536:(4 * q4 + 4) * 1536].rearrange("p (k n) -> p k n", k=4)),
                writes=[s_W1b], kind="dma")

        def qk_norm(col0, gain_col, dst_ap, dst_slot, hT, s_hT):
            nb = nrmR.next()
            ps, s_ps = psM.next()
            proj_fm(W1b, s_W1b, col0, 128, ps, s_ps, hT, s_hT)
            fw.op("dve", lambda e: e.tensor_copy(out=nb["raw"], in_=ps), reads=[s_ps], writes=[nb["s_raw"]])
            fw.op("dve", lambda e: e.tensor_tensor(out=nb["sq"], in0=nb["raw"], in1=nb["raw"], op=ALU.mult),
                  reads=[nb["s_raw"]], writes=[nb["s_sq"]])
            ps2, s_ps2 = psM.next()
            fw.op("pe", lambda e: e.matmul(ps2, lhsT=c_onesh, rhs=nb["sq"], start=True, stop=True),
                  reads=[nb["s_sq"], s_const], writes=[s_ps2], cost=0.3)
            fw.op("act", lambda e: e.activation(out=nb["ln"], in_=ps2, func=AF.Ln, bias=EPS_AP, scale=1.0),
                  reads=[s_ps2], writes=[nb["s_ln"]])
            fw.op("act", lambda e: e.activation(out=nb["ln"], in_=nb["ln"], func=AF.Exp, scale=-0.5),
                  reads=[nb["s_ln"]], writes=[nb["s_ln"]])
            fw.op("dve", lambda e: e.scalar_tensor_tensor(out=dst_ap, in0=nb["raw"], scalar=g128(gain_col), in1=nb["ln"],
                                                          op0=ALU.mult, op1=ALU.mult),
                  reads=[nb["s_raw"], nb["s_ln"], s_const], writes=[dst_slot])

        import os as _os
        for G in range(int(_os.environ.get("KNB", NCH))):
            fw.stop_if("b_w")
            hT, s_hT = hT_ring.next()
            fw.op("sp", lambda e, hT=hT, G=G: e.dma_start(out=hT.rearrange("p k t -> p (k t)"), in_=hbuf.ap()[G]),
                  reads=[s_hbuf[G]], writes=s_hT, kind="dma")
            QT, s_QT = QT_ring.next()
            fw.stop_if("b_h")
            for hh in range(4):
                qk_norm(hh * 128, 48, QT[:, hh, :], s_QT[hh], hT, s_hT)
                qk_norm(512 + hh * 128, 49, KT[:, hh, G * TC:(G + 1) * TC], s_KT[hh][G], hT, s_hT)
            fw.stop_if("b_qk")
            for tb in range(4):
                ps, s_ps = psM.next()

                def f(e, ps=ps, tb=tb, hT=hT):
                    for kc in range(NKC):
                        last = e.matmul(ps, lhsT=hT[:, kc, tb * 128:(tb + 1) * 128], rhs=W1b[:, kc, 1024:1536],
                                        start=(kc == 0), stop=(kc == NKC - 1))
                    return last
                fw.op("pe", f, reads=[s_W1b] + s_hT, writes=[s_ps])
                fw.op("dve", lambda e, ps=ps, tb=tb, G=G: e.tensor_copy(out=Vt[:, 4 * G + tb, :], in_=ps),
                      reads=[s_ps], writes=[s_V[4 * G + tb]])
            fw.stop_if("b_v")
            NT = 4 * G + 4
            s_cio = [Slot() for _ in range(4)]
            for hp in range(2):
                heads = [2 * hp, 2 * hp + 1]
                etile = {}
                sptile = {}
                wtile = {}

                def stageZ(s):
                    kb = NT - 1 - s
                    diag = kb - 4 * G
                    for li, hh in enumerate(heads):
                        ps, s_ps = psz[li]

                        def f(e, ps=ps, hh=hh, kb=kb, diag=diag, QT=QT):
                            last = e.matmul(ps, lhsT=KT[:, hh, kb * 128:(kb + 1) * 128], rhs=QT[:, hh, :], start=True, stop=(diag < 0))
                            if diag >= 0:
                                o0 = 384 - 128 * diag
                                last = e.matmul(ps, lhsT=c_ident, rhs=c_mask[:, o0:o0 + TC], start=False, stop=True)
                            return last
                        fw.op("pe", f, reads=[s_KT[hh][kb // 4], s_QT[hh], s_const], writes=[s_ps], cost=0.6 if diag >= 0 else 0.3)
                    eb, s_eb = e_r[s % 3]
                    etile[s] = (eb, s_eb)
                    fw.op("act", lambda e, eb=eb: e.activation(out=eb, in_=psz2, func=AF.Exp, scale=float(QSCALE)),
                          reads=[psz[0][1], psz[1][1]], writes=[s_eb], cost=1.1)
                    sb_, s_sb = sp_r[s % 3]
                    sptile[s] = (sb_, s_sb)
                    fw.op("act", lambda e, eb=eb, sb_=sb_: e.activation(out=sb_, in_=eb, func=AF.Ln, bias=ONE_AP, scale=1.0),
                          reads=[s_eb, s_const], writes=[s_sb], cost=1.1)

                def stageC(s):
                    sb_, s_sb = sptile[s]
                    for li, hh in enumerate(heads):
                        ps, s_ps = psc[li]
                        rd = [s_sb, s_const]
                        if s > 0:
                            sprev, s_sprev = sptile[s - 1]
                            rd += [s_sprev, s_ps]
                        else:
                            sprev = None

                        def f(e, ps=ps, sb_=sb_, sprev=sprev, s=s, li=li):
                            if s > 0:
                                e.matmul(ps, lhsT=c_SU, rhs=sprev[:, li, :], start=False, stop=False, skip_group_check=True)
                            return e.matmul(ps, lhsT=c_L, rhs=sb_[:, li, :], start=(s == 0), stop=True, skip_group_check=True)
                        fw.op("pe", f, reads=rd, writes=[s_ps], cost=0.6 if s > 0 else 0.3)
                    xb, s_xb = x_r[s % 2]
                    fw.op("act", lambda e, xb=xb: e.activation(out=xb, in_=psc2, func=AF.Exp, scale=-1.0),
                          reads=[psc[0][1], psc[1][1]], writes=[s_xb], cost=1.1)
                    eb, s_eb = etile[s]
                    wb, s_wb = w_r[s % 3]
                    wtile[s] = (wb, s_wb)
                    fw.op("dve", lambda e, xb=xb, eb=eb, wb=wb: e.tensor_tensor(out=wb, in0=eb, in1=xb, op=ALU.mult),
                          reads=[s_xb, s_eb], writes=[s_wb], cost=1.2)

                def stageO(s):
                    kb = NT - 1 - s
                    for li, hh in enumerate(heads):
                        ps, s_ps = pso[li]
                        wb, s_wb = wtile[s]
                        fw.op("pe", lambda e, ps=ps, wb=wb, hh=hh, kb=kb, s=s, li=li: e.matmul(
                            ps, lhsT=Vt[:, kb, hh * 128:(hh + 1) * 128], rhs=wb[:, li, :], start=(s == 0), stop=(s == NT - 1)),
                            reads=[s_wb, s_V[kb], s_ps] if s > 0 else [s_wb, s_V[kb]], writes=[s_ps], cost=0.3)

                for step in range(NT + 2):
                    if step < NT:
                        stageZ(step)
                    if 0 <= step - 1 < NT:
                        stageC(step - 1)
                    if 0 <= step - 2 < NT:
                        stageO(step - 2)
                for li, hh in enumerate(heads):
                    ps, s_ps = pso[li]
                    ob, s_ob = obfR.next()
                    fw.op("dve", lambda e, ps=ps, ob=ob: e.tensor_copy(out=ob, in_=ps), reads=[s_ps], writes=[s_ob])
                    fw.op("sp", lambda e, ob=ob, hh=hh, G=G: e.dma_start(
                        out=cin_o.ap()[G, hh * 128:(hh + 1) * 128, :], in_=ob), reads=[s_ob], writes=[s_cio[hh]], kind="dma")
            fw.op("pool", lambda e, G=G: e.collective_compute("AllGather", ALU.bypass, replica_groups=RG,
                                                              ins=[cin_o.ap()[G].opt()], outs=[cout_o.ap()[G].opt()]),
                  reads=s_cio, writes=[s_co[G]], kind="cc")

        fw.stop_if("passB")

        P = Arena(arena_ap)
        NU = 10
        wunits = [(P.alloc(128, (NKC, 128), BF16), Slot()) for _ in range(NU)]
        wu_i = [0]
        tmpf = [(P.alloc(128, (TC,), F32), Slot()) for _ in range(4)]
        tmpR = Ring(tmpf)
        R1 = P.alloc(128, (NKC, 1024), F32)
        R1b = R1.rearrange("p k t -> p (k t)").bitcast(BF16)
        h_a = R1b[:, 0:NKC * 1024].rearrange("p (k t) -> p k t", k=NKC)
        oT_a = R1b[:, NKC * 1024:2 * NKC * 1024].rearrange("p (k t) -> p k t", k=NKC)
        R2 = P.alloc(128, (22, 1024), BF16)
        R3 = P.alloc(128, (NKC, 1024), BF16)
        pTb = P.alloc(128, (2, 1024), BF16)
        s_xo = [[Slot() for _ in range(2)] for _ in range(NKC)]
        s_R1h = [Slot() for _ in range(8)]
        s_oT = Slot()
        s_R2 = Slot()
        s_R3 = [[Slot() for _ in range(2)] for _ in range(NKC)]
        s_h2 = [Slot() for _ in range(8)]
        s_pT = Slot()
        psP = Ring(list(zip([p[:, :] for p in psb], ps_slots)))
        sq_p = P.alloc(128, (NKC, SUB), BF16)
        lnm_p = P.alloc(128, (2 * SUB,), F32)
        scratchP = (sq_p, Slot(), lnm_p, Slot())
        xs_rp = [(R3.rearrange("p k t -> p (k t)")[:, 12 * 1024:16 * 1024].bitcast(F32).rearrange("p (k t) -> p k t", k=NKC), Slot())]

        fw.barrier(ENGS)

        def wload(src_ap, nk=NKC):
            u, s_u = wunits[wu_i[0] % NU]
            wu_i[0] += 1
            fw.op("pool", lambda e: e.dma_start(out=u[:, 0:nk, :].rearrange("p k n -> p (k n)"), in_=src_ap),
                  writes=[s_u], kind="dma")
            return u, s_u

        def acc_group(ps, s_ps, parts, tb, extra_reads):
            def f(e):
                tot = sum(p[2] for p in parts)
                i = 0
                for (u, s_u, nk, act, k0, rows) in parts:
                    for kc in range(nk):
                        last = e.matmul(ps, lhsT=u[0:rows, kc, :], rhs=act[0:rows, k0 + kc, tb * TC:(tb + 1) * TC],
                                        start=(i == 0), stop=(i == tot - 1))
                        i += 1
                return last
            fw.op("pe", f, reads=[p[1] for p in parts] + list(extra_reads), writes=[s_ps], cost=0.29 * sum(p[2] for p in parts))

        for tb in range(2):
            def f_hh(e, tb=tb):
                rank = e.partition_id() % 4
                return e.dma_start(out=h_a[:, :, tb * TC:(tb + 1) * TC],
                                   in_=hbuf.ap()[bass.ds(2 * rank + tb, 1)].rearrange("o p (k t) -> p (o k) t", k=NKC))
            fw.op("pool", f_hh, reads=s_hbuf, writes=s_R1h[4 * tb:4 * tb + 4], kind="dma")

        for tb in range(2):
            def f_yr(e, tb=tb):
                rank = e.partition_id() % 4
                return e.dma_start(out=R2[:, :, tb * TC:(tb + 1) * TC],
                                   in_=cout_yr.ap()[bass.ds(2 * rank + tb, 1)].rearrange("o (k p) t -> p (o k) t", p=128))
            fw.op("pool", f_yr, reads=s_cyr, writes=[s_R2], kind="dma")

            def f_o(e, tb=tb):
                rank = e.partition_id() % 4
                return e.dma_start(out=oT_a[:, :, tb * TC:(tb + 1) * TC],
                                   in_=cout_o.ap()[bass.ds(2 * rank + tb, 1)].rearrange("o (k p) t -> p (o k) t", p=128))
            fw.op("pool", f_o, reads=s_co, writes=[s_oT], kind="dma")
        fw.op("pool", lambda e: e.dma_start(out=pTb.rearrange("p k t -> p (k t)"), in_=pTt[:, :]), writes=[s_pT], kind="dma")

        fw.stop_if("s0")
        for cc in range(NKC):
            ugr = wload(wt_g[cc])
            uga = wload(wt_g[16 + cc])
            ur0 = wload(wt_rnn[cc, 0], 11)
            ur1 = wload(wt_rnn[cc, 1], 11)
            uat = wload(wt_att[cc])
            for tb in range(2):
                hs = s_R1h[4 * tb:4 * tb + 4]
                p_gr = psP.next()
                acc_group(p_gr[0], p_gr[1], [(ugr[0], ugr[1], NKC, h_a, 0, 128)], tb, hs)
                p_ga = psP.next()
                acc_group(p_ga[0], p_ga[1], [(uga[0], uga[1], NKC, h_a, 0, 128)], tb, hs)
                p_yr = psP.next()
                acc_group(p_yr[0], p_yr[1], [(ur0[0], ur0[1], 11, R2, 0, 128), (ur1[0], ur1[1], 11, R2, 11, 128)], tb, [s_R2])
                p_ya = psP.next()
                acc_group(p_ya[0], p_ya[1], [(uat[0], uat[1], NKC, oT_a, 0, 128)], tb, [s_oT])
                t_sr = tmpR.next()
                t_sa = tmpR.next()
                fw.op("act", lambda e, p=p_gr, t=t_sr: e.activation(out=t[0], in_=p[0], func=AF.Sigmoid), reads=[p_gr[1]], writes=[t_sr[1]])
                fw.op("act", lambda e, p=p_ga, t=t_sa: e.activation(out=t[0], in_=p[0], func=AF.Sigmoid), reads=[p_ga[1]], writes=[t_sa[1]])
                fw.op("dve", lambda e, p=p_yr, t=t_sr: e.tensor_tensor(out=t[0], in0=t[0], in1=p[0], op=ALU.mult),
                      reads=[p_yr[1], t_sr[1]], writes=[t_sr[1]])
                fw.op("dve", lambda e, p=p_ya, t=t_sa: e.tensor_tensor(out=t[0], in0=t[0], in1=p[0], op=ALU.mult),
                      reads=[p_ya[1], t_sa[1]], writes=[t_sa[1]])
                fw.op("dve", lambda e, a=t_sr, b=t_sa, cc=cc, tb=tb: e.tensor_tensor(
                    out=R3[:, cc, tb * TC:(tb + 1) * TC], in0=a[0], in1=b[0], op=ALU.add),
                    reads=[t_sr[1], t_sa[1]], writes=[s_R3[cc][tb]])

        fw.barrier(["pe", "act", "dve", "sp"])
        fw.stop_if("sa")
        mix_slots = [s_R3[c][t] for c in range(NKC) for t in range(2)]
        for cc in range(NKC):
            uo = wload(wt_o[cc])
            for tb in range(2):
                fw.op("sp", lambda e, cc=cc, tb=tb: e.dma_start(out=R1[:, cc, tb * TC:(tb + 1) * TC], in_=xo2[cc, tb]),
                      writes=[s_xo[cc][tb]], kind="dma")
                p = psP.next()
                acc_group(p[0], p[1], [(uo[0], uo[1], NKC, R3, 0, 128)], tb, [s_R3[c][tb] for c in range(NKC)])
                fw.op("dve", lambda e, p=p, cc=cc, tb=tb: e.tensor_tensor(
                    out=R1[:, cc, tb * TC:(tb + 1) * TC], in0=R1[:, cc, tb * TC:(tb + 1) * TC], in1=p[0], op=ALU.add),
                    reads=[p[1], s_xo[cc][tb]], writes=[s_xo[cc][tb]])

        fw.barrier(["pe", "act", "dve", "sp"])

        def rms_resident(gcol0, dst, dst_slots):
            for sc in range(8):
                tb = sc // 4
                rmsnorm_sub(None, R1[:, :, sc * SUB:(sc + 1) * SUB], [s_xo[c][tb] for c in range(NKC)], gcol0,
                            dst[:, :, sc * SUB:(sc + 1) * SUB], dst_slots[sc], SUB, scratchP, psP)

        fw.stop_if("sb")
        rms_resident(16, R3, s_h2)
        for half in range(2):
            for m in range(22):
                mm = half * 22 + m
                ug = wload(wt_gu[mm])
                uu = wload(wt_gu[44 + mm])
                for tb in range(2):
                    hs = s_h2[4 * tb:4 * tb + 4]
                    p_g = psP.next()
                    acc_group(p_g[0], p_g[1], [(ug[0], ug[1], NKC, R3, 0, 128)], tb, hs)
                    p_u = psP.next()
                    acc_group(p_u[0], p_u[1], [(uu[0], uu[1], NKC, R3, 0, 128)], tb, hs)
                    t_s = tmpR.next()
                    fw.op("act", lambda e, p=p_g, t=t_s: e.activation(out=t[0], in_=p[0], func=AF.Silu), reads=[p_g[1]], writes=[t_s[1]])
                    fw.op("dve", lambda e, p=p_u, t=t_s, m=m, tb=tb: e.tensor_tensor(
                        out=R2[:, m, tb * TC:(tb + 1) * TC], in0=t[0], in1=p[0], op=ALU.mult),
                        reads=[p_u[1], t_s[1]], writes=[s_R2])
            for cc in range(NKC):
                ud0 = wload(wt_dn[half, cc, 0], 11)
                ud1 = wload(wt_dn[half, cc, 1], 11)
                for tb in range(2):
                    p = psP.next()
                    acc_group(p[0], p[1], [(ud0[0], ud0[1], 11, R2, 0, 128), (ud1[0], ud1[1], 11, R2, 11, 128)], tb, [s_R2])
                    fw.op("dve", lambda e, p=p, cc=cc, tb=tb: e.tensor_tensor(
                        out=R1[:, cc, tb * TC:(tb + 1) * TC], in0=R1[:, cc, tb * TC:(tb + 1) * TC], in1=p[0], op=ALU.add),
                        reads=[p[1], s_xo[cc][tb]], writes=[s_xo[cc][tb]])
            fw.barrier(["pe", "act", "dve", "sp"])

        fw.stop_if("sc")
        h3 = R2[:, 0:NKC, :]
        s_h3 = [Slot() for _ in range(8)]
        rms_resident(32, h3, s_h3)
        out_ops = []
        for cc in range(NKC):
            upg = wload(wt_pg[cc])
            upp = wload(wt_pp[cc], 2)
            for tb in range(2):
                hs = s_h3[4 * tb:4 * tb + 4]
                p_g = psP.next()
                acc_group(p_g[0], p_g[1], [(upg[0], upg[1], NKC, h3, 0, 128)], tb, hs)
                p_p = psP.next()

                acc_group(p_p[0], p_p[1], [(upp[0], upp[1], 2, pTb, 0, 128)], tb, [s_pT])
                t_s = tmpR.next()
                fw.op("act", lambda e, p=p_g, t=t_s: e.activation(out=t[0], in_=p[0], func=AF.Sigmoid), reads=[p_g[1]], writes=[t_s[1]])
                fw.op("dve", lambda e, p=p_p, t=t_s: e.tensor_tensor(out=t[0], in0=t[0], in1=p[0], op=ALU.mult),
                      reads=[p_p[1], t_s[1]], writes=[t_s[1]])
                fw.op("dve", lambda e, t=t_s, cc=cc, tb=tb: e.tensor_tensor(
                    out=R1[:, cc, tb * TC:(tb + 1) * TC], in0=R1[:, cc, tb * TC:(tb + 1) * TC], in1=t[0], op=ALU.add),
                    reads=[t_s[1], s_xo[cc][tb]], writes=[s_xo[cc][tb]])
            out_ops.append(fw.op("sp", lambda e, cc=cc: e.dma_start(out=outT[cc], in_=R1[:, cc, :]),
                                 reads=[s_xo[cc][0], s_xo[cc][1]], kind="dma"))
        s_fin = Slot("fin")
        fin = fw.op("sp", lambda e: e.nop(), kind="c")
        fin.deps.update(o_ for o_ in out_ops if o_.kind != "skip")

        import os as _os2
        if _os2.environ.get("KNOSCHED", "") != "1":
            fw.schedule()
        fw.finalize()
        sems = {}
        for e in ENGS:
            sems[("eng", e)] = es.enter_context(nc.semaphore(f"s_{e}"))
            for i in range(FW.NDS):
                sems[("dma", e, i)] = es.enter_context(nc.semaphore(f"d_{e}{i}"))
        for i in range(fw.cc_count):
            sems[("cc", "pool", i)] = es.enter_context(nc.semaphore(f"cc{i}"))
        block = es.enter_context(nc.Block())

        @block.tensor
        def _(t):
            fw.emit("pe", t, sems)

        @block.scalar
        def _(a):
            fw.emit("act", a, sems)

        @block.vector
        def _(v):
            fw.emit("dve", v, sems)

        @block.gpsimd
        def _(g):
            fw.emit("pool", g, sems)

        @block.sync
        def _(sy):
            fw.emit("sp", sy, sems)
    return nc


def _tile_w(w, nk):
    K, N = w.shape
    assert K == nk * 128
    t = w.reshape(nk, 128, N // 128, 128).transpose(2, 1, 0, 3)
    return np.ascontiguousarray(t).reshape(N // 128, 128, nk * 128)


_NC_CACHE = {}


def kernel(x, p, g_mix, w_in, conv_w, conv_b, w_rg_a, b_rg_a, w_rg_x, b_rg_x, lru_lambda, q_gain, k_gain,
           w_rnn_out, w_attn_out, w_o, g_ffn, w_ffn_gu, w_ffn_down, g_ple, w_ple_gate, w_ple_proj):
    f = np.float32
    x = np.asarray(x, f)
    p = np.asarray(p, f)
    w_in0 = np.asarray(w_in, f)[0]
    if "nc" not in _NC_CACHE:
        _NC_CACHE["nc"] = build_program()
    nc = _NC_CACHE["nc"]

    vec128 = np.zeros((128, 50), f)
    vec128[:, 0:16] = np.asarray(g_mix, f)[0].reshape(16, 128).T
    vec128[:, 16:32] = np.asarray(g_ffn, f)[0].reshape(16, 128).T
    vec128[:, 32:48] = np.asarray(g_ple, f)[0].reshape(16, 128).T
    vec128[:, 48] = np.asarray(q_gain, f)[0]
    vec128[:, 49] = np.asarray(k_gain, f)[0]
    cmat = np.zeros((128, 384 + 896), f)
    cmat[:, 0:128] = np.eye(128, dtype=f)
    jj = np.arange(128)[:, None]
    ss = np.arange(128)[None, :]
    cmat[:, 128:256] = (jj >= ss).astype(f)
    cmat[:, 256:384] = (jj < ss).astype(f)
    xx = np.arange(896)[None, :]
    cmat[:, 384:] = np.where(xx <= jj + 384, NEG, 0.0).astype(f)
    wt_g = _tile_w(w_in0[:, 11776:15872], 16)
    wr = np.asarray(w_rnn_out, f)[0]
    wt_rnn = _tile_w(wr, 22).reshape(16, 128, 2, 11 * 128).transpose(0, 2, 1, 3).copy()
    wt_att = _tile_w(np.asarray(w_attn_out, f)[0], 16)
    wt_o = _tile_w(np.asarray(w_o, f)[0], 16)
    wt_gu = _tile_w(np.asarray(w_ffn_gu, f)[0], 16)
    wd = np.asarray(w_ffn_down, f)[0]
    wt_dn = np.stack([_tile_w(wd[h * 2816:(h + 1) * 2816], 22).reshape(16, 128, 2, 11 * 128).transpose(0, 2, 1, 3)
                      for h in range(2)]).copy()
    wt_pg = _tile_w(np.asarray(w_ple_gate, f)[0], 16)
    wpp = np.asarray(w_ple_proj, f)[0]
    wt_pp = _tile_w(wpp, 2)
    cw = np.asarray(conv_w, f)[0]
    cbv = np.asarray(conv_b, f)[0]
    ba = np.asarray(b_rg_a, f)[0]
    bx = np.asarray(b_rg_x, f)[0]
    lam = np.asarray(lru_lambda, f)[0]
    wa_all = np.asarray(w_rg_a, f)[0]
    wx_all = np.asarray(w_rg_x, f)[0]

    in_maps = []
    for c in range(8):
        b, g = c // 4, c % 4
        xb = x[b]
        xTt = np.ascontiguousarray(xb.reshape(32, 128, 16, 128).transpose(0, 3, 2, 1)).reshape(32, 128, 16 * 128)
        xo = xb[1024 * g:1024 * (g + 1)]
        xo2 = np.ascontiguousarray(xo.reshape(2, 512, 16, 128).transpose(2, 0, 3, 1))
        pb = p[0, b, 1024 * g:1024 * (g + 1)]
        pTt = np.ascontiguousarray(pb.reshape(1024, 2, 128).transpose(2, 1, 0)).reshape(128, 2 * 1024)
        w1a = np.concatenate([w_in0[:, 704 * g:704 * (g + 1)], w_in0[:, 2816 + 704 * g:2816 + 704 * (g + 1)]], axis=1)
        w1a = np.ascontiguousarray(w1a.reshape(16, 128, 1408).transpose(1, 0, 2)).reshape(128, 16 * 1408)
        w1b = np.concatenate([w_in0[:, 5632 + 512 * g:5632 + 512 * (g + 1)], w_in0[:, 7680 + 512 * g:7680 + 512 * (g + 1)],
                              w_in0[:, 9728 + 512 * g:9728 + 512 * (g + 1)]], axis=1)
        w1b = np.ascontiguousarray(w1b.reshape(16, 128, 1536).transpose(1, 0, 2)).reshape(128, 16 * 1536)
        ch = slice(704 * g, 704 * (g + 1))
        v88 = np.zeros((88, 8, 8), f)
        for k in range(4):
            v88[:, :, k] = cw[k, ch].reshape(8, 88).T
        v88[:, :, 4] = cbv[ch].reshape(8, 88).T
        v88[:, :, 5] = ba[ch].reshape(8, 88).T
        v88[:, :, 6] = bx[ch].reshape(8, 88).T
        v88[:, :, 7] = lam[ch].reshape(8, 88).T
        wax = np.stack([wa_all[4 * g:4 * g + 4], wx_all[4 * g:4 * g + 4]])
        wax = np.ascontiguousarray(wax.reshape(2, 4, 2, 88, 176).transpose(3, 0, 1, 2, 4)).reshape(88, 2 * 4 * 2 * 176)
        in_maps.append(dict(xTt=xTt, xo2=xo2, pTt=pTt, w1a=w1a, w1b=w1b,
                            vec88=v88.reshape(88, 64), vec128=vec128, wax=wax, cmat=cmat, wt_g=wt_g, wt_rnn=wt_rnn,
                            wt_att=wt_att, wt_o=wt_o, wt_gu=wt_gu, wt_dn=wt_dn, wt_pg=wt_pg, wt_pp=wt_pp))
    res = run_bass_kernel_spmd(nc, in_maps, core_ids=list(range(8)))
    out = np.empty((2, S, D), f)
    for c in range(8):
        b, g = c // 4, c % 4
        oT = np.asarray(res.results[c]["outT"], f).reshape(2048, 1024)
        out[b, 1024 * g:1024 * (g + 1), :] = oT.T
    return out
```

```python
import numpy as np
from contextlib import ExitStack
import concourse.bass as bass
import concourse.mybir as mybir
from concourse.bass_utils import run_bass_kernel_spmd

F32 = mybir.dt.float32
BF16 = mybir.dt.bfloat16
U8 = mybir.dt.uint8
AF = mybir.ActivationFunctionType
ALU = mybir.AluOpType

S = 4096
D = 2048
NKC = 16
TC = 512
NCH = S // TC
SUB = 128
EPS = 1e-6
QSCALE = 1.0 / np.sqrt(128.0)
NEG = -30000.0
ARENA = 198 * 1024
GELU_C = 0.7978845608028654

ENGS = ["pe", "act", "dve", "pool", "sp"]


class Slot:
    __slots__ = ("name", "w", "r", "excl")

    def __init__(self, name="", excl=False):
        self.name = name
        self.w = None
        self.r = []
        self.excl = excl


class Op:
    __slots__ = ("eng", "fn", "deps", "needs_inc", "seq", "kind", "sem", "semv", "pos", "raw", "cost", "idx",
                 "nd", "rt", "fin", "succ")


class FW:
    NDS = 8

    def __init__(self):
        self.ops = {e: [] for e in ENGS}
        self.cur_barrier = {e: None for e in ENGS}
        self.since_barrier = []
        self.cc_count = 0
        self.stopped = False
        self.all = []

    def stop_if(self, tag):
        import os
        if os.environ.get("KSTOP", "") == tag:
            self.stopped = True

    DEFCOST = {"pe": 4.6, "act": 0.65, "dve": 0.65, "pool": 5.0, "sp": 3.0}

    def op(self, eng, fn, reads=(), writes=(), kind="c", cost=None):
        o = Op()
        o.cost = cost if cost is not None else (30.0 if kind == "cc" else self.DEFCOST[eng])
        if self.stopped:
            o.eng, o.fn, o.kind, o.deps, o.raw, o.pos = eng, fn, "skip", set(), set(), -1
            return o
        o.eng, o.fn, o.kind = eng, fn, kind
        o.needs_inc = False
        o.seq = 0
        o.sem = None
        o.semv = 0
        deps = set()
        raw = set()
        for s in reads:
            if s.w is not None:
                deps.add(s.w)
                raw.add(s.w)
            if s.excl:
                deps.update(s.r)
        for s in writes:
            if s.w is not None:
                deps.add(s.w)
            deps.update(s.r)
        for s in reads:
            s.r.append(o)
        for s in writes:
            s.w = o
            s.r = []
        if self.cur_barrier[eng] is not None:
            deps.add(self.cur_barrier[eng])
        if kind == "cc":
            o.sem = ("cc", eng, self.cc_count)
            self.cc_count += 1
            o.semv = 1
        deps.discard(o)
        o.deps = deps
        o.raw = raw
        o.pos = len(self.ops[eng])
        self.ops[eng].append(o)
        o.idx = len(self.all)
        self.all.append(o)
        self.since_barrier.append(o)
        return o

    def schedule(self, reorder=("pe", "act", "dve", "sp"), W=200):
        import bisect
        for o in self.all:
            o.succ = []
        for o in self.all:
            o.nd = len(o.deps)
            o.rt = 0.0
            for d in o.deps:
                d.succ.append(o)
        blev = {}
        for o in reversed(self.all):
            m = 0.0
            for c in o.succ:
                v = blev[id(c)]
                if v > m:
                    m = v
            blev[id(o)] = m + o.cost
        import os as _os3
        use_bl = _os3.environ.get("KPRIO", "bl") == "bl"
        for o in self.all:
            o.seq = (-blev[id(o)] if use_bl else 0.0, o.idx)
        avail = {e: [] for e in ENGS}
        ptr = {e: 0 for e in ENGS}
        free_at = {e: 0.0 for e in ENGS}
        order = {e: [] for e in ENGS}
        for o in self.all:
            if o.nd == 0 and o.eng in reorder:
                bisect.insort(avail[o.eng], (o.seq, o))
        remaining = len(self.all)
        while remaining:
            best = None
            for e in ENGS:
                if e in reorder:
                    lst = avail[e]
                    if not lst:
                        continue
                    fa = free_at[e]
                    cb = None
                    for (ix, o) in lst[:W]:
                        st = o.rt if o.rt > fa else fa
                        if cb is None or st < cb[0] - 1e-9:
                            cb = (st, ix, o)
                        if st <= fa:
                            break
                else:
                    if ptr[e] >= len(self.ops[e]):
                        continue
                    o = self.ops[e][ptr[e]]
                    if o.nd > 0:
                        continue
                    st = max(o.rt, free_at[e])
                    cb = (st, o.seq, o)
                if best is None or (cb[0], cb[1]) < (best[0], best[1]):
                    best = cb
            assert best is not None, "scheduler stuck"
            st, ix, o = best
            e = o.eng
            if e in reorder:
                i = bisect.bisect_left(avail[e], (ix, ))
                assert avail[e][i][1] is o
                avail[e].pop(i)
            else:
                ptr[e] += 1
            if o.kind in ("dma", "cc"):
                free_at[e] = st + (1.0 if e == "pool" else 0.15)
            else:
                free_at[e] = st + o.cost
            o.fin = st + o.cost
            order[e].append(o)
            remaining -= 1
            for c in o.succ:
                c.nd -= 1
                if o.fin > c.rt:
                    c.rt = o.fin
                if c.nd == 0 and c.eng in reorder:
                    bisect.insort(avail[c.eng], (c.seq, c))
        for e in ENGS:
            self.ops[e] = order[e]
            for i, o in enumerate(order[e]):
                o.pos = i
        self.makespan = max(o.fin for o in self.all)

    def barrier(self, engines):
        if self.stopped:
            return
        prior = list(self.since_barrier)
        b = self.op("dve", lambda e: e.nop(), cost=0.1)
        b.deps.update(prior)
        b.deps.discard(b)
        self.since_barrier = [b]
        for e in engines:
            self.cur_barrier[e] = b

    def finalize(self):
        for e in ENGS:
            n = 0
            hist = []
            for o in self.ops[e]:
                if o.kind != "dma":
                    continue
                o.sem = ("dma", e, n % self.NDS)
                o.semv = 16 * (n // self.NDS + 1)
                if n >= self.NDS:
                    o.deps.add(hist[n - self.NDS])
                hist.append(o)
                n += 1
        for e in ENGS:
            for o in self.ops[e]:
                for d in o.deps:
                    if d.kind != "c":
                        continue
                    if d.eng != o.eng:
                        d.needs_inc = True
                    elif d in o.raw and o.pos - d.pos <= 1:
                        d.needs_inc = True
        for e in ENGS:
            n = 0
            for o in self.ops[e]:
                if o.kind == "c" and o.needs_inc:
                    n += 1
                    o.seq = n
                    o.sem = ("eng", e)
                    o.semv = n

    def emit(self, eng, handle, sems):
        waited = {}
        for o in self.ops[eng]:
            need = {}
            for d in o.deps:
                if d.kind == "c":
                    if d.eng == eng and not (d in o.raw and o.pos - d.pos <= 1):
                        continue
                if need.get(d.sem, 0) < d.semv:
                    need[d.sem] = d.semv
            for k, v in need.items():
                if waited.get(k, 0) >= v:
                    continue
                waited[k] = v
                handle.wait_ge(sems[k], v)
            inst = o.fn(handle)
            if o.kind == "dma":
                inst.then_inc(sems[o.sem], 16)
            elif o.kind == "cc":
                inst.then_inc(sems[o.sem], 1)
            elif o.needs_inc:
                inst.then_inc(sems[o.sem], 1)


class Arena:
    def __init__(self, ap):
        self.ap = ap
        self.top = 0

    def alloc(self, parts, free, dtype):
        esz = 4 if dtype == F32 else 2
        n = int(np.prod(free)) * esz
        off = (self.top + 63) // 64 * 64
        self.top = off + n
        assert self.top <= ARENA, f"arena overflow {self.top}"
        v = self.ap[0:parts, off:off + n].bitcast(dtype)
        if len(free) == 2:
            v = v.rearrange("p (a b) -> p a b", a=free[0])
        elif len(free) == 3:
            v = v.rearrange("p (a b c) -> p a b c", a=free[0], b=free[1])
        return v


def arena_at(arena, off, parts, free, dtype):
    esz = 4 if dtype == F32 else 2
    n = int(np.prod(free)) * esz
    assert off % 64 == 0 and off + n <= ARENA
    v = arena.ap[0:parts, off:off + n].bitcast(dtype)
    if len(free) == 2:
        v = v.rearrange("p (a b) -> p a b", a=free[0])
    arena.top = max(arena.top, off + n)
    return v


class Ring:
    def __init__(self, items):
        self.items = items
        self.i = 0

    def next(self):
        it = self.items[self.i % len(self.items)]
        self.i += 1
        return it


def build_program():
    nc = bass.Bass("TRN2", target_bir_lowering=False)

    def din(name, shape):
        return nc.dram_tensor(name, shape, F32, kind="ExternalInput").ap()

    xTt = din("xTt", [S // SUB, 128, NKC * SUB])
    xo2 = din("xo2", [16, 2, 128, 512])
    pTt = din("pTt", [128, 2 * 1024])
    w1a = din("w1a", [128, NKC * 1408])
    w1b = din("w1b", [128, NKC * 1536])
    vec88 = din("vec88", [88, 64])
    vec128 = din("vec128", [128, 50])
    wax = din("wax", [88, 2 * 4 * 2 * 176])
    cmat = din("cmat", [128, 384 + 896])
    wt_g = din("wt_g", [32, 128, 2048])
    wt_rnn = din("wt_rnn", [16, 2, 128, 11 * 128])
    wt_att = din("wt_att", [16, 128, 2048])
    wt_o = din("wt_o", [16, 128, 2048])
    wt_gu = din("wt_gu", [88, 128, 2048])
    wt_dn = din("wt_dn", [2, 16, 2, 128, 11 * 128])
    wt_pg = din("wt_pg", [16, 128, 2048])
    wt_pp = din("wt_pp", [16, 128, 2 * 128])
    outT = nc.dram_tensor("outT", [16, 128, 1024], F32, kind="ExternalOutput").ap()
    cin_yr = nc.dram_tensor("cin_yr", [NCH, 704, TC], BF16)
    cout_yr = nc.dram_tensor("cout_yr", [NCH, 2816, TC], BF16)
    cin_o = nc.dram_tensor("cin_o", [NCH, 512, TC], BF16)
    cout_o = nc.dram_tensor("cout_o", [NCH, 2048, TC], BF16)
    RG = [[0, 1, 2, 3], [4, 5, 6, 7]]
    hbuf = nc.dram_tensor("hbuf", [NCH, 128, NKC * TC], BF16)
    s_hbuf = [Slot() for _ in range(NCH)]
    s_cyr = [Slot() for _ in range(NCH)]
    s_co = [Slot() for _ in range(NCH)]

    fw = FW()
    es = ExitStack()
    with es:
        arena_t = es.enter_context(nc.sbuf_tensor("arena", [128, ARENA], U8))
        cst_t = es.enter_context(nc.sbuf_tensor("cst", [128, 2432], F32))
        psb2 = [es.enter_context(nc.psum_tensor(f"ps{i}", [128, 1024], F32)) for i in range(4)]
        psb = [psb2[i // 2][:, (i % 2) * 512:(i % 2 + 1) * 512] for i in range(8)]
        ps_slots = [Slot(f"ps{i}", excl=True) for i in range(8)]

        cst = cst_t
        c_vec128 = cst[:, 0:50]
        c_vec88 = cst[0:88, 64:128]
        c_der88 = cst[0:88, 128:160]
        c_num = cst[:, 160:168]
        cb = cst[:, 192:192 + 768].bitcast(BF16)
        c_ident = cb[:, 0:128]
        c_L = cb[:, 128:256]
        c_SU = cb[:, 256:384]
        c_mask = cb[:, 384:384 + 896]
        c_onesm = cb[:, 1280:1408]
        c_onesh = cb[:, 1408:1536]
        c_halo = cst[0:88, 968:968 + 24].rearrange("p (a b) -> p a b", a=8)
        c_hlast = cst[0:88, 1000:1008]
        c_wax = cst[0:88, 1016:1016 + 1408].bitcast(BF16).rearrange("p (w n i c) -> p w n i c", w=2, n=4, i=2)
        s_const = Slot("const")
        s_halo = [Slot() for _ in range(4)]
        s_hlast = [Slot() for _ in range(8)]
        EPS_AP = c_num[:, 0:1]
        ONE_AP = c_num[:, 1:2]

        def g128(col):
            return c_vec128[:, col:col + 1]

        fw.op("sp", lambda e: e.dma_start(out=c_vec128, in_=vec128[:, :]), writes=[s_const], kind="dma")
        fw.op("sp", lambda e: e.dma_start(out=c_vec88, in_=vec88[:, :]), writes=[s_const], kind="dma")
        fw.op("pool", lambda e: e.dma_start(out=cb[:, 0:1280], in_=cmat[:, :]), writes=[s_const], kind="dma")
        fw.op("pool", lambda e: e.dma_start(out=c_wax.rearrange("p w n i c -> p (w n i c)"), in_=wax[:, :]),
              writes=[s_const], kind="dma")

        def setup_consts(e):
            e.memset(c_num[:, 0:1], EPS)
            e.memset(c_num[:, 1:2], 1.0)
            e.memset(c_num[:, 2:3], 0.0)
            e.memset(c_onesm, 1.0 / 2048.0)
            e.memset(c_onesh, 1.0 / 128.0)
            e.memset(c_halo, 0.0)
            return e.memset(c_hlast, 0.0)
        fw.op("dve", setup_consts, writes=[s_const] + s_halo + s_hlast)

        v88 = c_vec88.rearrange("p (c f) -> p c f", c=8)
        hba = c_der88[:, 0:8]
        hbx = c_der88[:, 8:16]
        scl = c_der88[:, 16:24]
        tmp8 = c_der88[:, 24:32]
        fw.op("act", lambda e: e.activation(out=tmp8, in_=v88[:, :, 7], func=AF.Exp, scale=-1.0),
              reads=[s_const], writes=[s_const])
        fw.op("dve", lambda e: e.tensor_scalar(out=hba, in0=v88[:, :, 5], scalar1=0.5, scalar2=None, op0=ALU.mult),
              reads=[s_const], writes=[s_const])
        fw.op("act", lambda e: e.activation(out=scl, in_=tmp8, func=AF.Ln, bias=ONE_AP[0:88, :], scale=1.0),
              reads=[s_const], writes=[s_const])
        fw.op("dve", lambda e: e.tensor_scalar(out=hbx, in0=v88[:, :, 6], scalar1=0.5, scalar2=None, op0=ALU.mult),
              reads=[s_const], writes=[s_const])
        fw.op("dve", lambda e: e.tensor_scalar(out=scl, in0=scl, scalar1=-4.0, scalar2=None, op0=ALU.mult),
              reads=[s_const], writes=[s_const])

        arena_ap = arena_t[:, :]
        fw.stop_if("setup")

        def rmsnorm_sub(load_fn, src_ap, src_slots, gcol0, dst_ap, dst_slot, ntok, scratch, psring, sq_eng="act"):
            sq, s_sq, lnm, s_ln = scratch
            if load_fn is not None:
                load_fn()
            if sq_eng == "act":
                fw.op("act", lambda e: e.activation(out=sq, in_=src_ap, func=AF.Square),
                      reads=src_slots, writes=[s_sq], cost=2.0)
            else:
                fw.op("dve", lambda e: e.tensor_tensor(out=sq, in0=src_ap, in1=src_ap, op=ALU.mult),
                      reads=src_slots, writes=[s_sq], cost=2.4)
            ps, s_ps = psring.next()

            def f_mm(e):
                for kc in range(NKC):
                    last = e.matmul(ps[:, 0:ntok], lhsT=c_onesm, rhs=sq[:, kc, :], start=(kc == 0), stop=(kc == NKC - 1))
                return last
            fw.op("pe", f_mm, reads=[s_sq, s_const], writes=[s_ps], cost=16 * 0.12)
            fw.op("act", lambda e: e.activation(out=lnm[:, 0:ntok], in_=ps[:, 0:ntok], func=AF.Ln, bias=EPS_AP, scale=1.0),
                  reads=[s_ps], writes=[s_ln])
            fw.op("act", lambda e: e.activation(out=lnm[:, ntok:2 * ntok], in_=lnm[:, 0:ntok], func=AF.Exp, scale=-0.5),
                  reads=[s_ln], writes=[s_ln])

            def f_h(e):
                for kc in range(NKC):
                    last = e.scalar_tensor_tensor(out=dst_ap[:, kc, :], in0=src_ap[:, kc, :], scalar=g128(gcol0 + kc),
                                                  in1=lnm[:, ntok:2 * ntok], op0=ALU.mult, op1=ALU.mult)
                return last
            fw.op("dve", f_h, reads=list(src_slots) + [s_ln, s_const], writes=[dst_slot], cost=16 * 0.22)

        A = Arena(arena_ap)
        W1a = A.alloc(128, (NKC, 1408), BF16)
        s_W1a = Slot("W1a")
        off_xs = A.top
        xs_r = [(A.alloc(128, (NKC, SUB), F32), Slot()) for _ in range(2)]
        xsR = Ring(xs_r)
        sq_a = A.alloc(128, (NKC, SUB), BF16)
        lnm_a = A.alloc(128, (2 * SUB,), F32)
        scratchA = (sq_a, Slot(), lnm_a, Slot())
        off_hT0 = A.top
        hT_ring = Ring([(A.alloc(128, (NKC, TC), BF16), [Slot() for _ in range(TC // SUB)]) for _ in range(2)])
        hT0_slots_A = hT_ring.items[0][1]
        xs_slots_A = [xs_r[0][1], xs_r[1][1]]
        yrb = [(A.alloc(88, (8, TC), BF16), Slot()) for _ in range(2)]
        NSET = 2
        sets = []
        for _ in range(NSET):
            d = {}
            d["xr"] = A.alloc(88, (2, TC + 3), F32)
            d["gr"] = A.alloc(88, (2, TC), F32)
            d["xc"] = A.alloc(88, (2, TC), F32)
            d["xcb"] = A.alloc(88, (2, TC), BF16)
            for nm in ("rp", "ip", "a", "t", "s", "h"):
                d[nm] = A.alloc(88, (2, TC), F32)
            d["slots"] = {nm: [Slot(), Slot()] for nm in ("xr", "gr", "xc", "xcb", "rp", "ip", "a", "t", "s", "h")}
            sets.append(d)
        psA = Ring(list(zip([p[:, :] for p in psb[0:6]], ps_slots[0:6])))
        psA_ms = Ring(list(zip([p[:, :] for p in psb[6:8]], ps_slots[6:8])))

        for q4 in range(4):
            fw.op("pool", lambda e, q4=q4: e.dma_start(
                out=W1a[:, 4 * q4:4 * q4 + 4, :],
                in_=w1a[:, 4 * q4 * 1408:(4 * q4 + 4) * 1408].rearrange("p (k n) -> p k n", k=4)),
                writes=[s_W1a], kind="dma")

        def proj_fm(Wt, s_W, col0, M, ps, s_ps, h_ap, h_slots, ntok=TC):
            def f(e):
                for kc in range(NKC):
                    last = e.matmul(ps[0:M, 0:ntok], lhsT=Wt[:, kc, col0:col0 + M], rhs=h_ap[:, kc, :],
                                    start=(kc == 0), stop=(kc == NKC - 1))
                return last
            fw.op("pe", f, reads=[s_W] + list(h_slots), writes=[s_ps])

        def hT_chunk(G, xsrc, scratch, psring, sq_eng="act"):
            hT, s_hT = hT_ring.next()
            for sb in range(TC // SUB):
                sc = G * (TC // SUB) + sb
                xs, s_xs = xsR.next()
                fw.op("sp", lambda e, xs=xs, sc=sc: e.dma_start(out=xs.rearrange("p k t -> p (k t)"), in_=xsrc[sc]),
                      writes=[s_xs], kind="dma")
                rmsnorm_sub(None, xs, [s_xs], 0, hT[:, :, sb * SUB:(sb + 1) * SUB], s_hT[sb], SUB, scratch, psring, sq_eng)
            return hT, s_hT

        Wk = arena_at(A, off_xs, 128, (NKC, 512), BF16)
        Wq = arena_at(A, off_hT0, 128, (NKC, 512), BF16)
        s_Wq, s_Wk, s_Wv = Slot("Wq"), Slot("Wk"), Slot("Wv")
        w1b_v = w1b.rearrange("p (k n) -> p k n", k=NKC)
        for G in range(NCH):
            hT, s_hT = hT_chunk(G, xTt, scratchA, psA_ms)
            if G == NCH - 1 and not fw.stopped:
                fw.op("pool", lambda e: e.dma_start(out=Wq, in_=w1b_v[:, :, 0:512]), writes=[s_Wq] + hT0_slots_A, kind="dma", cost=12.0)
                fw.op("pool", lambda e: e.dma_start(out=Wk, in_=w1b_v[:, :, 512:1024]), writes=[s_Wk] + xs_slots_A, kind="dma", cost=12.0)
            fw.op("sp", lambda e, hT=hT, G=G: e.dma_start(out=hbuf.ap()[G], in_=hT.rearrange("p k t -> p (k t)")),
                  reads=s_hT, writes=[s_hbuf[G]], kind="dma")
            yb, s_yb = yrb[G % 2]
            for n in range(4):
                st = sets[(G * 4 + n) % NSET]
                sl = st["slots"]
                for j in range(2):
                    c8 = 2 * n + j
                    ps, s_ps = psA.next()
                    proj_fm(W1a, s_W1a, c8 * 88, 88, ps, s_ps, hT, s_hT)
                    fw.op("act", lambda e, ps=ps, st=st, j=j: e.activation(out=st["xr"][:, j, 3:TC + 3], in_=ps[0:88, :], func=AF.Copy),
                          reads=[s_ps], writes=[sl["xr"][j]])
                    ps, s_ps = psA.next()
                    proj_fm(W1a, s_W1a, 704 + c8 * 88, 88, ps, s_ps, hT, s_hT)
                    fw.op("act", lambda e, ps=ps, st=st, j=j: e.activation(out=st["gr"][:, j, :], in_=ps[0:88, :], func=AF.Copy),
                          reads=[s_ps], writes=[sl["gr"][j]])
                fw.op("dve", lambda e, st=st, n=n: e.tensor_copy(out=st["xr"][:, :, 0:3], in_=c_halo[:, 2 * n:2 * n + 2, :]),
                      reads=[s_halo[n]], writes=sl["xr"])
                fw.op("dve", lambda e, st=st, n=n: e.tensor_copy(out=c_halo[:, 2 * n:2 * n + 2, :], in_=st["xr"][:, :, TC:TC + 3]),
                      reads=sl["xr"], writes=[s_halo[n]])
                for k in range(4):
                    for j in range(2):
                        c8 = 2 * n + j
                        if k == 0:
                            fw.op("dve", lambda e, st=st, j=j, c8=c8: e.tensor_scalar(
                                out=st["xc"][:, j, :], in0=st["xr"][:, j, 0:TC], scalar1=v88[:, c8, 0:1], scalar2=v88[:, c8, 4:5],
                                op0=ALU.mult, op1=ALU.add), reads=[sl["xr"][j], s_const], writes=[sl["xc"][j]])
                        else:
                            fw.op("dve", lambda e, st=st, j=j, c8=c8, k=k: e.scalar_tensor_tensor(
                                out=st["xc"][:, j, :], in0=st["xr"][:, j, k:k + TC], scalar=v88[:, c8, k:k + 1], in1=st["xc"][:, j, :],
                                op0=ALU.mult, op1=ALU.add), reads=[sl["xr"][j], sl["xc"][j], s_const], writes=[sl["xc"][j]])
                for j in range(2):
                    fw.op("act", lambda e, st=st, j=j: e.activation(out=st["xcb"][:, j, :], in_=st["xc"][:, j, :], func=AF.Copy),
                          reads=[sl["xc"][j]], writes=[sl["xcb"][j]])
                gps = []
                for j in range(2):
                    for w in range(2):
                        ps, s_ps = psA.next()

                        def f(e, ps=ps, st=st, j=j, w=w, n=n):
                            for i in range(2):
                                last = e.matmul(ps[0:88, :], lhsT=c_wax[:, w, n, i, 88 * j:88 * j + 88], rhs=st["xcb"][:, i, :],
                                                start=(i == 0), stop=(i == 1))
                            return last
                        fw.op("pe", f, reads=sl["xcb"] + [s_const], writes=[s_ps], cost=0.9)
                        gps.append((ps, s_ps))
                for j in range(2):
                    c8 = 2 * n + j
                    ps, s_ps = gps[2 * j]
                    fw.op("act", lambda e, ps=ps, st=st, j=j, c8=c8: e.activation(
                        out=st["rp"][:, j, :], in_=ps[0:88, :], func=AF.Tanh, bias=hba[:, c8:c8 + 1], scale=0.5),
                        reads=[s_ps, s_const], writes=[sl["rp"][j]])
                    ps, s_ps = gps[2 * j + 1]
                    fw.op("act", lambda e, ps=ps, st=st, j=j, c8=c8: e.activation(
                        out=st["ip"][:, j, :], in_=ps[0:88, :], func=AF.Tanh, bias=hbx[:, c8:c8 + 1], scale=0.5),
                        reads=[s_ps, s_const], writes=[sl["ip"][j]])
                for j in range(2):
                    c8 = 2 * n + j
                    fw.op("act", lambda e, st=st, j=j, c8=c8: e.activation(
                        out=st["a"][:, j, :], in_=st["rp"][:, j, :], func=AF.Exp, bias=scl[:, c8:c8 + 1], scale=scl[:, c8:c8 + 1]),
                        reads=[sl["rp"][j], s_const], writes=[sl["a"][j]])
                for j in range(2):
                    fw.op("dve", lambda e, st=st, j=j: e.scalar_tensor_tensor(
                        out=st["t"][:, j, :], in0=st["ip"][:, j, :], scalar=1.0, in1=st["xc"][:, j, :], op0=ALU.add, op1=ALU.mult),
                        reads=[sl["ip"][j], sl["xc"][j]], writes=[sl["t"][j]])
                for j in range(2):
                    fw.op("act", lambda e, st=st, j=j: e.activation(out=st["s"][:, j, :], in_=st["a"][:, j, :], func=AF.Square),
                          reads=[sl["a"][j]], writes=[sl["s"][j]])
                for j in range(2):
                    fw.op("act", lambda e, st=st, j=j: e.activation(out=st["s"][:, j, :], in_=st["s"][:, j, :], func=AF.Sqrt,
                                                                    bias=ONE_AP[0:88, :], scale=-1.0),
                          reads=[sl["s"][j], s_const], writes=[sl["s"][j]])
                for j in range(2):
                    fw.op("dve", lambda e, st=st, j=j: e.scalar_tensor_tensor(
                        out=st["t"][:, j, :], in0=st["s"][:, j, :], scalar=0.5, in1=st["t"][:, j, :], op0=ALU.mult, op1=ALU.mult),
                        reads=[sl["s"][j], sl["t"][j]], writes=[sl["t"][j]])
                for j in range(2):
                    c8 = 2 * n + j
                    fw.op("dve", lambda e, st=st, j=j, c8=c8: e.tensor_tensor_scan(
                        out=st["h"][:, j, :], data0=st["a"][:, j, :], data1=st["t"][:, j, :], initial=c_hlast[:, c8:c8 + 1],
                        op0=ALU.mult, op1=ALU.add), reads=[sl["a"][j], sl["t"][j], s_hlast[c8]], writes=[sl["h"][j]])
                for j in range(2):
                    c8 = 2 * n + j
                    fw.op("act", lambda e, st=st, j=j, c8=c8: e.activation(out=c_hlast[:, c8:c8 + 1], in_=st["h"][:, j, TC - 1:TC], func=AF.Copy),
                          reads=[sl["h"][j]], writes=[s_hlast[c8]])
                for j in range(2):
                    fw.op("dve", lambda e, st=st, j=j: e.tensor_tensor(out=st["ip"][:, j, :], in0=st["gr"][:, j, :], in1=st["gr"][:, j, :], op=ALU.mult),
                          reads=[sl["gr"][j]], writes=[sl["ip"][j]])
                for j in range(2):
                    fw.op("dve", lambda e, st=st, j=j: e.tensor_scalar(out=st["ip"][:, j, :], in0=st["ip"][:, j, :], scalar1=0.044715, scalar2=1.0,
                                                                      op0=ALU.mult, op1=ALU.add), reads=[sl["ip"][j]], writes=[sl["ip"][j]])
                for j in range(2):
                    fw.op("dve", lambda e, st=st, j=j: e.tensor_tensor(out=st["ip"][:, j, :], in0=st["ip"][:, j, :], in1=st["gr"][:, j, :], op=ALU.mult),
                          reads=[sl["ip"][j], sl["gr"][j]], writes=[sl["ip"][j]])
                for j in range(2):
                    fw.op("act", lambda e, st=st, j=j: e.activation(out=st["rp"][:, j, :], in_=st["ip"][:, j, :], func=AF.Tanh, scale=GELU_C),
                          reads=[sl["ip"][j]], writes=[sl["rp"][j]])
                for j in range(2):
                    fw.op("dve", lambda e, st=st, j=j: e.scalar_tensor_tensor(
                        out=st["ip"][:, j, :], in0=st["rp"][:, j, :], scalar=1.0, in1=st["gr"][:, j, :], op0=ALU.add, op1=ALU.mult),
                        reads=[sl["rp"][j], sl["gr"][j]], writes=[sl["ip"][j]])
                for j in range(2):
                    c8 = 2 * n + j
                    fw.op("dve", lambda e, st=st, j=j, c8=c8, yb=yb: e.scalar_tensor_tensor(
                        out=yb[:, c8, :], in0=st["ip"][:, j, :], scalar=0.5, in1=st["h"][:, j, :], op0=ALU.mult, op1=ALU.mult),
                        reads=[sl["ip"][j], sl["h"][j]], writes=[s_yb])
            s_ci = Slot()
            fw.op("sp", lambda e, yb=yb, G=G: e.dma_start(
                out=cin_yr.ap()[G].rearrange("(c p) t -> p c t", p=88), in_=yb),
                reads=[s_yb], writes=[s_ci], kind="dma")
            fw.op("pool", lambda e, G=G: e.collective_compute("AllGather", ALU.bypass, replica_groups=RG,
                                                              ins=[cin_yr.ap()[G].opt()], outs=[cout_yr.ap()[G].opt()]),
                  reads=[s_ci], writes=[s_cyr[G]], kind="cc")
            if G == 0:
                fw.stop_if("passA1")
        fw.stop_if("passA")


        fw.stop_if("ag1")
        B = Arena(arena_ap)
        assert off_xs == 45056 and off_hT0 == 66560, (off_xs, off_hT0)
        KT = arena_at(B, 0, 128, (4, S), BF16)
        s_KT = [[Slot() for _ in range(NCH)] for _ in range(4)]
        e_r = [(arena_at(B, 32768 + 4096 * i, 128, (2, TC), F32), Slot()) for i in range(3)]
        nrm = [dict(raw=arena_at(B, 61440, 128, (TC,), F32), sq=arena_at(B, 63488, 128, (TC,), BF16),
                    ln=arena_at(B, 64512, 128, (TC,), F32), s_raw=Slot(), s_sq=Slot(), s_ln=Slot())]
        B.top = off_hT0 + 16384
        Wv = B.alloc(128, (NKC, 512), BF16)
        Vt = B.alloc(128, (S // 128, 512), BF16)
        s_V = [Slot() for _ in range(S // 128)]
        hT_ring.items = [(B.alloc(128, (NKC, TC), BF16), [Slot() for _ in range(TC // SUB)]) for _ in range(2)]
        QT_ring = Ring([(B.alloc(128, (4, TC), BF16), [Slot() for _ in range(4)]) for _ in range(2)])
        nrm.append(dict(raw=B.alloc(128, (TC,), F32), sq=B.alloc(128, (TC,), BF16), ln=B.alloc(128, (TC,), F32),
                        s_raw=Slot(), s_sq=Slot(), s_ln=Slot()))
        nrmR = Ring(nrm)
        sp_r = [(B.alloc(128, (2, TC), BF16), Slot()) for _ in range(3)]
        x_r = [(B.alloc(128, (2, TC), F32), Slot()) for _ in range(2)]
        w_r = [(B.alloc(128, (2, TC), BF16), Slot()) for _ in range(3)]
        psz2 = psb2[0][:, :].rearrange("p (a b) -> p a b", a=2)
        psc2 = psb2[1][:, :].rearrange("p (a b) -> p a b", a=2)
        obf = [(B.alloc(128, (TC,), BF16), Slot()) for _ in range(2)]
        obfR = Ring(obf)
        psz = [(psb[0][:, :], ps_slots[0]), (psb[1][:, :], ps_slots[1])]
        psc = [(psb[2][:, :], ps_slots[2]), (psb[3][:, :], ps_slots[3])]
        pso = [(psb[4][:, :], ps_slots[4]), (psb[5][:, :], ps_slots[5])]
        psM = Ring([(psb[6][:, :], ps_slots[6]), (psb[7][:, :], ps_slots[7])])

        fw.barrier(ENGS)
        fw.op("pool", lambda e: e.dma_start(out=Wv, in_=w1b_v[:, :, 1024:1536]), writes=[s_Wv], kind="dma", cost=12.0)

        def qk_norm(Wt, s_Wt, col0, gain_col, dst_ap, dst_slot, hT, s_hT):
            nb = nrmR.next()
            ps, s_ps = psM.next()
            proj_fm(Wt, s_Wt, col0, 128, ps, s_ps, hT, s_hT)
            fw.op("dve", lambda e: e.tensor_copy(out=nb["raw"], in_=ps), reads=[s_ps], writes=[nb["s_raw"]])
            fw.op("dve", lambda e: e.tensor_tensor(out=nb["sq"], in0=nb["raw"], in1=nb["raw"], op=ALU.mult),
                  reads=[nb["s_raw"]], writes=[nb["s_sq"]])
            ps2, s_ps2 = psM.next()
            fw.op("pe", lambda e: e.matmul(ps2, lhsT=c_onesh, rhs=nb["sq"], start=True, stop=True),
                  reads=[nb["s_sq"], s_const], writes=[s_ps2], cost=0.3)
            fw.op("act", lambda e: e.activation(out=nb["ln"], in_=ps2, func=AF.Ln, bias=EPS_AP, scale=1.0),
                  reads=[s_ps2], writes=[nb["s_ln"]])
            fw.op("act", lambda e: e.activation(out=nb["ln"], in_=nb["ln"], func=AF.Exp, scale=-0.5),
                  reads=[nb["s_ln"]], writes=[nb["s_ln"]])
            fw.op("dve", lambda e: e.scalar_tensor_tensor(out=dst_ap, in0=nb["raw"], scalar=g128(gain_col), in1=nb["ln"],
                                                          op0=ALU.mult, op1=ALU.mult),
                  reads=[nb["s_raw"], nb["s_ln"], s_const], writes=[dst_slot])

        import os as _os
        for G in range(int(_os.environ.get("KNB", NCH))):
            fw.stop_if("b_w")
            hT, s_hT = hT_ring.next()
            fw.op("sp", lambda e, hT=hT, G=G: e.dma_start(out=hT.rearrange("p k t -> p (k t)"), in_=hbuf.ap()[G]),
                  reads=[s_hbuf[G]], writes=s_hT, kind="dma")
            QT, s_QT = QT_ring.next()
            fw.stop_if("b_h")
            for hh in range(4):
                qk_norm(Wq, s_Wq, hh * 128, 48, QT[:, hh, :], s_QT[hh], hT, s_hT)
                qk_norm(Wk, s_Wk, hh * 128, 49, KT[:, hh, G * TC:(G + 1) * TC], s_KT[hh][G], hT, s_hT)
            fw.stop_if("b_qk")
            for tb in range(4):
                ps, s_ps = psM.next()

                def f(e, ps=ps, tb=tb, hT=hT):
                    for kc in range(NKC):
                        last = e.matmul(ps, lhsT=hT[:, kc, tb * 128:(tb + 1) * 128], rhs=Wv[:, kc, :],
                                        start=(kc == 0), stop=(kc == NKC - 1))
                    return last
                fw.op("pe", f, reads=[s_Wv] + s_hT, writes=[s_ps])
                fw.op("dve", lambda e, ps=ps, tb=tb, G=G: e.tensor_copy(out=Vt[:, 4 * G + tb, :], in_=ps),
                      reads=[s_ps], writes=[s_V[4 * G + tb]])
            fw.stop_if("b_v")
            NT = 4 * G + 4
            s_cio = [Slot() for _ in range(4)]
            for hp in range(2):
                heads = [2 * hp, 2 * hp + 1]
                etile = {}
                sptile = {}
                wtile = {}

                def stageZ(s):
                    kb = NT - 1 - s
                    diag = kb - 4 * G
                    for li, hh in enumerate(heads):
                        ps, s_ps = psz[li]

                        def f(e, ps=ps, hh=hh, kb=kb, diag=diag, QT=QT):
                            last = e.matmul(ps, lhsT=KT[:, hh, kb * 128:(kb + 1) * 128], rhs=QT[:, hh, :], start=True, stop=(diag < 0))
                            if diag >= 0:
                                o0 = 384 - 128 * diag
                                last = e.matmul(ps, lhsT=c_ident, rhs=c_mask[:, o0:o0 + TC], start=False, stop=True)
                            return last
                        fw.op("pe", f, reads=[s_KT[hh][kb // 4], s_QT[hh], s_const], writes=[s_ps], cost=0.6 if diag >= 0 else 0.3)
                    eb, s_eb = e_r[s % 3]
                    etile[s] = (eb, s_eb)
                    fw.op("act", lambda e, eb=eb: e.activation(out=eb, in_=psz2, func=AF.Exp, scale=float(QSCALE)),
                          reads=[psz[0][1], psz[1][1]], writes=[s_eb], cost=1.1)
                    sb_, s_sb = sp_r[s % 3]
                    sptile[s] = (sb_, s_sb)
                    fw.op("act", lambda e, eb=eb, sb_=sb_: e.activation(out=sb_, in_=eb, func=AF.Ln, bias=ONE_AP, scale=1.0),
                          reads=[s_eb, s_const], writes=[s_sb], cost=1.1)

                def stageC(s):
                    sb_, s_sb = sptile[s]
                    for li, hh in enumerate(heads):
                        ps, s_ps = psc[li]
                        rd = [s_sb, s_const]
                        if s > 0:
                            sprev, s_sprev = sptile[s - 1]
                            rd += [s_sprev, s_ps]
                        else:
                            sprev = None

                        def f(e, ps=ps, sb_=sb_, sprev=sprev, s=s, li=li):
                            if s > 0:
                                e.matmul(ps, lhsT=c_SU, rhs=sprev[:, li, :], start=False, stop=False, skip_group_check=True)
                            return e.matmul(ps, lhsT=c_L, rhs=sb_[:, li, :], start=(s == 0), stop=True, skip_group_check=True)
                        fw.op("pe", f, reads=rd, writes=[s_ps], cost=0.6 if s > 0 else 0.3)
                    xb, s_xb = x_r[s % 2]
                    fw.op("act", lambda e, xb=xb: e.activation(out=xb, in_=psc2, func=AF.Exp, scale=-1.0),
                          reads=[psc[0][1], psc[1][1]], writes=[s_xb], cost=1.1)
                    eb, s_eb = etile[s]
                    wb, s_wb = w_r[s % 3]
                    wtile[s] = (wb, s_wb)
                    fw.op("dve", lambda e, xb=xb, eb=eb, wb=wb: e.tensor_tensor(out=wb, in0=eb, in1=xb, op=ALU.mult),
                          reads=[s_xb, s_eb], writes=[s_wb], cost=1.2)

                def stageO(s):
                    kb = NT - 1 - s
                    for li, hh in enumerate(heads):
                        ps, s_ps = pso[li]
                        wb, s_wb = wtile[s]
                        fw.op("pe", lambda e, ps=ps, wb=wb, hh=hh, kb=kb, s=s, li=li: e.matmul(
                            ps, lhsT=Vt[:, kb, hh * 128:(hh + 1) * 128], rhs=wb[:, li, :], start=(s == 0), stop=(s == NT - 1)),
                            reads=[s_wb, s_V[kb], s_ps] if s > 0 else [s_wb, s_V[kb]], writes=[s_ps], cost=0.3)

                for step in range(NT + 2):
                    if step < NT:
                        stageZ(step)
                    if 0 <= step - 1 < NT:
                        stageC(step - 1)
                    if 0 <= step - 2 < NT:
                        stageO(step - 2)
                for li, hh in enumerate(heads):
                    ps, s_ps = pso[li]
                    ob, s_ob = obfR.next()
                    fw.op("dve", lambda e, ps=ps, ob=ob: e.tensor_copy(out=ob, in_=ps), reads=[s_ps], writes=[s_ob])
                    fw.op("sp", lambda e, ob=ob, hh=hh, G=G: e.dma_start(
                        out=cin_o.ap()[G, hh * 128:(hh + 1) * 128, :], in_=ob), reads=[s_ob], writes=[s_cio[hh]], kind="dma")
            fw.op("pool", lambda e, G=G: e.collective_compute("AllGather", ALU.bypass, replica_groups=RG,
                                                              ins=[cin_o.ap()[G].opt()], outs=[cout_o.ap()[G].opt()]),
                  reads=s_cio, writes=[s_co[G]], kind="cc")

        fw.stop_if("passB")

        P = Arena(arena_ap)
        NU = 10
        wunits = [(P.alloc(128, (NKC, 128), BF16), Slot()) for _ in range(NU)]
        wu_i = [0]
        tmpf = [(P.alloc(128, (TC,), F32), Slot()) for _ in range(4)]
        tmpR = Ring(tmpf)
        R1 = P.alloc(128, (NKC, 1024), F32)
        R1b = R1.rearrange("p k t -> p (k t)").bitcast(BF16)
        h_a = R1b[:, 0:NKC * 1024].rearrange("p (k t) -> p k t", k=NKC)
        oT_a = R1b[:, NKC * 1024:2 * NKC * 1024].rearrange("p (k t) -> p k t", k=NKC)
        R2 = P.alloc(128, (22, 1024), BF16)
        R3 = P.alloc(128, (NKC, 1024), BF16)
        pTb = P.alloc(128, (2, 1024), BF16)
        s_xo = [[Slot() for _ in range(2)] for _ in range(NKC)]
        s_R1h = [Slot() for _ in range(8)]
        s_oT = Slot()
        s_R2 = Slot()
        s_R3 = [[Slot() for _ in range(2)] for _ in range(NKC)]
        s_h2 = [Slot() for _ in range(8)]
        s_pT = Slot()
        psP = Ring(list(zip([p[:, :] for p in psb], ps_slots)))
        sq_p = P.alloc(128, (NKC, SUB), BF16)
        lnm_p = P.alloc(128, (2 * SUB,), F32)
        scratchP = (sq_p, Slot(), lnm_p, Slot())
        xs_rp = [(R3.rearrange("p k t -> p (k t)")[:, 12 * 1024:16 * 1024].bitcast(F32).rearrange("p (k t) -> p k t", k=NKC), Slot())]

        fw.barrier(ENGS)

        def wload(src_ap, nk=NKC):
            u, s_u = wunits[wu_i[0] % NU]
            wu_i[0] += 1
            fw.op("pool", lambda e: e.dma_start(out=u[:, 0:nk, :].rearrange("p k n -> p (k n)"), in_=src_ap),
                  writes=[s_u], kind="dma")
            return u, s_u

        def acc_group(ps, s_ps, parts, tb, extra_reads):
            def f(e):
                tot = sum(p[2] for p in parts)
                i = 0
                for (u, s_u, nk, act, k0, rows) in parts:
                    for kc in range(nk):
                        last = e.matmul(ps, lhsT=u[0:rows, kc, :], rhs=act[0:rows, k0 + kc, tb * TC:(tb + 1) * TC],
                                        start=(i == 0), stop=(i == tot - 1))
                        i += 1
                return last
            fw.op("pe", f, reads=[p[1] for p in parts] + list(extra_reads), writes=[s_ps], cost=0.29 * sum(p[2] for p in parts))

        for tb in range(2):
            def f_hh(e, tb=tb):
                rank = e.partition_id() % 4
                return e.dma_start(out=h_a[:, :, tb * TC:(tb + 1) * TC],
                                   in_=hbuf.ap()[bass.ds(2 * rank + tb, 1)].rearrange("o p (k t) -> p (o k) t", k=NKC))
            fw.op("pool", f_hh, reads=s_hbuf, writes=s_R1h[4 * tb:4 * tb + 4], kind="dma")

        for tb in range(2):
            def f_yr(e, tb=tb):
                rank = e.partition_id() % 4
                return e.dma_start(out=R2[:, :, tb * TC:(tb + 1) * TC],
                                   in_=cout_yr.ap()[bass.ds(2 * rank + tb, 1)].rearrange("o (k p) t -> p (o k) t", p=128))
            fw.op("pool", f_yr, reads=s_cyr, writes=[s_R2], kind="dma")

            def f_o(e, tb=tb):
                rank = e.partition_id() % 4
                return e.dma_start(out=oT_a[:, :, tb * TC:(tb + 1) * TC],
                                   in_=cout_o.ap()[bass.ds(2 * rank + tb, 1)].rearrange("o (k p) t -> p (o k) t", p=128))
            fw.op("pool", f_o, reads=s_co, writes=[s_oT], kind="dma")
        fw.op("pool", lambda e: e.dma_start(out=pTb.rearrange("p k t -> p (k t)"), in_=pTt[:, :]), writes=[s_pT], kind="dma")

        fw.stop_if("s0")
        for cc in range(NKC):
            ugr = wload(wt_g[cc])
            uga = wload(wt_g[16 + cc])
            ur0 = wload(wt_rnn[cc, 0], 11)
            ur1 = wload(wt_rnn[cc, 1], 11)
            uat = wload(wt_att[cc])
            for tb in range(2):
                hs = s_R1h[4 * tb:4 * tb + 4]
                p_gr = psP.next()
                acc_group(p_gr[0], p_gr[1], [(ugr[0], ugr[1], NKC, h_a, 0, 128)], tb, hs)
                p_ga = psP.next()
                acc_group(p_ga[0], p_ga[1], [(uga[0], uga[1], NKC, h_a, 0, 128)], tb, hs)
                p_yr = psP.next()
                acc_group(p_yr[0], p_yr[1], [(ur0[0], ur0[1], 11, R2, 0, 128), (ur1[0], ur1[1], 11, R2, 11, 128)], tb, [s_R2])
                p_ya = psP.next()
                acc_group(p_ya[0], p_ya[1], [(uat[0], uat[1], NKC, oT_a, 0, 128)], tb, [s_oT])
                t_sr = tmpR.next()
                t_sa = tmpR.next()
                fw.op("act", lambda e, p=p_gr, t=t_sr: e.activation(out=t[0], in_=p[0], func=AF.Sigmoid), reads=[p_gr[1]], writes=[t_sr[1]])
                fw.op("act", lambda e, p=p_ga, t=t_sa: e.activation(out=t[0], in_=p[0], func=AF.Sigmoid), reads=[p_ga[1]], writes=[t_sa[1]])
                fw.op("dve", lambda e, p=p_yr, t=t_sr: e.tensor_tensor(out=t[0], in0=t[0], in1=p[0], op=ALU.mult),
                      reads=[p_yr[1], t_sr[1]], writes=[t_sr[1]])
                fw.op("dve", lambda e, p=p_ya, t=t_sa: e.tensor_tensor(out=t[0], in0=t[0], in1=p[0], op=ALU.mult),
                      reads=[p_ya[1], t_sa[1]], writes=[t_sa[1]])
                fw.op("dve", lambda e, a=t_sr, b=t_sa, cc=cc, tb=tb: e.tensor_tensor(
                    out=R3[:, cc, tb * TC:(tb + 1) * TC], in0=a[0], in1=b[0], op=ALU.add),
                    reads=[t_sr[1], t_sa[1]], writes=[s_R3[cc][tb]])

        fw.barrier(["pe", "act", "dve", "sp"])
        fw.stop_if("sa")
        mix_slots = [s_R3[c][t] for c in range(NKC) for t in range(2)]
        for cc in range(NKC):
            uo = wload(wt_o[cc])
            for tb in range(2):
                fw.op("sp", lambda e, cc=cc, tb=tb: e.dma_start(out=R1[:, cc, tb * TC:(tb + 1) * TC], in_=xo2[cc, tb]),
                      writes=[s_xo[cc][tb]], kind="dma")
                p = psP.next()
                acc_group(p[0], p[1], [(uo[0], uo[1], NKC, R3, 0, 128)], tb, [s_R3[c][tb] for c in range(NKC)])
                fw.op("dve", lambda e, p=p, cc=cc, tb=tb: e.tensor_tensor(
                    out=R1[:, cc, tb * TC:(tb + 1) * TC], in0=R1[:, cc, tb * TC:(tb + 1) * TC], in1=p[0], op=ALU.add),
                    reads=[p[1], s_xo[cc][tb]], writes=[s_xo[cc][tb]])

        fw.barrier(["pe", "act", "dve", "sp"])

        def rms_resident(gcol0, dst, dst_slots):
            for sc in range(8):
                tb = sc // 4
                rmsnorm_sub(None, R1[:, :, sc * SUB:(sc + 1) * SUB], [s_xo[c][tb] for c in range(NKC)], gcol0,
                            dst[:, :, sc * SUB:(sc + 1) * SUB], dst_slots[sc], SUB, scratchP, psP)

        fw.stop_if("sb")
        rms_resident(16, R3, s_h2)
        for half in range(2):
            for m in range(22):
                mm = half * 22 + m
                ug = wload(wt_gu[mm])
                uu = wload(wt_gu[44 + mm])
                for tb in range(2):
                    hs = s_h2[4 * tb:4 * tb + 4]
                    p_g = psP.next()
                    acc_group(p_g[0], p_g[1], [(ug[0], ug[1], NKC, R3, 0, 128)], tb, hs)
                    p_u = psP.next()
                    acc_group(p_u[0], p_u[1], [(uu[0], uu[1], NKC, R3, 0, 128)], tb, hs)
                    t_s = tmpR.next()
                    fw.op("act", lambda e, p=p_g, t=t_s: e.activation(out=t[0], in_=p[0], func=AF.Silu), reads=[p_g[1]], writes=[t_s[1]])
                    fw.op("dve", lambda e, p=p_u, t=t_s, m=m, tb=tb: e.tensor_tensor(
                        out=R2[:, m, tb * TC:(tb + 1) * TC], in0=t[0], in1=p[0], op=ALU.mult),
                        reads=[p_u[1], t_s[1]], writes=[s_R2])
            for cc in range(NKC):
                ud0 = wload(wt_dn[half, cc, 0], 11)
                ud1 = wload(wt_dn[half, cc, 1], 11)
                for tb in range(2):
                    p = psP.next()
                    acc_group(p[0], p[1], [(ud0[0], ud0[1], 11, R2, 0, 128), (ud1[0], ud1[1], 11, R2, 11, 128)], tb, [s_R2])
                    fw.op("dve", lambda e, p=p, cc=cc, tb=tb: e.tensor_tensor(
                        out=R1[:, cc, tb * TC:(tb + 1) * TC], in0=R1[:, cc, tb * TC:(tb + 1) * TC], in1=p[0], op=ALU.add),
                        reads=[p[1], s_xo[cc][tb]], writes=[s_xo[cc][tb]])
            fw.barrier(["pe", "act", "dve", "sp"])

        fw.stop_if("sc")
        h3 = R2[:, 0:NKC, :]
        s_h3 = [Slot() for _ in range(8)]
        rms_resident(32, h3, s_h3)
        out_ops = []
        for cc in range(NKC):
            upg = wload(wt_pg[cc])
            upp = wload(wt_pp[cc], 2)
            for tb in range(2):
                hs = s_h3[4 * tb:4 * tb + 4]
                p_g = psP.next()
                acc_group(p_g[0], p_g[1], [(upg[0], upg[1], NKC, h3, 0, 128)], tb, hs)
                p_p = psP.next()

                acc_group(p_p[0], p_p[1], [(upp[0], upp[1], 2, pTb, 0, 128)], tb, [s_pT])
                t_s = tmpR.next()
                fw.op("act", lambda e, p=p_g, t=t_s: e.activation(out=t[0], in_=p[0], func=AF.Sigmoid), reads=[p_g[1]], writes=[t_s[1]])
                fw.op("dve", lambda e, p=p_p, t=t_s: e.tensor_tensor(out=t[0], in0=t[0], in1=p[0], op=ALU.mult),
                      reads=[p_p[1], t_s[1]], writes=[t_s[1]])
                fw.op("dve", lambda e, t=t_s, cc=cc, tb=tb: e.tensor_tensor(
                    out=R1[:, cc, tb * TC:(tb + 1) * TC], in0=R1[:, cc, tb * TC:(tb + 1) * TC], in1=t[0], op=ALU.add),
                    reads=[t_s[1], s_xo[cc][tb]], writes=[s_xo[cc][tb]])
            out_ops.append(fw.op("sp", lambda e, cc=cc: e.dma_start(out=outT[cc], in_=R1[:, cc, :]),
                                 reads=[s_xo[cc][0], s_xo[cc][1]], kind="dma"))
        s_fin = Slot("fin")
        fin = fw.op("sp", lambda e: e.nop(), kind="c")
        fin.deps.update(o_ for o_ in out_ops if o_.kind != "skip")

        import os as _os2
        if _os2.environ.get("KNOSCHED", "") != "1":
            fw.schedule()
        fw.finalize()
        sems = {}
        for e in ENGS:
            sems[("eng", e)] = es.enter_context(nc.semaphore(f"s_{e}"))
            for i in range(FW.NDS):
                sems[("dma", e, i)] = es.enter_context(nc.semaphore(f"d_{e}{i}"))
        for i in range(fw.cc_count):
            sems[("cc", "pool", i)] = es.enter_context(nc.semaphore(f"cc{i}"))
        block = es.enter_context(nc.Block())

        @block.tensor
        def _(t):
            fw.emit("pe", t, sems)

        @block.scalar
        def _(a):
            fw.emit("act", a, sems)

        @block.vector
        def _(v):
            fw.emit("dve", v, sems)

        @block.gpsimd
        def _(g):
            fw.emit("pool", g, sems)

        @block.sync
        def _(sy):
            fw.emit("sp", sy, sems)
    return nc


def _tile_w(w, nk):
    K, N = w.shape
    assert K == nk * 128
    t = w.reshape(nk, 128, N // 128, 128).transpose(2, 1, 0, 3)
    return np.ascontiguousarray(t).reshape(N // 128, 128, nk * 128)


_NC_CACHE = {}


def kernel(x, p, g_mix, w_in, conv_w, conv_b, w_rg_a, b_rg_a, w_rg_x, b_rg_x, lru_lambda, q_gain, k_gain,
           w_rnn_out, w_attn_out, w_o, g_ffn, w_ffn_gu, w_ffn_down, g_ple, w_ple_gate, w_ple_proj):
    f = np.float32
    x = np.asarray(x, f)
    p = np.asarray(p, f)
    w_in0 = np.asarray(w_in, f)[0]
    if "nc" not in _NC_CACHE:
        _NC_CACHE["nc"] = build_program()
    nc = _NC_CACHE["nc"]

    vec128 = np.zeros((128, 50), f)
    vec128[:, 0:16] = np.asarray(g_mix, f)[0].reshape(16, 128).T
    vec128[:, 16:32] = np.asarray(g_ffn, f)[0].reshape(16, 128).T
    vec128[:, 32:48] = np.asarray(g_ple, f)[0].reshape(16, 128).T
    vec128[:, 48] = np.asarray(q_gain, f)[0]
    vec128[:, 49] = np.asarray(k_gain, f)[0]
    cmat = np.zeros((128, 384 + 896), f)
    cmat[:, 0:128] = np.eye(128, dtype=f)
    jj = np.arange(128)[:, None]
    ss = np.arange(128)[None, :]
    cmat[:, 128:256] = (jj >= ss).astype(f)
    cmat[:, 256:384] = (jj < ss).astype(f)
    xx = np.arange(896)[None, :]
    cmat[:, 384:] = np.where(xx <= jj + 384, NEG, 0.0).astype(f)
    wt_g = _tile_w(w_in0[:, 11776:15872], 16)
    wr = np.asarray(w_rnn_out, f)[0]
    wt_rnn = _tile_w(wr, 22).reshape(16, 128, 2, 11 * 128).transpose(0, 2, 1, 3).copy()
    wt_att = _tile_w(np.asarray(w_attn_out, f)[0], 16)
    wt_o = _tile_w(np.asarray(w_o, f)[0], 16)
    wt_gu = _tile_w(np.asarray(w_ffn_gu, f)[0], 16)
    wd = np.asarray(w_ffn_down, f)[0]
    wt_dn = np.stack([_tile_w(wd[h * 2816:(h + 1) * 2816], 22).reshape(16, 128, 2, 11 * 128).transpose(0, 2, 1, 3)
                      for h in range(2)]).copy()
    wt_pg = _tile_w(np.asarray(w_ple_gate, f)[0], 16)
    wpp = np.asarray(w_ple_proj, f)[0]
    wt_pp = _tile_w(wpp, 2)
    cw = np.asarray(conv_w, f)[0]
    cbv = np.asarray(conv_b, f)[0]
    ba = np.asarray(b_rg_a, f)[0]
    bx = np.asarray(b_rg_x, f)[0]
    lam = np.asarray(lru_lambda, f)[0]
    wa_all = np.asarray(w_rg_a, f)[0]
    wx_all = np.asarray(w_rg_x, f)[0]

    in_maps = []
    for c in range(8):
        b, g = c // 4, c % 4
        xb = x[b]
        xTt = np.ascontiguousarray(xb.reshape(32, 128, 16, 128).transpose(0, 3, 2, 1)).reshape(32, 128, 16 * 128)
        xo = xb[1024 * g:1024 * (g + 1)]
        xo2 = np.ascontiguousarray(xo.reshape(2, 512, 16, 128).transpose(2, 0, 3, 1))
        pb = p[0, b, 1024 * g:1024 * (g + 1)]
        pTt = np.ascontiguousarray(pb.reshape(1024, 2, 128).transpose(2, 1, 0)).reshape(128, 2 * 1024)
        w1a = np.concatenate([w_in0[:, 704 * g:704 * (g + 1)], w_in0[:, 2816 + 704 * g:2816 + 704 * (g + 1)]], axis=1)
        w1a = np.ascontiguousarray(w1a.reshape(16, 128, 1408).transpose(1, 0, 2)).reshape(128, 16 * 1408)
        w1b = np.concatenate([w_in0[:, 5632 + 512 * g:5632 + 512 * (g + 1)], w_in0[:, 7680 + 512 * g:7680 + 512 * (g + 1)],
                              w_in0[:, 9728 + 512 * g:9728 + 512 * (g + 1)]], axis=1)
        w1b = np.ascontiguousarray(w1b.reshape(16, 128, 1536).transpose(1, 0, 2)).reshape(128, 16 * 1536)
        ch = slice(704 * g, 704 * (g + 1))
        v88 = np.zeros((88, 8, 8), f)
        for k in range(4):
            v88[:, :, k] = cw[k, ch].reshape(8, 88).T
        v88[:, :, 4] = cbv[ch].reshape(8, 88).T
        v88[:, :, 5] = ba[ch].reshape(8, 88).T
        v88[:, :, 6] = bx[ch].reshape(8, 88).T
        v88[:, :, 7] = lam[ch].reshape(8, 88).T
        wax = np.stack([wa_all[4 * g:4 * g + 4], wx_all[4 * g:4 * g + 4]])
        wax = np.ascontiguousarray(wax.reshape(2, 4, 2, 88, 176).transpose(3, 0, 1, 2, 4)).reshape(88, 2 * 4 * 2 * 176)
        in_maps.append(dict(xTt=xTt, xo2=xo2, pTt=pTt, w1a=w1a, w1b=w1b,
                            vec88=v88.reshape(88, 64), vec128=vec128, wax=wax, cmat=cmat, wt_g=wt_g, wt_rnn=wt_rnn,
                            wt_att=wt_att, wt_o=wt_o, wt_gu=wt_gu, wt_dn=wt_dn, wt_pg=wt_pg, wt_pp=wt_pp))
    res = run_bass_kernel_spmd(nc, in_maps, core_ids=list(range(8)))
    out = np.empty((2, S, D), f)
    for c in range(8):
        b, g = c // 4, c % 4
        oT = np.asarray(res.results[c]["outT"], f).reshape(2048, 1024)
        out[b, 1024 * g:1024 * (g + 1), :] = oT.T
    return out
```

```python
import numpy as np
from contextlib import ExitStack
import concourse.bass as bass
import concourse.mybir as mybir
from concourse.bass_utils import run_bass_kernel_spmd

F32 = mybir.dt.float32
BF16 = mybir.dt.bfloat16
U8 = mybir.dt.uint8
AF = mybir.ActivationFunctionType
ALU = mybir.AluOpType

S = 4096
D = 2048
NKC = 16
TC = 512
NCH = S // TC
SUB = 128
EPS = 1e-6
QSCALE = 1.0 / np.sqrt(128.0)
NEG = -30000.0
ARENA = 198 * 1024
GELU_C = 0.7978845608028654

ENGS = ["pe", "act", "dve", "pool", "sp"]


class Slot:
    __slots__ = ("name", "w", "r", "excl")

    def __init__(self, name="", excl=False):
        self.name = name
        self.w = None
        self.r = []
        self.excl = excl


class Op:
    __slots__ = ("eng", "fn", "deps", "needs_inc", "seq", "kind", "sem", "semv", "pos", "raw", "cost", "idx",
                 "nd", "rt", "fin", "succ")


class FW:
    NDS = 8

    def __init__(self):
        self.ops = {e: [] for e in ENGS}
        self.cur_barrier = {e: None for e in ENGS}
        self.since_barrier = []
        self.cc_count = 0
        self.stopped = False
        self.all = []

    def stop_if(self, tag):
        import os
        if os.environ.get("KSTOP", "") == tag:
            self.stopped = True

    DEFCOST = {"pe": 4.6, "act": 0.65, "dve": 0.65, "pool": 5.0, "sp": 3.0}

    def op(self, eng, fn, reads=(), writes=(), kind="c", cost=None):
        o = Op()
        o.cost = cost if cost is not None else (30.0 if kind == "cc" else self.DEFCOST[eng])
        if self.stopped:
            o.eng, o.fn, o.kind, o.deps, o.raw, o.pos = eng, fn, "skip", set(), set(), -1
            return o
        o.eng, o.fn, o.kind = eng, fn, kind
        o.needs_inc = False
        o.seq = 0
        o.sem = None
        o.semv = 0
        deps = set()
        raw = set()
        for s in reads:
            if s.w is not None:
                deps.add(s.w)
                raw.add(s.w)
            if s.excl:
                deps.update(s.r)
        for s in writes:
            if s.w is not None:
                deps.add(s.w)
            deps.update(s.r)
        for s in reads:
            s.r.append(o)
        for s in writes:
            s.w = o
            s.r = []
        if self.cur_barrier[eng] is not None:
            deps.add(self.cur_barrier[eng])
        if kind == "cc":
            o.sem = ("cc", eng, self.cc_count)
            self.cc_count += 1
            o.semv = 1
        deps.discard(o)
        o.deps = deps
        o.raw = raw
        o.pos = len(self.ops[eng])
        self.ops[eng].append(o)
        o.idx = len(self.all)
        self.all.append(o)
        self.since_barrier.append(o)
        return o

    def schedule(self, reorder=("pe", "act", "dve", "sp"), W=200):
        import bisect
        for o in self.all:
            o.succ = []
        for o in self.all:
            o.nd = len(o.deps)
            o.rt = 0.0
            for d in o.deps:
                d.succ.append(o)
        blev = {}
        for o in reversed(self.all):
            m = 0.0
            for c in o.succ:
                v = blev[id(c)]
                if v > m:
                    m = v
            blev[id(o)] = m + o.cost
        import os as _os3
        use_bl = _os3.environ.get("KPRIO", "bl") == "bl"
        for o in self.all:
            o.seq = (-blev[id(o)] if use_bl else 0.0, o.idx)
        avail = {e: [] for e in ENGS}
        ptr = {e: 0 for e in ENGS}
        free_at = {e: 0.0 for e in ENGS}
        order = {e: [] for e in ENGS}
        for o in self.all:
            if o.nd == 0 and o.eng in reorder:
                bisect.insort(avail[o.eng], (o.seq, o))
        remaining = len(self.all)
        while remaining:
            best = None
            for e in ENGS:
                if e in reorder:
                    lst = avail[e]
                    if not lst:
                        continue
                    fa = free_at[e]
                    cb = None
                    for (ix, o) in lst[:W]:
                        st = o.rt if o.rt > fa else fa
                        if cb is None or st < cb[0] - 1e-9:
                            cb = (st, ix, o)
                        if st <= fa:
                            break
                else:
                    if ptr[e] >= len(self.ops[e]):
                        continue
                    o = self.ops[e][ptr[e]]
                    if o.nd > 0:
                        continue
                    st = max(o.rt, free_at[e])
                    cb = (st, o.seq, o)
                if best is None or (cb[0], cb[1]) < (best[0], best[1]):
                    best = cb
            assert best is not None, "scheduler stuck"
            st, ix, o = best
            e = o.eng
            if e in reorder:
                i = bisect.bisect_left(avail[e], (ix, ))
                assert avail[e][i][1] is o
                avail[e].pop(i)
            else:
                ptr[e] += 1
            if o.kind in ("dma", "cc"):
                free_at[e] = st + (1.0 if e == "pool" else 0.15)
            else:
                free_at[e] = st + o.cost
            o.fin = st + o.cost
            order[e].append(o)
            remaining -= 1
            for c in o.succ:
                c.nd -= 1
                if o.fin > c.rt:
                    c.rt = o.fin
                if c.nd == 0 and c.eng in reorder:
                    bisect.insort(avail[c.eng], (c.seq, c))
        for e in ENGS:
            self.ops[e] = order[e]
            for i, o in enumerate(order[e]):
                o.pos = i
        self.makespan = max(o.fin for o in self.all)

    def barrier(self, engines):
        if self.stopped:
            return
        prior = list(self.since_barrier)
        b = self.op("dve", lambda e: e.nop(), cost=0.1)
        b.deps.update(prior)
        b.deps.discard(b)
        self.since_barrier = [b]
        for e in engines:
            self.cur_barrier[e] = b

    def finalize(self):
        for e in ENGS:
            n = 0
            hist = []
            for o in self.ops[e]:
                if o.kind != "dma":
                    continue
                o.sem = ("dma", e, n % self.NDS)
                o.semv = 16 * (n // self.NDS + 1)
                if n >= self.NDS:
                    o.deps.add(hist[n - self.NDS])
                hist.append(o)
                n += 1
        for e in ENGS:
            for o in self.ops[e]:
                for d in o.deps:
                    if d.kind != "c":
                        continue
                    if d.eng != o.eng:
                        d.needs_inc = True
                    elif d in o.raw and o.pos - d.pos <= 1:
                        d.needs_inc = True
        for e in ENGS:
            n = 0
            for o in self.ops[e]:
                if o.kind == "c" and o.needs_inc:
                    n += 1
                    o.seq = n
                    o.sem = ("eng", e)
                    o.semv = n

    def emit(self, eng, handle, sems):
        waited = {}
        for o in self.ops[eng]:
            need = {}
            for d in o.deps:
                if d.kind == "c":
                    if d.eng == eng and not (d in o.raw and o.pos - d.pos <= 1):
                        continue
                if need.get(d.sem, 0) < d.semv:
                    need[d.sem] = d.semv
            for k, v in need.items():
                if waited.get(k, 0) >= v:
                    continue
                waited[k] = v
                handle.wait_ge(sems[k], v)
            inst = o.fn(handle)
            if o.kind == "dma":
                inst.then_inc(sems[o.sem], 16)
            elif o.kind == "cc":
                inst.then_inc(sems[o.sem], 1)
            elif o.needs_inc:
                inst.then_inc(sems[o.sem], 1)


class Arena:
    def __init__(self, ap):
        self.ap = ap
        self.top = 0

    def alloc(self, parts, free, dtype):
        esz = 4 if dtype == F32 else 2
        n = int(np.prod(free)) * esz
        off = (self.top + 63) // 64 * 64
        self.top = off + n
        assert self.top <= ARENA, f"arena overflow {self.top}"
        v = self.ap[0:parts, off:off + n].bitcast(dtype)
        if len(free) == 2:
            v = v.rearrange("p (a b) -> p a b", a=free[0])
        elif len(free) == 3:
            v = v.rearrange("p (a b c) -> p a b c", a=free[0], b=free[1])
        return v


def arena_at(arena, off, parts, free, dtype):
    esz = 4 if dtype == F32 else 2
    n = int(np.prod(free)) * esz
    assert off % 64 == 0 and off + n <= ARENA
    v = arena.ap[0:parts, off:off + n].bitcast(dtype)
    if len(free) == 2:
        v = v.rearrange("p (a b) -> p a b", a=free[0])
    arena.top = max(arena.top, off + n)
    return v


class Ring:
    def __init__(self, items):
        self.items = items
        self.i = 0

    def next(self):
        it = self.items[self.i % len(self.items)]
        self.i += 1
        return it


def build_program():
    nc = bass.Bass("TRN2", target_bir_lowering=False)

    def din(name, shape):
        return nc.dram_tensor(name, shape, F32, kind="ExternalInput").ap()

    xTt = din("xTt", [S // SUB, 128, NKC * SUB])
    xo2 = din("xo2", [16, 2, 128, 512])
    pTt = din("pTt", [128, 2 * 1024])
    w1a = din("w1a", [128, NKC * 1408])
    w1b = din("w1b", [128, NKC * 1536])
    vec88 = din("vec88", [88, 64])
    vec128 = din("vec128", [128, 50])
    wax = din("wax", [88, 2 * 4 * 2 * 176])
    cmat = din("cmat", [128, 384 + 896])
    wt_g = din("wt_g", [32, 128, 2048])
    wt_rnn = din("wt_rnn", [16, 2, 128, 11 * 128])
    wt_att = din("wt_att", [16, 128, 2048])
    wt_o = din("wt_o", [16, 128, 2048])
    wt_gu = din("wt_gu", [88, 128, 2048])
    wt_dn = din("wt_dn", [2, 16, 2, 128, 11 * 128])
    wt_pg = din("wt_pg", [16, 128, 2048])
    wt_pp = din("wt_pp", [16, 128, 2 * 128])
    outT = nc.dram_tensor("outT", [16, 128, 1024], F32, kind="ExternalOutput").ap()
    cin_yr = nc.dram_tensor("cin_yr", [NCH, 704, TC], BF16)
    cout_yr = nc.dram_tensor("cout_yr", [NCH, 2816, TC], BF16)
    cin_o = nc.dram_tensor("cin_o", [NCH, 512, TC], BF16)
    cout_o = nc.dram_tensor("cout_o", [NCH, 2048, TC], BF16)
    RG = [[0, 1, 2, 3], [4, 5, 6, 7]]
    hbuf = nc.dram_tensor("hbuf", [NCH, 128, NKC * TC], BF16)
    s_hbuf = [Slot() for _ in range(NCH)]
    s_cyr = [Slot() for _ in range(NCH)]
    s_co = [Slot() for _ in range(NCH)]

    fw = FW()
    es = ExitStack()
    with es:
        arena_t = es.enter_context(nc.sbuf_tensor("arena", [128, ARENA], U8))
        cst_t = es.enter_context(nc.sbuf_tensor("cst", [128, 2432], F32))
        psb2 = [es.enter_context(nc.psum_tensor(f"ps{i}", [128, 1024], F32)) for i in range(4)]
        psb = [psb2[i // 2][:, (i % 2) * 512:(i % 2 + 1) * 512] for i in range(8)]
        ps_slots = [Slot(f"ps{i}", excl=True) for i in range(8)]

        cst = cst_t
        c_vec128 = cst[:, 0:50]
        c_vec88 = cst[0:88, 64:128]
        c_der88 = cst[0:88, 128:160]
        c_num = cst[:, 160:168]
        cb = cst[:, 192:192 + 768].bitcast(BF16)
        c_ident = cb[:, 0:128]
        c_L = cb[:, 128:256]
        c_SU = cb[:, 256:384]
        c_mask = cb[:, 384:384 + 896]
        c_onesm = cb[:, 1280:1408]
        c_onesh = cb[:, 1408:1536]
        c_halo = cst[0:88, 968:968 + 24].rearrange("p (a b) -> p a b", a=8)
        c_hlast = cst[0:88, 1000:1008]
        c_wax = cst[0:88, 1016:1016 + 1408].bitcast(BF16).rearrange("p (w n i c) -> p w n i c", w=2, n=4, i=2)
        s_const = Slot("const")
        s_halo = [Slot() for _ in range(4)]
        s_hlast = [Slot() for _ in range(8)]
        EPS_AP = c_num[:, 0:1]
        ONE_AP = c_num[:, 1:2]

        def g128(col):
            return c_vec128[:, col:col + 1]

        fw.op("sp", lambda e: e.dma_start(out=c_vec128, in_=vec128[:, :]), writes=[s_const], kind="dma")
        fw.op("sp", lambda e: e.dma_start(out=c_vec88, in_=vec88[:, :]), writes=[s_const], kind="dma")
        fw.op("pool", lambda e: e.dma_start(out=cb[:, 0:1280], in_=cmat[:, :]), writes=[s_const], kind="dma")
        fw.op("pool", lambda e: e.dma_start(out=c_wax.rearrange("p w n i c -> p (w n i c)"), in_=wax[:, :]),
              writes=[s_const], kind="dma")

        def setup_consts(e):
            e.memset(c_num[:, 0:1], EPS)
            e.memset(c_num[:, 1:2], 1.0)
            e.memset(c_num[:, 2:3], 0.0)
            e.memset(c_onesm, 1.0 / 2048.0)
            e.memset(c_onesh, 1.0 / 128.0)
            e.memset(c_halo, 0.0)
            return e.memset(c_hlast, 0.0)
        fw.op("dve", setup_consts, writes=[s_const] + s_halo + s_hlast)

        v88 = c_vec88.rearrange("p (c f) -> p c f", c=8)
        hba = c_der88[:, 0:8]
        hbx = c_der88[:, 8:16]
        scl = c_der88[:, 16:24]
        tmp8 = c_der88[:, 24:32]
        fw.op("act", lambda e: e.activation(out=tmp8, in_=v88[:, :, 7], func=AF.Exp, scale=-1.0),
              reads=[s_const], writes=[s_const])
        fw.op("dve", lambda e: e.tensor_scalar(out=hba, in0=v88[:, :, 5], scalar1=0.5, scalar2=None, op0=ALU.mult),
              reads=[s_const], writes=[s_const])
        fw.op("act", lambda e: e.activation(out=scl, in_=tmp8, func=AF.Ln, bias=ONE_AP[0:88, :], scale=1.0),
              reads=[s_const], writes=[s_const])
        fw.op("dve", lambda e: e.tensor_scalar(out=hbx, in0=v88[:, :, 6], scalar1=0.5, scalar2=None, op0=ALU.mult),
              reads=[s_const], writes=[s_const])
        fw.op("dve", lambda e: e.tensor_scalar(out=scl, in0=scl, scalar1=-4.0, scalar2=None, op0=ALU.mult),
              reads=[s_const], writes=[s_const])

        arena_ap = arena_t[:, :]
        fw.stop_if("setup")

        def rmsnorm_sub(load_fn, src_ap, src_slots, gcol0, dst_ap, dst_slot, ntok, scratch, psring, sq_eng="act"):
            sq, s_sq, lnm, s_ln = scratch
            if load_fn is not None:
                load_fn()
            if sq_eng == "act":
                fw.op("act", lambda e: e.activation(out=sq, in_=src_ap, func=AF.Square),
                      reads=src_slots, writes=[s_sq], cost=2.0)
            else:
                fw.op("dve", lambda e: e.tensor_tensor(out=sq, in0=src_ap, in1=src_ap, op=ALU.mult),
                      reads=src_slots, writes=[s_sq], cost=2.4)
            ps, s_ps = psring.next()

            def f_mm(e):
                for kc in range(NKC):
                    last = e.matmul(ps[:, 0:ntok], lhsT=c_onesm, rhs=sq[:, kc, :], start=(kc == 0), stop=(kc == NKC - 1))
                return last
            fw.op("pe", f_mm, reads=[s_sq, s_const], writes=[s_ps], cost=16 * 0.12)
            fw.op("act", lambda e: e.activation(out=lnm[:, 0:ntok], in_=ps[:, 0:ntok], func=AF.Ln, bias=EPS_AP, scale=1.0),
                  reads=[s_ps], writes=[s_ln])
            fw.op("act", lambda e: e.activation(out=lnm[:, ntok:2 * ntok], in_=lnm[:, 0:ntok], func=AF.Exp, scale=-0.5),
                  reads=[s_ln], writes=[s_ln])

            def f_h(e):
                for kc in range(NKC):
                    last = e.scalar_tensor_tensor(out=dst_ap[:, kc, :], in0=src_ap[:, kc, :], scalar=g128(gcol0 + kc),
                                                  in1=lnm[:, ntok:2 * ntok], op0=ALU.mult, op1=ALU.mult)
                return last
            fw.op("dve", f_h, reads=list(src_slots) + [s_ln, s_const], writes=[dst_slot], cost=16 * 0.22)

        A = Arena(arena_ap)
        W1a = A.alloc(128, (NKC, 1408), BF16)
        s_W1a = Slot("W1a")
        off_xs = A.top
        xs_r = [(A.alloc(128, (NKC, SUB), F32), Slot()) for _ in range(2)]
        xsR = Ring(xs_r)
        sq_a = A.alloc(128, (NKC, SUB), BF16)
        lnm_a = A.alloc(128, (2 * SUB,), F32)
        scratchA = (sq_a, Slot(), lnm_a, Slot())
        off_hT0 = A.top
        hT_ring = Ring([(A.alloc(128, (NKC, TC), BF16), [Slot() for _ in range(TC // SUB)]) for _ in range(2)])
        hT0_slots_A = hT_ring.items[0][1]
        xs_slots_A = [xs_r[0][1], xs_r[1][1]]
        yrb = [(A.alloc(88, (8, TC), BF16), Slot()) for _ in range(2)]
        NSET = 2
        sets = []
        for _ in range(NSET):
            d = {}
            d["xr"] = A.alloc(88, (2, TC + 3), F32)
            d["gr"] = A.alloc(88, (2, TC), F32)
            d["xc"] = A.alloc(88, (2, TC), F32)
            d["xcb"] = A.alloc(88, (2, TC), BF16)
            for nm in ("rp", "ip", "a", "t", "s", "h"):
                d[nm] = A.alloc(88, (2, TC), F32)
            d["slots"] = {nm: [Slot(), Slot()] for nm in ("xr", "gr", "xc", "xcb", "rp", "ip", "a", "t", "s", "h")}
            sets.append(d)
        psA = Ring(list(zip([p[:, :] for p in psb[0:6]], ps_slots[0:6])))
        psA_ms = Ring(list(zip([p[:, :] for p in psb[6:8]], ps_slots[6:8])))

        for q4 in range(4):
            fw.op("pool", lambda e, q4=q4: e.dma_start(
                out=W1a[:, 4 * q4:4 * q4 + 4, :],
                in_=w1a[:, 4 * q4 * 1408:(4 * q4 + 4) * 1408].rearrange("p (k n) -> p k n", k=4)),
                writes=[s_W1a], kind="dma")

        def proj_fm(Wt, s_W, col0, M, ps, s_ps, h_ap, h_slots, ntok=TC):
            def f(e):
                for kc in range(NKC):
                    last = e.matmul(ps[0:M, 0:ntok], lhsT=Wt[:, kc, col0:col0 + M], rhs=h_ap[:, kc, :],
                                    start=(kc == 0), stop=(kc == NKC - 1))
                return last
            fw.op("pe", f, reads=[s_W] + list(h_slots), writes=[s_ps])

        def hT_chunk(G, xsrc, scratch, psring, sq_eng="act"):
            hT, s_hT = hT_ring.next()
            for sb in range(TC // SUB):
                sc = G * (TC // SUB) + sb
                xs, s_xs = xsR.next()
                fw.op("sp", lambda e, xs=xs, sc=sc: e.dma_start(out=xs.rearrange("p k t -> p (k t)"), in_=xsrc[sc]),
                      writes=[s_xs], kind="dma")
                rmsnorm_sub(None, xs, [s_xs], 0, hT[:, :, sb * SUB:(sb + 1) * SUB], s_hT[sb], SUB, scratch, psring, sq_eng)
            return hT, s_hT

        Wk = arena_at(A, off_xs, 128, (NKC, 512), BF16)
        Wq = arena_at(A, off_hT0, 128, (NKC, 512), BF16)
        s_Wq, s_Wk, s_Wv = Slot("Wq"), Slot("Wk"), Slot("Wv")
        w1b_v = w1b.rearrange("p (k n) -> p k n", k=NKC)
        for G in range(NCH):
            hT, s_hT = hT_chunk(G, xTt, scratchA, psA_ms)
            if G == NCH - 1 and not fw.stopped:
                fw.op("pool", lambda e: e.dma_start(out=Wq, in_=w1b_v[:, :, 0:512]), writes=[s_Wq] + hT0_slots_A, kind="dma", cost=12.0)
                fw.op("pool", lambda e: e.dma_start(out=Wk, in_=w1b_v[:, :, 512:1024]), writes=[s_Wk] + xs_slots_A, kind="dma", cost=12.0)
            fw.op("sp", lambda e, hT=hT, G=G: e.dma_start(out=hbuf.ap()[G], in_=hT.rearrange("p k t -> p (k t)")),
                  reads=s_hT, writes=[s_hbuf[G]], kind="dma")
            yb, s_yb = yrb[G % 2]
            for n in range(4):
                st = sets[(G * 4 + n) % NSET]
                sl = st["slots"]
                for j in range(2):
                    c8 = 2 * n + j
                    ps, s_ps = psA.next()
                    proj_fm(W1a, s_W1a, c8 * 88, 88, ps, s_ps, hT, s_hT)
                    fw.op("act", lambda e, ps=ps, st=st, j=j: e.activation(out=st["xr"][:, j, 3:TC + 3], in_=ps[0:88, :], func=AF.Copy),
                          reads=[s_ps], writes=[sl["xr"][j]])
                    ps, s_ps = psA.next()
                    proj_fm(W1a, s_W1a, 704 + c8 * 88, 88, ps, s_ps, hT, s_hT)
                    fw.op("act", lambda e, ps=ps, st=st, j=j: e.activation(out=st["gr"][:, j, :], in_=ps[0:88, :], func=AF.Copy),
                          reads=[s_ps], writes=[sl["gr"][j]])
                fw.op("dve", lambda e, st=st, n=n: e.tensor_copy(out=st["xr"][:, :, 0:3], in_=c_halo[:, 2 * n:2 * n + 2, :]),
                      reads=[s_halo[n]], writes=sl["xr"])
                fw.op("dve", lambda e, st=st, n=n: e.tensor_copy(out=c_halo[:, 2 * n:2 * n + 2, :], in_=st["xr"][:, :, TC:TC + 3]),
                      reads=sl["xr"], writes=[s_halo[n]])
                for k in range(4):
                    for j in range(2):
                        c8 = 2 * n + j
                        if k == 0:
                            fw.op("dve", lambda e, st=st, j=j, c8=c8: e.tensor_scalar(
                                out=st["xc"][:, j, :], in0=st["xr"][:, j, 0:TC], scalar1=v88[:, c8, 0:1], scalar2=v88[:, c8, 4:5],
                                op0=ALU.mult, op1=ALU.add), reads=[sl["xr"][j], s_const], writes=[sl["xc"][j]])
                        else:
                            fw.op("dve", lambda e, st=st, j=j, c8=c8, k=k: e.scalar_tensor_tensor(
                                out=st["xc"][:, j, :], in0=st["xr"][:, j, k:k + TC], scalar=v88[:, c8, k:k + 1], in1=st["xc"][:, j, :],
                                op0=ALU.mult, op1=ALU.add), reads=[sl["xr"][j], sl["xc"][j], s_const], writes=[sl["xc"][j]])
                for j in range(2):
                    fw.op("act", lambda e, st=st, j=j: e.activation(out=st["xcb"][:, j, :], in_=st["xc"][:, j, :], func=AF.Copy),
                          reads=[sl["xc"][j]], writes=[sl["xcb"][j]])
                gps = []
                for j in range(2):
                    for w in range(2):
                        ps, s_ps = psA.next()

                        def f(e, ps=ps, st=st, j=j, w=w, n=n):
                            for i in range(2):
                                last = e.matmul(ps[0:88, :], lhsT=c_wax[:, w, n, i, 88 * j:88 * j + 88], rhs=st["xcb"][:, i, :],
                                                start=(i == 0), stop=(i == 1))
                            return last
                        fw.op("pe", f, reads=sl["xcb"] + [s_const], writes=[s_ps], cost=0.9)
                        gps.append((ps, s_ps))
                for j in range(2):
                    c8 = 2 * n + j
                    ps, s_ps = gps[2 * j]
                    fw.op("act", lambda e, ps=ps, st=st, j=j, c8=c8: e.activation(
                        out=st["rp"][:, j, :], in_=ps[0:88, :], func=AF.Tanh, bias=hba[:, c8:c8 + 1], scale=0.5),
                        reads=[s_ps, s_const], writes=[sl["rp"][j]])
                    ps, s_ps = gps[2 * j + 1]
                    fw.op("act", lambda e, ps=ps, st=st, j=j, c8=c8: e.activation(
                        out=st["ip"][:, j, :], in_=ps[0:88, :], func=AF.Tanh, bias=hbx[:, c8:c8 + 1], scale=0.5),
                        reads=[s_ps, s_const], writes=[sl["ip"][j]])
                for j in range(2):
                    c8 = 2 * n + j
                    fw.op("act", lambda e, st=st, j=j, c8=c8: e.activation(
                        out=st["a"][:, j, :], in_=st["rp"][:, j, :], func=AF.Exp, bias=scl[:, c8:c8 + 1], scale=scl[:, c8:c8 + 1]),
                        reads=[sl["rp"][j], s_const], writes=[sl["a"][j]])
                for j in range(2):
                    fw.op("dve", lambda e, st=st, j=j: e.scalar_tensor_tensor(
                        out=st["t"][:, j, :], in0=st["ip"][:, j, :], scalar=1.0, in1=st["xc"][:, j, :], op0=ALU.add, op1=ALU.mult),
                        reads=[sl["ip"][j], sl["xc"][j]], writes=[sl["t"][j]])
                for j in range(2):
                    fw.op("act", lambda e, st=st, j=j: e.activation(out=st["s"][:, j, :], in_=st["a"][:, j, :], func=AF.Square),
                          reads=[sl["a"][j]], writes=[sl["s"][j]])
                for j in range(2):
                    fw.op("act", lambda e, st=st, j=j: e.activation(out=st["s"][:, j, :], in_=st["s"][:, j, :], func=AF.Sqrt,
                                                                    bias=ONE_AP[0:88, :], scale=-1.0),
                          reads=[sl["s"][j], s_const], writes=[sl["s"][j]])
                for j in range(2):
                    fw.op("dve", lambda e, st=st, j=j: e.scalar_tensor_tensor(
                        out=st["t"][:, j, :], in0=st["s"][:, j, :], scalar=0.5, in1=st["t"][:, j, :], op0=ALU.mult, op1=ALU.mult),
                        reads=[sl["s"][j], sl["t"][j]], writes=[sl["t"][j]])
                for j in range(2):
                    c8 = 2 * n + j
                    fw.op("dve", lambda e, st=st, j=j, c8=c8: e.tensor_tensor_scan(
                        out=st["h"][:, j, :], data0=st["a"][:, j, :], data1=st["t"][:, j, :], initial=c_hlast[:, c8:c8 + 1],
                        op0=ALU.mult, op1=ALU.add), reads=[sl["a"][j], sl["t"][j], s_hlast[c8]], writes=[sl["h"][j]])
                for j in range(2):
                    c8 = 2 * n + j
                    fw.op("act", lambda e, st=st, j=j, c8=c8: e.activation(out=c_hlast[:, c8:c8 + 1], in_=st["h"][:, j, TC - 1:TC], func=AF.Copy),
                          reads=[sl["h"][j]], writes=[s_hlast[c8]])
                for j in range(2):
                    fw.op("dve", lambda e, st=st, j=j: e.tensor_tensor(out=st["ip"][:, j, :], in0=st["gr"][:, j, :], in1=st["gr"][:, j, :], op=ALU.mult),
                          reads=[sl["gr"][j]], writes=[sl["ip"][j]])
                for j in range(2):
                    fw.op("dve", lambda e, st=st, j=j: e.tensor_scalar(out=st["ip"][:, j, :], in0=st["ip"][:, j, :], scalar1=0.044715, scalar2=1.0,
                                                                      op0=ALU.mult, op1=ALU.add), reads=[sl["ip"][j]], writes=[sl["ip"][j]])
                for j in range(2):
                    fw.op("dve", lambda e, st=st, j=j: e.tensor_tensor(out=st["ip"][:, j, :], in0=st["ip"][:, j, :], in1=st["gr"][:, j, :], op=ALU.mult),
                          reads=[sl["ip"][j], sl["gr"][j]], writes=[sl["ip"][j]])
                for j in range(2):
                    fw.op("act", lambda e, st=st, j=j: e.activation(out=st["rp"][:, j, :], in_=st["ip"][:, j, :], func=AF.Tanh, scale=GELU_C),
                          reads=[sl["ip"][j]], writes=[sl["rp"][j]])
                for j in range(2):
                    fw.op("dve", lambda e, st=st, j=j: e.scalar_tensor_tensor(
                        out=st["ip"][:, j, :], in0=st["rp"][:, j, :], scalar=1.0, in1=st["gr"][:, j, :], op0=ALU.add, op1=ALU.mult),
                        reads=[sl["rp"][j], sl["gr"][j]], writes=[sl["ip"][j]])
                for j in range(2):
                    c8 = 2 * n + j
                    fw.op("dve", lambda e, st=st, j=j, c8=c8, yb=yb: e.scalar_tensor_tensor(
                        out=yb[:, c8, :], in0=st["ip"][:, j, :], scalar=0.5, in1=st["h"][:, j, :], op0=ALU.mult, op1=ALU.mult),
                        reads=[sl["ip"][j], sl["h"][j]], writes=[s_yb])
            s_ci = Slot()
            fw.op("sp", lambda e, yb=yb, G=G: e.dma_start(
                out=cin_yr.ap()[G].rearrange("(c p) t -> p c t", p=88), in_=yb),
                reads=[s_yb], writes=[s_ci], kind="dma")
            fw.op("pool", lambda e, G=G: e.collective_compute("AllGather", ALU.bypass, replica_groups=RG,
                                                              ins=[cin_yr.ap()[G].opt()], outs=[cout_yr.ap()[G].opt()]),
                  reads=[s_ci], writes=[s_cyr[G]], kind="cc")
            if G == 0:
                fw.stop_if("passA1")
        fw.stop_if("passA")


        fw.stop_if("ag1")
        B = Arena(arena_ap)
        assert off_xs == 45056 and off_hT0 == 66560, (off_xs, off_hT0)
        KT = arena_at(B, 0, 128, (4, S), BF16)
        s_KT = [[Slot() for _ in range(NCH)] for _ in range(4)]
        e_r = [(arena_at(B, 32768 + 4096 * i, 128, (2, TC), F32), Slot()) for i in range(3)]
        nrm = [dict(raw=arena_at(B, 61440, 128, (TC,), F32), sq=arena_at(B, 63488, 128, (TC,), BF16),
                    ln=arena_at(B, 64512, 128, (TC,), F32), s_raw=Slot(), s_sq=Slot(), s_ln=Slot())]
        B.top = off_hT0 + 16384
        Wv = B.alloc(128, (NKC, 512), BF16)
        Vt = B.alloc(128, (S // 128, 512), BF16)
        s_V = [Slot() for _ in range(S // 128)]
        hT_ring.items = [(B.alloc(128, (NKC, TC), BF16), [Slot() for _ in range(TC // SUB)]) for _ in range(2)]
        QT_ring = Ring([(B.alloc(128, (4, TC), BF16), [Slot() for _ in range(4)]) for _ in range(2)])
        nrm.append(dict(raw=B.alloc(128, (TC,), F32), sq=B.alloc(128, (TC,), BF16), ln=B.alloc(128, (TC,), F32),
                        s_raw=Slot(), s_sq=Slot(), s_ln=Slot()))
        nrmR = Ring(nrm)
        sp_r = [(B.alloc(128, (2, TC), BF16), Slot()) for _ in range(3)]
        x_r = [(B.alloc(128, (2, TC), F32), Slot()) for _ in range(2)]
        w_r = [(B.alloc(128, (2, TC), BF16), Slot()) for _ in range(3)]
        psz2 = psb2[0][:, :].rearrange("p (a b) -> p a b", a=2)
        psc2 = psb2[1][:, :].rearrange("p (a b) -> p a b", a=2)
        obf = [(B.alloc(128, (TC,), BF16), Slot()) for _ in range(2)]
        obfR = Ring(obf)
        psz = [(psb[0][:, :], ps_slots[0]), (psb[1][:, :], ps_slots[1])]
        psc = [(psb[2][:, :], ps_slots[2]), (psb[3][:, :], ps_slots[3])]
        pso = [(psb[4][:, :], ps_slots[4]), (psb[5][:, :], ps_slots[5])]
        psM = Ring([(psb[6][:, :], ps_slots[6]), (psb[7][:, :], ps_slots[7])])

        fw.barrier(ENGS)
        fw.op("pool", lambda e: e.dma_start(out=Wv, in_=w1b_v[:, :, 1024:1536]), writes=[s_Wv], kind="dma", cost=12.0)

        def qk_norm(Wt, s_Wt, col0, gain_col, dst_ap, dst_slot, hT, s_hT):
            nb = nrmR.next()
            ps, s_ps = psM.next()
            proj_fm(Wt, s_Wt, col0, 128, ps, s_ps, hT, s_hT)
            fw.op("dve", lambda e: e.tensor_copy(out=nb["raw"], in_=ps), reads=[s_ps], writes=[nb["s_raw"]])
            fw.op("dve", lambda e: e.tensor_tensor(out=nb["sq"], in0=nb["raw"], in1=nb["raw"], op=ALU.mult),
                  reads=[nb["s_raw"]], writes=[nb["s_sq"]])
            ps2, s_ps2 = psM.next()
            fw.op("pe", lambda e: e.matmul(ps2, lhsT=c_onesh, rhs=nb["sq"], start=True, stop=True),
                  reads=[nb["s_sq"], s_const], writes=[s_ps2], cost=0.3)
            fw.op("act", lambda e: e.activation(out=nb["ln"], in_=ps2, func=AF.Ln, bias=EPS_AP, scale=1.0),
                  reads=[s_ps2], writes=[nb["s_ln"]])
            fw.op("act", lambda e: e.activation(out=nb["ln"], in_=nb["ln"], func=AF.Exp, scale=-0.5),
                  reads=[nb["s_ln"]], writes=[nb["s_ln"]])
            fw.op("dve", lambda e: e.scalar_tensor_tensor(out=dst_ap, in0=nb["raw"], scalar=g128(gain_col), in1=nb["ln"],
                                                          op0=ALU.mult, op1=ALU.mult),
                  reads=[nb["s_raw"], nb["s_ln"], s_const], writes=[dst_slot])

        import os as _os
        for G in range(int(_os.environ.get("KNB", NCH))):
            fw.stop_if("b_w")
            hT, s_hT = hT_ring.next()
            fw.op("sp", lambda e, hT=hT, G=G: e.dma_start(out=hT.rearrange("p k t -> p (k t)"), in_=hbuf.ap()[G]),
                  reads=[s_hbuf[G]], writes=s_hT, kind="dma")
            QT, s_QT = QT_ring.next()
            fw.stop_if("b_h")
            for hh in range(4):
                qk_norm(Wq, s_Wq, hh * 128, 48, QT[:, hh, :], s_QT[hh], hT, s_hT)
                qk_norm(Wk, s_Wk, hh * 128, 49, KT[:, hh, G * TC:(G + 1) * TC], s_KT[hh][G], hT, s_hT)
            fw.stop_if("b_qk")
            for tb in range(4):
                ps, s_ps = psM.next()

                def f(e, ps=ps, tb=tb, hT=hT):
                    for kc in range(NKC):
                        last = e.matmul(ps, lhsT=hT[:, kc, tb * 128:(tb + 1) * 128], rhs=Wv[:, kc, :],
                                        start=(kc == 0), stop=(kc == NKC - 1))
                    return last
                fw.op("pe", f, reads=[s_Wv] + s_hT, writes=[s_ps])
                fw.op("dve", lambda e, ps=ps, tb=tb, G=G: e.tensor_copy(out=Vt[:, 4 * G + tb, :], in_=ps),
                      reads=[s_ps], writes=[s_V[4 * G + tb]])
            fw.stop_if("b_v")
            NT = 4 * G + 4
            s_cio = [Slot() for _ in range(4)]
            for hp in range(2):
                heads = [2 * hp, 2 * hp + 1]
                etile = {}
                sptile = {}
                wtile = {}

                def lo_of(s):
                    d = (NT - 1 - s) - 4 * G
                    return 128 * d if d > 0 else 0

                def stageZ(s):
                    kb = NT - 1 - s
                    diag = kb - 4 * G
                    lo = lo_of(s)
                    for li, hh in enumerate(heads):
                        ps, s_ps = psz[li]

                        def f(e, ps=ps, hh=hh, kb=kb, diag=diag, QT=QT, lo=lo):
                            last = e.matmul(ps[:, lo:TC], lhsT=KT[:, hh, kb * 128:(kb + 1) * 128], rhs=QT[:, hh, lo:TC],
                                            start=True, stop=(diag < 0), skip_group_check=True)
                            if diag >= 0:
                                o0 = 384 - 128 * diag
                                last = e.matmul(ps[:, lo:TC], lhsT=c_ident, rhs=c_mask[:, o0 + lo:o0 + TC], start=False, stop=True,
                                                skip_group_check=True)
                            return last
                        fw.op("pe", f, reads=[s_KT[hh][kb // 4], s_QT[hh], s_const], writes=[s_ps], cost=0.6 if diag >= 0 else 0.3)
                    eb, s_eb = e_r[s % 3]
                    etile[s] = (eb, s_eb)
                    fw.op("act", lambda e, eb=eb, lo=lo: e.activation(out=eb[:, :, lo:TC], in_=psz2[:, :, lo:TC], func=AF.Exp, scale=float(QSCALE)),
                          reads=[psz[0][1], psz[1][1]], writes=[s_eb], cost=1.1)
                    sb_, s_sb = sp_r[s % 3]
                    sptile[s] = (sb_, s_sb)
                    fw.op("act", lambda e, eb=eb, sb_=sb_, lo=lo: e.activation(out=sb_[:, :, lo:TC], in_=eb[:, :, lo:TC], func=AF.Ln, bias=ONE_AP, scale=1.0),
                          reads=[s_eb, s_const], writes=[s_sb], cost=1.1)

                def stageC(s):
                    sb_, s_sb = sptile[s]
                    lo = lo_of(s)
                    lp = lo_of(s - 1) if s > 0 else 0
                    for li, hh in enumerate(heads):
                        ps, s_ps = psc[li]
                        rd = [s_sb, s_const]
                        if s > 0:
                            sprev, s_sprev = sptile[s - 1]
                            rd += [s_sprev, s_ps]
                        else:
                            sprev = None

                        def f(e, ps=ps, sb_=sb_, sprev=sprev, s=s, li=li, lo=lo, lp=lp):
                            if s > 0:
                                e.matmul(ps[:, lp:TC], lhsT=c_SU, rhs=sprev[:, li, lp:TC], start=False, stop=False, skip_group_check=True)
                            return e.matmul(ps[:, lo:TC], lhsT=c_L, rhs=sb_[:, li, lo:TC], start=(s == 0), stop=True, skip_group_check=True)
                        fw.op("pe", f, reads=rd, writes=[s_ps], cost=0.6 if s > 0 else 0.3)
                    xb, s_xb = x_r[s % 2]
                    fw.op("act", lambda e, xb=xb, lo=lo: e.activation(out=xb[:, :, lo:TC], in_=psc2[:, :, lo:TC], func=AF.Exp, scale=-1.0),
                          reads=[psc[0][1], psc[1][1]], writes=[s_xb], cost=1.1)
                    eb, s_eb = etile[s]
                    wb, s_wb = w_r[s % 3]
                    wtile[s] = (wb, s_wb)
                    fw.op("dve", lambda e, xb=xb, eb=eb, wb=wb, lo=lo: e.tensor_tensor(out=wb[:, :, lo:TC], in0=eb[:, :, lo:TC], in1=xb[:, :, lo:TC], op=ALU.mult),
                          reads=[s_xb, s_eb], writes=[s_wb], cost=1.2)

                def stageO(s):
                    kb = NT - 1 - s
                    lo = lo_of(s)
                    for li, hh in enumerate(heads):
                        ps, s_ps = pso[li]
                        wb, s_wb = wtile[s]
                        fw.op("pe", lambda e, ps=ps, wb=wb, hh=hh, kb=kb, s=s, li=li, lo=lo: e.matmul(
                            ps[:, lo:TC], lhsT=Vt[:, kb, hh * 128:(hh + 1) * 128], rhs=wb[:, li, lo:TC], start=(s == 0), stop=(s == NT - 1),
                            skip_group_check=True),
                            reads=[s_wb, s_V[kb], s_ps] if s > 0 else [s_wb, s_V[kb]], writes=[s_ps], cost=0.3)

                for step in range(NT + 2):
                    if step < NT:
                        stageZ(step)
                    if 0 <= step - 1 < NT:
                        stageC(step - 1)
                    if 0 <= step - 2 < NT:
                        stageO(step - 2)
                for li, hh in enumerate(heads):
                    ps, s_ps = pso[li]
                    ob, s_ob = obfR.next()
                    fw.op("dve", lambda e, ps=ps, ob=ob: e.tensor_copy(out=ob, in_=ps), reads=[s_ps], writes=[s_ob])
                    fw.op("sp", lambda e, ob=ob, hh=hh, G=G: e.dma_start(
                        out=cin_o.ap()[G, hh * 128:(hh + 1) * 128, :], in_=ob), reads=[s_ob], writes=[s_cio[hh]], kind="dma")
            fw.op("pool", lambda e, G=G: e.collective_compute("AllGather", ALU.bypass, replica_groups=RG,
                                                              ins=[cin_o.ap()[G].opt()], outs=[cout_o.ap()[G].opt()]),
                  reads=s_cio, writes=[s_co[G]], kind="cc")

        fw.stop_if("passB")

        P = Arena(arena_ap)
        NU = 10
        wunits = [(P.alloc(128, (NKC, 128), BF16), Slot()) for _ in range(NU)]
        wu_i = [0]
        tmpf = [(P.alloc(128, (TC,), F32), Slot()) for _ in range(4)]
        tmpR = Ring(tmpf)
        R1 = P.alloc(128, (NKC, 1024), F32)
        R1b = R1.rearrange("p k t -> p (k t)").bitcast(BF16)
        h_a = R1b[:, 0:NKC * 1024].rearrange("p (k t) -> p k t", k=NKC)
        oT_a = R1b[:, NKC * 1024:2 * NKC * 1024].rearrange("p (k t) -> p k t", k=NKC)
        R2 = P.alloc(128, (22, 1024), BF16)
        R3 = P.alloc(128, (NKC, 1024), BF16)
        pTb = P.alloc(128, (2, 1024), BF16)
        s_xo = [[Slot() for _ in range(2)] for _ in range(NKC)]
        s_R1h = [Slot() for _ in range(8)]
        s_oT = Slot()
        s_R2 = Slot()
        s_R3 = [[Slot() for _ in range(2)] for _ in range(NKC)]
        s_h2 = [Slot() for _ in range(8)]
        s_pT = Slot()
        psP = Ring(list(zip([p[:, :] for p in psb], ps_slots)))
        sq_p = P.alloc(128, (NKC, SUB), BF16)
        lnm_p = P.alloc(128, (2 * SUB,), F32)
        scratchP = (sq_p, Slot(), lnm_p, Slot())
        xs_rp = [(R3.rearrange("p k t -> p (k t)")[:, 12 * 1024:16 * 1024].bitcast(F32).rearrange("p (k t) -> p k t", k=NKC), Slot())]

        fw.barrier(ENGS)

        def wload(src_ap, nk=NKC):
            u, s_u = wunits[wu_i[0] % NU]
            wu_i[0] += 1
            fw.op("pool", lambda e: e.dma_start(out=u[:, 0:nk, :].rearrange("p k n -> p (k n)"), in_=src_ap),
                  writes=[s_u], kind="dma")
            return u, s_u

        def acc_group(ps, s_ps, parts, tb, extra_reads):
            def f(e):
                tot = sum(p[2] for p in parts)
                i = 0
                for (u, s_u, nk, act, k0, rows) in parts:
                    for kc in range(nk):
                        last = e.matmul(ps, lhsT=u[0:rows, kc, :], rhs=act[0:rows, k0 + kc, tb * TC:(tb + 1) * TC],
                                        start=(i == 0), stop=(i == tot - 1))
                        i += 1
                return last
            fw.op("pe", f, reads=[p[1] for p in parts] + list(extra_reads), writes=[s_ps], cost=0.29 * sum(p[2] for p in parts))

        for tb in range(2):
            def f_hh(e, tb=tb):
                rank = e.partition_id() % 4
                return e.dma_start(out=h_a[:, :, tb * TC:(tb + 1) * TC],
                                   in_=hbuf.ap()[bass.ds(2 * rank + tb, 1)].rearrange("o p (k t) -> p (o k) t", k=NKC))
            fw.op("pool", f_hh, reads=s_hbuf, writes=s_R1h[4 * tb:4 * tb + 4], kind="dma")

        def decl_yr():
            for tb in range(2):
                def f_yr(e, tb=tb):
                    rank = e.partition_id() % 4
                    return e.dma_start(out=R2[:, :, tb * TC:(tb + 1) * TC],
                                       in_=cout_yr.ap()[bass.ds(2 * rank + tb, 1)].rearrange("o (k p) t -> p (o k) t", p=128))
                fw.op("pool", f_yr, reads=s_cyr, writes=[s_R2], kind="dma")

        def decl_o():
            for tb in range(2):
                def f_o(e, tb=tb):
                    rank = e.partition_id() % 4
                    return e.dma_start(out=oT_a[:, :, tb * TC:(tb + 1) * TC],
                                       in_=cout_o.ap()[bass.ds(2 * rank + tb, 1)].rearrange("o (k p) t -> p (o k) t", p=128))
                fw.op("pool", f_o, reads=s_co, writes=[s_oT], kind="dma")

        def decl_p():
            fw.op("pool", lambda e: e.dma_start(out=pTb.rearrange("p k t -> p (k t)"), in_=pTt[:, :]), writes=[s_pT], kind="dma")

        fw.stop_if("s0")
        for cc in range(NKC):
            ugr = wload(wt_g[cc])
            uga = wload(wt_g[16 + cc])
            if cc == 0:
                decl_yr()
            ur0 = wload(wt_rnn[cc, 0], 11)
            ur1 = wload(wt_rnn[cc, 1], 11)
            if cc == 0:
                decl_o()
            uat = wload(wt_att[cc])
            if cc == 0:
                decl_p()
            for tb in range(2):
                hs = s_R1h[4 * tb:4 * tb + 4]
                p_gr = psP.next()
                acc_group(p_gr[0], p_gr[1], [(ugr[0], ugr[1], NKC, h_a, 0, 128)], tb, hs)
                p_ga = psP.next()
                acc_group(p_ga[0], p_ga[1], [(uga[0], uga[1], NKC, h_a, 0, 128)], tb, hs)
                p_yr = psP.next()
                acc_group(p_yr[0], p_yr[1], [(ur0[0], ur0[1], 11, R2, 0, 128), (ur1[0], ur1[1], 11, R2, 11, 128)], tb, [s_R2])
                p_ya = psP.next()
                acc_group(p_ya[0], p_ya[1], [(uat[0], uat[1], NKC, oT_a, 0, 128)], tb, [s_oT])
                t_sr = tmpR.next()
                t_sa = tmpR.next()
                fw.op("act", lambda e, p=p_gr, t=t_sr: e.activation(out=t[0], in_=p[0], func=AF.Sigmoid), reads=[p_gr[1]], writes=[t_sr[1]])
                fw.op("act", lambda e, p=p_ga, t=t_sa: e.activation(out=t[0], in_=p[0], func=AF.Sigmoid), reads=[p_ga[1]], writes=[t_sa[1]])
                fw.op("dve", lambda e, p=p_yr, t=t_sr: e.tensor_tensor(out=t[0], in0=t[0], in1=p[0], op=ALU.mult),
                      reads=[p_yr[1], t_sr[1]], writes=[t_sr[1]])
                fw.op("dve", lambda e, p=p_ya, t=t_sa: e.tensor_tensor(out=t[0], in0=t[0], in1=p[0], op=ALU.mult),
                      reads=[p_ya[1], t_sa[1]], writes=[t_sa[1]])
                fw.op("dve", lambda e, a=t_sr, b=t_sa, cc=cc, tb=tb: e.tensor_tensor(
                    out=R3[:, cc, tb * TC:(tb + 1) * TC], in0=a[0], in1=b[0], op=ALU.add),
                    reads=[t_sr[1], t_sa[1]], writes=[s_R3[cc][tb]])

        fw.barrier(["pe", "act", "dve", "sp"])
        fw.stop_if("sa")
        mix_slots = [s_R3[c][t] for c in range(NKC) for t in range(2)]
        for cc in range(NKC):
            uo = wload(wt_o[cc])
            for tb in range(2):
                fw.op("sp", lambda e, cc=cc, tb=tb: e.dma_start(out=R1[:, cc, tb * TC:(tb + 1) * TC], in_=xo2[cc, tb]),
                      writes=[s_xo[cc][tb]], kind="dma")
                p = psP.next()
                acc_group(p[0], p[1], [(uo[0], uo[1], NKC, R3, 0, 128)], tb, [s_R3[c][tb] for c in range(NKC)])
                fw.op("dve", lambda e, p=p, cc=cc, tb=tb: e.tensor_tensor(
                    out=R1[:, cc, tb * TC:(tb + 1) * TC], in0=R1[:, cc, tb * TC:(tb + 1) * TC], in1=p[0], op=ALU.add),
                    reads=[p[1], s_xo[cc][tb]], writes=[s_xo[cc][tb]])

        fw.barrier(["pe", "act", "dve", "sp"])

        def rms_resident(gcol0, dst, dst_slots):
            for sc in range(8):
                tb = sc // 4
                rmsnorm_sub(None, R1[:, :, sc * SUB:(sc + 1) * SUB], [s_xo[c][tb] for c in range(NKC)], gcol0,
                            dst[:, :, sc * SUB:(sc + 1) * SUB], dst_slots[sc], SUB, scratchP, psP)

        fw.stop_if("sb")
        rms_resident(16, R3, s_h2)
        for half in range(2):
            for m in range(22):
                mm = half * 22 + m
                ug = wload(wt_gu[mm])
                uu = wload(wt_gu[44 + mm])
                for tb in range(2):
                    hs = s_h2[4 * tb:4 * tb + 4]
                    p_g = psP.next()
                    acc_group(p_g[0], p_g[1], [(ug[0], ug[1], NKC, R3, 0, 128)], tb, hs)
                    p_u = psP.next()
                    acc_group(p_u[0], p_u[1], [(uu[0], uu[1], NKC, R3, 0, 128)], tb, hs)
                    t_s = tmpR.next()
                    fw.op("act", lambda e, p=p_g, t=t_s: e.activation(out=t[0], in_=p[0], func=AF.Silu), reads=[p_g[1]], writes=[t_s[1]])
                    fw.op("dve", lambda e, p=p_u, t=t_s, m=m, tb=tb: e.tensor_tensor(
                        out=R2[:, m, tb * TC:(tb + 1) * TC], in0=t[0], in1=p[0], op=ALU.mult),
                        reads=[p_u[1], t_s[1]], writes=[s_R2])
            for cc in range(NKC):
                ud0 = wload(wt_dn[half, cc, 0], 11)
                ud1 = wload(wt_dn[half, cc, 1], 11)
                for tb in range(2):
                    p = psP.next()
                    acc_group(p[0], p[1], [(ud0[0], ud0[1], 11, R2, 0, 128), (ud1[0], ud1[1], 11, R2, 11, 128)], tb, [s_R2])
                    fw.op("dve", lambda e, p=p, cc=cc, tb=tb: e.tensor_tensor(
                        out=R1[:, cc, tb * TC:(tb + 1) * TC], in0=R1[:, cc, tb * TC:(tb + 1) * TC], in1=p[0], op=ALU.add),
                        reads=[p[1], s_xo[cc][tb]], writes=[s_xo[cc][tb]])
            fw.barrier(["pe", "act", "dve", "sp"])

        fw.stop_if("sc")
        h3 = R2[:, 0:NKC, :]
        s_h3 = [Slot() for _ in range(8)]
        rms_resident(32, h3, s_h3)
        out_ops = []
        for cc in range(NKC):
            upg = wload(wt_pg[cc])
            upp = wload(wt_pp[cc], 2)
            for tb in range(2):
                hs = s_h3[4 * tb:4 * tb + 4]
                p_g = psP.next()
                acc_group(p_g[0], p_g[1], [(upg[0], upg[1], NKC, h3, 0, 128)], tb, hs)
                p_p = psP.next()

                acc_group(p_p[0], p_p[1], [(upp[0], upp[1], 2, pTb, 0, 128)], tb, [s_pT])
                t_s = tmpR.next()
                fw.op("act", lambda e, p=p_g, t=t_s: e.activation(out=t[0], in_=p[0], func=AF.Sigmoid), reads=[p_g[1]], writes=[t_s[1]])
                fw.op("dve", lambda e, p=p_p, t=t_s: e.tensor_tensor(out=t[0], in0=t[0], in1=p[0], op=ALU.mult),
                      reads=[p_p[1], t_s[1]], writes=[t_s[1]])
                fw.op("dve", lambda e, t=t_s, cc=cc, tb=tb: e.tensor_tensor(
                    out=R1[:, cc, tb * TC:(tb + 1) * TC], in0=R1[:, cc, tb * TC:(tb + 1) * TC], in1=t[0], op=ALU.add),
                    reads=[t_s[1], s_xo[cc][tb]], writes=[s_xo[cc][tb]])
            out_ops.append(fw.op("sp", lambda e, cc=cc: e.dma_start(out=outT[cc], in_=R1[:, cc, :]),
                                 reads=[s_xo[cc][0], s_xo[cc][1]], kind="dma"))
        s_fin = Slot("fin")
        fin = fw.op("sp", lambda e: e.nop(), kind="c")
        fin.deps.update(o_ for o_ in out_ops if o_.kind != "skip")

        import os as _os2
        if _os2.environ.get("KNOSCHED", "") != "1":
            fw.schedule()
        fw.finalize()
        sems = {}
        for e in ENGS:
            sems[("eng", e)] = es.enter_context(nc.semaphore(f"s_{e}"))
            for i in range(FW.NDS):
                sems[("dma", e, i)] = es.enter_context(nc.semaphore(f"d_{e}{i}"))
        for i in range(fw.cc_count):
            sems[("cc", "pool", i)] = es.enter_context(nc.semaphore(f"cc{i}"))
        block = es.enter_context(nc.Block())

        @block.tensor
        def _(t):
            fw.emit("pe", t, sems)

        @block.scalar
        def _(a):
            fw.emit("act", a, sems)

        @block.vector
        def _(v):
            fw.emit("dve", v, sems)

        @block.gpsimd
        def _(g):
            fw.emit("pool", g, sems)

        @block.sync
        def _(sy):
            fw.emit("sp", sy, sems)
    return nc


def _tile_w(w, nk):
    K, N = w.shape
    assert K == nk * 128
    t = w.reshape(nk, 128, N // 128, 128).transpose(2, 1, 0, 3)
    return np.ascontiguousarray(t).reshape(N // 128, 128, nk * 128)


_NC_CACHE = {}


def kernel(x, p, g_mix, w_in, conv_w, conv_b, w_rg_a, b_rg_a, w_rg_x, b_rg_x, lru_lambda, q_gain, k_gain,
           w_rnn_out, w_attn_out, w_o, g_ffn, w_ffn_gu, w_ffn_down, g_ple, w_ple_gate, w_ple_proj):
    f = np.float32
    x = np.asarray(x, f)
    p = np.asarray(p, f)
    w_in0 = np.asarray(w_in, f)[0]
    if "nc" not in _NC_CACHE:
        _NC_CACHE["nc"] = build_program()
    nc = _NC_CACHE["nc"]

    vec128 = np.zeros((128, 50), f)
    vec128[:, 0:16] = np.asarray(g_mix, f)[0].reshape(16, 128).T
    vec128[:, 16:32] = np.asarray(g_ffn, f)[0].reshape(16, 128).T
    vec128[:, 32:48] = np.asarray(g_ple, f)[0].reshape(16, 128).T
    vec128[:, 48] = np.asarray(q_gain, f)[0]
    vec128[:, 49] = np.asarray(k_gain, f)[0]
    cmat = np.zeros((128, 384 + 896), f)
    cmat[:, 0:128] = np.eye(128, dtype=f)
    jj = np.arange(128)[:, None]
    ss = np.arange(128)[None, :]
    cmat[:, 128:256] = (jj >= ss).astype(f)
    cmat[:, 256:384] = (jj < ss).astype(f)
    xx = np.arange(896)[None, :]
    cmat[:, 384:] = np.where(xx <= jj + 384, NEG, 0.0).astype(f)
    wt_g = _tile_w(w_in0[:, 11776:15872], 16)
    wr = np.asarray(w_rnn_out, f)[0]
    wt_rnn = _tile_w(wr, 22).reshape(16, 128, 2, 11 * 128).transpose(0, 2, 1, 3).copy()
    wt_att = _tile_w(np.asarray(w_attn_out, f)[0], 16)
    wt_o = _tile_w(np.asarray(w_o, f)[0], 16)
    wt_gu = _tile_w(np.asarray(w_ffn_gu, f)[0], 16)
    wd = np.asarray(w_ffn_down, f)[0]
    wt_dn = np.stack([_tile_w(wd[h * 2816:(h + 1) * 2816], 22).reshape(16, 128, 2, 11 * 128).transpose(0, 2, 1, 3)
                      for h in range(2)]).copy()
    wt_pg = _tile_w(np.asarray(w_ple_gate, f)[0], 16)
    wpp = np.asarray(w_ple_proj, f)[0]
    wt_pp = _tile_w(wpp, 2)
    cw = np.asarray(conv_w, f)[0]
    cbv = np.asarray(conv_b, f)[0]
    ba = np.asarray(b_rg_a, f)[0]
    bx = np.asarray(b_rg_x, f)[0]
    lam = np.asarray(lru_lambda, f)[0]
    wa_all = np.asarray(w_rg_a, f)[0]
    wx_all = np.asarray(w_rg_x, f)[0]

    in_maps = []
    for c in range(8):
        b, g = c // 4, c % 4
        xb = x[b]
        xTt = np.ascontiguousarray(xb.reshape(32, 128, 16, 128).transpose(0, 3, 2, 1)).reshape(32, 128, 16 * 128)
        xo = xb[1024 * g:1024 * (g + 1)]
        xo2 = np.ascontiguousarray(xo.reshape(2, 512, 16, 128).transpose(2, 0, 3, 1))
        pb = p[0, b, 1024 * g:1024 * (g + 1)]
        pTt = np.ascontiguousarray(pb.reshape(1024, 2, 128).transpose(2, 1, 0)).reshape(128, 2 * 1024)
        w1a = np.concatenate([w_in0[:, 704 * g:704 * (g + 1)], w_in0[:, 2816 + 704 * g:2816 + 704 * (g + 1)]], axis=1)
        w1a = np.ascontiguousarray(w1a.reshape(16, 128, 1408).transpose(1, 0, 2)).reshape(128, 16 * 1408)
        w1b = np.concatenate([w_in0[:, 5632 + 512 * g:5632 + 512 * (g + 1)], w_in0[:, 7680 + 512 * g:7680 + 512 * (g + 1)],
                              w_in0[:, 9728 + 512 * g:9728 + 512 * (g + 1)]], axis=1)
        w1b = np.ascontiguousarray(w1b.reshape(16, 128, 1536).transpose(1, 0, 2)).reshape(128, 16 * 1536)
        ch = slice(704 * g, 704 * (g + 1))
        v88 = np.zeros((88, 8, 8), f)
        for k in range(4):
            v88[:, :, k] = cw[k, ch].reshape(8, 88).T
        v88[:, :, 4] = cbv[ch].reshape(8, 88).T
        v88[:, :, 5] = ba[ch].reshape(8, 88).T
        v88[:, :, 6] = bx[ch].reshape(8, 88).T
        v88[:, :, 7] = lam[ch].reshape(8, 88).T
        wax = np.stack([wa_all[4 * g:4 * g + 4], wx_all[4 * g:4 * g + 4]])
        wax = np.ascontiguousarray(wax.reshape(2, 4, 2, 88, 176).transpose(3, 0, 1, 2, 4)).reshape(88, 2 * 4 * 2 * 176)
        in_maps.append(dict(xTt=xTt, xo2=xo2, pTt=pTt, w1a=w1a, w1b=w1b,
                            vec88=v88.reshape(88, 64), vec128=vec128, wax=wax, cmat=cmat, wt_g=wt_g, wt_rnn=wt_rnn,
                            wt_att=wt_att, wt_o=wt_o, wt_gu=wt_gu, wt_dn=wt_dn, wt_pg=wt_pg, wt_pp=wt_pp))
    res = run_bass_kernel_spmd(nc, in_maps, core_ids=list(range(8)))
    out = np.empty((2, S, D), f)
    for c in range(8):
        b, g = c // 4, c % 4
        oT = np.asarray(res.results[c]["outT"], f).reshape(2048, 1024)
        out[b, 1024 * g:1024 * (g + 1), :] = oT.T
    return out
```

```python
import numpy as np
from contextlib import ExitStack
import concourse.bass as bass
import concourse.mybir as mybir
from concourse.bass_utils import run_bass_kernel_spmd

F32 = mybir.dt.float32
BF16 = mybir.dt.bfloat16
U8 = mybir.dt.uint8
AF = mybir.ActivationFunctionType
ALU = mybir.AluOpType

S = 4096
D = 2048
NKC = 16
TC = 512
NCH = S // TC
SUB = 128
EPS = 1e-6
QSCALE = 1.0 / np.sqrt(128.0)
NEG = -30000.0
ARENA = 198 * 1024
GELU_C = 0.7978845608028654

ENGS = ["pe", "act", "dve", "pool", "sp"]


class Slot:
    __slots__ = ("name", "w", "r", "excl")

    def __init__(self, name="", excl=False):
        self.name = name
        self.w = None
        self.r = []
        self.excl = excl


class Op:
    __slots__ = ("eng", "fn", "deps", "needs_inc", "seq", "kind", "sem", "semv", "pos", "raw", "cost", "idx",
                 "nd", "rt", "fin", "succ")


class FW:
    NDS = 8

    def __init__(self):
        self.ops = {e: [] for e in ENGS}
        self.cur_barrier = {e: None for e in ENGS}
        self.since_barrier = []
        self.cc_count = 0
        self.stopped = False
        self.all = []

    def stop_if(self, tag):
        import os
        if os.environ.get("KSTOP", "") == tag:
            self.stopped = True

    DEFCOST = {"pe": 4.6, "act": 0.65, "dve": 0.65, "pool": 5.0, "sp": 3.0}

    def op(self, eng, fn, reads=(), writes=(), kind="c", cost=None):
        o = Op()
        o.cost = cost if cost is not None else (30.0 if kind == "cc" else self.DEFCOST[eng])
        if self.stopped:
            o.eng, o.fn, o.kind, o.deps, o.raw, o.pos = eng, fn, "skip", set(), set(), -1
            return o
        o.eng, o.fn, o.kind = eng, fn, kind
        o.needs_inc = False
        o.seq = 0
        o.sem = None
        o.semv = 0
        deps = set()
        raw = set()
        for s in reads:
            if s.w is not None:
                deps.add(s.w)
                raw.add(s.w)
            if s.excl:
                deps.update(s.r)
        for s in writes:
            if s.w is not None:
                deps.add(s.w)
            deps.update(s.r)
        for s in reads:
            s.r.append(o)
        for s in writes:
            s.w = o
            s.r = []
        if self.cur_barrier[eng] is not None:
            deps.add(self.cur_barrier[eng])
        if kind == "cc":
            o.sem = ("cc", eng, self.cc_count)
            self.cc_count += 1
            o.semv = 1
        deps.discard(o)
        o.deps = deps
        o.raw = raw
        o.pos = len(self.ops[eng])
        self.ops[eng].append(o)
        o.idx = len(self.all)
        self.all.append(o)
        self.since_barrier.append(o)
        return o

    def schedule(self, reorder=("pe", "act", "dve", "sp"), W=200):
        import bisect
        for o in self.all:
            o.succ = []
        for o in self.all:
            o.nd = len(o.deps)
            o.rt = 0.0
            for d in o.deps:
                d.succ.append(o)
        blev = {}
        for o in reversed(self.all):
            m = 0.0
            for c in o.succ:
                v = blev[id(c)]
                if v > m:
                    m = v
            blev[id(o)] = m + o.cost
        import os as _os3
        use_bl = _os3.environ.get("KPRIO", "bl") == "bl"
        for o in self.all:
            o.seq = (-blev[id(o)] if use_bl else 0.0, o.idx)
        avail = {e: [] for e in ENGS}
        ptr = {e: 0 for e in ENGS}
        free_at = {e: 0.0 for e in ENGS}
        order = {e: [] for e in ENGS}
        for o in self.all:
            if o.nd == 0 and o.eng in reorder:
                bisect.insort(avail[o.eng], (o.seq, o))
        remaining = len(self.all)
        while remaining:
            best = None
            for e in ENGS:
                if e in reorder:
                    lst = avail[e]
                    if not lst:
                        continue
                    fa = free_at[e]
                    cb = None
                    for (ix, o) in lst[:W]:
                        st = o.rt if o.rt > fa else fa
                        if cb is None or st < cb[0] - 1e-9:
                            cb = (st, ix, o)
                        if st <= fa:
                            break
                else:
                    if ptr[e] >= len(self.ops[e]):
                        continue
                    o = self.ops[e][ptr[e]]
                    if o.nd > 0:
                        continue
                    st = max(o.rt, free_at[e])
                    cb = (st, o.seq, o)
                if best is None or (cb[0], cb[1]) < (best[0], best[1]):
                    best = cb
            assert best is not None, "scheduler stuck"
            st, ix, o = best
            e = o.eng
            if e in reorder:
                i = bisect.bisect_left(avail[e], (ix, ))
                assert avail[e][i][1] is o
                avail[e].pop(i)
            else:
                ptr[e] += 1
            if o.kind in ("dma", "cc"):
                free_at[e] = st + (1.0 if e == "pool" else 0.15)
            else:
                free_at[e] = st + o.cost
            o.fin = st + o.cost
            order[e].append(o)
            remaining -= 1
            for c in o.succ:
                c.nd -= 1
                if o.fin > c.rt:
                    c.rt = o.fin
                if c.nd == 0 and c.eng in reorder:
                    bisect.insort(avail[c.eng], (c.seq, c))
        for e in ENGS:
            self.ops[e] = order[e]
            for i, o in enumerate(order[e]):
                o.pos = i
        self.makespan = max(o.fin for o in self.all)

    def barrier(self, engines):
        if self.stopped:
            return
        prior = list(self.since_barrier)
        b = self.op("dve", lambda e: e.nop(), cost=0.1)
        b.deps.update(prior)
        b.deps.discard(b)
        self.since_barrier = [b]
        for e in engines:
            self.cur_barrier[e] = b

    def finalize(self):
        for e in ENGS:
            n = 0
            hist = []
            for o in self.ops[e]:
                if o.kind != "dma":
                    continue
                o.sem = ("dma", e, n % self.NDS)
                o.semv = 16 * (n // self.NDS + 1)
                if n >= self.NDS:
                    o.deps.add(hist[n - self.NDS])
                hist.append(o)
                n += 1
        for e in ENGS:
            for o in self.ops[e]:
                for d in o.deps:
                    if d.kind != "c":
                        continue
                    if d.eng != o.eng:
                        d.needs_inc = True
                    elif d in o.raw and o.pos - d.pos <= 1:
                        d.needs_inc = True
        for e in ENGS:
            n = 0
            for o in self.ops[e]:
                if o.kind == "c" and o.needs_inc:
                    n += 1
                    o.seq = n
                    o.sem = ("eng", e)
                    o.semv = n

    def emit(self, eng, handle, sems):
        waited = {}
        for o in self.ops[eng]:
            need = {}
            for d in o.deps:
                if d.kind == "c":
                    if d.eng == eng and not (d in o.raw and o.pos - d.pos <= 1):
                        continue
                if need.get(d.sem, 0) < d.semv:
                    need[d.sem] = d.semv
            for k, v in need.items():
                if waited.get(k, 0) >= v:
                    continue
                waited[k] = v
                handle.wait_ge(sems[k], v)
            inst = o.fn(handle)
            if o.kind == "dma":
                inst.then_inc(sems[o.sem], 16)
            elif o.kind == "cc":
                inst.then_inc(sems[o.sem], 1)
            elif o.needs_inc:
                inst.then_inc(sems[o.sem], 1)


class Arena:
    def __init__(self, ap):
        self.ap = ap
        self.top = 0

    def alloc(self, parts, free, dtype):
        esz = 4 if dtype == F32 else 2
        n = int(np.prod(free)) * esz
        off = (self.top + 63) // 64 * 64
        self.top = off + n
        assert self.top <= ARENA, f"arena overflow {self.top}"
        v = self.ap[0:parts, off:off + n].bitcast(dtype)
        if len(free) == 2:
            v = v.rearrange("p (a b) -> p a b", a=free[0])
        elif len(free) == 3:
            v = v.rearrange("p (a b c) -> p a b c", a=free[0], b=free[1])
        return v


def arena_at(arena, off, parts, free, dtype):
    esz = 4 if dtype == F32 else 2
    n = int(np.prod(free)) * esz
    assert off % 64 == 0 and off + n <= ARENA
    v = arena.ap[0:parts, off:off + n].bitcast(dtype)
    if len(free) == 2:
        v = v.rearrange("p (a b) -> p a b", a=free[0])
    arena.top = max(arena.top, off + n)
    return v


class Ring:
    def __init__(self, items):
        self.items = items
        self.i = 0

    def next(self):
        it = self.items[self.i % len(self.items)]
        self.i += 1
        return it


def build_program():
    nc = bass.Bass("TRN2", target_bir_lowering=False)

    def din(name, shape):
        return nc.dram_tensor(name, shape, F32, kind="ExternalInput").ap()

    xTt = din("xTt", [S // SUB, 128, NKC * SUB])
    xo2 = din("xo2", [16, 2, 128, 512])
    pTt = din("pTt", [128, 2 * 1024])
    w1a = din("w1a", [128, NKC * 1408])
    w1b = din("w1b", [128, NKC * 1536])
    vec88 = din("vec88", [88, 64])
    vec128 = din("vec128", [128, 50])
    wax = din("wax", [88, 2 * 4 * 2 * 176])
    cmat = din("cmat", [128, 384 + 896])
    wt_g = din("wt_g", [32, 128, 2048])
    wt_rnn = din("wt_rnn", [16, 2, 128, 11 * 128])
    wt_att = din("wt_att", [16, 128, 2048])
    wt_o = din("wt_o", [16, 128, 2048])
    wt_gu = din("wt_gu", [88, 128, 2048])
    wt_dn = din("wt_dn", [2, 16, 2, 128, 11 * 128])
    wt_pg = din("wt_pg", [16, 128, 2048])
    wt_pp = din("wt_pp", [16, 128, 2 * 128])
    outT = nc.dram_tensor("outT", [16, 128, 1024], F32, kind="ExternalOutput").ap()
    cin_yr = nc.dram_tensor("cin_yr", [NCH, 704, TC], BF16)
    cout_yr = nc.dram_tensor("cout_yr", [NCH, 2816, TC], BF16)
    cin_o = nc.dram_tensor("cin_o", [NCH, 512, TC], BF16)
    cout_o = nc.dram_tensor("cout_o", [NCH, 2048, TC], BF16)
    RG = [[0, 1, 2, 3], [4, 5, 6, 7]]
    hbuf = nc.dram_tensor("hbuf", [NCH, 128, NKC * TC], BF16)
    s_hbuf = [Slot() for _ in range(NCH)]
    s_cyr = [Slot() for _ in range(NCH)]
    s_co = [Slot() for _ in range(NCH)]

    fw = FW()
    es = ExitStack()
    with es:
        arena_t = es.enter_context(nc.sbuf_tensor("arena", [128, ARENA], U8))
        cst_t = es.enter_context(nc.sbuf_tensor("cst", [128, 2432], F32))
        psb2 = [es.enter_context(nc.psum_tensor(f"ps{i}", [128, 1024], F32)) for i in range(4)]
        psb = [psb2[i // 2][:, (i % 2) * 512:(i % 2 + 1) * 512] for i in range(8)]
        ps_slots = [Slot(f"ps{i}", excl=True) for i in range(8)]

        cst = cst_t
        c_vec128 = cst[:, 0:50]
        c_vec88 = cst[0:88, 64:128]
        c_der88 = cst[0:88, 128:160]
        c_num = cst[:, 160:168]
        cb = cst[:, 192:192 + 768].bitcast(BF16)
        c_ident = cb[:, 0:128]
        c_L = cb[:, 128:256]
        c_SU = cb[:, 256:384]
        c_mask = cb[:, 384:384 + 896]
        c_onesm = cb[:, 1280:1408]
        c_onesh = cb[:, 1408:1536]
        c_halo = cst[0:88, 968:968 + 24].rearrange("p (a b) -> p a b", a=8)
        c_hlast = cst[0:88, 1000:1008]
        c_wax = cst[0:88, 1016:1016 + 1408].bitcast(BF16).rearrange("p (w n i c) -> p w n i c", w=2, n=4, i=2)
        s_const = Slot("const")
        s_halo = [Slot() for _ in range(4)]
        s_hlast = [Slot() for _ in range(8)]
        EPS_AP = c_num[:, 0:1]
        ONE_AP = c_num[:, 1:2]

        def g128(col):
            return c_vec128[:, col:col + 1]

        fw.op("sp", lambda e: e.dma_start(out=c_vec128, in_=vec128[:, :]), writes=[s_const], kind="dma")
        fw.op("sp", lambda e: e.dma_start(out=c_vec88, in_=vec88[:, :]), writes=[s_const], kind="dma")
        fw.op("pool", lambda e: e.dma_start(out=cb[:, 0:1280], in_=cmat[:, :]), writes=[s_const], kind="dma")
        fw.op("pool", lambda e: e.dma_start(out=c_wax.rearrange("p w n i c -> p (w n i c)"), in_=wax[:, :]),
              writes=[s_const], kind="dma")

        def setup_consts(e):
            e.memset(c_num[:, 0:1], EPS)
            e.memset(c_num[:, 1:2], 1.0)
            e.memset(c_num[:, 2:3], 0.0)
            e.memset(c_onesm, 1.0 / 2048.0)
            e.memset(c_onesh, 1.0 / 128.0)
            e.memset(c_halo, 0.0)
            return e.memset(c_hlast, 0.0)
        fw.op("dve", setup_consts, writes=[s_const] + s_halo + s_hlast)

        v88 = c_vec88.rearrange("p (c f) -> p c f", c=8)
        hba = c_der88[:, 0:8]
        hbx = c_der88[:, 8:16]
        scl = c_der88[:, 16:24]
        tmp8 = c_der88[:, 24:32]
        fw.op("act", lambda e: e.activation(out=tmp8, in_=v88[:, :, 7], func=AF.Exp, scale=-1.0),
              reads=[s_const], writes=[s_const])
        fw.op("dve", lambda e: e.tensor_scalar(out=hba, in0=v88[:, :, 5], scalar1=0.5, scalar2=None, op0=ALU.mult),
              reads=[s_const], writes=[s_const])
        fw.op("act", lambda e: e.activation(out=scl, in_=tmp8, func=AF.Ln, bias=ONE_AP[0:88, :], scale=1.0),
              reads=[s_const], writes=[s_const])
        fw.op("dve", lambda e: e.tensor_scalar(out=hbx, in0=v88[:, :, 6], scalar1=0.5, scalar2=None, op0=ALU.mult),
              reads=[s_const], writes=[s_const])
        fw.op("dve", lambda e: e.tensor_scalar(out=scl, in0=scl, scalar1=-4.0, scalar2=None, op0=ALU.mult),
              reads=[s_const], writes=[s_const])

        arena_ap = arena_t[:, :]
        fw.stop_if("setup")

        def rmsnorm_sub(load_fn, src_ap, src_slots, gcol0, dst_ap, dst_slot, ntok, scratch, psring, sq_eng="act"):
            sq, s_sq, lnm, s_ln = scratch
            if load_fn is not None:
                load_fn()
            if sq_eng == "act":
                fw.op("act", lambda e: e.activation(out=sq, in_=src_ap, func=AF.Square),
                      reads=src_slots, writes=[s_sq], cost=2.0)
            else:
                fw.op("dve", lambda e: e.tensor_tensor(out=sq, in0=src_ap, in1=src_ap, op=ALU.mult),
                      reads=src_slots, writes=[s_sq], cost=2.4)
            ps, s_ps = psring.next()

            def f_mm(e):
                for kc in range(NKC):
                    last = e.matmul(ps[:, 0:ntok], lhsT=c_onesm, rhs=sq[:, kc, :], start=(kc == 0), stop=(kc == NKC - 1))
                return last
            fw.op("pe", f_mm, reads=[s_sq, s_const], writes=[s_ps], cost=16 * 0.12)
            fw.op("act", lambda e: e.activation(out=lnm[:, 0:ntok], in_=ps[:, 0:ntok], func=AF.Ln, bias=EPS_AP, scale=1.0),
                  reads=[s_ps], writes=[s_ln])
            fw.op("act", lambda e: e.activation(out=lnm[:, ntok:2 * ntok], in_=lnm[:, 0:ntok], func=AF.Exp, scale=-0.5),
                  reads=[s_ln], writes=[s_ln])

            def f_h(e):
                for kc in range(NKC):
                    last = e.scalar_tensor_tensor(out=dst_ap[:, kc, :], in0=src_ap[:, kc, :], scalar=g128(gcol0 + kc),
                                                  in1=lnm[:, ntok:2 * ntok], op0=ALU.mult, op1=ALU.mult)
                return last
            fw.op("dve", f_h, reads=list(src_slots) + [s_ln, s_const], writes=[dst_slot], cost=16 * 0.22)

        A = Arena(arena_ap)
        W1a = A.alloc(128, (NKC, 1408), BF16)
        s_W1a = Slot("W1a")
        off_xs = A.top
        xs_r = [(A.alloc(128, (NKC, SUB), F32), Slot()) for _ in range(2)]
        xsR = Ring(xs_r)
        sq_a = A.alloc(128, (NKC, SUB), BF16)
        lnm_a = A.alloc(128, (2 * SUB,), F32)
        scratchA = (sq_a, Slot(), lnm_a, Slot())
        off_hT0 = A.top
        hT_ring = Ring([(A.alloc(128, (NKC, TC), BF16), [Slot() for _ in range(TC // SUB)]) for _ in range(2)])
        hT0_slots_A = hT_ring.items[0][1]
        xs_slots_A = [xs_r[0][1], xs_r[1][1]]
        yrb = [(A.alloc(88, (8, TC), BF16), Slot()) for _ in range(2)]
        NSET = 2
        sets = []
        for _ in range(NSET):
            d = {}
            d["xr"] = A.alloc(88, (2, TC + 3), F32)
            d["gr"] = A.alloc(88, (2, TC), F32)
            d["xc"] = A.alloc(88, (2, TC), F32)
            d["xcb"] = A.alloc(88, (2, TC), BF16)
            for nm in ("rp", "ip", "a", "t", "s", "h"):
                d[nm] = A.alloc(88, (2, TC), F32)
            d["slots"] = {nm: [Slot(), Slot()] for nm in ("xr", "gr", "xc", "xcb", "rp", "ip", "a", "t", "s", "h")}
            sets.append(d)
        psA = Ring(list(zip([p[:, :] for p in psb[0:6]], ps_slots[0:6])))
        psA_ms = Ring(list(zip([p[:, :] for p in psb[6:8]], ps_slots[6:8])))

        for q4 in range(4):
            fw.op("pool", lambda e, q4=q4: e.dma_start(
                out=W1a[:, 4 * q4:4 * q4 + 4, :],
                in_=w1a[:, 4 * q4 * 1408:(4 * q4 + 4) * 1408].rearrange("p (k n) -> p k n", k=4)),
                writes=[s_W1a], kind="dma")

        def proj_fm(Wt, s_W, col0, M, ps, s_ps, h_ap, h_slots, ntok=TC):
            def f(e):
                for kc in range(NKC):
                    last = e.matmul(ps[0:M, 0:ntok], lhsT=Wt[:, kc, col0:col0 + M], rhs=h_ap[:, kc, :],
                                    start=(kc == 0), stop=(kc == NKC - 1))
                return last
            fw.op("pe", f, reads=[s_W] + list(h_slots), writes=[s_ps])

        def hT_chunk(G, xsrc, scratch, psring, sq_eng="act"):
            hT, s_hT = hT_ring.next()
            for sb in range(TC // SUB):
                sc = G * (TC // SUB) + sb
                xs, s_xs = xsR.next()
                fw.op("sp", lambda e, xs=xs, sc=sc: e.dma_start(out=xs.rearrange("p k t -> p (k t)"), in_=xsrc[sc]),
                      writes=[s_xs], kind="dma")
                rmsnorm_sub(None, xs, [s_xs], 0, hT[:, :, sb * SUB:(sb + 1) * SUB], s_hT[sb], SUB, scratch, psring, sq_eng)
            return hT, s_hT

        Wk = arena_at(A, off_xs, 128, (NKC, 512), BF16)
        Wq = arena_at(A, off_hT0, 128, (NKC, 512), BF16)
        s_Wq, s_Wk, s_Wv = Slot("Wq"), Slot("Wk"), Slot("Wv")
        w1b_v = w1b.rearrange("p (k n) -> p k n", k=NKC)
        for G in range(NCH):
            hT, s_hT = hT_chunk(G, xTt, scratchA, psA_ms)
            if G == NCH - 1 and not fw.stopped:
                fw.op("pool", lambda e: e.dma_start(out=Wq, in_=w1b_v[:, :, 0:512]), writes=[s_Wq] + hT0_slots_A, kind="dma", cost=12.0)
                fw.op("pool", lambda e: e.dma_start(out=Wk, in_=w1b_v[:, :, 512:1024]), writes=[s_Wk] + xs_slots_A, kind="dma", cost=12.0)
            fw.op("sp", lambda e, hT=hT, G=G: e.dma_start(out=hbuf.ap()[G], in_=hT.rearrange("p k t -> p (k t)")),
                  reads=s_hT, writes=[s_hbuf[G]], kind="dma")
            yb, s_yb = yrb[G % 2]
            for n in range(4):
                st = sets[(G * 4 + n) % NSET]
                sl = st["slots"]
                for j in range(2):
                    c8 = 2 * n + j
                    ps, s_ps = psA.next()
                    proj_fm(W1a, s_W1a, c8 * 88, 88, ps, s_ps, hT, s_hT)
                    fw.op("act", lambda e, ps=ps, st=st, j=j: e.activation(out=st["xr"][:, j, 3:TC + 3], in_=ps[0:88, :], func=AF.Copy),
                          reads=[s_ps], writes=[sl["xr"][j]])
                    ps, s_ps = psA.next()
                    proj_fm(W1a, s_W1a, 704 + c8 * 88, 88, ps, s_ps, hT, s_hT)
                    fw.op("act", lambda e, ps=ps, st=st, j=j: e.activation(out=st["gr"][:, j, :], in_=ps[0:88, :], func=AF.Copy),
                          reads=[s_ps], writes=[sl["gr"][j]])
                fw.op("dve", lambda e, st=st, n=n: e.tensor_copy(out=st["xr"][:, :, 0:3], in_=c_halo[:, 2 * n:2 * n + 2, :]),
                      reads=[s_halo[n]], writes=sl["xr"])
                fw.op("dve", lambda e, st=st, n=n: e.tensor_copy(out=c_halo[:, 2 * n:2 * n + 2, :], in_=st["xr"][:, :, TC:TC + 3]),
                      reads=sl["xr"], writes=[s_halo[n]])
                for k in range(4):
                    for j in range(2):
                        c8 = 2 * n + j
                        if k == 0:
                            fw.op("dve", lambda e, st=st, j=j, c8=c8: e.tensor_scalar(
                                out=st["xc"][:, j, :], in0=st["xr"][:, j, 0:TC], scalar1=v88[:, c8, 0:1], scalar2=v88[:, c8, 4:5],
                                op0=ALU.mult, op1=ALU.add), reads=[sl["xr"][j], s_const], writes=[sl["xc"][j]])
                        else:
                            fw.op("dve", lambda e, st=st, j=j, c8=c8, k=k: e.scalar_tensor_tensor(
                                out=st["xc"][:, j, :], in0=st["xr"][:, j, k:k + TC], scalar=v88[:, c8, k:k + 1], in1=st["xc"][:, j, :],
                                op0=ALU.mult, op1=ALU.add), reads=[sl["xr"][j], sl["xc"][j], s_const], writes=[sl["xc"][j]])
                for j in range(2):
                    fw.op("act", lambda e, st=st, j=j: e.activation(out=st["xcb"][:, j, :], in_=st["xc"][:, j, :], func=AF.Copy),
                          reads=[sl["xc"][j]], writes=[sl["xcb"][j]])
                gps = []
                for j in range(2):
                    for w in range(2):
                        ps, s_ps = psA.next()

                        def f(e, ps=ps, st=st, j=j, w=w, n=n):
                            for i in range(2):
                                last = e.matmul(ps[0:88, :], lhsT=c_wax[:, w, n, i, 88 * j:88 * j + 88], rhs=st["xcb"][:, i, :],
                                                start=(i == 0), stop=(i == 1))
                            return last
                        fw.op("pe", f, reads=sl["xcb"] + [s_const], writes=[s_ps], cost=0.9)
                        gps.append((ps, s_ps))
                for j in range(2):
                    c8 = 2 * n + j
                    ps, s_ps = gps[2 * j]
                    fw.op("act", lambda e, ps=ps, st=st, j=j, c8=c8: e.activation(
                        out=st["rp"][:, j, :], in_=ps[0:88, :], func=AF.Tanh, bias=hba[:, c8:c8 + 1], scale=0.5),
                        reads=[s_ps, s_const], writes=[sl["rp"][j]])
                    ps, s_ps = gps[2 * j + 1]
                    fw.op("act", lambda e, ps=ps, st=st, j=j, c8=c8: e.activation(
                        out=st["ip"][:, j, :], in_=ps[0:88, :], func=AF.Tanh, bias=hbx[:, c8:c8 + 1], scale=0.5),
                        reads=[s_ps, s_const], writes=[sl["ip"][j]])
                for j in range(2):
                    c8 = 2 * n + j
                    fw.op("act", lambda e, st=st, j=j, c8=c8: e.activation(
                        out=st["a"][:, j, :], in_=st["rp"][:, j, :], func=AF.Exp, bias=scl[:, c8:c8 + 1], scale=scl[:, c8:c8 + 1]),
                        reads=[sl["rp"][j], s_const], writes=[sl["a"][j]])
                for j in range(2):
                    fw.op("dve", lambda e, st=st, j=j: e.scalar_tensor_tensor(
                        out=st["t"][:, j, :], in0=st["ip"][:, j, :], scalar=1.0, in1=st["xc"][:, j, :], op0=ALU.add, op1=ALU.mult),
                        reads=[sl["ip"][j], sl["xc"][j]], writes=[sl["t"][j]])
                for j in range(2):
                    fw.op("act", lambda e, st=st, j=j: e.activation(out=st["s"][:, j, :], in_=st["a"][:, j, :], func=AF.Square),
                          reads=[sl["a"][j]], writes=[sl["s"][j]])
                for j in range(2):
                    fw.op("act", lambda e, st=st, j=j: e.activation(out=st["s"][:, j, :], in_=st["s"][:, j, :], func=AF.Sqrt,
                                                                    bias=ONE_AP[0:88, :], scale=-1.0),
                          reads=[sl["s"][j], s_const], writes=[sl["s"][j]])
                for j in range(2):
                    fw.op("dve", lambda e, st=st, j=j: e.scalar_tensor_tensor(
                        out=st["t"][:, j, :], in0=st["s"][:, j, :], scalar=0.5, in1=st["t"][:, j, :], op0=ALU.mult, op1=ALU.mult),
                        reads=[sl["s"][j], sl["t"][j]], writes=[sl["t"][j]])
                for j in range(2):
                    c8 = 2 * n + j
                    fw.op("dve", lambda e, st=st, j=j, c8=c8: e.tensor_tensor_scan(
                        out=st["h"][:, j, :], data0=st["a"][:, j, :], data1=st["t"][:, j, :], initial=c_hlast[:, c8:c8 + 1],
                        op0=ALU.mult, op1=ALU.add), reads=[sl["a"][j], sl["t"][j], s_hlast[c8]], writes=[sl["h"][j]])
                for j in range(2):
                    c8 = 2 * n + j
                    fw.op("act", lambda e, st=st, j=j, c8=c8: e.activation(out=c_hlast[:, c8:c8 + 1], in_=st["h"][:, j, TC - 1:TC], func=AF.Copy),
                          reads=[sl["h"][j]], writes=[s_hlast[c8]])
                for j in range(2):
                    fw.op("dve", lambda e, st=st, j=j: e.tensor_tensor(out=st["ip"][:, j, :], in0=st["gr"][:, j, :], in1=st["gr"][:, j, :], op=ALU.mult),
                          reads=[sl["gr"][j]], writes=[sl["ip"][j]])
                for j in range(2):
                    fw.op("dve", lambda e, st=st, j=j: e.tensor_scalar(out=st["ip"][:, j, :], in0=st["ip"][:, j, :], scalar1=0.044715, scalar2=1.0,
                                                                      op0=ALU.mult, op1=ALU.add), reads=[sl["ip"][j]], writes=[sl["ip"][j]])
                for j in range(2):
                    fw.op("dve", lambda e, st=st, j=j: e.tensor_tensor(out=st["ip"][:, j, :], in0=st["ip"][:, j, :], in1=st["gr"][:, j, :], op=ALU.mult),
                          reads=[sl["ip"][j], sl["gr"][j]], writes=[sl["ip"][j]])
                for j in range(2):
                    fw.op("act", lambda e, st=st, j=j: e.activation(out=st["rp"][:, j, :], in_=st["ip"][:, j, :], func=AF.Tanh, scale=GELU_C),
                          reads=[sl["ip"][j]], writes=[sl["rp"][j]])
                for j in range(2):
                    fw.op("dve", lambda e, st=st, j=j: e.scalar_tensor_tensor(
                        out=st["ip"][:, j, :], in0=st["rp"][:, j, :], scalar=1.0, in1=st["gr"][:, j, :], op0=ALU.add, op1=ALU.mult),
                        reads=[sl["rp"][j], sl["gr"][j]], writes=[sl["ip"][j]])
                for j in range(2):
                    c8 = 2 * n + j
                    fw.op("dve", lambda e, st=st, j=j, c8=c8, yb=yb: e.scalar_tensor_tensor(
                        out=yb[:, c8, :], in0=st["ip"][:, j, :], scalar=0.5, in1=st["h"][:, j, :], op0=ALU.mult, op1=ALU.mult),
                        reads=[sl["ip"][j], sl["h"][j]], writes=[s_yb])
            s_ci = Slot()
            fw.op("sp", lambda e, yb=yb, G=G: e.dma_start(
                out=cin_yr.ap()[G].rearrange("(c p) t -> p c t", p=88), in_=yb),
                reads=[s_yb], writes=[s_ci], kind="dma")
            fw.op("pool", lambda e, G=G: e.collective_compute("AllGather", ALU.bypass, replica_groups=RG,
                                                              ins=[cin_yr.ap()[G].opt()], outs=[cout_yr.ap()[G].opt()]),
                  reads=[s_ci], writes=[s_cyr[G]], kind="cc")
            if G == 0:
                fw.stop_if("passA1")
        fw.stop_if("passA")


        fw.stop_if("ag1")
        B = Arena(arena_ap)
        assert off_xs == 45056 and off_hT0 == 66560, (off_xs, off_hT0)
        KT = arena_at(B, 0, 128, (4, S), BF16)
        s_KT = [[Slot() for _ in range(NCH)] for _ in range(4)]
        e_r = [(arena_at(B, 32768 + 4096 * i, 128, (2, TC), F32), Slot()) for i in range(3)]
        nrm = [dict(raw=arena_at(B, 61440, 128, (TC,), F32), sq=arena_at(B, 63488, 128, (TC,), BF16),
                    ln=arena_at(B, 64512, 128, (TC,), F32), s_raw=Slot(), s_sq=Slot(), s_ln=Slot())]
        B.top = off_hT0 + 16384
        Wv = B.alloc(128, (NKC, 512), BF16)
        Vt = B.alloc(128, (S // 128, 512), BF16)
        s_V = [Slot() for _ in range(S // 128)]
        hT_ring.items = [(B.alloc(128, (NKC, TC), BF16), [Slot() for _ in range(TC // SUB)]) for _ in range(2)]
        QT_ring = Ring([(B.alloc(128, (4, TC), BF16), [Slot() for _ in range(4)]) for _ in range(2)])
        nrm.append(dict(raw=B.alloc(128, (TC,), F32), sq=B.alloc(128, (TC,), BF16), ln=B.alloc(128, (TC,), F32),
                        s_raw=Slot(), s_sq=Slot(), s_ln=Slot()))
        nrmR = Ring(nrm)
        sp_r = [(B.alloc(128, (2, TC), BF16), Slot()) for _ in range(3)]
        x_r = [(B.alloc(128, (2, TC), F32), Slot()) for _ in range(2)]
        w_r = [(B.alloc(128, (2, TC), BF16), Slot()) for _ in range(3)]
        psz2 = psb2[0][:, :].rearrange("p (a b) -> p a b", a=2)
        psc2 = psb2[1][:, :].rearrange("p (a b) -> p a b", a=2)
        obf = [(B.alloc(128, (TC,), BF16), Slot()) for _ in range(2)]
        obfR = Ring(obf)
        psz = [(psb[0][:, :], ps_slots[0]), (psb[1][:, :], ps_slots[1])]
        psc = [(psb[2][:, :], ps_slots[2]), (psb[3][:, :], ps_slots[3])]
        pso = [(psb[4][:, :], ps_slots[4]), (psb[5][:, :], ps_slots[5])]
        psM = Ring([(psb[6][:, :], ps_slots[6]), (psb[7][:, :], ps_slots[7])])

        fw.barrier(ENGS)
        fw.op("pool", lambda e: e.dma_start(out=Wv, in_=w1b_v[:, :, 1024:1536]), writes=[s_Wv], kind="dma", cost=12.0)

        def qk_norm(Wt, s_Wt, col0, gain_col, dst_ap, dst_slot, hT, s_hT):
            nb = nrmR.next()
            ps, s_ps = psM.next()
            proj_fm(Wt, s_Wt, col0, 128, ps, s_ps, hT, s_hT)
            fw.op("dve", lambda e: e.tensor_copy(out=nb["raw"], in_=ps), reads=[s_ps], writes=[nb["s_raw"]])
            fw.op("dve", lambda e: e.tensor_tensor(out=nb["sq"], in0=nb["raw"], in1=nb["raw"], op=ALU.mult),
                  reads=[nb["s_raw"]], writes=[nb["s_sq"]])
            ps2, s_ps2 = psM.next()
            fw.op("pe", lambda e: e.matmul(ps2, lhsT=c_onesh, rhs=nb["sq"], start=True, stop=True),
                  reads=[nb["s_sq"], s_const], writes=[s_ps2], cost=0.3)
            fw.op("act", lambda e: e.activation(out=nb["ln"], in_=ps2, func=AF.Ln, bias=EPS_AP, scale=1.0),
                  reads=[s_ps2], writes=[nb["s_ln"]])
            fw.op("act", lambda e: e.activation(out=nb["ln"], in_=nb["ln"], func=AF.Exp, scale=-0.5),
                  reads=[nb["s_ln"]], writes=[nb["s_ln"]])
            fw.op("dve", lambda e: e.scalar_tensor_tensor(out=dst_ap, in0=nb["raw"], scalar=g128(gain_col), in1=nb["ln"],
                                                          op0=ALU.mult, op1=ALU.mult),
                  reads=[nb["s_raw"], nb["s_ln"], s_const], writes=[dst_slot])

        import os as _os
        for G in range(int(_os.environ.get("KNB", NCH))):
            fw.stop_if("b_w")
            hT, s_hT = hT_ring.next()
            fw.op("sp", lambda e, hT=hT, G=G: e.dma_start(out=hT.rearrange("p k t -> p (k t)"), in_=hbuf.ap()[G]),
                  reads=[s_hbuf[G]], writes=s_hT, kind="dma")
            QT, s_QT = QT_ring.next()
            fw.stop_if("b_h")
            for hh in range(4):
                qk_norm(Wq, s_Wq, hh * 128, 48, QT[:, hh, :], s_QT[hh], hT, s_hT)
                qk_norm(Wk, s_Wk, hh * 128, 49, KT[:, hh, G * TC:(G + 1) * TC], s_KT[hh][G], hT, s_hT)
            fw.stop_if("b_qk")
            for tb in range(4):
                ps, s_ps = psM.next()

                def f(e, ps=ps, tb=tb, hT=hT):
                    for kc in range(NKC):
                        last = e.matmul(ps, lhsT=hT[:, kc, tb * 128:(tb + 1) * 128], rhs=Wv[:, kc, :],
                                        start=(kc == 0), stop=(kc == NKC - 1))
                    return last
                fw.op("pe", f, reads=[s_Wv] + s_hT, writes=[s_ps])
                fw.op("dve", lambda e, ps=ps, tb=tb, G=G: e.tensor_copy(out=Vt[:, 4 * G + tb, :], in_=ps),
                      reads=[s_ps], writes=[s_V[4 * G + tb]])
            fw.stop_if("b_v")
            NT = 4 * G + 4
            s_cio = [Slot() for _ in range(4)]
            for hp in range(2):
                heads = [2 * hp, 2 * hp + 1]
                etile = {}
                sptile = {}
                wtile = {}

                def lo_of(s):
                    d = (NT - 1 - s) - 4 * G
                    return 128 * d if d > 0 else 0

                def stageZ(s):
                    kb = NT - 1 - s
                    diag = kb - 4 * G
                    lo = lo_of(s)
                    for li, hh in enumerate(heads):
                        ps, s_ps = psz[li]

                        def f(e, ps=ps, hh=hh, kb=kb, diag=diag, QT=QT, lo=lo):
                            last = e.matmul(ps[:, lo:TC], lhsT=KT[:, hh, kb * 128:(kb + 1) * 128], rhs=QT[:, hh, lo:TC],
                                            start=True, stop=(diag < 0), skip_group_check=True)
                            if diag >= 0:
                                o0 = 384 - 128 * diag
                                last = e.matmul(ps[:, lo:TC], lhsT=c_ident, rhs=c_mask[:, o0 + lo:o0 + TC], start=False, stop=True,
                                                skip_group_check=True)
                            return last
                        fw.op("pe", f, reads=[s_KT[hh][kb // 4], s_QT[hh], s_const], writes=[s_ps], cost=0.6 if diag >= 0 else 0.3)
                    eb, s_eb = e_r[s % 3]
                    etile[s] = (eb, s_eb)
                    fw.op("act", lambda e, eb=eb, lo=lo: e.activation(out=eb[:, :, lo:TC], in_=psz2[:, :, lo:TC], func=AF.Exp, scale=float(QSCALE)),
                          reads=[psz[0][1], psz[1][1]], writes=[s_eb], cost=1.1)
                    sb_, s_sb = sp_r[s % 3]
                    sptile[s] = (sb_, s_sb)
                    fw.op("act", lambda e, eb=eb, sb_=sb_, lo=lo: e.activation(out=sb_[:, :, lo:TC], in_=eb[:, :, lo:TC], func=AF.Ln, bias=ONE_AP, scale=1.0),
                          reads=[s_eb, s_const], writes=[s_sb], cost=1.1)

                def stageC(s):
                    sb_, s_sb = sptile[s]
                    lo = lo_of(s)
                    lp = lo_of(s - 1) if s > 0 else 0
                    for li, hh in enumerate(heads):
                        ps, s_ps = psc[li]
                        rd = [s_sb, s_const]
                        if s > 0:
                            sprev, s_sprev = sptile[s - 1]
                            rd += [s_sprev, s_ps]
                        else:
                            sprev = None

                        def f(e, ps=ps, sb_=sb_, sprev=sprev, s=s, li=li, lo=lo, lp=lp):
                            if s > 0:
                                e.matmul(ps[:, lp:TC], lhsT=c_SU, rhs=sprev[:, li, lp:TC], start=False, stop=False, skip_group_check=True)
                            return e.matmul(ps[:, lo:TC], lhsT=c_L, rhs=sb_[:, li, lo:TC], start=(s == 0), stop=True, skip_group_check=True)
                        fw.op("pe", f, reads=rd, writes=[s_ps], cost=0.6 if s > 0 else 0.3)
                    xb, s_xb = x_r[s % 2]
                    fw.op("act", lambda e, xb=xb, lo=lo: e.activation(out=xb[:, :, lo:TC], in_=psc2[:, :, lo:TC], func=AF.Exp, scale=-1.0),
                          reads=[psc[0][1], psc[1][1]], writes=[s_xb], cost=1.1)
                    eb, s_eb = etile[s]
                    wb, s_wb = w_r[s % 3]
                    wtile[s] = (wb, s_wb)
                    fw.op("dve", lambda e, xb=xb, eb=eb, wb=wb, lo=lo: e.tensor_tensor(out=wb[:, :, lo:TC], in0=eb[:, :, lo:TC], in1=xb[:, :, lo:TC], op=ALU.mult),
                          reads=[s_xb, s_eb], writes=[s_wb], cost=1.2)

                def stageO(s):
                    kb = NT - 1 - s
                    lo = lo_of(s)
                    for li, hh in enumerate(heads):
                        ps, s_ps = pso[li]
                        wb, s_wb = wtile[s]
                        fw.op("pe", lambda e, ps=ps, wb=wb, hh=hh, kb=kb, s=s, li=li, lo=lo: e.matmul(
                            ps[:, lo:TC], lhsT=Vt[:, kb, hh * 128:(hh + 1) * 128], rhs=wb[:, li, lo:TC], start=(s == 0), stop=(s == NT - 1),
                            skip_group_check=True),
                            reads=[s_wb, s_V[kb], s_ps] if s > 0 else [s_wb, s_V[kb]], writes=[s_ps], cost=0.3)

                for step in range(NT + 2):
                    if step < NT:
                        stageZ(step)
                    if 0 <= step - 1 < NT:
                        stageC(step - 1)
                    if 0 <= step - 2 < NT:
                        stageO(step - 2)
                for li, hh in enumerate(heads):
                    ps, s_ps = pso[li]
                    ob, s_ob = obfR.next()
                    fw.op("dve", lambda e, ps=ps, ob=ob: e.tensor_copy(out=ob, in_=ps), reads=[s_ps], writes=[s_ob])
                    fw.op("sp", lambda e, ob=ob, hh=hh, G=G: e.dma_start(
                        out=cin_o.ap()[G, hh * 128:(hh + 1) * 128, :], in_=ob), reads=[s_ob], writes=[s_cio[hh]], kind="dma")
            fw.op("pool", lambda e, G=G: e.collective_compute("AllGather", ALU.bypass, replica_groups=RG,
                                                              ins=[cin_o.ap()[G].opt()], outs=[cout_o.ap()[G].opt()]),
                  reads=s_cio, writes=[s_co[G]], kind="cc")

        fw.stop_if("passB")

        P = Arena(arena_ap)
        NU = 10
        wunits = [(P.alloc(128, (NKC, 128), BF16), Slot()) for _ in range(NU)]
        wu_i = [0]
        tmpf = [(P.alloc(128, (TC,), F32), Slot()) for _ in range(4)]
        tmpR = Ring(tmpf)
        R1 = P.alloc(128, (NKC, 1024), F32)
        R1b = R1.rearrange("p k t -> p (k t)").bitcast(BF16)
        h_a = R1b[:, 0:NKC * 1024].rearrange("p (k t) -> p k t", k=NKC)
        oT_a = R1b[:, NKC * 1024:2 * NKC * 1024].rearrange("p (k t) -> p k t", k=NKC)
        R2 = P.alloc(128, (22, 1024), BF16)
        R3 = P.alloc(128, (NKC, 1024), BF16)
        pTb = P.alloc(128, (2, 1024), BF16)
        s_xo = [[Slot() for _ in range(2)] for _ in range(NKC)]
        s_R1h = [Slot() for _ in range(8)]
        s_oT = Slot()
        s_R2 = Slot()
        s_R3 = [[Slot() for _ in range(2)] for _ in range(NKC)]
        s_h2 = [Slot() for _ in range(8)]
        s_pT = Slot()
        psP = Ring(list(zip([p[:, :] for p in psb], ps_slots)))
        sq_p = P.alloc(128, (NKC, SUB), BF16)
        lnm_p = P.alloc(128, (2 * SUB,), F32)
        scratchP = (sq_p, Slot(), lnm_p, Slot())
        xs_rp = [(R3.rearrange("p k t -> p (k t)")[:, 12 * 1024:16 * 1024].bitcast(F32).rearrange("p (k t) -> p k t", k=NKC), Slot())]

        fw.barrier(ENGS)

        def wload(src_ap, nk=NKC):
            u, s_u = wunits[wu_i[0] % NU]
            wu_i[0] += 1
            fw.op("pool", lambda e: e.dma_start(out=u[:, 0:nk, :].rearrange("p k n -> p (k n)"), in_=src_ap),
                  writes=[s_u], kind="dma")
            return u, s_u

        def acc_group(ps, s_ps, parts, tb, extra_reads):
            def f(e):
                tot = sum(p[2] for p in parts)
                i = 0
                for (u, s_u, nk, act, k0, rows) in parts:
                    for kc in range(nk):
                        last = e.matmul(ps, lhsT=u[0:rows, kc, :], rhs=act[0:rows, k0 + kc, tb * TC:(tb + 1) * TC],
                                        start=(i == 0), stop=(i == tot - 1))
                        i += 1
                return last
            fw.op("pe", f, reads=[p[1] for p in parts] + list(extra_reads), writes=[s_ps], cost=0.29 * sum(p[2] for p in parts))

        for tb in range(2):
            def f_hh(e, tb=tb):
                rank = e.partition_id() % 4
                return e.dma_start(out=h_a[:, :, tb * TC:(tb + 1) * TC],
                                   in_=hbuf.ap()[bass.ds(2 * rank + tb, 1)].rearrange("o p (k t) -> p (o k) t", k=NKC))
            fw.op("pool", f_hh, reads=s_hbuf, writes=s_R1h[4 * tb:4 * tb + 4], kind="dma")

        def decl_yr():
            for tb in range(2):
                def f_yr(e, tb=tb):
                    rank = e.partition_id() % 4
                    return e.dma_start(out=R2[:, :, tb * TC:(tb + 1) * TC],
                                       in_=cout_yr.ap()[bass.ds(2 * rank + tb, 1)].rearrange("o (k p) t -> p (o k) t", p=128))
                fw.op("pool", f_yr, reads=s_cyr, writes=[s_R2], kind="dma")

        def decl_o():
            for tb in range(2):
                def f_o(e, tb=tb):
                    rank = e.partition_id() % 4
                    return e.dma_start(out=oT_a[:, :, tb * TC:(tb + 1) * TC],
                                       in_=cout_o.ap()[bass.ds(2 * rank + tb, 1)].rearrange("o (k p) t -> p (o k) t", p=128))
                fw.op("pool", f_o, reads=s_co, writes=[s_oT], kind="dma")

        def decl_p():
            fw.op("pool", lambda e: e.dma_start(out=pTb.rearrange("p k t -> p (k t)"), in_=pTt[:, :]), writes=[s_pT], kind="dma")

        fw.stop_if("s0")
        for cc in range(NKC):
            ugr = wload(wt_g[cc])
            uga = wload(wt_g[16 + cc])
            if cc == 0:
                decl_yr()
            ur0 = wload(wt_rnn[cc, 0], 11)
            ur1 = wload(wt_rnn[cc, 1], 11)
            if cc == 0:
                decl_o()
            uat = wload(wt_att[cc])
            if cc == 0:
                decl_p()
            for tb in range(2):
                hs = s_R1h[4 * tb:4 * tb + 4]
                p_gr = psP.next()
                acc_group(p_gr[0], p_gr[1], [(ugr[0], ugr[1], NKC, h_a, 0, 128)], tb, hs)
                p_ga = psP.next()
                acc_group(p_ga[0], p_ga[1], [(uga[0], uga[1], NKC, h_a, 0, 128)], tb, hs)
                p_yr = psP.next()
                acc_group(p_yr[0], p_yr[1], [(ur0[0], ur0[1], 11, R2, 0, 128), (ur1[0], ur1[1], 11, R2, 11, 128)], tb, [s_R2])
                p_ya = psP.next()
                acc_group(p_ya[0], p_ya[1], [(uat[0], uat[1], NKC, oT_a, 0, 128)], tb, [s_oT])
                t_sr = tmpR.next()
                t_sa = tmpR.next()
                fw.op("act", lambda e, p=p_gr, t=t_sr: e.activation(out=t[0], in_=p[0], func=AF.Sigmoid), reads=[p_gr[1]], writes=[t_sr[1]])
                fw.op("act", lambda e, p=p_ga, t=t_sa: e.activation(out=t[0], in_=p[0], func=AF.Sigmoid), reads=[p_ga[1]], writes=[t_sa[1]])
                fw.op("dve", lambda e, p=p_yr, t=t_sr: e.tensor_tensor(out=t[0], in0=t[0], in1=p[0], op=ALU.mult),
                      reads=[p_yr[1], t_sr[1]], writes=[t_sr[1]])
                fw.op("dve", lambda e, p=p_ya, t=t_sa: e.tensor_tensor(out=t[0], in0=t[0], in1=p[0], op=ALU.mult),
                      reads=[p_ya[1], t_sa[1]], writes=[t_sa[1]])
                fw.op("dve", lambda e, a=t_sr, b=t_sa, cc=cc, tb=tb: e.tensor_tensor(
                    out=R3[:, cc, tb * TC:(tb + 1) * TC], in0=a[0], in1=b[0], op=ALU.add),
                    reads=[t_sr[1], t_sa[1]], writes=[s_R3[cc][tb]])

        fw.barrier(["pe", "act", "dve", "sp"])
        fw.stop_if("sa")
        mix_slots = [s_R3[c][t] for c in range(NKC) for t in range(2)]
        for cc in range(NKC):
            uo = wload(wt_o[cc])
            for tb in range(2):
                fw.op("sp", lambda e, cc=cc, tb=tb: e.dma_start(out=R1[:, cc, tb * TC:(tb + 1) * TC], in_=xo2[cc, tb]),
                      writes=[s_xo[cc][tb]], kind="dma")
                p = psP.next()
                acc_group(p[0], p[1], [(uo[0], uo[1], NKC, R3, 0, 128)], tb, [s_R3[c][tb] for c in range(NKC)])
                fw.op("dve", lambda e, p=p, cc=cc, tb=tb: e.tensor_tensor(
                    out=R1[:, cc, tb * TC:(tb + 1) * TC], in0=R1[:, cc, tb * TC:(tb + 1) * TC], in1=p[0], op=ALU.add),
                    reads=[p[1], s_xo[cc][tb]], writes=[s_xo[cc][tb]])


        def rms_resident(gcol0, dst, dst_slots):
            for sc in range(8):
                tb = sc // 4
                rmsnorm_sub(None, R1[:, :, sc * SUB:(sc + 1) * SUB], [s_xo[c][tb] for c in range(NKC)], gcol0,
                            dst[:, :, sc * SUB:(sc + 1) * SUB], dst_slots[sc], SUB, scratchP, psP)

        fw.stop_if("sb")
        rms_resident(16, R3, s_h2)
        for half in range(2):
            for m in range(22):
                mm = half * 22 + m
                ug = wload(wt_gu[mm])
                uu = wload(wt_gu[44 + mm])
                for tb in range(2):
                    hs = s_h2[4 * tb:4 * tb + 4]
                    p_g = psP.next()
                    acc_group(p_g[0], p_g[1], [(ug[0], ug[1], NKC, R3, 0, 128)], tb, hs)
                    p_u = psP.next()
                    acc_group(p_u[0], p_u[1], [(uu[0], uu[1], NKC, R3, 0, 128)], tb, hs)
                    t_s = tmpR.next()
                    fw.op("act", lambda e, p=p_g, t=t_s: e.activation(out=t[0], in_=p[0], func=AF.Silu), reads=[p_g[1]], writes=[t_s[1]])
                    fw.op("dve", lambda e, p=p_u, t=t_s, m=m, tb=tb: e.tensor_tensor(
                        out=R2[:, m, tb * TC:(tb + 1) * TC], in0=t[0], in1=p[0], op=ALU.mult),
                        reads=[p_u[1], t_s[1]], writes=[s_R2])
            for cc in range(NKC):
                ud0 = wload(wt_dn[half, cc, 0], 11)
                ud1 = wload(wt_dn[half, cc, 1], 11)
                for tb in range(2):
                    p = psP.next()
                    acc_group(p[0], p[1], [(ud0[0], ud0[1], 11, R2, 0, 128), (ud1[0], ud1[1], 11, R2, 11, 128)], tb, [s_R2])
                    fw.op("dve", lambda e, p=p, cc=cc, tb=tb: e.tensor_tensor(
                        out=R1[:, cc, tb * TC:(tb + 1) * TC], in0=R1[:, cc, tb * TC:(tb + 1) * TC], in1=p[0], op=ALU.add),
                        reads=[p[1], s_xo[cc][tb]], writes=[s_xo[cc][tb]])

        fw.stop_if("sc")
        h3 = R2[:, 0:NKC, :]
        s_h3 = [Slot() for _ in range(8)]
        rms_resident(32, h3, s_h3)
        out_ops = []
        for cc in range(NKC):
            upg = wload(wt_pg[cc])
            upp = wload(wt_pp[cc], 2)
            for tb in range(2):
                hs = s_h3[4 * tb:4 * tb + 4]
                p_g = psP.next()
                acc_group(p_g[0], p_g[1], [(upg[0], upg[1], NKC, h3, 0, 128)], tb, hs)
                p_p = psP.next()

                acc_group(p_p[0], p_p[1], [(upp[0], upp[1], 2, pTb, 0, 128)], tb, [s_pT])
                t_s = tmpR.next()
                fw.op("act", lambda e, p=p_g, t=t_s: e.activation(out=t[0], in_=p[0], func=AF.Sigmoid), reads=[p_g[1]], writes=[t_s[1]])
                fw.op("dve", lambda e, p=p_p, t=t_s: e.tensor_tensor(out=t[0], in0=t[0], in1=p[0], op=ALU.mult),
                      reads=[p_p[1], t_s[1]], writes=[t_s[1]])
                fw.op("dve", lambda e, t=t_s, cc=cc, tb=tb: e.tensor_tensor(
                    out=R1[:, cc, tb * TC:(tb + 1) * TC], in0=R1[:, cc, tb * TC:(tb + 1) * TC], in1=t[0], op=ALU.add),
                    reads=[t_s[1], s_xo[cc][tb]], writes=[s_xo[cc][tb]])
            out_ops.append(fw.op("sp", lambda e, cc=cc: e.dma_start(out=outT[cc], in_=R1[:, cc, :]),
                                 reads=[s_xo[cc][0], s_xo[cc][1]], kind="dma"))
        s_fin = Slot("fin")
        fin = fw.op("sp", lambda e: e.nop(), kind="c")
        fin.deps.update(o_ for o_ in out_ops if o_.kind != "skip")

        import os as _os2
        if _os2.environ.get("KNOSCHED", "") != "1":
            fw.schedule()
        fw.finalize()
        sems = {}
        for e in ENGS:
            sems[("eng", e)] = es.enter_context(nc.semaphore(f"s_{e}"))
            for i in range(FW.NDS):
                sems[("dma", e, i)] = es.enter_context(nc.semaphore(f"d_{e}{i}"))
        for i in range(fw.cc_count):
            sems[("cc", "pool", i)] = es.enter_context(nc.semaphore(f"cc{i}"))
        block = es.enter_context(nc.Block())

        @block.tensor
        def _(t):
            fw.emit("pe", t, sems)

        @block.scalar
        def _(a):
            fw.emit("act", a, sems)

        @block.vector
        def _(v):
            fw.emit("dve", v, sems)

        @block.gpsimd
        def _(g):
            fw.emit("pool", g, sems)

        @block.sync
        def _(sy):
            fw.emit("sp", sy, sems)
    return nc


def _tile_w(w, nk):
    K, N = w.shape
    assert K == nk * 128
    t = w.reshape(nk, 128, N // 128, 128).transpose(2, 1, 0, 3)
    return np.ascontiguousarray(t).reshape(N // 128, 128, nk * 128)


_NC_CACHE = {}


def kernel(x, p, g_mix, w_in, conv_w, conv_b, w_rg_a, b_rg_a, w_rg_x, b_rg_x, lru_lambda, q_gain, k_gain,
           w_rnn_out, w_attn_out, w_o, g_ffn, w_ffn_gu, w_ffn_down, g_ple, w_ple_gate, w_ple_proj):
    f = np.float32
    x = np.asarray(x, f)
    p = np.asarray(p, f)
    w_in0 = np.asarray(w_in, f)[0]
    if "nc" not in _NC_CACHE:
        _NC_CACHE["nc"] = build_program()
    nc = _NC_CACHE["nc"]

    vec128 = np.zeros((128, 50), f)
    vec128[:, 0:16] = np.asarray(g_mix, f)[0].reshape(16, 128).T
    vec128[:, 16:32] = np.asarray(g_ffn, f)[0].reshape(16, 128).T
    vec128[:, 32:48] = np.asarray(g_ple, f)[0].reshape(16, 128).T
    vec128[:, 48] = np.asarray(q_gain, f)[0]
    vec128[:, 49] = np.asarray(k_gain, f)[0]
    cmat = np.zeros((128, 384 + 896), f)
    cmat[:, 0:128] = np.eye(128, dtype=f)
    jj = np.arange(128)[:, None]
    ss = np.arange(128)[None, :]
    cmat[:, 128:256] = (jj >= ss).astype(f)
    cmat[:, 256:384] = (jj < ss).astype(f)
    xx = np.arange(896)[None, :]
    cmat[:, 384:] = np.where(xx <= jj + 384, NEG, 0.0).astype(f)
    wt_g = _tile_w(w_in0[:, 11776:15872], 16)
    wr = np.asarray(w_rnn_out, f)[0]
    wt_rnn = _tile_w(wr, 22).reshape(16, 128, 2, 11 * 128).transpose(0, 2, 1, 3).copy()
    wt_att = _tile_w(np.asarray(w_attn_out, f)[0], 16)
    wt_o = _tile_w(np.asarray(w_o, f)[0], 16)
    wt_gu = _tile_w(np.asarray(w_ffn_gu, f)[0], 16)
    wd = np.asarray(w_ffn_down, f)[0]
    wt_dn = np.stack([_tile_w(wd[h * 2816:(h + 1) * 2816], 22).reshape(16, 128, 2, 11 * 128).transpose(0, 2, 1, 3)
                      for h in range(2)]).copy()
    wt_pg = _tile_w(np.asarray(w_ple_gate, f)[0], 16)
    wpp = np.asarray(w_ple_proj, f)[0]
    wt_pp = _tile_w(wpp, 2)
    cw = np.asarray(conv_w, f)[0]
    cbv = np.asarray(conv_b, f)[0]
    ba = np.asarray(b_rg_a, f)[0]
    bx = np.asarray(b_rg_x, f)[0]
    lam = np.asarray(lru_lambda, f)[0]
    wa_all = np.asarray(w_rg_a, f)[0]
    wx_all = np.asarray(w_rg_x, f)[0]

    in_maps = []
    for c in range(8):
        b, g = c // 4, c % 4
        xb = x[b]
        xTt = np.ascontiguousarray(xb.reshape(32, 128, 16, 128).transpose(0, 3, 2, 1)).reshape(32, 128, 16 * 128)
        xo = xb[1024 * g:1024 * (g + 1)]
        xo2 = np.ascontiguousarray(xo.reshape(2, 512, 16, 128).transpose(2, 0, 3, 1))
        pb = p[0, b, 1024 * g:1024 * (g + 1)]
        pTt = np.ascontiguousarray(pb.reshape(1024, 2, 128).transpose(2, 1, 0)).reshape(128, 2 * 1024)
        w1a = np.concatenate([w_in0[:, 704 * g:704 * (g + 1)], w_in0[:, 2816 + 704 * g:2816 + 704 * (g + 1)]], axis=1)
        w1a = np.ascontiguousarray(w1a.reshape(16, 128, 1408).transpose(1, 0, 2)).reshape(128, 16 * 1408)
        w1b = np.concatenate([w_in0[:, 5632 + 512 * g:5632 + 512 * (g + 1)], w_in0[:, 7680 + 512 * g:7680 + 512 * (g + 1)],
                              w_in0[:, 9728 + 512 * g:9728 + 512 * (g + 1)]], axis=1)
        w1b = np.ascontiguousarray(w1b.reshape(16, 128, 1536).transpose(1, 0, 2)).reshape(128, 16 * 1536)
        ch = slice(704 * g, 704 * (g + 1))
        v88 = np.zeros((88, 8, 8), f)
        for k in range(4):
            v88[:, :, k] = cw[k, ch].reshape(8, 88).T
        v88[:, :, 4] = cbv[ch].reshape(8, 88).T
        v88[:, :, 5] = ba[ch].reshape(8, 88).T
        v88[:, :, 6] = bx[ch].reshape(8, 88).T
        v88[:, :, 7] = lam[ch].reshape(8, 88).T
        wax = np.stack([wa_all[4 * g:4 * g + 4], wx_all[4 * g:4 * g + 4]])
        wax = np.ascontiguousarray(wax.reshape(2, 4, 2, 88, 176).transpose(3, 0, 1, 2, 4)).reshape(88, 2 * 4 * 2 * 176)
        in_maps.append(dict(xTt=xTt, xo2=xo2, pTt=pTt, w1a=w1a, w1b=w1b,
                            vec88=v88.reshape(88, 64), vec128=vec128, wax=wax, cmat=cmat, wt_g=wt_g, wt_rnn=wt_rnn,
                            wt_att=wt_att, wt_o=wt_o, wt_gu=wt_gu, wt_dn=wt_dn, wt_pg=wt_pg, wt_pp=wt_pp))
    res = run_bass_kernel_spmd(nc, in_maps, core_ids=list(range(8)))
    out = np.empty((2, S, D), f)
    for c in range(8):
        b, g = c // 4, c % 4
        oT = np.asarray(res.results[c]["outT"], f).reshape(2048, 1024)
        out[b, 1024 * g:1024 * (g + 1), :] = oT.T
    return out
```

```python
import numpy as np
from contextlib import ExitStack
import concourse.bass as bass
import concourse.mybir as mybir
from concourse.bass_utils import run_bass_kernel_spmd

F32 = mybir.dt.float32
BF16 = mybir.dt.bfloat16
U8 = mybir.dt.uint8
AF = mybir.ActivationFunctionType
ALU = mybir.AluOpType

S = 4096
D = 2048
NKC = 16
TC = 512
NCH = S // TC
SUB = 128
EPS = 1e-6
QSCALE = 1.0 / np.sqrt(128.0)
NEG = -30000.0
ARENA = 198 * 1024
GELU_C = 0.7978845608028654

ENGS = ["pe", "act", "dve", "pool", "sp"]


class Slot:
    __slots__ = ("name", "w", "r", "excl")

    def __init__(self, name="", excl=False):
        self.name = name
        self.w = None
        self.r = []
        self.excl = excl


class Op:
    __slots__ = ("eng", "fn", "deps", "needs_inc", "seq", "kind", "sem", "semv", "pos", "raw", "cost", "idx",
                 "nd", "rt", "fin", "succ")


class FW:
    NDS = 8

    def __init__(self):
        self.ops = {e: [] for e in ENGS}
        self.cur_barrier = {e: None for e in ENGS}
        self.since_barrier = []
        self.cc_count = 0
        self.stopped = False
        self.all = []

    def stop_if(self, tag):
        import os
        if os.environ.get("KSTOP", "") == tag:
            self.stopped = True

    DEFCOST = {"pe": 4.6, "act": 0.65, "dve": 0.65, "pool": 5.0, "sp": 3.0}

    def op(self, eng, fn, reads=(), writes=(), kind="c", cost=None):
        o = Op()
        o.cost = cost if cost is not None else (30.0 if kind == "cc" else self.DEFCOST[eng])
        if self.stopped:
            o.eng, o.fn, o.kind, o.deps, o.raw, o.pos = eng, fn, "skip", set(), set(), -1
            return o
        o.eng, o.fn, o.kind = eng, fn, kind
        o.needs_inc = False
        o.seq = 0
        o.sem = None
        o.semv = 0
        deps = set()
        raw = set()
        for s in reads:
            if s.w is not None:
                deps.add(s.w)
                raw.add(s.w)
            if s.excl:
                deps.update(s.r)
        for s in writes:
            if s.w is not None:
                deps.add(s.w)
            deps.update(s.r)
        for s in reads:
            s.r.append(o)
        for s in writes:
            s.w = o
            s.r = []
        if self.cur_barrier[eng] is not None:
            deps.add(self.cur_barrier[eng])
        if kind == "cc":
            o.sem = ("cc", eng, self.cc_count)
            self.cc_count += 1
            o.semv = 1
        deps.discard(o)
        o.deps = deps
        o.raw = raw
        o.pos = len(self.ops[eng])
        self.ops[eng].append(o)
        o.idx = len(self.all)
        self.all.append(o)
        self.since_barrier.append(o)
        return o

    def schedule(self, reorder=("pe", "act", "dve", "sp"), W=200):
        import bisect
        for o in self.all:
            o.succ = []
        for o in self.all:
            o.nd = len(o.deps)
            o.rt = 0.0
            for d in o.deps:
                d.succ.append(o)
        blev = {}
        for o in reversed(self.all):
            m = 0.0
            for c in o.succ:
                v = blev[id(c)]
                if v > m:
                    m = v
            blev[id(o)] = m + o.cost
        import os as _os3
        use_bl = _os3.environ.get("KPRIO", "bl") == "bl"
        for o in self.all:
            o.seq = (-blev[id(o)] if use_bl else 0.0, o.idx)
        avail = {e: [] for e in ENGS}
        ptr = {e: 0 for e in ENGS}
        free_at = {e: 0.0 for e in ENGS}
        order = {e: [] for e in ENGS}
        for o in self.all:
            if o.nd == 0 and o.eng in reorder:
                bisect.insort(avail[o.eng], (o.seq, o))
        remaining = len(self.all)
        while remaining:
            best = None
            for e in ENGS:
                if e in reorder:
                    lst = avail[e]
                    if not lst:
                        continue
                    fa = free_at[e]
                    cb = None
                    for (ix, o) in lst[:W]:
                        st = o.rt if o.rt > fa else fa
                        if cb is None or st < cb[0] - 1e-9:
                            cb = (st, ix, o)
                        if st <= fa:
                            break
                else:
                    if ptr[e] >= len(self.ops[e]):
                        continue
                    o = self.ops[e][ptr[e]]
                    if o.nd > 0:
                        continue
                    st = max(o.rt, free_at[e])
                    cb = (st, o.seq, o)
                if best is None or (cb[0], cb[1]) < (best[0], best[1]):
                    best = cb
            assert best is not None, "scheduler stuck"
            st, ix, o = best
            e = o.eng
            if e in reorder:
                i = bisect.bisect_left(avail[e], (ix, ))
                assert avail[e][i][1] is o
                avail[e].pop(i)
            else:
                ptr[e] += 1
            if o.kind in ("dma", "cc"):
                free_at[e] = st + (1.0 if e == "pool" else 0.15)
            else:
                free_at[e] = st + o.cost
            o.fin = st + o.cost
            order[e].append(o)
            remaining -= 1
            for c in o.succ:
                c.nd -= 1
                if o.fin > c.rt:
                    c.rt = o.fin
                if c.nd == 0 and c.eng in reorder:
                    bisect.insort(avail[c.eng], (c.seq, c))
        for e in ENGS:
            self.ops[e] = order[e]
            for i, o in enumerate(order[e]):
                o.pos = i
        self.makespan = max(o.fin for o in self.all)

    def barrier(self, engines, split_cc=False):
        if self.stopped:
            return
        prior = list(self.since_barrier)
        b = self.op("dve", lambda e: e.nop(), cost=0.1)
        b.deps.update(o for o in prior if not (split_cc and o.kind == "cc"))
        b.deps.discard(b)
        self.since_barrier = [b]
        for e in engines:
            self.cur_barrier[e] = b
        if split_cc and "pool" in engines:
            bp = self.op("pool", lambda e: e.nop(), cost=0.1)
            bp.deps.update(prior)
            bp.deps.add(b)
            bp.deps.discard(bp)
            self.cur_barrier["pool"] = bp

    def finalize(self):
        for e in ENGS:
            n = 0
            hist = []
            for o in self.ops[e]:
                if o.kind != "dma":
                    continue
                o.sem = ("dma", e, n % self.NDS)
                o.semv = 16 * (n // self.NDS + 1)
                if n >= self.NDS:
                    o.deps.add(hist[n - self.NDS])
                hist.append(o)
                n += 1
        for e in ENGS:
            for o in self.ops[e]:
                for d in o.deps:
                    if d.kind != "c":
                        continue
                    if d.eng != o.eng:
                        d.needs_inc = True
                    elif d in o.raw and o.pos - d.pos <= 1:
                        d.needs_inc = True
        for e in ENGS:
            n = 0
            for o in self.ops[e]:
                if o.kind == "c" and o.needs_inc:
                    n += 1
                    o.seq = n
                    o.sem = ("eng", e)
                    o.semv = n

    def emit(self, eng, handle, sems):
        waited = {}
        for o in self.ops[eng]:
            need = {}
            for d in o.deps:
                if d.kind == "c":
                    if d.eng == eng and not (d in o.raw and o.pos - d.pos <= 1):
                        continue
                if need.get(d.sem, 0) < d.semv:
                    need[d.sem] = d.semv
            for k, v in need.items():
                if waited.get(k, 0) >= v:
                    continue
                waited[k] = v
                handle.wait_ge(sems[k], v)
            inst = o.fn(handle)
            if o.kind == "dma":
                inst.then_inc(sems[o.sem], 16)
            elif o.kind == "cc":
                inst.then_inc(sems[o.sem], 1)
            elif o.needs_inc:
                inst.then_inc(sems[o.sem], 1)


class Arena:
    def __init__(self, ap):
        self.ap = ap
        self.top = 0

    def alloc(self, parts, free, dtype):
        esz = 4 if dtype == F32 else 2
        n = int(np.prod(free)) * esz
        off = (self.top + 63) // 64 * 64
        self.top = off + n
        assert self.top <= ARENA, f"arena overflow {self.top}"
        v = self.ap[0:parts, off:off + n].bitcast(dtype)
        if len(free) == 2:
            v = v.rearrange("p (a b) -> p a b", a=free[0])
        elif len(free) == 3:
            v = v.rearrange("p (a b c) -> p a b c", a=free[0], b=free[1])
        return v


def arena_at(arena, off, parts, free, dtype):
    esz = 4 if dtype == F32 else 2
    n = int(np.prod(free)) * esz
    assert off % 64 == 0 and off + n <= ARENA
    v = arena.ap[0:parts, off:off + n].bitcast(dtype)
    if len(free) == 2:
        v = v.rearrange("p (a b) -> p a b", a=free[0])
    arena.top = max(arena.top, off + n)
    return v


class Ring:
    def __init__(self, items):
        self.items = items
        self.i = 0

    def next(self):
        it = self.items[self.i % len(self.items)]
        self.i += 1
        return it


def build_program():
    nc = bass.Bass("TRN2", target_bir_lowering=False)

    def din(name, shape):
        return nc.dram_tensor(name, shape, F32, kind="ExternalInput").ap()

    xTt = din("xTt", [S // SUB, 128, NKC * SUB])
    xo2 = din("xo2", [16, 2, 128, 512])
    pTt = din("pTt", [128, 2 * 1024])
    w1a = din("w1a", [128, NKC * 1408])
    w1b = din("w1b", [128, NKC * 1536])
    vec88 = din("vec88", [88, 64])
    vec128 = din("vec128", [128, 50])
    wax = din("wax", [88, 2 * 4 * 2 * 176])
    cmat = din("cmat", [128, 384 + 896])
    wt_g = din("wt_g", [32, 128, 2048])
    wt_rnn = din("wt_rnn", [16, 2, 128, 11 * 128])
    wt_att = din("wt_att", [16, 128, 2048])
    wt_o = din("wt_o", [16, 128, 2048])
    wt_gu = din("wt_gu", [88, 128, 2048])
    wt_dn = din("wt_dn", [2, 16, 2, 128, 11 * 128])
    wt_pg = din("wt_pg", [16, 128, 2048])
    wt_pp = din("wt_pp", [16, 128, 2 * 128])
    outT = nc.dram_tensor("outT", [16, 128, 1024], F32, kind="ExternalOutput").ap()
    cin_yr = nc.dram_tensor("cin_yr", [NCH, 704, TC], BF16)
    cout_yr = nc.dram_tensor("cout_yr", [NCH, 2816, TC], BF16)
    cin_o = nc.dram_tensor("cin_o", [NCH, 512, TC], BF16)
    cout_o = nc.dram_tensor("cout_o", [NCH, 2048, TC], BF16)
    RG = [[0, 1, 2, 3], [4, 5, 6, 7]]
    hbuf = nc.dram_tensor("hbuf", [NCH, 128, NKC * TC], BF16)
    s_hbuf = [Slot() for _ in range(NCH)]
    s_cyr = [Slot() for _ in range(NCH)]
    s_co = [Slot() for _ in range(NCH)]

    fw = FW()
    es = ExitStack()
    with es:
        arena_t = es.enter_context(nc.sbuf_tensor("arena", [128, ARENA], U8))
        cst_t = es.enter_context(nc.sbuf_tensor("cst", [128, 2432], F32))
        psb2 = [es.enter_context(nc.psum_tensor(f"ps{i}", [128, 1024], F32)) for i in range(4)]
        psb = [psb2[i // 2][:, (i % 2) * 512:(i % 2 + 1) * 512] for i in range(8)]
        ps_slots = [Slot(f"ps{i}", excl=True) for i in range(8)]

        cst = cst_t
        c_vec128 = cst[:, 0:50]
        c_vec88 = cst[0:88, 64:128]
        c_der88 = cst[0:88, 128:160]
        c_num = cst[:, 160:168]
        cb = cst[:, 192:192 + 768].bitcast(BF16)
        c_ident = cb[:, 0:128]
        c_L = cb[:, 128:256]
        c_SU = cb[:, 256:384]
        c_mask = cb[:, 384:384 + 896]
        c_onesm = cb[:, 1280:1408]
        c_onesh = cb[:, 1408:1536]
        c_halo = cst[0:88, 968:968 + 24].rearrange("p (a b) -> p a b", a=8)
        c_hlast = cst[0:88, 1000:1008]
        c_wax = cst[0:88, 1016:1016 + 1408].bitcast(BF16).rearrange("p (w n i c) -> p w n i c", w=2, n=4, i=2)
        s_const = Slot("const")
        s_halo = [Slot() for _ in range(4)]
        s_hlast = [Slot() for _ in range(8)]
        EPS_AP = c_num[:, 0:1]
        ONE_AP = c_num[:, 1:2]

        def g128(col):
            return c_vec128[:, col:col + 1]

        fw.op("sp", lambda e: e.dma_start(out=c_vec128, in_=vec128[:, :]), writes=[s_const], kind="dma")
        fw.op("sp", lambda e: e.dma_start(out=c_vec88, in_=vec88[:, :]), writes=[s_const], kind="dma")
        fw.op("pool", lambda e: e.dma_start(out=cb[:, 0:1280], in_=cmat[:, :]), writes=[s_const], kind="dma")
        fw.op("pool", lambda e: e.dma_start(out=c_wax.rearrange("p w n i c -> p (w n i c)"), in_=wax[:, :]),
              writes=[s_const], kind="dma")

        def setup_consts(e):
            e.memset(c_num[:, 0:1], EPS)
            e.memset(c_num[:, 1:2], 1.0)
            e.memset(c_num[:, 2:3], 0.0)
            e.memset(c_onesm, 1.0 / 2048.0)
            e.memset(c_onesh, 1.0 / 128.0)
            e.memset(c_halo, 0.0)
            return e.memset(c_hlast, 0.0)
        fw.op("dve", setup_consts, writes=[s_const] + s_halo + s_hlast)

        v88 = c_vec88.rearrange("p (c f) -> p c f", c=8)
        hba = c_der88[:, 0:8]
        hbx = c_der88[:, 8:16]
        scl = c_der88[:, 16:24]
        tmp8 = c_der88[:, 24:32]
        fw.op("act", lambda e: e.activation(out=tmp8, in_=v88[:, :, 7], func=AF.Exp, scale=-1.0),
              reads=[s_const], writes=[s_const])
        fw.op("dve", lambda e: e.tensor_scalar(out=hba, in0=v88[:, :, 5], scalar1=0.5, scalar2=None, op0=ALU.mult),
              reads=[s_const], writes=[s_const])
        fw.op("act", lambda e: e.activation(out=scl, in_=tmp8, func=AF.Ln, bias=ONE_AP[0:88, :], scale=1.0),
              reads=[s_const], writes=[s_const])
        fw.op("dve", lambda e: e.tensor_scalar(out=hbx, in0=v88[:, :, 6], scalar1=0.5, scalar2=None, op0=ALU.mult),
              reads=[s_const], writes=[s_const])
        fw.op("dve", lambda e: e.tensor_scalar(out=scl, in0=scl, scalar1=-4.0, scalar2=None, op0=ALU.mult),
              reads=[s_const], writes=[s_const])

        arena_ap = arena_t[:, :]
        fw.stop_if("setup")

        def rmsnorm_sub(load_fn, src_ap, src_slots, gcol0, dst_ap, dst_slot, ntok, scratch, psring, sq_eng="act"):
            sq, s_sq, lnm, s_ln = scratch
            if load_fn is not None:
                load_fn()
            if sq_eng == "act":
                fw.op("act", lambda e: e.activation(out=sq, in_=src_ap, func=AF.Square),
                      reads=src_slots, writes=[s_sq], cost=2.0)
            else:
                fw.op("dve", lambda e: e.tensor_tensor(out=sq, in0=src_ap, in1=src_ap, op=ALU.mult),
                      reads=src_slots, writes=[s_sq], cost=2.4)
            ps, s_ps = psring.next()

            def f_mm(e):
                for kc in range(NKC):
                    last = e.matmul(ps[:, 0:ntok], lhsT=c_onesm, rhs=sq[:, kc, :], start=(kc == 0), stop=(kc == NKC - 1))
                return last
            fw.op("pe", f_mm, reads=[s_sq, s_const], writes=[s_ps], cost=16 * 0.12)
            fw.op("act", lambda e: e.activation(out=lnm[:, 0:ntok], in_=ps[:, 0:ntok], func=AF.Ln, bias=EPS_AP, scale=1.0),
                  reads=[s_ps], writes=[s_ln])
            fw.op("act", lambda e: e.activation(out=lnm[:, ntok:2 * ntok], in_=lnm[:, 0:ntok], func=AF.Exp, scale=-0.5),
                  reads=[s_ln], writes=[s_ln])

            def f_h(e):
                for kc in range(NKC):
                    last = e.scalar_tensor_tensor(out=dst_ap[:, kc, :], in0=src_ap[:, kc, :], scalar=g128(gcol0 + kc),
                                                  in1=lnm[:, ntok:2 * ntok], op0=ALU.mult, op1=ALU.mult)
                return last
            fw.op("dve", f_h, reads=list(src_slots) + [s_ln, s_const], writes=[dst_slot], cost=16 * 0.22)

        A = Arena(arena_ap)
        W1a = A.alloc(128, (NKC, 1408), BF16)
        s_W1a = Slot("W1a")
        off_xs = A.top
        xs_r = [(A.alloc(128, (NKC, SUB), F32), Slot()) for _ in range(2)]
        xsR = Ring(xs_r)
        sq_a = A.alloc(128, (NKC, SUB), BF16)
        lnm_a = A.alloc(128, (2 * SUB,), F32)
        scratchA = (sq_a, Slot(), lnm_a, Slot())
        off_hT0 = A.top
        hT_ring = Ring([(A.alloc(128, (NKC, TC), BF16), [Slot() for _ in range(TC // SUB)]) for _ in range(2)])
        hT0_slots_A = hT_ring.items[0][1]
        xs_slots_A = [xs_r[0][1], xs_r[1][1]]
        yrb = [(A.alloc(88, (8, TC), BF16), Slot()) for _ in range(2)]
        NSET = 2
        sets = []
        for _ in range(NSET):
            d = {}
            d["xr"] = A.alloc(88, (2, TC + 3), F32)
            d["gr"] = A.alloc(88, (2, TC), F32)
            d["xc"] = A.alloc(88, (2, TC), F32)
            d["xcb"] = A.alloc(88, (2, TC), BF16)
            for nm in ("rp", "ip", "a", "t", "s", "h"):
                d[nm] = A.alloc(88, (2, TC), F32)
            d["slots"] = {nm: [Slot(), Slot()] for nm in ("xr", "gr", "xc", "xcb", "rp", "ip", "a", "t", "s", "h")}
            sets.append(d)
        psA = Ring(list(zip([p[:, :] for p in psb[0:6]], ps_slots[0:6])))
        psA_ms = Ring(list(zip([p[:, :] for p in psb[6:8]], ps_slots[6:8])))

        for q4 in range(4):
            fw.op("pool", lambda e, q4=q4: e.dma_start(
                out=W1a[:, 4 * q4:4 * q4 + 4, :],
                in_=w1a[:, 4 * q4 * 1408:(4 * q4 + 4) * 1408].rearrange("p (k n) -> p k n", k=4)),
                writes=[s_W1a], kind="dma")

        def proj_fm(Wt, s_W, col0, M, ps, s_ps, h_ap, h_slots, ntok=TC):
            def f(e):
                for kc in range(NKC):
                    last = e.matmul(ps[0:M, 0:ntok], lhsT=Wt[:, kc, col0:col0 + M], rhs=h_ap[:, kc, :],
                                    start=(kc == 0), stop=(kc == NKC - 1))
                return last
            fw.op("pe", f, reads=[s_W] + list(h_slots), writes=[s_ps])

        def hT_chunk(G, xsrc, scratch, psring, sq_eng="act"):
            hT, s_hT = hT_ring.next()
            for sb in range(TC // SUB):
                sc = G * (TC // SUB) + sb
                xs, s_xs = xsR.next()
                fw.op("sp", lambda e, xs=xs, sc=sc: e.dma_start(out=xs.rearrange("p k t -> p (k t)"), in_=xsrc[sc]),
                      writes=[s_xs], kind="dma")
                rmsnorm_sub(None, xs, [s_xs], 0, hT[:, :, sb * SUB:(sb + 1) * SUB], s_hT[sb], SUB, scratch, psring, sq_eng)
            return hT, s_hT

        Wk = arena_at(A, off_xs, 128, (NKC, 512), BF16)
        Wq = arena_at(A, off_hT0, 128, (NKC, 512), BF16)
        s_Wq, s_Wk, s_Wv = Slot("Wq"), Slot("Wk"), Slot("Wv")
        w1b_v = w1b.rearrange("p (k n) -> p k n", k=NKC)
        for G in range(NCH):
            hT, s_hT = hT_chunk(G, xTt, scratchA, psA_ms)
            if G == NCH - 1 and not fw.stopped:
                fw.op("pool", lambda e: e.dma_start(out=Wq, in_=w1b_v[:, :, 0:512]), writes=[s_Wq] + hT0_slots_A, kind="dma", cost=12.0)
                fw.op("pool", lambda e: e.dma_start(out=Wk, in_=w1b_v[:, :, 512:1024]), writes=[s_Wk] + xs_slots_A, kind="dma", cost=12.0)
            fw.op("sp", lambda e, hT=hT, G=G: e.dma_start(out=hbuf.ap()[G], in_=hT.rearrange("p k t -> p (k t)")),
                  reads=s_hT, writes=[s_hbuf[G]], kind="dma")
            yb, s_yb = yrb[G % 2]
            for n in range(4):
                st = sets[(G * 4 + n) % NSET]
                sl = st["slots"]
                for j in range(2):
                    c8 = 2 * n + j
                    ps, s_ps = psA.next()
                    proj_fm(W1a, s_W1a, c8 * 88, 88, ps, s_ps, hT, s_hT)
                    fw.op("act", lambda e, ps=ps, st=st, j=j: e.activation(out=st["xr"][:, j, 3:TC + 3], in_=ps[0:88, :], func=AF.Copy),
                          reads=[s_ps], writes=[sl["xr"][j]])
                    ps, s_ps = psA.next()
                    proj_fm(W1a, s_W1a, 704 + c8 * 88, 88, ps, s_ps, hT, s_hT)
                    fw.op("act", lambda e, ps=ps, st=st, j=j: e.activation(out=st["gr"][:, j, :], in_=ps[0:88, :], func=AF.Copy),
                          reads=[s_ps], writes=[sl["gr"][j]])
                fw.op("dve", lambda e, st=st, n=n: e.tensor_copy(out=st["xr"][:, :, 0:3], in_=c_halo[:, 2 * n:2 * n + 2, :]),
                      reads=[s_halo[n]], writes=sl["xr"])
                fw.op("dve", lambda e, st=st, n=n: e.tensor_copy(out=c_halo[:, 2 * n:2 * n + 2, :], in_=st["xr"][:, :, TC:TC + 3]),
                      reads=sl["xr"], writes=[s_halo[n]])
                for k in range(4):
                    for j in range(2):
                        c8 = 2 * n + j
                        if k == 0:
                            fw.op("dve", lambda e, st=st, j=j, c8=c8: e.tensor_scalar(
                                out=st["xc"][:, j, :], in0=st["xr"][:, j, 0:TC], scalar1=v88[:, c8, 0:1], scalar2=v88[:, c8, 4:5],
                                op0=ALU.mult, op1=ALU.add), reads=[sl["xr"][j], s_const], writes=[sl["xc"][j]])
                        else:
                            fw.op("dve", lambda e, st=st, j=j, c8=c8, k=k: e.scalar_tensor_tensor(
                                out=st["xc"][:, j, :], in0=st["xr"][:, j, k:k + TC], scalar=v88[:, c8, k:k + 1], in1=st["xc"][:, j, :],
                                op0=ALU.mult, op1=ALU.add), reads=[sl["xr"][j], sl["xc"][j], s_const], writes=[sl["xc"][j]])
                for j in range(2):
                    fw.op("act", lambda e, st=st, j=j: e.activation(out=st["xcb"][:, j, :], in_=st["xc"][:, j, :], func=AF.Copy),
                          reads=[sl["xc"][j]], writes=[sl["xcb"][j]])
                gps = []
                for j in range(2):
                    for w in range(2):
                        ps, s_ps = psA.next()

                        def f(e, ps=ps, st=st, j=j, w=w, n=n):
                            for i in range(2):
                                last = e.matmul(ps[0:88, :], lhsT=c_wax[:, w, n, i, 88 * j:88 * j + 88], rhs=st["xcb"][:, i, :],
                                                start=(i == 0), stop=(i == 1))
                            return last
                        fw.op("pe", f, reads=sl["xcb"] + [s_const], writes=[s_ps], cost=0.9)
                        gps.append((ps, s_ps))
                for j in range(2):
                    c8 = 2 * n + j
                    ps, s_ps = gps[2 * j]
                    fw.op("act", lambda e, ps=ps, st=st, j=j, c8=c8: e.activation(
                        out=st["rp"][:, j, :], in_=ps[0:88, :], func=AF.Tanh, bias=hba[:, c8:c8 + 1], scale=0.5),
                        reads=[s_ps, s_const], writes=[sl["rp"][j]])
                    ps, s_ps = gps[2 * j + 1]
                    fw.op("act", lambda e, ps=ps, st=st, j=j, c8=c8: e.activation(
                        out=st["ip"][:, j, :], in_=ps[0:88, :], func=AF.Tanh, bias=hbx[:, c8:c8 + 1], scale=0.5),
                        reads=[s_ps, s_const], writes=[sl["ip"][j]])
                for j in range(2):
                    c8 = 2 * n + j
                    fw.op("act", lambda e, st=st, j=j, c8=c8: e.activation(
                        out=st["a"][:, j, :], in_=st["rp"][:, j, :], func=AF.Exp, bias=scl[:, c8:c8 + 1], scale=scl[:, c8:c8 + 1]),
                        reads=[sl["rp"][j], s_const], writes=[sl["a"][j]])
                for j in range(2):
                    fw.op("dve", lambda e, st=st, j=j: e.scalar_tensor_tensor(
                        out=st["t"][:, j, :], in0=st["ip"][:, j, :], scalar=1.0, in1=st["xc"][:, j, :], op0=ALU.add, op1=ALU.mult),
                        reads=[sl["ip"][j], sl["xc"][j]], writes=[sl["t"][j]])
                for j in range(2):
                    fw.op("act", lambda e, st=st, j=j: e.activation(out=st["s"][:, j, :], in_=st["a"][:, j, :], func=AF.Square),
                          reads=[sl["a"][j]], writes=[sl["s"][j]])
                for j in range(2):
                    fw.op("act", lambda e, st=st, j=j: e.activation(out=st["s"][:, j, :], in_=st["s"][:, j, :], func=AF.Sqrt,
                                                                    bias=ONE_AP[0:88, :], scale=-1.0),
                          reads=[sl["s"][j], s_const], writes=[sl["s"][j]])
                for j in range(2):
                    fw.op("dve", lambda e, st=st, j=j: e.scalar_tensor_tensor(
                        out=st["t"][:, j, :], in0=st["s"][:, j, :], scalar=0.5, in1=st["t"][:, j, :], op0=ALU.mult, op1=ALU.mult),
                        reads=[sl["s"][j], sl["t"][j]], writes=[sl["t"][j]])
                for j in range(2):
                    c8 = 2 * n + j
                    fw.op("dve", lambda e, st=st, j=j, c8=c8: e.tensor_tensor_scan(
                        out=st["h"][:, j, :], data0=st["a"][:, j, :], data1=st["t"][:, j, :], initial=c_hlast[:, c8:c8 + 1],
                        op0=ALU.mult, op1=ALU.add), reads=[sl["a"][j], sl["t"][j], s_hlast[c8]], writes=[sl["h"][j]])
                for j in range(2):
                    c8 = 2 * n + j
                    fw.op("act", lambda e, st=st, j=j, c8=c8: e.activation(out=c_hlast[:, c8:c8 + 1], in_=st["h"][:, j, TC - 1:TC], func=AF.Copy),
                          reads=[sl["h"][j]], writes=[s_hlast[c8]])
                for j in range(2):
                    fw.op("dve", lambda e, st=st, j=j: e.tensor_tensor(out=st["ip"][:, j, :], in0=st["gr"][:, j, :], in1=st["gr"][:, j, :], op=ALU.mult),
                          reads=[sl["gr"][j]], writes=[sl["ip"][j]])
                for j in range(2):
                    fw.op("dve", lambda e, st=st, j=j: e.tensor_scalar(out=st["ip"][:, j, :], in0=st["ip"][:, j, :], scalar1=0.044715, scalar2=1.0,
                                                                      op0=ALU.mult, op1=ALU.add), reads=[sl["ip"][j]], writes=[sl["ip"][j]])
                for j in range(2):
                    fw.op("dve", lambda e, st=st, j=j: e.tensor_tensor(out=st["ip"][:, j, :], in0=st["ip"][:, j, :], in1=st["gr"][:, j, :], op=ALU.mult),
                          reads=[sl["ip"][j], sl["gr"][j]], writes=[sl["ip"][j]])
                for j in range(2):
                    fw.op("act", lambda e, st=st, j=j: e.activation(out=st["rp"][:, j, :], in_=st["ip"][:, j, :], func=AF.Tanh, scale=GELU_C),
                          reads=[sl["ip"][j]], writes=[sl["rp"][j]])
                for j in range(2):
                    fw.op("dve", lambda e, st=st, j=j: e.scalar_tensor_tensor(
                        out=st["ip"][:, j, :], in0=st["rp"][:, j, :], scalar=1.0, in1=st["gr"][:, j, :], op0=ALU.add, op1=ALU.mult),
                        reads=[sl["rp"][j], sl["gr"][j]], writes=[sl["ip"][j]])
                for j in range(2):
                    c8 = 2 * n + j
                    fw.op("dve", lambda e, st=st, j=j, c8=c8, yb=yb: e.scalar_tensor_tensor(
                        out=yb[:, c8, :], in0=st["ip"][:, j, :], scalar=0.5, in1=st["h"][:, j, :], op0=ALU.mult, op1=ALU.mult),
                        reads=[sl["ip"][j], sl["h"][j]], writes=[s_yb])
            s_ci = Slot()
            fw.op("sp", lambda e, yb=yb, G=G: e.dma_start(
                out=cin_yr.ap()[G].rearrange("(c p) t -> p c t", p=88), in_=yb),
                reads=[s_yb], writes=[s_ci], kind="dma")
            fw.op("pool", lambda e, G=G: e.collective_compute("AllGather", ALU.bypass, replica_groups=RG,
                                                              ins=[cin_yr.ap()[G].opt()], outs=[cout_yr.ap()[G].opt()]),
                  reads=[s_ci], writes=[s_cyr[G]], kind="cc")
            if G == 0:
                fw.stop_if("passA1")
        fw.stop_if("passA")


        fw.stop_if("ag1")
        B = Arena(arena_ap)
        assert off_xs == 45056 and off_hT0 == 66560, (off_xs, off_hT0)
        KT = arena_at(B, 0, 128, (4, S), BF16)
        s_KT = [[Slot() for _ in range(NCH)] for _ in range(4)]
        e_r = [(arena_at(B, 32768 + 4096 * i, 128, (2, TC), F32), Slot()) for i in range(3)]
        nrm = [dict(raw=arena_at(B, 61440, 128, (TC,), F32), sq=arena_at(B, 63488, 128, (TC,), BF16),
                    ln=arena_at(B, 64512, 128, (TC,), F32), s_raw=Slot(), s_sq=Slot(), s_ln=Slot())]
        B.top = off_hT0 + 16384
        Wv = B.alloc(128, (NKC, 512), BF16)
        Vt = B.alloc(128, (S // 128, 512), BF16)
        s_V = [Slot() for _ in range(S // 128)]
        hT_ring.items = [(B.alloc(128, (NKC, TC), BF16), [Slot() for _ in range(TC // SUB)]) for _ in range(2)]
        QT_ring = Ring([(B.alloc(128, (4, TC), BF16), [Slot() for _ in range(4)]) for _ in range(2)])
        nrm.append(dict(raw=B.alloc(128, (TC,), F32), sq=B.alloc(128, (TC,), BF16), ln=B.alloc(128, (TC,), F32),
                        s_raw=Slot(), s_sq=Slot(), s_ln=Slot()))
        nrmR = Ring(nrm)
        sp_r = [(B.alloc(128, (2, TC), BF16), Slot()) for _ in range(3)]
        x_r = [(B.alloc(128, (2, TC), F32), Slot()) for _ in range(2)]
        w_r = [(B.alloc(128, (2, TC), BF16), Slot()) for _ in range(3)]
        psz2 = psb2[0][:, :].rearrange("p (a b) -> p a b", a=2)
        psc2 = psb2[1][:, :].rearrange("p (a b) -> p a b", a=2)
        obf = [(B.alloc(128, (TC,), BF16), Slot()) for _ in range(2)]
        obfR = Ring(obf)
        psz = [(psb[0][:, :], ps_slots[0]), (psb[1][:, :], ps_slots[1])]
        psc = [(psb[2][:, :], ps_slots[2]), (psb[3][:, :], ps_slots[3])]
        pso = [(psb[4][:, :], ps_slots[4]), (psb[5][:, :], ps_slots[5])]
        psM = Ring([(psb[6][:, :], ps_slots[6]), (psb[7][:, :], ps_slots[7])])

        fw.barrier(ENGS, split_cc=True)
        fw.op("pool", lambda e: e.dma_start(out=Wv, in_=w1b_v[:, :, 1024:1536]), writes=[s_Wv], kind="dma", cost=12.0)

        def qk_norm(Wt, s_Wt, col0, gain_col, dst_ap, dst_slot, hT, s_hT):
            nb = nrmR.next()
            ps, s_ps = psM.next()
            proj_fm(Wt, s_Wt, col0, 128, ps, s_ps, hT, s_hT)
            fw.op("dve", lambda e: e.tensor_copy(out=nb["raw"], in_=ps), reads=[s_ps], writes=[nb["s_raw"]])
            fw.op("dve", lambda e: e.tensor_tensor(out=nb["sq"], in0=nb["raw"], in1=nb["raw"], op=ALU.mult),
                  reads=[nb["s_raw"]], writes=[nb["s_sq"]])
            ps2, s_ps2 = psM.next()
            fw.op("pe", lambda e: e.matmul(ps2, lhsT=c_onesh, rhs=nb["sq"], start=True, stop=True),
                  reads=[nb["s_sq"], s_const], writes=[s_ps2], cost=0.3)
            fw.op("act", lambda e: e.activation(out=nb["ln"], in_=ps2, func=AF.Ln, bias=EPS_AP, scale=1.0),
                  reads=[s_ps2], writes=[nb["s_ln"]])
            fw.op("act", lambda e: e.activation(out=nb["ln"], in_=nb["ln"], func=AF.Exp, scale=-0.5),
                  reads=[nb["s_ln"]], writes=[nb["s_ln"]])
            fw.op("dve", lambda e: e.scalar_tensor_tensor(out=dst_ap, in0=nb["raw"], scalar=g128(gain_col), in1=nb["ln"],
                                                          op0=ALU.mult, op1=ALU.mult),
                  reads=[nb["s_raw"], nb["s_ln"], s_const], writes=[dst_slot])

        import os as _os
        for G in range(int(_os.environ.get("KNB", NCH))):
            fw.stop_if("b_w")
            hT, s_hT = hT_ring.next()
            fw.op("sp", lambda e, hT=hT, G=G: e.dma_start(out=hT.rearrange("p k t -> p (k t)"), in_=hbuf.ap()[G]),
                  reads=[s_hbuf[G]], writes=s_hT, kind="dma")
            QT, s_QT = QT_ring.next()
            fw.stop_if("b_h")
            for hh in range(4):
                qk_norm(Wq, s_Wq, hh * 128, 48, QT[:, hh, :], s_QT[hh], hT, s_hT)
                qk_norm(Wk, s_Wk, hh * 128, 49, KT[:, hh, G * TC:(G + 1) * TC], s_KT[hh][G], hT, s_hT)
            fw.stop_if("b_qk")
            for tb in range(4):
                ps, s_ps = psM.next()

                def f(e, ps=ps, tb=tb, hT=hT):
                    for kc in range(NKC):
                        last = e.matmul(ps, lhsT=hT[:, kc, tb * 128:(tb + 1) * 128], rhs=Wv[:, kc, :],
                                        start=(kc == 0), stop=(kc == NKC - 1))
                    return last
                fw.op("pe", f, reads=[s_Wv] + s_hT, writes=[s_ps])
                fw.op("dve", lambda e, ps=ps, tb=tb, G=G: e.tensor_copy(out=Vt[:, 4 * G + tb, :], in_=ps),
                      reads=[s_ps], writes=[s_V[4 * G + tb]])
            fw.stop_if("b_v")
            NT = 4 * G + 4
            s_cio = [Slot() for _ in range(4)]
            for hp in range(2):
                heads = [2 * hp, 2 * hp + 1]
                etile = {}
                sptile = {}
                wtile = {}

                def lo_of(s):
                    d = (NT - 1 - s) - 4 * G
                    return 128 * d if d > 0 else 0

                def stageZ(s):
                    kb = NT - 1 - s
                    diag = kb - 4 * G
                    lo = lo_of(s)
                    for li, hh in enumerate(heads):
                        ps, s_ps = psz[li]

                        def f(e, ps=ps, hh=hh, kb=kb, diag=diag, QT=QT, lo=lo):
                            last = e.matmul(ps[:, lo:TC], lhsT=KT[:, hh, kb * 128:(kb + 1) * 128], rhs=QT[:, hh, lo:TC],
                                            start=True, stop=(diag < 0), skip_group_check=True)
                            if diag >= 0:
                                o0 = 384 - 128 * diag
                                last = e.matmul(ps[:, lo:TC], lhsT=c_ident, rhs=c_mask[:, o0 + lo:o0 + TC], start=False, stop=True,
                                                skip_group_check=True)
                            return last
                        fw.op("pe", f, reads=[s_KT[hh][kb // 4], s_QT[hh], s_const], writes=[s_ps], cost=0.6 if diag >= 0 else 0.3)
                    eb, s_eb = e_r[s % 3]
                    etile[s] = (eb, s_eb)
                    fw.op("act", lambda e, eb=eb, lo=lo: e.activation(out=eb[:, :, lo:TC], in_=psz2[:, :, lo:TC], func=AF.Exp, scale=float(QSCALE)),
                          reads=[psz[0][1], psz[1][1]], writes=[s_eb], cost=1.1)
                    sb_, s_sb = sp_r[s % 3]
                    sptile[s] = (sb_, s_sb)
                    fw.op("act", lambda e, eb=eb, sb_=sb_, lo=lo: e.activation(out=sb_[:, :, lo:TC], in_=eb[:, :, lo:TC], func=AF.Ln, bias=ONE_AP, scale=1.0),
                          reads=[s_eb, s_const], writes=[s_sb], cost=1.1)

                def stageC(s):
                    sb_, s_sb = sptile[s]
                    lo = lo_of(s)
                    lp = lo_of(s - 1) if s > 0 else 0
                    for li, hh in enumerate(heads):
                        ps, s_ps = psc[li]
                        rd = [s_sb, s_const]
                        if s > 0:
                            sprev, s_sprev = sptile[s - 1]
                            rd += [s_sprev, s_ps]
                        else:
                            sprev = None

                        def f(e, ps=ps, sb_=sb_, sprev=sprev, s=s, li=li, lo=lo, lp=lp):
                            if s > 0:
                                e.matmul(ps[:, lp:TC], lhsT=c_SU, rhs=sprev[:, li, lp:TC], start=False, stop=False, skip_group_check=True)
                            return e.matmul(ps[:, lo:TC], lhsT=c_L, rhs=sb_[:, li, lo:TC], start=(s == 0), stop=True, skip_group_check=True)
                        fw.op("pe", f, reads=rd, writes=[s_ps], cost=0.6 if s > 0 else 0.3)
                    xb, s_xb = x_r[s % 2]
                    fw.op("act", lambda e, xb=xb, lo=lo: e.activation(out=xb[:, :, lo:TC], in_=psc2[:, :, lo:TC], func=AF.Exp, scale=-1.0),
                          reads=[psc[0][1], psc[1][1]], writes=[s_xb], cost=1.1)
                    eb, s_eb = etile[s]
                    wb, s_wb = w_r[s % 3]
                    wtile[s] = (wb, s_wb)
                    fw.op("dve", lambda e, xb=xb, eb=eb, wb=wb, lo=lo: e.tensor_tensor(out=wb[:, :, lo:TC], in0=eb[:, :, lo:TC], in1=xb[:, :, lo:TC], op=ALU.mult),
                          reads=[s_xb, s_eb], writes=[s_wb], cost=1.2)

                def stageO(s):
                    kb = NT - 1 - s
                    lo = lo_of(s)
                    for li, hh in enumerate(heads):
                        ps, s_ps = pso[li]
                        wb, s_wb = wtile[s]
                        fw.op("pe", lambda e, ps=ps, wb=wb, hh=hh, kb=kb, s=s, li=li, lo=lo: e.matmul(
                            ps[:, lo:TC], lhsT=Vt[:, kb, hh * 128:(hh + 1) * 128], rhs=wb[:, li, lo:TC], start=(s == 0), stop=(s == NT - 1),
                            skip_group_check=True),
                            reads=[s_wb, s_V[kb], s_ps] if s > 0 else [s_wb, s_V[kb]], writes=[s_ps], cost=0.3)

                for step in range(NT + 2):
                    if step < NT:
                        stageZ(step)
                    if 0 <= step - 1 < NT:
                        stageC(step - 1)
                    if 0 <= step - 2 < NT:
                        stageO(step - 2)
                for li, hh in enumerate(heads):
                    ps, s_ps = pso[li]
                    ob, s_ob = obfR.next()
                    fw.op("dve", lambda e, ps=ps, ob=ob: e.tensor_copy(out=ob, in_=ps), reads=[s_ps], writes=[s_ob])
                    fw.op("sp", lambda e, ob=ob, hh=hh, G=G: e.dma_start(
                        out=cin_o.ap()[G, hh * 128:(hh + 1) * 128, :], in_=ob), reads=[s_ob], writes=[s_cio[hh]], kind="dma")
            fw.op("pool", lambda e, G=G: e.collective_compute("AllGather", ALU.bypass, replica_groups=RG,
                                                              ins=[cin_o.ap()[G].opt()], outs=[cout_o.ap()[G].opt()]),
                  reads=s_cio, writes=[s_co[G]], kind="cc")

        fw.stop_if("passB")

        P = Arena(arena_ap)
        NU = 10
        wunits = [(P.alloc(128, (NKC, 128), BF16), Slot()) for _ in range(NU)]
        wu_i = [0]
        tmpf = [(P.alloc(128, (TC,), F32), Slot()) for _ in range(4)]
        tmpR = Ring(tmpf)
        R1 = P.alloc(128, (NKC, 1024), F32)
        R1b = R1.rearrange("p k t -> p (k t)").bitcast(BF16)
        h_a = R1b[:, 0:NKC * 1024].rearrange("p (k t) -> p k t", k=NKC)
        oT_a = R1b[:, NKC * 1024:2 * NKC * 1024].rearrange("p (k t) -> p k t", k=NKC)
        R2 = P.alloc(128, (22, 1024), BF16)
        R3 = P.alloc(128, (NKC, 1024), BF16)
        pTb = P.alloc(128, (2, 1024), BF16)
        s_xo = [[Slot() for _ in range(2)] for _ in range(NKC)]
        s_R1h = [Slot() for _ in range(8)]
        s_oT = Slot()
        s_R2 = Slot()
        s_R3 = [[Slot() for _ in range(2)] for _ in range(NKC)]
        s_h2 = [Slot() for _ in range(8)]
        s_pT = Slot()
        psP = Ring(list(zip([p[:, :] for p in psb], ps_slots)))
        sq_p = P.alloc(128, (NKC, SUB), BF16)
        lnm_p = P.alloc(128, (2 * SUB,), F32)
        scratchP = (sq_p, Slot(), lnm_p, Slot())
        xs_rp = [(R3.rearrange("p k t -> p (k t)")[:, 12 * 1024:16 * 1024].bitcast(F32).rearrange("p (k t) -> p k t", k=NKC), Slot())]

        fw.barrier(ENGS)

        def wload(src_ap, nk=NKC):
            u, s_u = wunits[wu_i[0] % NU]
            wu_i[0] += 1
            fw.op("pool", lambda e: e.dma_start(out=u[:, 0:nk, :].rearrange("p k n -> p (k n)"), in_=src_ap),
                  writes=[s_u], kind="dma")
            return u, s_u

        def acc_group(ps, s_ps, parts, tb, extra_reads):
            def f(e):
                tot = sum(p[2] for p in parts)
                i = 0
                for (u, s_u, nk, act, k0, rows) in parts:
                    for kc in range(nk):
                        last = e.matmul(ps, lhsT=u[0:rows, kc, :], rhs=act[0:rows, k0 + kc, tb * TC:(tb + 1) * TC],
                                        start=(i == 0), stop=(i == tot - 1))
                        i += 1
                return last
            fw.op("pe", f, reads=[p[1] for p in parts] + list(extra_reads), writes=[s_ps], cost=0.29 * sum(p[2] for p in parts))

        for tb in range(2):
            def f_hh(e, tb=tb):
                rank = e.partition_id() % 4
                return e.dma_start(out=h_a[:, :, tb * TC:(tb + 1) * TC],
                                   in_=hbuf.ap()[bass.ds(2 * rank + tb, 1)].rearrange("o p (k t) -> p (o k) t", k=NKC))
            fw.op("pool", f_hh, reads=s_hbuf, writes=s_R1h[4 * tb:4 * tb + 4], kind="dma")

        def decl_yr():
            for tb in range(2):
                def f_yr(e, tb=tb):
                    rank = e.partition_id() % 4
                    return e.dma_start(out=R2[:, :, tb * TC:(tb + 1) * TC],
                                       in_=cout_yr.ap()[bass.ds(2 * rank + tb, 1)].rearrange("o (k p) t -> p (o k) t", p=128))
                fw.op("pool", f_yr, reads=s_cyr, writes=[s_R2], kind="dma")

        def decl_o():
            for tb in range(2):
                def f_o(e, tb=tb):
                    rank = e.partition_id() % 4
                    return e.dma_start(out=oT_a[:, :, tb * TC:(tb + 1) * TC],
                                       in_=cout_o.ap()[bass.ds(2 * rank + tb, 1)].rearrange("o (k p) t -> p (o k) t", p=128))
                fw.op("pool", f_o, reads=s_co, writes=[s_oT], kind="dma")

        def decl_p():
            fw.op("pool", lambda e: e.dma_start(out=pTb.rearrange("p k t -> p (k t)"), in_=pTt[:, :]), writes=[s_pT], kind="dma")

        fw.stop_if("s0")
        for cc in range(NKC):
            ugr = wload(wt_g[cc])
            uga = wload(wt_g[16 + cc])
            if cc == 0:
                decl_yr()
            ur0 = wload(wt_rnn[cc, 0], 11)
            ur1 = wload(wt_rnn[cc, 1], 11)
            if cc == 0:
                decl_o()
            uat = wload(wt_att[cc])
            if cc == 0:
                decl_p()
            for tb in range(2):
                hs = s_R1h[4 * tb:4 * tb + 4]
                p_gr = psP.next()
                acc_group(p_gr[0], p_gr[1], [(ugr[0], ugr[1], NKC, h_a, 0, 128)], tb, hs)
                p_ga = psP.next()
                acc_group(p_ga[0], p_ga[1], [(uga[0], uga[1], NKC, h_a, 0, 128)], tb, hs)
                p_yr = psP.next()
                acc_group(p_yr[0], p_yr[1], [(ur0[0], ur0[1], 11, R2, 0, 128), (ur1[0], ur1[1], 11, R2, 11, 128)], tb, [s_R2])
                p_ya = psP.next()
                acc_group(p_ya[0], p_ya[1], [(uat[0], uat[1], NKC, oT_a, 0, 128)], tb, [s_oT])
                t_sr = tmpR.next()
                t_sa = tmpR.next()
                fw.op("act", lambda e, p=p_gr, t=t_sr: e.activation(out=t[0], in_=p[0], func=AF.Sigmoid), reads=[p_gr[1]], writes=[t_sr[1]])
                fw.op("act", lambda e, p=p_ga, t=t_sa: e.activation(out=t[0], in_=p[0], func=AF.Sigmoid), reads=[p_ga[1]], writes=[t_sa[1]])
                fw.op("dve", lambda e, p=p_yr, t=t_sr: e.tensor_tensor(out=t[0], in0=t[0], in1=p[0], op=ALU.mult),
                      reads=[p_yr[1], t_sr[1]], writes=[t_sr[1]])
                fw.op("dve", lambda e, p=p_ya, t=t_sa: e.tensor_tensor(out=t[0], in0=t[0], in1=p[0], op=ALU.mult),
                      reads=[p_ya[1], t_sa[1]], writes=[t_sa[1]])
                fw.op("dve", lambda e, a=t_sr, b=t_sa, cc=cc, tb=tb: e.tensor_tensor(
                    out=R3[:, cc, tb * TC:(tb + 1) * TC], in0=a[0], in1=b[0], op=ALU.add),
                    reads=[t_sr[1], t_sa[1]], writes=[s_R3[cc][tb]])

        fw.barrier(["pe", "act", "dve", "sp"])
        fw.stop_if("sa")
        mix_slots = [s_R3[c][t] for c in range(NKC) for t in range(2)]
        for cc in range(NKC):
            uo = wload(wt_o[cc])
            for tb in range(2):
                fw.op("sp", lambda e, cc=cc, tb=tb: e.dma_start(out=R1[:, cc, tb * TC:(tb + 1) * TC], in_=xo2[cc, tb]),
                      writes=[s_xo[cc][tb]], kind="dma")
                p = psP.next()
                acc_group(p[0], p[1], [(uo[0], uo[1], NKC, R3, 0, 128)], tb, [s_R3[c][tb] for c in range(NKC)])
                fw.op("dve", lambda e, p=p, cc=cc, tb=tb: e.tensor_tensor(
                    out=R1[:, cc, tb * TC:(tb + 1) * TC], in0=R1[:, cc, tb * TC:(tb + 1) * TC], in1=p[0], op=ALU.add),
                    reads=[p[1], s_xo[cc][tb]], writes=[s_xo[cc][tb]])


        def rms_resident(gcol0, dst, dst_slots):
            for sc in range(8):
                tb = sc // 4
                rmsnorm_sub(None, R1[:, :, sc * SUB:(sc + 1) * SUB], [s_xo[c][tb] for c in range(NKC)], gcol0,
                            dst[:, :, sc * SUB:(sc + 1) * SUB], dst_slots[sc], SUB, scratchP, psP)

        fw.stop_if("sb")
        rms_resident(16, R3, s_h2)
        for half in range(2):
            for m in range(22):
                mm = half * 22 + m
                ug = wload(wt_gu[mm])
                uu = wload(wt_gu[44 + mm])
                for tb in range(2):
                    hs = s_h2[4 * tb:4 * tb + 4]
                    p_g = psP.next()
                    acc_group(p_g[0], p_g[1], [(ug[0], ug[1], NKC, R3, 0, 128)], tb, hs)
                    p_u = psP.next()
                    acc_group(p_u[0], p_u[1], [(uu[0], uu[1], NKC, R3, 0, 128)], tb, hs)
                    t_s = tmpR.next()
                    fw.op("act", lambda e, p=p_g, t=t_s: e.activation(out=t[0], in_=p[0], func=AF.Silu), reads=[p_g[1]], writes=[t_s[1]])
                    fw.op("dve", lambda e, p=p_u, t=t_s, m=m, tb=tb: e.tensor_tensor(
                        out=R2[:, m, tb * TC:(tb + 1) * TC], in0=t[0], in1=p[0], op=ALU.mult),
                        reads=[p_u[1], t_s[1]], writes=[s_R2])
            for cc in range(NKC):
                ud0 = wload(wt_dn[half, cc, 0], 11)
                ud1 = wload(wt_dn[half, cc, 1], 11)
                for tb in range(2):
                    p = psP.next()
                    acc_group(p[0], p[1], [(ud0[0], ud0[1], 11, R2, 0, 128), (ud1[0], ud1[1], 11, R2, 11, 128)], tb, [s_R2])
                    fw.op("dve", lambda e, p=p, cc=cc, tb=tb: e.tensor_tensor(
                        out=R1[:, cc, tb * TC:(tb + 1) * TC], in0=R1[:, cc, tb * TC:(tb + 1) * TC], in1=p[0], op=ALU.add),
                        reads=[p[1], s_xo[cc][tb]], writes=[s_xo[cc][tb]])

        fw.stop_if("sc")
        h3 = R2[:, 0:NKC, :]
        s_h3 = [Slot() for _ in range(8)]
        rms_resident(32, h3, s_h3)
        out_ops = []
        for cc in range(NKC):
            upg = wload(wt_pg[cc])
            upp = wload(wt_pp[cc], 2)
            for tb in range(2):
                hs = s_h3[4 * tb:4 * tb + 4]
                p_g = psP.next()
                acc_group(p_g[0], p_g[1], [(upg[0], upg[1], NKC, h3, 0, 128)], tb, hs)
                p_p = psP.next()

                acc_group(p_p[0], p_p[1], [(upp[0], upp[1], 2, pTb, 0, 128)], tb, [s_pT])
                t_s = tmpR.next()
                fw.op("act", lambda e, p=p_g, t=t_s: e.activation(out=t[0], in_=p[0], func=AF.Sigmoid), reads=[p_g[1]], writes=[t_s[1]])
                fw.op("dve", lambda e, p=p_p, t=t_s: e.tensor_tensor(out=t[0], in0=t[0], in1=p[0], op=ALU.mult),
                      reads=[p_p[1], t_s[1]], writes=[t_s[1]])
                fw.op("dve", lambda e, t=t_s, cc=cc, tb=tb: e.tensor_tensor(
                    out=R1[:, cc, tb * TC:(tb + 1) * TC], in0=R1[:, cc, tb * TC:(tb + 1) * TC], in1=t[0], op=ALU.add),
                    reads=[t_s[1], s_xo[cc][tb]], writes=[s_xo[cc][tb]])
            out_ops.append(fw.op("sp", lambda e, cc=cc: e.dma_start(out=outT[cc], in_=R1[:, cc, :]),
                                 reads=[s_xo[cc][0], s_xo[cc][1]], kind="dma"))
        s_fin = Slot("fin")
        fin = fw.op("sp", lambda e: e.nop(), kind="c")
        fin.deps.update(o_ for o_ in out_ops if o_.kind != "skip")

        import os as _os2
        if _os2.environ.get("KNOSCHED", "") != "1":
            fw.schedule()
        fw.finalize()
        sems = {}
        for e in ENGS:
            sems[("eng", e)] = es.enter_context(nc.semaphore(f"s_{e}"))
            for i in range(FW.NDS):
                sems[("dma", e, i)] = es.enter_context(nc.semaphore(f"d_{e}{i}"))
        for i in range(fw.cc_count):
            sems[("cc", "pool", i)] = es.enter_context(nc.semaphore(f"cc{i}"))
        block = es.enter_context(nc.Block())

        @block.tensor
        def _(t):
            fw.emit("pe", t, sems)

        @block.scalar
        def _(a):
            fw.emit("act", a, sems)

        @block.vector
        def _(v):
            fw.emit("dve", v, sems)

        @block.gpsimd
        def _(g):
            fw.emit("pool", g, sems)

        @block.sync
        def _(sy):
            fw.emit("sp", sy, sems)
    return nc


def _tile_w(w, nk):
    K, N = w.shape
    assert K == nk * 128
    t = w.reshape(nk, 128, N // 128, 128).transpose(2, 1, 0, 3)
    return np.ascontiguousarray(t).reshape(N // 128, 128, nk * 128)


_NC_CACHE = {}


def kernel(x, p, g_mix, w_in, conv_w, conv_b, w_rg_a, b_rg_a, w_rg_x, b_rg_x, lru_lambda, q_gain, k_gain,
           w_rnn_out, w_attn_out, w_o, g_ffn, w_ffn_gu, w_ffn_down, g_ple, w_ple_gate, w_ple_proj):
    f = np.float32
    x = np.asarray(x, f)
    p = np.asarray(p, f)
    w_in0 = np.asarray(w_in, f)[0]
    if "nc" not in _NC_CACHE:
        _NC_CACHE["nc"] = build_program()
    nc = _NC_CACHE["nc"]

    vec128 = np.zeros((128, 50), f)
    vec128[:, 0:16] = np.asarray(g_mix, f)[0].reshape(16, 128).T
    vec128[:, 16:32] = np.asarray(g_ffn, f)[0].reshape(16, 128).T
    vec128[:, 32:48] = np.asarray(g_ple, f)[0].reshape(16, 128).T
    vec128[:, 48] = np.asarray(q_gain, f)[0]
    vec128[:, 49] = np.asarray(k_gain, f)[0]
    cmat = np.zeros((128, 384 + 896), f)
    cmat[:, 0:128] = np.eye(128, dtype=f)
    jj = np.arange(128)[:, None]
    ss = np.arange(128)[None, :]
    cmat[:, 128:256] = (jj >= ss).astype(f)
    cmat[:, 256:384] = (jj < ss).astype(f)
    xx = np.arange(896)[None, :]
    cmat[:, 384:] = np.where(xx <= jj + 384, NEG, 0.0).astype(f)
    wt_g = _tile_w(w_in0[:, 11776:15872], 16)
    wr = np.asarray(w_rnn_out, f)[0]
    wt_rnn = _tile_w(wr, 22).reshape(16, 128, 2, 11 * 128).transpose(0, 2, 1, 3).copy()
    wt_att = _tile_w(np.asarray(w_attn_out, f)[0], 16)
    wt_o = _tile_w(np.asarray(w_o, f)[0], 16)
    wt_gu = _tile_w(np.asarray(w_ffn_gu, f)[0], 16)
    wd = np.asarray(w_ffn_down, f)[0]
    wt_dn = np.stack([_tile_w(wd[h * 2816:(h + 1) * 2816], 22).reshape(16, 128, 2, 11 * 128).transpose(0, 2, 1, 3)
                      for h in range(2)]).copy()
    wt_pg = _tile_w(np.asarray(w_ple_gate, f)[0], 16)
    wpp = np.asarray(w_ple_proj, f)[0]
    wt_pp = _tile_w(wpp, 2)
    cw = np.asarray(conv_w, f)[0]
    cbv = np.asarray(conv_b, f)[0]
    ba = np.asarray(b_rg_a, f)[0]
    bx = np.asarray(b_rg_x, f)[0]
    lam = np.asarray(lru_lambda, f)[0]
    wa_all = np.asarray(w_rg_a, f)[0]
    wx_all = np.asarray(w_rg_x, f)[0]

    in_maps = []
    for c in range(8):
        b, g = c // 4, c % 4
        xb = x[b]
        xTt = np.ascontiguousarray(xb.reshape(32, 128, 16, 128).transpose(0, 3, 2, 1)).reshape(32, 128, 16 * 128)
        xo = xb[1024 * g:1024 * (g + 1)]
        xo2 = np.ascontiguousarray(xo.reshape(2, 512, 16, 128).transpose(2, 0, 3, 1))
        pb = p[0, b, 1024 * g:1024 * (g + 1)]
        pTt = np.ascontiguousarray(pb.reshape(1024, 2, 128).transpose(2, 1, 0)).reshape(128, 2 * 1024)
        w1a = np.concatenate([w_in0[:, 704 * g:704 * (g + 1)], w_in0[:, 2816 + 704 * g:2816 + 704 * (g + 1)]], axis=1)
        w1a = np.ascontiguousarray(w1a.reshape(16, 128, 1408).transpose(1, 0, 2)).reshape(128, 16 * 1408)
        w1b = np.concatenate([w_in0[:, 5632 + 512 * g:5632 + 512 * (g + 1)], w_in0[:, 7680 + 512 * g:7680 + 512 * (g + 1)],
                              w_in0[:, 9728 + 512 * g:9728 + 512 * (g + 1)]], axis=1)
        w1b = np.ascontiguousarray(w1b.reshape(16, 128, 1536).transpose(1, 0, 2)).reshape(128, 16 * 1536)
        ch = slice(704 * g, 704 * (g + 1))
        v88 = np.zeros((88, 8, 8), f)
        for k in range(4):
            v88[:, :, k] = cw[k, ch].reshape(8, 88).T
        v88[:, :, 4] = cbv[ch].reshape(8, 88).T
        v88[:, :, 5] = ba[ch].reshape(8, 88).T
        v88[:, :, 6] = bx[ch].reshape(8, 88).T
        v88[:, :, 7] = lam[ch].reshape(8, 88).T
        wax = np.stack([wa_all[4 * g:4 * g + 4], wx_all[4 * g:4 * g + 4]])
        wax = np.ascontiguousarray(wax.reshape(2, 4, 2, 88, 176).transpose(3, 0, 1, 2, 4)).reshape(88, 2 * 4 * 2 * 176)
        in_maps.append(dict(xTt=xTt, xo2=xo2, pTt=pTt, w1a=w1a, w1b=w1b,
                            vec88=v88.reshape(88, 64), vec128=vec128, wax=wax, cmat=cmat, wt_g=wt_g, wt_rnn=wt_rnn,
                            wt_att=wt_att, wt_o=wt_o, wt_gu=wt_gu, wt_dn=wt_dn, wt_pg=wt_pg, wt_pp=wt_pp))
    res = run_bass_kernel_spmd(nc, in_maps, core_ids=list(range(8)))
    out = np.empty((2, S, D), f)
    for c in range(8):
        b, g = c // 4, c % 4
        oT = np.asarray(res.results[c]["outT"], f).reshape(2048, 1024)
        out[b, 1024 * g:1024 * (g + 1), :] = oT.T
    return out
```

```python
import numpy as np
from contextlib import ExitStack
import concourse.bass as bass
import concourse.mybir as mybir
from concourse.bass_utils import run_bass_kernel_spmd

F32 = mybir.dt.float32
BF16 = mybir.dt.bfloat16
U8 = mybir.dt.uint8
AF = mybir.ActivationFunctionType
ALU = mybir.AluOpType

S = 4096
D = 2048
NKC = 16
TC = 512
NCH = S // TC
SUB = 128
EPS = 1e-6
QSCALE = 1.0 / np.sqrt(128.0)
NEG = -30000.0
ARENA = 198 * 1024
GELU_C = 0.7978845608028654

ENGS = ["pe", "act", "dve", "pool", "sp"]


class Slot:
    __slots__ = ("name", "w", "r", "excl")

    def __init__(self, name="", excl=False):
        self.name = name
        self.w = None
        self.r = []
        self.excl = excl


class Op:
    __slots__ = ("eng", "fn", "deps", "needs_inc", "seq", "kind", "sem", "semv", "pos", "raw", "cost", "idx",
                 "nd", "rt", "fin", "succ")


class FW:
    NDS = 8

    def __init__(self):
        self.ops = {e: [] for e in ENGS}
        self.cur_barrier = {e: None for e in ENGS}
        self.since_barrier = []
        self.cc_count = 0
        self.stopped = False
        self.all = []

    def stop_if(self, tag):
        import os
        if os.environ.get("KSTOP", "") == tag:
            self.stopped = True

    DEFCOST = {"pe": 4.6, "act": 0.65, "dve": 0.65, "pool": 5.0, "sp": 3.0}

    def op(self, eng, fn, reads=(), writes=(), kind="c", cost=None):
        o = Op()
        o.cost = cost if cost is not None else (30.0 if kind == "cc" else self.DEFCOST[eng])
        if self.stopped:
            o.eng, o.fn, o.kind, o.deps, o.raw, o.pos = eng, fn, "skip", set(), set(), -1
            return o
        o.eng, o.fn, o.kind = eng, fn, kind
        o.needs_inc = False
        o.seq = 0
        o.sem = None
        o.semv = 0
        deps = set()
        raw = set()
        for s in reads:
            if s.w is not None:
                deps.add(s.w)
                raw.add(s.w)
            if s.excl:
                deps.update(s.r)
        for s in writes:
            if s.w is not None:
                deps.add(s.w)
            deps.update(s.r)
        for s in reads:
            s.r.append(o)
        for s in writes:
            s.w = o
            s.r = []
        if self.cur_barrier[eng] is not None:
            deps.add(self.cur_barrier[eng])
        if kind == "cc":
            o.sem = ("cc", eng, self.cc_count)
            self.cc_count += 1
            o.semv = 1
        deps.discard(o)
        o.deps = deps
        o.raw = raw
        o.pos = len(self.ops[eng])
        self.ops[eng].append(o)
        o.idx = len(self.all)
        self.all.append(o)
        self.since_barrier.append(o)
        return o

    def schedule(self, reorder=("pe", "act", "dve", "sp"), W=200):
        import bisect
        for o in self.all:
            o.succ = []
        for o in self.all:
            o.nd = len(o.deps)
            o.rt = 0.0
            for d in o.deps:
                d.succ.append(o)
        blev = {}
        for o in reversed(self.all):
            m = 0.0
            for c in o.succ:
                v = blev[id(c)]
                if v > m:
                    m = v
            blev[id(o)] = m + o.cost
        import os as _os3
        use_bl = _os3.environ.get("KPRIO", "bl") == "bl"
        for o in self.all:
            o.seq = (-blev[id(o)] if use_bl else 0.0, o.idx)
        avail = {e: [] for e in ENGS}
        ptr = {e: 0 for e in ENGS}
        free_at = {e: 0.0 for e in ENGS}
        order = {e: [] for e in ENGS}
        for o in self.all:
            if o.nd == 0 and o.eng in reorder:
                bisect.insort(avail[o.eng], (o.seq, o))
        remaining = len(self.all)
        while remaining:
            best = None
            for e in ENGS:
                if e in reorder:
                    lst = avail[e]
                    if not lst:
                        continue
                    fa = free_at[e]
                    cb = None
                    for (ix, o) in lst[:W]:
                        st = o.rt if o.rt > fa else fa
                        if cb is None or st < cb[0] - 1e-9:
                            cb = (st, ix, o)
                        if st <= fa:
                            break
                else:
                    if ptr[e] >= len(self.ops[e]):
                        continue
                    o = self.ops[e][ptr[e]]
                    if o.nd > 0:
                        continue
                    st = max(o.rt, free_at[e])
                    cb = (st, o.seq, o)
                if best is None or (cb[0], cb[1]) < (best[0], best[1]):
                    best = cb
            assert best is not None, "scheduler stuck"
            st, ix, o = best
            e = o.eng
            if e in reorder:
                i = bisect.bisect_left(avail[e], (ix, ))
                assert avail[e][i][1] is o
                avail[e].pop(i)
            else:
                ptr[e] += 1
            if o.kind in ("dma", "cc"):
                free_at[e] = st + (1.0 if e == "pool" else 0.15)
            else:
                free_at[e] = st + o.cost
            o.fin = st + o.cost
            order[e].append(o)
            remaining -= 1
            for c in o.succ:
                c.nd -= 1
                if o.fin > c.rt:
                    c.rt = o.fin
                if c.nd == 0 and c.eng in reorder:
                    bisect.insort(avail[c.eng], (c.seq, c))
        for e in ENGS:
            self.ops[e] = order[e]
            for i, o in enumerate(order[e]):
                o.pos = i
        self.makespan = max(o.fin for o in self.all)

    def barrier(self, engines, split_cc=False):
        if self.stopped:
            return
        prior = list(self.since_barrier)
        b = self.op("dve", lambda e: e.nop(), cost=0.1)
        b.deps.update(o for o in prior if not (split_cc and o.kind == "cc"))
        b.deps.discard(b)
        self.since_barrier = [b]
        for e in engines:
            self.cur_barrier[e] = b
        if split_cc and "pool" in engines:
            bp = self.op("pool", lambda e: e.nop(), cost=0.1)
            bp.deps.update(prior)
            bp.deps.add(b)
            bp.deps.discard(bp)
            self.cur_barrier["pool"] = bp

    def finalize(self):
        for e in ENGS:
            n = 0
            hist = []
            for o in self.ops[e]:
                if o.kind != "dma":
                    continue
                o.sem = ("dma", e, n % self.NDS)
                o.semv = 16 * (n // self.NDS + 1)
                if n >= self.NDS:
                    o.deps.add(hist[n - self.NDS])
                hist.append(o)
                n += 1
        for e in ENGS:
            for o in self.ops[e]:
                for d in o.deps:
                    if d.kind != "c":
                        continue
                    if d.eng != o.eng:
                        d.needs_inc = True
                    elif d in o.raw and o.pos - d.pos <= 1:
                        d.needs_inc = True
        for e in ENGS:
            n = 0
            for o in self.ops[e]:
                if o.kind == "c" and o.needs_inc:
                    n += 1
                    o.seq = n
                    o.sem = ("eng", e)
                    o.semv = n

    def emit(self, eng, handle, sems):
        waited = {}
        for o in self.ops[eng]:
            need = {}
            for d in o.deps:
                if d.kind == "c":
                    if d.eng == eng and not (d in o.raw and o.pos - d.pos <= 1):
                        continue
                if need.get(d.sem, 0) < d.semv:
                    need[d.sem] = d.semv
            for k, v in need.items():
                if waited.get(k, 0) >= v:
                    continue
                waited[k] = v
                handle.wait_ge(sems[k], v)
            inst = o.fn(handle)
            if o.kind == "dma":
                inst.then_inc(sems[o.sem], 16)
            elif o.kind == "cc":
                inst.then_inc(sems[o.sem], 1)
            elif o.needs_inc:
                inst.then_inc(sems[o.sem], 1)


class Arena:
    def __init__(self, ap):
        self.ap = ap
        self.top = 0

    def alloc(self, parts, free, dtype):
        esz = 4 if dtype == F32 else 2
        n = int(np.prod(free)) * esz
        off = (self.top + 63) // 64 * 64
        self.top = off + n
        assert self.top <= ARENA, f"arena overflow {self.top}"
        v = self.ap[0:parts, off:off + n].bitcast(dtype)
        if len(free) == 2:
            v = v.rearrange("p (a b) -> p a b", a=free[0])
        elif len(free) == 3:
            v = v.rearrange("p (a b c) -> p a b c", a=free[0], b=free[1])
        return v


def arena_at(arena, off, parts, free, dtype):
    esz = 4 if dtype == F32 else 2
    n = int(np.prod(free)) * esz
    assert off % 64 == 0 and off + n <= ARENA
    v = arena.ap[0:parts, off:off + n].bitcast(dtype)
    if len(free) == 2:
        v = v.rearrange("p (a b) -> p a b", a=free[0])
    arena.top = max(arena.top, off + n)
    return v


class Ring:
    def __init__(self, items):
        self.items = items
        self.i = 0

    def next(self):
        it = self.items[self.i % len(self.items)]
        self.i += 1
        return it


def build_program():
    nc = bass.Bass("TRN2", target_bir_lowering=False)

    def din(name, shape):
        return nc.dram_tensor(name, shape, F32, kind="ExternalInput").ap()

    xTt = din("xTt", [S // SUB, 128, NKC * SUB])
    xo2 = din("xo2", [16, 2, 128, 512])
    pTt = din("pTt", [128, 2 * 1024])
    w1a = din("w1a", [128, NKC * 1408])
    w1b = din("w1b", [128, NKC * 1536])
    vec88 = din("vec88", [88, 64])
    vec128 = din("vec128", [128, 50])
    wax = din("wax", [88, 2 * 4 * 2 * 176])
    cmat = din("cmat", [128, 384 + 896])
    wt_g = din("wt_g", [32, 128, 2048])
    wt_rnn = din("wt_rnn", [16, 2, 128, 11 * 128])
    wt_att = din("wt_att", [16, 128, 2048])
    wt_o = din("wt_o", [16, 128, 2048])
    wt_gu = din("wt_gu", [88, 128, 2048])
    wt_dn = din("wt_dn", [2, 16, 2, 128, 11 * 128])
    wt_pg = din("wt_pg", [16, 128, 2048])
    wt_pp = din("wt_pp", [16, 128, 2 * 128])
    outT = nc.dram_tensor("outT", [16, 128, 1024], F32, kind="ExternalOutput").ap()
    cin_yr = nc.dram_tensor("cin_yr", [NCH, 704, TC], BF16)
    cout_yr = nc.dram_tensor("cout_yr", [NCH, 2816, TC], BF16)
    cin_o = nc.dram_tensor("cin_o", [NCH, 512, TC], BF16)
    cout_o = nc.dram_tensor("cout_o", [NCH, 2048, TC], BF16)
    RG = [[0, 1, 2, 3], [4, 5, 6, 7]]
    hbuf = nc.dram_tensor("hbuf", [NCH, 128, NKC * TC], BF16)
    s_hbuf = [Slot() for _ in range(NCH)]
    s_cyr = [Slot() for _ in range(NCH)]
    s_co = [Slot() for _ in range(NCH)]

    fw = FW()
    es = ExitStack()
    with es:
        arena_t = es.enter_context(nc.sbuf_tensor("arena", [128, ARENA], U8))
        cst_t = es.enter_context(nc.sbuf_tensor("cst", [128, 2432], F32))
        psb2 = [es.enter_context(nc.psum_tensor(f"ps{i}", [128, 1024], F32)) for i in range(4)]
        psb = [psb2[i // 2][:, (i % 2) * 512:(i % 2 + 1) * 512] for i in range(8)]
        ps_slots = [Slot(f"ps{i}", excl=True) for i in range(8)]

        cst = cst_t
        c_vec128 = cst[:, 0:50]
        c_vec88 = cst[0:88, 64:128]
        c_der88 = cst[0:88, 128:160]
        c_num = cst[:, 160:168]
        cb = cst[:, 192:192 + 768].bitcast(BF16)
        c_ident = cb[:, 0:128]
        c_L = cb[:, 128:256]
        c_SU = cb[:, 256:384]
        c_mask = cb[:, 384:384 + 896]
        c_onesm = cb[:, 1280:1408]
        c_onesh = cb[:, 1408:1536]
        c_halo = cst[0:88, 968:968 + 24].rearrange("p (a b) -> p a b", a=8)
        c_hlast = cst[0:88, 1000:1008]
        c_wax = cst[0:88, 1016:1016 + 1408].bitcast(BF16).rearrange("p (w n i c) -> p w n i c", w=2, n=4, i=2)
        s_const = Slot("const")
        s_halo = [Slot() for _ in range(4)]
        s_hlast = [Slot() for _ in range(8)]
        EPS_AP = c_num[:, 0:1]
        ONE_AP = c_num[:, 1:2]

        def g128(col):
            return c_vec128[:, col:col + 1]

        fw.op("sp", lambda e: e.dma_start(out=c_vec128, in_=vec128[:, :]), writes=[s_const], kind="dma")
        fw.op("sp", lambda e: e.dma_start(out=c_vec88, in_=vec88[:, :]), writes=[s_const], kind="dma")
        fw.op("pool", lambda e: e.dma_start(out=cb[:, 0:1280], in_=cmat[:, :]), writes=[s_const], kind="dma")
        fw.op("pool", lambda e: e.dma_start(out=c_wax.rearrange("p w n i c -> p (w n i c)"), in_=wax[:, :]),
              writes=[s_const], kind="dma")

        def setup_consts(e):
            e.memset(c_num[:, 0:1], EPS)
            e.memset(c_num[:, 1:2], 1.0)
            e.memset(c_num[:, 2:3], 0.0)
            e.memset(c_onesm, 1.0 / 2048.0)
            e.memset(c_onesh, 1.0 / 128.0)
            e.memset(c_halo, 0.0)
            return e.memset(c_hlast, 0.0)
        fw.op("dve", setup_consts, writes=[s_const] + s_halo + s_hlast)

        v88 = c_vec88.rearrange("p (c f) -> p c f", c=8)
        hba = c_der88[:, 0:8]
        hbx = c_der88[:, 8:16]
        scl = c_der88[:, 16:24]
        tmp8 = c_der88[:, 24:32]
        fw.op("act", lambda e: e.activation(out=tmp8, in_=v88[:, :, 7], func=AF.Exp, scale=-1.0),
              reads=[s_const], writes=[s_const])
        fw.op("dve", lambda e: e.tensor_scalar(out=hba, in0=v88[:, :, 5], scalar1=0.5, scalar2=None, op0=ALU.mult),
              reads=[s_const], writes=[s_const])
        fw.op("act", lambda e: e.activation(out=scl, in_=tmp8, func=AF.Ln, bias=ONE_AP[0:88, :], scale=1.0),
              reads=[s_const], writes=[s_const])
        fw.op("dve", lambda e: e.tensor_scalar(out=hbx, in0=v88[:, :, 6], scalar1=0.5, scalar2=None, op0=ALU.mult),
              reads=[s_const], writes=[s_const])
        fw.op("dve", lambda e: e.tensor_scalar(out=scl, in0=scl, scalar1=-4.0, scalar2=None, op0=ALU.mult),
              reads=[s_const], writes=[s_const])

        arena_ap = arena_t[:, :]
        fw.stop_if("setup")

        def rmsnorm_sub(load_fn, src_ap, src_slots, gcol0, dst_ap, dst_slot, ntok, scratch, psring, sq_eng="act"):
            sq, s_sq, lnm, s_ln = scratch
            if load_fn is not None:
                load_fn()
            if sq_eng == "act":
                fw.op("act", lambda e: e.activation(out=sq, in_=src_ap, func=AF.Square),
                      reads=src_slots, writes=[s_sq], cost=2.0)
            else:
                fw.op("dve", lambda e: e.tensor_tensor(out=sq, in0=src_ap, in1=src_ap, op=ALU.mult),
                      reads=src_slots, writes=[s_sq], cost=2.4)
            ps, s_ps = psring.next()

            def f_mm(e):
                for kc in range(NKC):
                    last = e.matmul(ps[:, 0:ntok], lhsT=c_onesm, rhs=sq[:, kc, :], start=(kc == 0), stop=(kc == NKC - 1))
                return last
            fw.op("pe", f_mm, reads=[s_sq, s_const], writes=[s_ps], cost=16 * 0.12)
            fw.op("act", lambda e: e.activation(out=lnm[:, 0:ntok], in_=ps[:, 0:ntok], func=AF.Ln, bias=EPS_AP, scale=1.0),
                  reads=[s_ps], writes=[s_ln])
            fw.op("act", lambda e: e.activation(out=lnm[:, ntok:2 * ntok], in_=lnm[:, 0:ntok], func=AF.Exp, scale=-0.5),
                  reads=[s_ln], writes=[s_ln])

            def f_h(e):
                for kc in range(NKC):
                    last = e.scalar_tensor_tensor(out=dst_ap[:, kc, :], in0=src_ap[:, kc, :], scalar=g128(gcol0 + kc),
                                                  in1=lnm[:, ntok:2 * ntok], op0=ALU.mult, op1=ALU.mult)
                return last
            fw.op("dve", f_h, reads=list(src_slots) + [s_ln, s_const], writes=[dst_slot], cost=16 * 0.22)

        A = Arena(arena_ap)
        W1a = A.alloc(128, (NKC, 1408), BF16)
        s_W1a = Slot("W1a")
        off_xs = A.top
        xs_r = [(A.alloc(128, (NKC, SUB), F32), Slot()) for _ in range(2)]
        xsR = Ring(xs_r)
        sq_a = A.alloc(128, (NKC, SUB), BF16)
        lnm_a = A.alloc(128, (2 * SUB,), F32)
        scratchA = (sq_a, Slot(), lnm_a, Slot())
        off_hT0 = A.top
        hT_ring = Ring([(A.alloc(128, (NKC, TC), BF16), [Slot() for _ in range(TC // SUB)]) for _ in range(2)])
        hT0_slots_A = hT_ring.items[0][1]
        xs_slots_A = [xs_r[0][1], xs_r[1][1]]
        yrb = [(A.alloc(88, (8, TC), BF16), Slot()) for _ in range(2)]
        NSET = 2
        sets = []
        for _ in range(NSET):
            d = {}
            d["xr"] = A.alloc(88, (2, TC + 3), F32)
            d["gr"] = A.alloc(88, (2, TC), F32)
            d["xc"] = A.alloc(88, (2, TC), F32)
            d["xcb"] = A.alloc(88, (2, TC), BF16)
            for nm in ("rp", "ip", "a", "t", "s", "h"):
                d[nm] = A.alloc(88, (2, TC), F32)
            d["slots"] = {nm: [Slot(), Slot()] for nm in ("xr", "gr", "xc", "xcb", "rp", "ip", "a", "t", "s", "h")}
            sets.append(d)
        psA = Ring(list(zip([p[:, :] for p in psb[0:6]], ps_slots[0:6])))
        psA_ms = Ring(list(zip([p[:, :] for p in psb[6:8]], ps_slots[6:8])))

        for q4 in range(4):
            fw.op("pool", lambda e, q4=q4: e.dma_start(
                out=W1a[:, 4 * q4:4 * q4 + 4, :],
                in_=w1a[:, 4 * q4 * 1408:(4 * q4 + 4) * 1408].rearrange("p (k n) -> p k n", k=4)),
                writes=[s_W1a], kind="dma")

        def proj_fm(Wt, s_W, col0, M, ps, s_ps, h_ap, h_slots, ntok=TC):
            def f(e):
                for kc in range(NKC):
                    last = e.matmul(ps[0:M, 0:ntok], lhsT=Wt[:, kc, col0:col0 + M], rhs=h_ap[:, kc, :],
                                    start=(kc == 0), stop=(kc == NKC - 1))
                return last
            fw.op("pe", f, reads=[s_W] + list(h_slots), writes=[s_ps])

        def hT_chunk(G, xsrc, scratch, psring, sq_eng="act"):
            hT, s_hT = hT_ring.next()
            for sb in range(TC // SUB):
                sc = G * (TC // SUB) + sb
                xs, s_xs = xsR.next()
                fw.op("sp", lambda e, xs=xs, sc=sc: e.dma_start(out=xs.rearrange("p k t -> p (k t)"), in_=xsrc[sc]),
                      writes=[s_xs], kind="dma")
                rmsnorm_sub(None, xs, [s_xs], 0, hT[:, :, sb * SUB:(sb + 1) * SUB], s_hT[sb], SUB, scratch, psring, sq_eng)
            return hT, s_hT

        Wk = arena_at(A, off_xs, 128, (NKC, 512), BF16)
        Wq = arena_at(A, off_hT0, 128, (NKC, 512), BF16)
        s_Wq, s_Wk, s_Wv = Slot("Wq"), Slot("Wk"), Slot("Wv")
        w1b_v = w1b.rearrange("p (k n) -> p k n", k=NKC)
        for G in range(NCH):
            hT, s_hT = hT_chunk(G, xTt, scratchA, psA_ms)
            if G == NCH - 1 and not fw.stopped:
                fw.op("pool", lambda e: e.dma_start(out=Wq, in_=w1b_v[:, :, 0:512]), writes=[s_Wq] + hT0_slots_A, kind="dma", cost=12.0)
                fw.op("pool", lambda e: e.dma_start(out=Wk, in_=w1b_v[:, :, 512:1024]), writes=[s_Wk] + xs_slots_A, kind="dma", cost=12.0)
            fw.op("sp", lambda e, hT=hT, G=G: e.dma_start(out=hbuf.ap()[G], in_=hT.rearrange("p k t -> p (k t)")),
                  reads=s_hT, writes=[s_hbuf[G]], kind="dma")
            yb, s_yb = yrb[G % 2]
            for n in range(4):
                st = sets[(G * 4 + n) % NSET]
                sl = st["slots"]
                for j in range(2):
                    c8 = 2 * n + j
                    ps, s_ps = psA.next()
                    proj_fm(W1a, s_W1a, c8 * 88, 88, ps, s_ps, hT, s_hT)
                    fw.op("act", lambda e, ps=ps, st=st, j=j: e.activation(out=st["xr"][:, j, 3:TC + 3], in_=ps[0:88, :], func=AF.Copy),
                          reads=[s_ps], writes=[sl["xr"][j]])
                    ps, s_ps = psA.next()
                    proj_fm(W1a, s_W1a, 704 + c8 * 88, 88, ps, s_ps, hT, s_hT)
                    fw.op("act", lambda e, ps=ps, st=st, j=j: e.activation(out=st["gr"][:, j, :], in_=ps[0:88, :], func=AF.Copy),
                          reads=[s_ps], writes=[sl["gr"][j]])
                fw.op("dve", lambda e, st=st, n=n: e.tensor_copy(out=st["xr"][:, :, 0:3], in_=c_halo[:, 2 * n:2 * n + 2, :]),
                      reads=[s_halo[n]], writes=sl["xr"])
                fw.op("dve", lambda e, st=st, n=n: e.tensor_copy(out=c_halo[:, 2 * n:2 * n + 2, :], in_=st["xr"][:, :, TC:TC + 3]),
                      reads=sl["xr"], writes=[s_halo[n]])
                for k in range(4):
                    for j in range(2):
                        c8 = 2 * n + j
                        if k == 0:
                            fw.op("dve", lambda e, st=st, j=j, c8=c8: e.tensor_scalar(
                                out=st["xc"][:, j, :], in0=st["xr"][:, j, 0:TC], scalar1=v88[:, c8, 0:1], scalar2=v88[:, c8, 4:5],
                                op0=ALU.mult, op1=ALU.add), reads=[sl["xr"][j], s_const], writes=[sl["xc"][j]])
                        else:
                            fw.op("dve", lambda e, st=st, j=j, c8=c8, k=k: e.scalar_tensor_tensor(
                                out=st["xc"][:, j, :], in0=st["xr"][:, j, k:k + TC], scalar=v88[:, c8, k:k + 1], in1=st["xc"][:, j, :],
                                op0=ALU.mult, op1=ALU.add), reads=[sl["xr"][j], sl["xc"][j], s_const], writes=[sl["xc"][j]])
                for j in range(2):
                    fw.op("act", lambda e, st=st, j=j: e.activation(out=st["xcb"][:, j, :], in_=st["xc"][:, j, :], func=AF.Copy),
                          reads=[sl["xc"][j]], writes=[sl["xcb"][j]])
                gps = []
                for j in range(2):
                    for w in range(2):
                        ps, s_ps = psA.next()

                        def f(e, ps=ps, st=st, j=j, w=w, n=n):
                            for i in range(2):
                                last = e.matmul(ps[0:88, :], lhsT=c_wax[:, w, n, i, 88 * j:88 * j + 88], rhs=st["xcb"][:, i, :],
                                                start=(i == 0), stop=(i == 1))
                            return last
                        fw.op("pe", f, reads=sl["xcb"] + [s_const], writes=[s_ps], cost=0.9)
                        gps.append((ps, s_ps))
                for j in range(2):
                    c8 = 2 * n + j
                    ps, s_ps = gps[2 * j]
                    fw.op("act", lambda e, ps=ps, st=st, j=j, c8=c8: e.activation(
                        out=st["rp"][:, j, :], in_=ps[0:88, :], func=AF.Tanh, bias=hba[:, c8:c8 + 1], scale=0.5),
                        reads=[s_ps, s_const], writes=[sl["rp"][j]])
                    ps, s_ps = gps[2 * j + 1]
                    fw.op("act", lambda e, ps=ps, st=st, j=j, c8=c8: e.activation(
                        out=st["ip"][:, j, :], in_=ps[0:88, :], func=AF.Tanh, bias=hbx[:, c8:c8 + 1], scale=0.5),
                        reads=[s_ps, s_const], writes=[sl["ip"][j]])
                for j in range(2):
                    c8 = 2 * n + j
                    fw.op("act", lambda e, st=st, j=j, c8=c8: e.activation(
                        out=st["a"][:, j, :], in_=st["rp"][:, j, :], func=AF.Exp, bias=scl[:, c8:c8 + 1], scale=scl[:, c8:c8 + 1]),
                        reads=[sl["rp"][j], s_const], writes=[sl["a"][j]])
                for j in range(2):
                    fw.op("dve", lambda e, st=st, j=j: e.scalar_tensor_tensor(
                        out=st["t"][:, j, :], in0=st["ip"][:, j, :], scalar=1.0, in1=st["xc"][:, j, :], op0=ALU.add, op1=ALU.mult),
                        reads=[sl["ip"][j], sl["xc"][j]], writes=[sl["t"][j]])
                for j in range(2):
                    fw.op("act", lambda e, st=st, j=j: e.activation(out=st["s"][:, j, :], in_=st["a"][:, j, :], func=AF.Square),
                          reads=[sl["a"][j]], writes=[sl["s"][j]])
                for j in range(2):
                    fw.op("act", lambda e, st=st, j=j: e.activation(out=st["s"][:, j, :], in_=st["s"][:, j, :], func=AF.Sqrt,
                                                                    bias=ONE_AP[0:88, :], scale=-1.0),
                          reads=[sl["s"][j], s_const], writes=[sl["s"][j]])
                for j in range(2):
                    fw.op("dve", lambda e, st=st, j=j: e.scalar_tensor_tensor(
                        out=st["t"][:, j, :], in0=st["s"][:, j, :], scalar=0.5, in1=st["t"][:, j, :], op0=ALU.mult, op1=ALU.mult),
                        reads=[sl["s"][j], sl["t"][j]], writes=[sl["t"][j]])
                for j in range(2):
                    c8 = 2 * n + j
                    fw.op("dve", lambda e, st=st, j=j, c8=c8: e.tensor_tensor_scan(
                        out=st["h"][:, j, :], data0=st["a"][:, j, :], data1=st["t"][:, j, :], initial=c_hlast[:, c8:c8 + 1],
                        op0=ALU.mult, op1=ALU.add), reads=[sl["a"][j], sl["t"][j], s_hlast[c8]], writes=[sl["h"][j]])
                for j in range(2):
                    c8 = 2 * n + j
                    fw.op("act", lambda e, st=st, j=j, c8=c8: e.activation(out=c_hlast[:, c8:c8 + 1], in_=st["h"][:, j, TC - 1:TC], func=AF.Copy),
                          reads=[sl["h"][j]], writes=[s_hlast[c8]])
                for j in range(2):
                    fw.op("dve", lambda e, st=st, j=j: e.tensor_tensor(out=st["ip"][:, j, :], in0=st["gr"][:, j, :], in1=st["gr"][:, j, :], op=ALU.mult),
                          reads=[sl["gr"][j]], writes=[sl["ip"][j]])
                for j in range(2):
                    fw.op("dve", lambda e, st=st, j=j: e.tensor_scalar(out=st["ip"][:, j, :], in0=st["ip"][:, j, :], scalar1=0.044715, scalar2=1.0,
                                                                      op0=ALU.mult, op1=ALU.add), reads=[sl["ip"][j]], writes=[sl["ip"][j]])
                for j in range(2):
                    fw.op("dve", lambda e, st=st, j=j: e.tensor_tensor(out=st["ip"][:, j, :], in0=st["ip"][:, j, :], in1=st["gr"][:, j, :], op=ALU.mult),
                          reads=[sl["ip"][j], sl["gr"][j]], writes=[sl["ip"][j]])
                for j in range(2):
                    fw.op("act", lambda e, st=st, j=j: e.activation(out=st["rp"][:, j, :], in_=st["ip"][:, j, :], func=AF.Tanh, scale=GELU_C),
                          reads=[sl["ip"][j]], writes=[sl["rp"][j]])
                for j in range(2):
                    fw.op("dve", lambda e, st=st, j=j: e.scalar_tensor_tensor(
                        out=st["ip"][:, j, :], in0=st["rp"][:, j, :], scalar=1.0, in1=st["gr"][:, j, :], op0=ALU.add, op1=ALU.mult),
                        reads=[sl["rp"][j], sl["gr"][j]], writes=[sl["ip"][j]])
                for j in range(2):
                    c8 = 2 * n + j
                    fw.op("dve", lambda e, st=st, j=j, c8=c8, yb=yb: e.scalar_tensor_tensor(
                        out=yb[:, c8, :], in0=st["ip"][:, j, :], scalar=0.5, in1=st["h"][:, j, :], op0=ALU.mult, op1=ALU.mult),
                        reads=[sl["ip"][j], sl["h"][j]], writes=[s_yb])
            s_ci = Slot()
            fw.op("sp", lambda e, yb=yb, G=G: e.dma_start(
                out=cin_yr.ap()[G].rearrange("(c p) t -> p c t", p=88), in_=yb),
                reads=[s_yb], writes=[s_ci], kind="dma")
            fw.op("pool", lambda e, G=G: e.collective_compute("AllGather", ALU.bypass, replica_groups=RG,
                                                              ins=[cin_yr.ap()[G].opt()], outs=[cout_yr.ap()[G].opt()]),
                  reads=[s_ci], writes=[s_cyr[G]], kind="cc")
            if G == 0:
                fw.stop_if("passA1")
        fw.stop_if("passA")


        fw.stop_if("ag1")
        B = Arena(arena_ap)
        assert off_xs == 45056 and off_hT0 == 66560, (off_xs, off_hT0)
        KT = arena_at(B, 0, 128, (4, S), BF16)
        s_KT = [[Slot() for _ in range(NCH)] for _ in range(4)]
        e_r = [(arena_at(B, 32768 + 4096 * i, 128, (2, TC), F32), Slot()) for i in range(3)]
        nrm = [dict(raw=arena_at(B, 61440, 128, (TC,), F32), sq=arena_at(B, 63488, 128, (TC,), BF16),
                    ln=arena_at(B, 64512, 128, (TC,), F32), s_raw=Slot(), s_sq=Slot(), s_ln=Slot())]
        B.top = off_hT0 + 16384
        Wv = B.alloc(128, (NKC, 512), BF16)
        Vt = B.alloc(128, (S // 128, 512), BF16)
        s_V = [Slot() for _ in range(S // 128)]
        hT_ring.items = [(B.alloc(128, (NKC, TC), BF16), [Slot() for _ in range(TC // SUB)]) for _ in range(2)]
        QT_ring = Ring([(B.alloc(128, (4, TC), BF16), [Slot() for _ in range(4)]) for _ in range(2)])
        nrm.append(dict(raw=B.alloc(128, (TC,), F32), sq=B.alloc(128, (TC,), BF16), ln=B.alloc(128, (TC,), F32),
                        s_raw=Slot(), s_sq=Slot(), s_ln=Slot()))
        nrmR = Ring(nrm)
        sp_r = [(B.alloc(128, (2, TC), BF16), Slot()) for _ in range(3)]
        x_r = [(B.alloc(128, (2, TC), F32), Slot()) for _ in range(2)]
        w_r = [(B.alloc(128, (2, TC), BF16), Slot()) for _ in range(3)]
        psz2 = psb2[0][:, :].rearrange("p (a b) -> p a b", a=2)
        psc2 = psb2[1][:, :].rearrange("p (a b) -> p a b", a=2)
        obf = [(B.alloc(128, (TC,), BF16), Slot()) for _ in range(2)]
        obfR = Ring(obf)
        psz = [(psb[0][:, :], ps_slots[0]), (psb[1][:, :], ps_slots[1])]
        psc = [(psb[2][:, :], ps_slots[2]), (psb[3][:, :], ps_slots[3])]
        pso = [(psb[4][:, :], ps_slots[4]), (psb[5][:, :], ps_slots[5])]
        psM = Ring([(psb[6][:, :], ps_slots[6]), (psb[7][:, :], ps_slots[7])])

        fw.barrier(ENGS, split_cc=True)
        fw.op("pool", lambda e: e.dma_start(out=Wv, in_=w1b_v[:, :, 1024:1536]), writes=[s_Wv], kind="dma", cost=12.0)

        P = Arena(arena_ap)
        NU = 10
        wunits = [(arena_at(P, 45056 + 4096 * i, 128, (NKC, 128), BF16), Slot()) for i in range(NU)]
        wu_i = [0]

        def wload(src_ap, nk=NKC, extra_reads=(), extra_writes=()):
            u, s_u = wunits[wu_i[0] % NU]
            wu_i[0] += 1
            fw.op("pool", lambda e: e.dma_start(out=u[:, 0:nk, :].rearrange("p k n -> p (k n)"), in_=src_ap),
                  reads=list(extra_reads), writes=[s_u] + list(extra_writes), kind="dma")
            return u, s_u
        pref = {}

        def qk_norm(Wt, s_Wt, col0, gain_col, dst_ap, dst_slot, hT, s_hT):
            nb = nrmR.next()
            ps, s_ps = psM.next()
            proj_fm(Wt, s_Wt, col0, 128, ps, s_ps, hT, s_hT)
            fw.op("dve", lambda e: e.tensor_copy(out=nb["raw"], in_=ps), reads=[s_ps], writes=[nb["s_raw"]])
            fw.op("dve", lambda e: e.tensor_tensor(out=nb["sq"], in0=nb["raw"], in1=nb["raw"], op=ALU.mult),
                  reads=[nb["s_raw"]], writes=[nb["s_sq"]])
            ps2, s_ps2 = psM.next()
            fw.op("pe", lambda e: e.matmul(ps2, lhsT=c_onesh, rhs=nb["sq"], start=True, stop=True),
                  reads=[nb["s_sq"], s_const], writes=[s_ps2], cost=0.3)
            fw.op("act", lambda e: e.activation(out=nb["ln"], in_=ps2, func=AF.Ln, bias=EPS_AP, scale=1.0),
                  reads=[s_ps2], writes=[nb["s_ln"]])
            fw.op("act", lambda e: e.activation(out=nb["ln"], in_=nb["ln"], func=AF.Exp, scale=-0.5),
                  reads=[nb["s_ln"]], writes=[nb["s_ln"]])
            fw.op("dve", lambda e: e.scalar_tensor_tensor(out=dst_ap, in0=nb["raw"], scalar=g128(gain_col), in1=nb["ln"],
                                                          op0=ALU.mult, op1=ALU.mult),
                  reads=[nb["s_raw"], nb["s_ln"], s_const], writes=[dst_slot])

        import os as _os
        for G in range(int(_os.environ.get("KNB", NCH))):
            fw.stop_if("b_w")
            hT, s_hT = hT_ring.next()
            fw.op("sp", lambda e, hT=hT, G=G: e.dma_start(out=hT.rearrange("p k t -> p (k t)"), in_=hbuf.ap()[G]),
                  reads=[s_hbuf[G]], writes=s_hT, kind="dma")
            QT, s_QT = QT_ring.next()
            fw.stop_if("b_h")
            for hh in range(4):
                qk_norm(Wq, s_Wq, hh * 128, 48, QT[:, hh, :], s_QT[hh], hT, s_hT)
                qk_norm(Wk, s_Wk, hh * 128, 49, KT[:, hh, G * TC:(G + 1) * TC], s_KT[hh][G], hT, s_hT)
            fw.stop_if("b_qk")
            for tb in range(4):
                ps, s_ps = psM.next()

                def f(e, ps=ps, tb=tb, hT=hT):
                    for kc in range(NKC):
                        last = e.matmul(ps, lhsT=hT[:, kc, tb * 128:(tb + 1) * 128], rhs=Wv[:, kc, :],
                                        start=(kc == 0), stop=(kc == NKC - 1))
                    return last
                fw.op("pe", f, reads=[s_Wv] + s_hT, writes=[s_ps])
                fw.op("dve", lambda e, ps=ps, tb=tb, G=G: e.tensor_copy(out=Vt[:, 4 * G + tb, :], in_=ps),
                      reads=[s_ps], writes=[s_V[4 * G + tb]])
            fw.stop_if("b_v")
            if G == NCH - 1 and not fw.stopped:
                dead = [s_Wk, s_Wq, s_Wv, nrm[0]["s_raw"], nrm[0]["s_sq"], nrm[0]["s_ln"]]
                for cc_ in range(2):
                    kw = dict(extra_reads=[s_co[NCH - 2]], extra_writes=dead)
                    pref[cc_] = (wload(wt_g[cc_], **kw), wload(wt_g[16 + cc_], **kw), wload(wt_rnn[cc_, 0], 11, **kw),
                                 wload(wt_rnn[cc_, 1], 11, **kw), wload(wt_att[cc_], **kw))
            NT = 4 * G + 4
            s_cio = [Slot() for _ in range(4)]
            for hp in range(2):
                heads = [2 * hp, 2 * hp + 1]
                etile = {}
                sptile = {}
                wtile = {}

                def lo_of(s):
                    d = (NT - 1 - s) - 4 * G
                    return 128 * d if d > 0 else 0

                def stageZ(s):
                    kb = NT - 1 - s
                    diag = kb - 4 * G
                    lo = lo_of(s)
                    for li, hh in enumerate(heads):
                        ps, s_ps = psz[li]

                        def f(e, ps=ps, hh=hh, kb=kb, diag=diag, QT=QT, lo=lo):
                            last = e.matmul(ps[:, lo:TC], lhsT=KT[:, hh, kb * 128:(kb + 1) * 128], rhs=QT[:, hh, lo:TC],
                                            start=True, stop=(diag < 0), skip_group_check=True)
                            if diag >= 0:
                                o0 = 384 - 128 * diag
                                last = e.matmul(ps[:, lo:TC], lhsT=c_ident, rhs=c_mask[:, o0 + lo:o0 + TC], start=False, stop=True,
                                                skip_group_check=True)
                            return last
                        fw.op("pe", f, reads=[s_KT[hh][kb // 4], s_QT[hh], s_const], writes=[s_ps], cost=0.6 if diag >= 0 else 0.3)
                    eb, s_eb = e_r[s % 3]
                    etile[s] = (eb, s_eb)
                    fw.op("act", lambda e, eb=eb, lo=lo: e.activation(out=eb[:, :, lo:TC], in_=psz2[:, :, lo:TC], func=AF.Exp, scale=float(QSCALE)),
                          reads=[psz[0][1], psz[1][1]], writes=[s_eb], cost=1.1)
                    sb_, s_sb = sp_r[s % 3]
                    sptile[s] = (sb_, s_sb)
                    fw.op("act", lambda e, eb=eb, sb_=sb_, lo=lo: e.activation(out=sb_[:, :, lo:TC], in_=eb[:, :, lo:TC], func=AF.Ln, bias=ONE_AP, scale=1.0),
                          reads=[s_eb, s_const], writes=[s_sb], cost=1.1)

                def stageC(s):
                    sb_, s_sb = sptile[s]
                    lo = lo_of(s)
                    lp = lo_of(s - 1) if s > 0 else 0
                    for li, hh in enumerate(heads):
                        ps, s_ps = psc[li]
                        rd = [s_sb, s_const]
                        if s > 0:
                            sprev, s_sprev = sptile[s - 1]
                            rd += [s_sprev, s_ps]
                        else:
                            sprev = None

                        def f(e, ps=ps, sb_=sb_, sprev=sprev, s=s, li=li, lo=lo, lp=lp):
                            if s > 0:
                                e.matmul(ps[:, lp:TC], lhsT=c_SU, rhs=sprev[:, li, lp:TC], start=False, stop=False, skip_group_check=True)
                            return e.matmul(ps[:, lo:TC], lhsT=c_L, rhs=sb_[:, li, lo:TC], start=(s == 0), stop=True, skip_group_check=True)
                        fw.op("pe", f, reads=rd, writes=[s_ps], cost=0.6 if s > 0 else 0.3)
                    xb, s_xb = x_r[s % 2]
                    fw.op("act", lambda e, xb=xb, lo=lo: e.activation(out=xb[:, :, lo:TC], in_=psc2[:, :, lo:TC], func=AF.Exp, scale=-1.0),
                          reads=[psc[0][1], psc[1][1]], writes=[s_xb], cost=1.1)
                    eb, s_eb = etile[s]
                    wb, s_wb = w_r[s % 3]
                    wtile[s] = (wb, s_wb)
                    fw.op("dve", lambda e, xb=xb, eb=eb, wb=wb, lo=lo: e.tensor_tensor(out=wb[:, :, lo:TC], in0=eb[:, :, lo:TC], in1=xb[:, :, lo:TC], op=ALU.mult),
                          reads=[s_xb, s_eb], writes=[s_wb], cost=1.2)

                def stageO(s):
                    kb = NT - 1 - s
                    lo = lo_of(s)
                    for li, hh in enumerate(heads):
                        ps, s_ps = pso[li]
                        wb, s_wb = wtile[s]
                        fw.op("pe", lambda e, ps=ps, wb=wb, hh=hh, kb=kb, s=s, li=li, lo=lo: e.matmul(
                            ps[:, lo:TC], lhsT=Vt[:, kb, hh * 128:(hh + 1) * 128], rhs=wb[:, li, lo:TC], start=(s == 0), stop=(s == NT - 1),
                            skip_group_check=True),
                            reads=[s_wb, s_V[kb], s_ps] if s > 0 else [s_wb, s_V[kb]], writes=[s_ps], cost=0.3)

                for step in range(NT + 2):
                    if step < NT:
                        stageZ(step)
                    if 0 <= step - 1 < NT:
                        stageC(step - 1)
                    if 0 <= step - 2 < NT:
                        stageO(step - 2)
                for li, hh in enumerate(heads):
                    ps, s_ps = pso[li]
                    ob, s_ob = obfR.next()
                    fw.op("dve", lambda e, ps=ps, ob=ob: e.tensor_copy(out=ob, in_=ps), reads=[s_ps], writes=[s_ob])
                    fw.op("sp", lambda e, ob=ob, hh=hh, G=G: e.dma_start(
                        out=cin_o.ap()[G, hh * 128:(hh + 1) * 128, :], in_=ob), reads=[s_ob], writes=[s_cio[hh]], kind="dma")
            fw.op("pool", lambda e, G=G: e.collective_compute("AllGather", ALU.bypass, replica_groups=RG,
                                                              ins=[cin_o.ap()[G].opt()], outs=[cout_o.ap()[G].opt()]),
                  reads=s_cio, writes=[s_co[G]], kind="cc")

        fw.stop_if("passB")

        R2 = arena_at(P, 0, 128, (22, 1024), BF16)
        R1 = arena_at(P, 86016, 128, (NKC, 1024), F32)
        R1b = R1.rearrange("p k t -> p (k t)").bitcast(BF16)
        h_a = R1b[:, 0:NKC * 1024].rearrange("p (k t) -> p k t", k=NKC)
        oT_a = R1b[:, NKC * 1024:2 * NKC * 1024].rearrange("p (k t) -> p k t", k=NKC)
        R3 = arena_at(P, 151552, 128, (NKC, 1024), BF16)
        tmpf = [(arena_at(P, 184320 + 2048 * i, 128, (TC,), F32), Slot()) for i in range(4)]
        tmpR = Ring(tmpf)
        pTb = arena_at(P, 192512, 128, (2, 1024), BF16)
        s_xo = [[Slot() for _ in range(2)] for _ in range(NKC)]
        s_R1h = [Slot() for _ in range(8)]
        s_oT = Slot()
        s_R2 = Slot()
        s_R3 = [[Slot() for _ in range(2)] for _ in range(NKC)]
        s_h2 = [Slot() for _ in range(8)]
        s_pT = Slot()
        psP = Ring(list(zip([p[:, :] for p in psb], ps_slots)))
        sq_p = arena_at(P, 196608, 128, (NKC, SUB), BF16)
        lnm_p = arena_at(P, 200704, 128, (2 * SUB,), F32)
        scratchP = (sq_p, Slot(), lnm_p, Slot())

        fw.barrier(ENGS, split_cc=True)

        def acc_group(ps, s_ps, parts, tb, extra_reads):
            def f(e):
                tot = sum(p[2] for p in parts)
                i = 0
                for (u, s_u, nk, act, k0, rows) in parts:
                    for kc in range(nk):
                        last = e.matmul(ps, lhsT=u[0:rows, kc, :], rhs=act[0:rows, k0 + kc, tb * TC:(tb + 1) * TC],
                                        start=(i == 0), stop=(i == tot - 1))
                        i += 1
                return last
            fw.op("pe", f, reads=[p[1] for p in parts] + list(extra_reads), writes=[s_ps], cost=0.29 * sum(p[2] for p in parts))

        for tb in range(2):
            def f_hh(e, tb=tb):
                rank = e.partition_id() % 4
                return e.dma_start(out=h_a[:, :, tb * TC:(tb + 1) * TC],
                                   in_=hbuf.ap()[bass.ds(2 * rank + tb, 1)].rearrange("o p (k t) -> p (o k) t", k=NKC))
            fw.op("sp", f_hh, reads=s_hbuf, writes=s_R1h[4 * tb:4 * tb + 4], kind="dma")

        def decl_yr():
            for tb in range(2):
                def f_yr(e, tb=tb):
                    rank = e.partition_id() % 4
                    return e.dma_start(out=R2[:, :, tb * TC:(tb + 1) * TC],
                                       in_=cout_yr.ap()[bass.ds(2 * rank + tb, 1)].rearrange("o (k p) t -> p (o k) t", p=128))
                fw.op("sp", f_yr, reads=s_cyr, writes=[s_R2], kind="dma")

        def decl_o():
            for tb in range(2):
                def f_o(e, tb=tb):
                    rank = e.partition_id() % 4
                    return e.dma_start(out=oT_a[:, :, tb * TC:(tb + 1) * TC],
                                       in_=cout_o.ap()[bass.ds(2 * rank + tb, 1)].rearrange("o (k p) t -> p (o k) t", p=128))
                fw.op("sp", f_o, reads=s_co, writes=[s_oT], kind="dma")

        def decl_p():
            fw.op("pool", lambda e: e.dma_start(out=pTb.rearrange("p k t -> p (k t)"), in_=pTt[:, :]), writes=[s_pT], kind="dma")

        fw.stop_if("s0")
        for cc in range(NKC):
            if cc in pref:
                ugr, uga, ur0, ur1, uat = pref[cc]
                if cc == 0:
                    decl_yr()
                    decl_o()
                    decl_p()
            else:
                ugr = wload(wt_g[cc])
                uga = wload(wt_g[16 + cc])
                ur0 = wload(wt_rnn[cc, 0], 11)
                ur1 = wload(wt_rnn[cc, 1], 11)
                uat = wload(wt_att[cc])
            for tb in range(2):
                hs = s_R1h[4 * tb:4 * tb + 4]
                p_gr = psP.next()
                acc_group(p_gr[0], p_gr[1], [(ugr[0], ugr[1], NKC, h_a, 0, 128)], tb, hs)
                p_ga = psP.next()
                acc_group(p_ga[0], p_ga[1], [(uga[0], uga[1], NKC, h_a, 0, 128)], tb, hs)
                p_yr = psP.next()
                acc_group(p_yr[0], p_yr[1], [(ur0[0], ur0[1], 11, R2, 0, 128), (ur1[0], ur1[1], 11, R2, 11, 128)], tb, [s_R2])
                p_ya = psP.next()
                acc_group(p_ya[0], p_ya[1], [(uat[0], uat[1], NKC, oT_a, 0, 128)], tb, [s_oT])
                t_sr = tmpR.next()
                t_sa = tmpR.next()
                fw.op("act", lambda e, p=p_gr, t=t_sr: e.activation(out=t[0], in_=p[0], func=AF.Sigmoid), reads=[p_gr[1]], writes=[t_sr[1]])
                fw.op("act", lambda e, p=p_ga, t=t_sa: e.activation(out=t[0], in_=p[0], func=AF.Sigmoid), reads=[p_ga[1]], writes=[t_sa[1]])
                fw.op("dve", lambda e, p=p_yr, t=t_sr: e.tensor_tensor(out=t[0], in0=t[0], in1=p[0], op=ALU.mult),
                      reads=[p_yr[1], t_sr[1]], writes=[t_sr[1]])
                fw.op("dve", lambda e, p=p_ya, t=t_sa: e.tensor_tensor(out=t[0], in0=t[0], in1=p[0], op=ALU.mult),
                      reads=[p_ya[1], t_sa[1]], writes=[t_sa[1]])
                fw.op("dve", lambda e, a=t_sr, b=t_sa, cc=cc, tb=tb: e.tensor_tensor(
                    out=R3[:, cc, tb * TC:(tb + 1) * TC], in0=a[0], in1=b[0], op=ALU.add),
                    reads=[t_sr[1], t_sa[1]], writes=[s_R3[cc][tb]])

        fw.barrier(["pe", "act", "dve", "sp"])
        fw.stop_if("sa")
        mix_slots = [s_R3[c][t] for c in range(NKC) for t in range(2)]
        for cc in range(NKC):
            uo = wload(wt_o[cc])
            for tb in range(2):
                fw.op("sp", lambda e, cc=cc, tb=tb: e.dma_start(out=R1[:, cc, tb * TC:(tb + 1) * TC], in_=xo2[cc, tb]),
                      writes=[s_xo[cc][tb]], kind="dma")
                p = psP.next()
                acc_group(p[0], p[1], [(uo[0], uo[1], NKC, R3, 0, 128)], tb, [s_R3[c][tb] for c in range(NKC)])
                fw.op("dve", lambda e, p=p, cc=cc, tb=tb: e.tensor_tensor(
                    out=R1[:, cc, tb * TC:(tb + 1) * TC], in0=R1[:, cc, tb * TC:(tb + 1) * TC], in1=p[0], op=ALU.add),
                    reads=[p[1], s_xo[cc][tb]], writes=[s_xo[cc][tb]])


        def rms_resident(gcol0, dst, dst_slots):
            for sc in range(8):
                tb = sc // 4
                rmsnorm_sub(None, R1[:, :, sc * SUB:(sc + 1) * SUB], [s_xo[c][tb] for c in range(NKC)], gcol0,
                            dst[:, :, sc * SUB:(sc + 1) * SUB], dst_slots[sc], SUB, scratchP, psP)

        fw.stop_if("sb")
        rms_resident(16, R3, s_h2)
        for half in range(2):
            for m in range(22):
                mm = half * 22 + m
                ug = wload(wt_gu[mm])
                uu = wload(wt_gu[44 + mm])
                for tb in range(2):
                    hs = s_h2[4 * tb:4 * tb + 4]
                    p_g = psP.next()
                    acc_group(p_g[0], p_g[1], [(ug[0], ug[1], NKC, R3, 0, 128)], tb, hs)
                    p_u = psP.next()
                    acc_group(p_u[0], p_u[1], [(uu[0], uu[1], NKC, R3, 0, 128)], tb, hs)
                    t_s = tmpR.next()
                    fw.op("act", lambda e, p=p_g, t=t_s: e.activation(out=t[0], in_=p[0], func=AF.Silu), reads=[p_g[1]], writes=[t_s[1]])
                    fw.op("dve", lambda e, p=p_u, t=t_s, m=m, tb=tb: e.tensor_tensor(
                        out=R2[:, m, tb * TC:(tb + 1) * TC], in0=t[0], in1=p[0], op=ALU.mult),
                        reads=[p_u[1], t_s[1]], writes=[s_R2])
            for cc in range(NKC):
                ud0 = wload(wt_dn[half, cc, 0], 11)
                ud1 = wload(wt_dn[half, cc, 1], 11)
                for tb in range(2):
                    p = psP.next()
                    acc_group(p[0], p[1], [(ud0[0], ud0[1], 11, R2, 0, 128), (ud1[0], ud1[1], 11, R2, 11, 128)], tb, [s_R2])
                    fw.op("dve", lambda e, p=p, cc=cc, tb=tb: e.tensor_tensor(
                        out=R1[:, cc, tb * TC:(tb + 1) * TC], in0=R1[:, cc, tb * TC:(tb + 1) * TC], in1=p[0], op=ALU.add),
                        reads=[p[1], s_xo[cc][tb]], writes=[s_xo[cc][tb]])

        fw.stop_if("sc")
        h3 = R2[:, 0:NKC, :]
        s_h3 = [Slot() for _ in range(8)]
        rms_resident(32, h3, s_h3)
        out_ops = []
        for cc in range(NKC):
            upg = wload(wt_pg[cc])
            upp = wload(wt_pp[cc], 2)
            for tb in range(2):
                hs = s_h3[4 * tb:4 * tb + 4]
                p_g = psP.next()
                acc_group(p_g[0], p_g[1], [(upg[0], upg[1], NKC, h3, 0, 128)], tb, hs)
                p_p = psP.next()

                acc_group(p_p[0], p_p[1], [(upp[0], upp[1], 2, pTb, 0, 128)], tb, [s_pT])
                t_s = tmpR.next()
                fw.op("act", lambda e, p=p_g, t=t_s: e.activation(out=t[0], in_=p[0], func=AF.Sigmoid), reads=[p_g[1]], writes=[t_s[1]])
                fw.op("dve", lambda e, p=p_p, t=t_s: e.tensor_tensor(out=t[0], in0=t[0], in1=p[0], op=ALU.mult),
                      reads=[p_p[1], t_s[1]], writes=[t_s[1]])
                fw.op("dve", lambda e, t=t_s, cc=cc, tb=tb: e.tensor_tensor(
                    out=R1[:, cc, tb * TC:(tb + 1) * TC], in0=R1[:, cc, tb * TC:(tb + 1) * TC], in1=t[0], op=ALU.add),
                    reads=[t_s[1], s_xo[cc][tb]], writes=[s_xo[cc][tb]])
            out_ops.append(fw.op("sp", lambda e, cc=cc: e.dma_start(out=outT[cc], in_=R1[:, cc, :]),
                                 reads=[s_xo[cc][0], s_xo[cc][1]], kind="dma"))
        s_fin = Slot("fin")
        fin = fw.op("sp", lambda e: e.nop(), kind="c")
        fin.deps.update(o_ for o_ in out_ops if o_.kind != "skip")

        import os as _os2
        if _os2.environ.get("KNOSCHED", "") != "1":
            fw.schedule()
        fw.finalize()
        sems = {}
        for e in ENGS:
            sems[("eng", e)] = es.enter_context(nc.semaphore(f"s_{e}"))
            for i in range(FW.NDS):
                sems[("dma", e, i)] = es.enter_context(nc.semaphore(f"d_{e}{i}"))
        for i in range(fw.cc_count):
            sems[("cc", "pool", i)] = es.enter_context(nc.semaphore(f"cc{i}"))
        block = es.enter_context(nc.Block())

        @block.tensor
        def _(t):
            fw.emit("pe", t, sems)

        @block.scalar
        def _(a):
            fw.emit("act", a, sems)

        @block.vector
        def _(v):
            fw.emit("dve", v, sems)

        @block.gpsimd
        def _(g):
            fw.emit("pool", g, sems)

        @block.sync
        def _(sy):
            fw.emit("sp", sy, sems)
    return nc


def _tile_w(w, nk):
    K, N = w.shape
    assert K == nk * 128
    t = w.reshape(nk, 128, N // 128, 128).transpose(2, 1, 0, 3)
    return np.ascontiguousarray(t).reshape(N // 128, 128, nk * 128)


_NC_CACHE = {}


def kernel(x, p, g_mix, w_in, conv_w, conv_b, w_rg_a, b_rg_a, w_rg_x, b_rg_x, lru_lambda, q_gain, k_gain,
           w_rnn_out, w_attn_out, w_o, g_ffn, w_ffn_gu, w_ffn_down, g_ple, w_ple_gate, w_ple_proj):
    f = np.float32
    x = np.asarray(x, f)
    p = np.asarray(p, f)
    w_in0 = np.asarray(w_in, f)[0]
    if "nc" not in _NC_CACHE:
        _NC_CACHE["nc"] = build_program()
    nc = _NC_CACHE["nc"]

    vec128 = np.zeros((128, 50), f)
    vec128[:, 0:16] = np.asarray(g_mix, f)[0].reshape(16, 128).T
    vec128[:, 16:32] = np.asarray(g_ffn, f)[0].reshape(16, 128).T
    vec128[:, 32:48] = np.asarray(g_ple, f)[0].reshape(16, 128).T
    vec128[:, 48] = np.asarray(q_gain, f)[0]
    vec128[:, 49] = np.asarray(k_gain, f)[0]
    cmat = np.zeros((128, 384 + 896), f)
    cmat[:, 0:128] = np.eye(128, dtype=f)
    jj = np.arange(128)[:, None]
    ss = np.arange(128)[None, :]
    cmat[:, 128:256] = (jj >= ss).astype(f)
    cmat[:, 256:384] = (jj < ss).astype(f)
    xx = np.arange(896)[None, :]
    cmat[:, 384:] = np.where(xx <= jj + 384, NEG, 0.0).astype(f)
    wt_g = _tile_w(w_in0[:, 11776:15872], 16)
    wr = np.asarray(w_rnn_out, f)[0]
    wt_rnn = _tile_w(wr, 22).reshape(16, 128, 2, 11 * 128).transpose(0, 2, 1, 3).copy()
    wt_att = _tile_w(np.asarray(w_attn_out, f)[0], 16)
    wt_o = _tile_w(np.asarray(w_o, f)[0], 16)
    wt_gu = _tile_w(np.asarray(w_ffn_gu, f)[0], 16)
    wd = np.asarray(w_ffn_down, f)[0]
    wt_dn = np.stack([_tile_w(wd[h * 2816:(h + 1) * 2816], 22).reshape(16, 128, 2, 11 * 128).transpose(0, 2, 1, 3)
                      for h in range(2)]).copy()
    wt_pg = _tile_w(np.asarray(w_ple_gate, f)[0], 16)
    wpp = np.asarray(w_ple_proj, f)[0]
    wt_pp = _tile_w(wpp, 2)
    cw = np.asarray(conv_w, f)[0]
    cbv = np.asarray(conv_b, f)[0]
    ba = np.asarray(b_rg_a, f)[0]
    bx = np.asarray(b_rg_x, f)[0]
    lam = np.asarray(lru_lambda, f)[0]
    wa_all = np.asarray(w_rg_a, f)[0]
    wx_all = np.asarray(w_rg_x, f)[0]

    in_maps = []
    for c in range(8):
        b, g = c // 4, c % 4
        xb = x[b]
        xTt = np.ascontiguousarray(xb.reshape(32, 128, 16, 128).transpose(0, 3, 2, 1)).reshape(32, 128, 16 * 128)
        xo = xb[1024 * g:1024 * (g + 1)]
        xo2 = np.ascontiguousarray(xo.reshape(2, 512, 16, 128).transpose(2, 0, 3, 1))
        pb = p[0, b, 1024 * g:1024 * (g + 1)]
        pTt = np.ascontiguousarray(pb.reshape(1024, 2, 128).transpose(2, 1, 0)).reshape(128, 2 * 1024)
        w1a = np.concatenate([w_in0[:, 704 * g:704 * (g + 1)], w_in0[:, 2816 + 704 * g:2816 + 704 * (g + 1)]], axis=1)
        w1a = np.ascontiguousarray(w1a.reshape(16, 128, 1408).transpose(1, 0, 2)).reshape(128, 16 * 1408)
        w1b = np.concatenate([w_in0[:, 5632 + 512 * g:5632 + 512 * (g + 1)], w_in0[:, 7680 + 512 * g:7680 + 512 * (g + 1)],
                              w_in0[:, 9728 + 512 * g:9728 + 512 * (g + 1)]], axis=1)
        w1b = np.ascontiguousarray(w1b.reshape(16, 128, 1536).transpose(1, 0, 2)).reshape(128, 16 * 1536)
        ch = slice(704 * g, 704 * (g + 1))
        v88 = np.zeros((88, 8, 8), f)
        for k in range(4):
            v88[:, :, k] = cw[k, ch].reshape(8, 88).T
        v88[:, :, 4] = cbv[ch].reshape(8, 88).T
        v88[:, :, 5] = ba[ch].reshape(8, 88).T
        v88[:, :, 6] = bx[ch].reshape(8, 88).T
        v88[:, :, 7] = lam[ch].reshape(8, 88).T
        wax = np.stack([wa_all[4 * g:4 * g + 4], wx_all[4 * g:4 * g + 4]])
        wax = np.ascontiguousarray(wax.reshape(2, 4, 2, 88, 176).transpose(3, 0, 1, 2, 4)).reshape(88, 2 * 4 * 2 * 176)
        in_maps.append(dict(xTt=xTt, xo2=xo2, pTt=pTt, w1a=w1a, w1b=w1b,
                            vec88=v88.reshape(88, 64), vec128=vec128, wax=wax, cmat=cmat, wt_g=wt_g, wt_rnn=wt_rnn,
                            wt_att=wt_att, wt_o=wt_o, wt_gu=wt_gu, wt_dn=wt_dn, wt_pg=wt_pg, wt_pp=wt_pp))
    res = run_bass_kernel_spmd(nc, in_maps, core_ids=list(range(8)))
    out = np.empty((2, S, D), f)
    for c in range(8):
        b, g = c // 4, c % 4
        oT = np.asarray(res.results[c]["outT"], f).reshape(2048, 1024)
        out[b, 1024 * g:1024 * (g + 1), :] = oT.T
    return out
```

```python
import numpy as np
from contextlib import ExitStack
import concourse.bass as bass
import concourse.mybir as mybir
from concourse.bass_utils import run_bass_kernel_spmd

F32 = mybir.dt.float32
BF16 = mybir.dt.bfloat16
U8 = mybir.dt.uint8
AF = mybir.ActivationFunctionType
ALU = mybir.AluOpType

S = 4096
D = 2048
NKC = 16
TC = 512
NCH = S // TC
SUB = 128
EPS = 1e-6
QSCALE = 1.0 / np.sqrt(128.0)
NEG = -30000.0
ARENA = 198 * 1024
GELU_C = 0.7978845608028654

ENGS = ["pe", "act", "dve", "pool", "sp"]


class Slot:
    __slots__ = ("name", "w", "r", "excl")

    def __init__(self, name="", excl=False):
        self.name = name
        self.w = None
        self.r = []
        self.excl = excl


class Op:
    __slots__ = ("eng", "fn", "deps", "needs_inc", "seq", "kind", "sem", "semv", "pos", "raw", "cost", "idx",
                 "nd", "rt", "fin", "succ")


class FW:
    NDS = 8

    def __init__(self):
        self.ops = {e: [] for e in ENGS}
        self.cur_barrier = {e: None for e in ENGS}
        self.since_barrier = []
        self.cc_count = 0
        self.stopped = False
        self.all = []

    def stop_if(self, tag):
        import os
        if os.environ.get("KSTOP", "") == tag:
            self.stopped = True

    DEFCOST = {"pe": 4.4, "act": 0.6, "dve": 0.65, "pool": 5.0, "sp": 3.0}

    def op(self, eng, fn, reads=(), writes=(), kind="c", cost=None):
        o = Op()
        o.cost = cost if cost is not None else (30.0 if kind == "cc" else self.DEFCOST[eng])
        if self.stopped:
            o.eng, o.fn, o.kind, o.deps, o.raw, o.pos = eng, fn, "skip", set(), set(), -1
            return o
        o.eng, o.fn, o.kind = eng, fn, kind
        o.needs_inc = False
        o.seq = 0
        o.sem = None
        o.semv = 0
        deps = set()
        raw = set()
        for s in reads:
            if s.w is not None:
                deps.add(s.w)
                raw.add(s.w)
            if s.excl:
                deps.update(s.r)
        for s in writes:
            if s.w is not None:
                deps.add(s.w)
            deps.update(s.r)
        for s in reads:
            s.r.append(o)
        for s in writes:
            s.w = o
            s.r = []
        if self.cur_barrier[eng] is not None:
            deps.add(self.cur_barrier[eng])
        if kind == "cc":
            o.sem = ("cc", eng, self.cc_count)
            self.cc_count += 1
            o.semv = 1
        deps.discard(o)
        o.deps = deps
        o.raw = raw
        o.pos = len(self.ops[eng])
        self.ops[eng].append(o)
        o.idx = len(self.all)
        self.all.append(o)
        self.since_barrier.append(o)
        return o

    def schedule(self, reorder=("pe", "act", "dve", "sp"), W=200):
        import bisect
        for o in self.all:
            o.succ = []
        for o in self.all:
            o.nd = len(o.deps)
            o.rt = 0.0
            for d in o.deps:
                d.succ.append(o)
        blev = {}
        for o in reversed(self.all):
            m = 0.0
            for c in o.succ:
                v = blev[id(c)]
                if v > m:
                    m = v
            blev[id(o)] = m + o.cost
        import os as _os3
        use_bl = _os3.environ.get("KPRIO", "bl") == "bl"
        for o in self.all:
            o.seq = (-blev[id(o)] if use_bl else 0.0, o.idx)
        avail = {e: [] for e in ENGS}
        ptr = {e: 0 for e in ENGS}
        free_at = {e: 0.0 for e in ENGS}
        order = {e: [] for e in ENGS}
        for o in self.all:
            if o.nd == 0 and o.eng in reorder:
                bisect.insort(avail[o.eng], (o.seq, o))
        remaining = len(self.all)
        while remaining:
            best = None
            for e in ENGS:
                if e in reorder:
                    lst = avail[e]
                    if not lst:
                        continue
                    fa = free_at[e]
                    cb = None
                    for (ix, o) in lst[:W]:
                        st = o.rt if o.rt > fa else fa
                        if cb is None or st < cb[0] - 1e-9:
                            cb = (st, ix, o)
                        if st <= fa:
                            break
                else:
                    if ptr[e] >= len(self.ops[e]):
                        continue
                    o = self.ops[e][ptr[e]]
                    if o.nd > 0:
                        continue
                    st = max(o.rt, free_at[e])
                    cb = (st, o.seq, o)
                if best is None or (cb[0], cb[1]) < (best[0], best[1]):
                    best = cb
            assert best is not None, "scheduler stuck"
            st, ix, o = best
            e = o.eng
            if e in reorder:
                i = bisect.bisect_left(avail[e], (ix, ))
                assert avail[e][i][1] is o
                avail[e].pop(i)
            else:
                ptr[e] += 1
            if o.kind in ("dma", "cc"):
                free_at[e] = st + (1.0 if e == "pool" else 0.15)
            else:
                free_at[e] = st + o.cost
            o.fin = st + o.cost
            order[e].append(o)
            remaining -= 1
            for c in o.succ:
                c.nd -= 1
                if o.fin > c.rt:
                    c.rt = o.fin
                if c.nd == 0 and c.eng in reorder:
                    bisect.insort(avail[c.eng], (c.seq, c))
        for e in ENGS:
            self.ops[e] = order[e]
            for i, o in enumerate(order[e]):
                o.pos = i
        self.makespan = max(o.fin for o in self.all)

    def barrier(self, engines, split_cc=False):
        if self.stopped:
            return
        prior = list(self.since_barrier)
        b = self.op("dve", lambda e: e.nop(), cost=0.1)
        b.deps.update(o for o in prior if not (split_cc and o.kind == "cc"))
        b.deps.discard(b)
        self.since_barrier = [b]
        for e in engines:
            self.cur_barrier[e] = b
        if split_cc and "pool" in engines:
            bp = self.op("pool", lambda e: e.nop(), cost=0.1)
            bp.deps.update(prior)
            bp.deps.add(b)
            bp.deps.discard(bp)
            self.cur_barrier["pool"] = bp

    def finalize(self):
        for e in ENGS:
            n = 0
            hist = []
            for o in self.ops[e]:
                if o.kind != "dma":
                    continue
                o.sem = ("dma", e, n % self.NDS)
                o.semv = 16 * (n // self.NDS + 1)
                if n >= self.NDS:
                    o.deps.add(hist[n - self.NDS])
                hist.append(o)
                n += 1
        for e in ENGS:
            for o in self.ops[e]:
                for d in o.deps:
                    if d.kind != "c":
                        continue
                    if d.eng != o.eng:
                        d.needs_inc = True
                    elif d in o.raw and o.pos - d.pos <= 1:
                        d.needs_inc = True
        for e in ENGS:
            n = 0
            for o in self.ops[e]:
                if o.kind == "c" and o.needs_inc:
                    n += 1
                    o.seq = n
                    o.sem = ("eng", e)
                    o.semv = n

    def emit(self, eng, handle, sems):
        waited = {}
        for o in self.ops[eng]:
            need = {}
            for d in o.deps:
                if d.kind == "c":
                    if d.eng == eng and not (d in o.raw and o.pos - d.pos <= 1):
                        continue
                if need.get(d.sem, 0) < d.semv:
                    need[d.sem] = d.semv
            for k, v in need.items():
                if waited.get(k, 0) >= v:
                    continue
                waited[k] = v
                handle.wait_ge(sems[k], v)
            inst = o.fn(handle)
            if o.kind == "dma":
                inst.then_inc(sems[o.sem], 16)
            elif o.kind == "cc":
                inst.then_inc(sems[o.sem], 1)
            elif o.needs_inc:
                inst.then_inc(sems[o.sem], 1)


class Arena:
    def __init__(self, ap):
        self.ap = ap
        self.top = 0

    def alloc(self, parts, free, dtype):
        esz = 4 if dtype == F32 else 2
        n = int(np.prod(free)) * esz
        off = (self.top + 63) // 64 * 64
        self.top = off + n
        assert self.top <= ARENA, f"arena overflow {self.top}"
        v = self.ap[0:parts, off:off + n].bitcast(dtype)
        if len(free) == 2:
            v = v.rearrange("p (a b) -> p a b", a=free[0])
        elif len(free) == 3:
            v = v.rearrange("p (a b c) -> p a b c", a=free[0], b=free[1])
        return v


def arena_at(arena, off, parts, free, dtype):
    esz = 4 if dtype == F32 else 2
    n = int(np.prod(free)) * esz
    assert off % 64 == 0 and off + n <= ARENA
    v = arena.ap[0:parts, off:off + n].bitcast(dtype)
    if len(free) == 2:
        v = v.rearrange("p (a b) -> p a b", a=free[0])
    arena.top = max(arena.top, off + n)
    return v


class Ring:
    def __init__(self, items):
        self.items = items
        self.i = 0

    def next(self):
        it = self.items[self.i % len(self.items)]
        self.i += 1
        return it


def build_program():
    nc = bass.Bass("TRN2", target_bir_lowering=False)

    def din(name, shape):
        return nc.dram_tensor(name, shape, F32, kind="ExternalInput").ap()

    xTt = din("xTt", [S // SUB, 128, NKC * SUB])
    xo2 = din("xo2", [16, 2, 128, 512])
    pTt = din("pTt", [128, 2 * 1024])
    w1a = din("w1a", [128, NKC * 1408])
    w1b = din("w1b", [128, NKC * 1536])
    vec88 = din("vec88", [88, 64])
    vec128 = din("vec128", [128, 50])
    wax = din("wax", [88, 2 * 4 * 2 * 176])
    cmat = din("cmat", [128, 384 + 896])
    wt_g = din("wt_g", [32, 128, 2048])
    wt_rnn = din("wt_rnn", [16, 2, 128, 11 * 128])
    wt_att = din("wt_att", [16, 128, 2048])
    wt_o = din("wt_o", [16, 128, 2048])
    wt_gu = din("wt_gu", [88, 128, 2048])
    wt_dn = din("wt_dn", [2, 16, 2, 128, 11 * 128])
    wt_pg = din("wt_pg", [16, 128, 2048])
    wt_pp = din("wt_pp", [16, 128, 2 * 128])
    outT = nc.dram_tensor("outT", [16, 128, 1024], F32, kind="ExternalOutput").ap()
    cin_yr = nc.dram_tensor("cin_yr", [NCH, 704, TC], BF16)
    cout_yr = nc.dram_tensor("cout_yr", [NCH, 2816, TC], BF16)
    cin_o = nc.dram_tensor("cin_o", [NCH, 512, TC], BF16)
    cout_o = nc.dram_tensor("cout_o", [NCH, 2048, TC], BF16)
    RG = [[0, 1, 2, 3], [4, 5, 6, 7]]
    hbuf = nc.dram_tensor("hbuf", [NCH, 128, NKC * TC], BF16)
    s_hbuf = [Slot() for _ in range(NCH)]
    s_cyr = [Slot() for _ in range(NCH)]
    s_co = [Slot() for _ in range(NCH)]

    fw = FW()
    es = ExitStack()
    with es:
        arena_t = es.enter_context(nc.sbuf_tensor("arena", [128, ARENA], U8))
        cst_t = es.enter_context(nc.sbuf_tensor("cst", [128, 2432], F32))
        psb2 = [es.enter_context(nc.psum_tensor(f"ps{i}", [128, 1024], F32)) for i in range(4)]
        psb = [psb2[i // 2][:, (i % 2) * 512:(i % 2 + 1) * 512] for i in range(8)]
        ps_slots = [Slot(f"ps{i}", excl=True) for i in range(8)]

        cst = cst_t
        c_vec128 = cst[:, 0:50]
        c_vec88 = cst[0:88, 64:128]
        c_der88 = cst[0:88, 128:160]
        c_num = cst[:, 160:168]
        cb = cst[:, 192:192 + 768].bitcast(BF16)
        c_ident = cb[:, 0:128]
        c_L = cb[:, 128:256]
        c_SU = cb[:, 256:384]
        c_mask = cb[:, 384:384 + 896]
        c_onesm = cb[:, 1280:1408]
        c_onesh = cb[:, 1408:1536]
        c_halo = cst[0:88, 968:968 + 24].rearrange("p (a b) -> p a b", a=8)
        c_hlast = cst[0:88, 1000:1008]
        c_wax = cst[0:88, 1016:1016 + 1408].bitcast(BF16).rearrange("p (w n i c) -> p w n i c", w=2, n=4, i=2)
        s_const = Slot("const")
        s_halo = [Slot() for _ in range(4)]
        s_hlast = [Slot() for _ in range(8)]
        EPS_AP = c_num[:, 0:1]
        ONE_AP = c_num[:, 1:2]

        def g128(col):
            return c_vec128[:, col:col + 1]

        fw.op("sp", lambda e: e.dma_start(out=c_vec128, in_=vec128[:, :]), writes=[s_const], kind="dma")
        fw.op("sp", lambda e: e.dma_start(out=c_vec88, in_=vec88[:, :]), writes=[s_const], kind="dma")
        fw.op("pool", lambda e: e.dma_start(out=cb[:, 0:1280], in_=cmat[:, :]), writes=[s_const], kind="dma")
        fw.op("pool", lambda e: e.dma_start(out=c_wax.rearrange("p w n i c -> p (w n i c)"), in_=wax[:, :]),
              writes=[s_const], kind="dma")

        def setup_consts(e):
            e.memset(c_num[:, 0:1], EPS)
            e.memset(c_num[:, 1:2], 1.0)
            e.memset(c_num[:, 2:3], 0.0)
            e.memset(c_onesm, 1.0 / 2048.0)
            e.memset(c_onesh, 1.0 / 128.0)
            e.memset(c_halo, 0.0)
            return e.memset(c_hlast, 0.0)
        fw.op("dve", setup_consts, writes=[s_const] + s_halo + s_hlast)

        v88 = c_vec88.rearrange("p (c f) -> p c f", c=8)
        hba = c_der88[:, 0:8]
        hbx = c_der88[:, 8:16]
        scl = c_der88[:, 16:24]
        tmp8 = c_der88[:, 24:32]
        fw.op("act", lambda e: e.activation(out=tmp8, in_=v88[:, :, 7], func=AF.Exp, scale=-1.0),
              reads=[s_const], writes=[s_const])
        fw.op("dve", lambda e: e.tensor_scalar(out=hba, in0=v88[:, :, 5], scalar1=0.5, scalar2=None, op0=ALU.mult),
              reads=[s_const], writes=[s_const])
        fw.op("act", lambda e: e.activation(out=scl, in_=tmp8, func=AF.Ln, bias=ONE_AP[0:88, :], scale=1.0),
              reads=[s_const], writes=[s_const])
        fw.op("dve", lambda e: e.tensor_scalar(out=hbx, in0=v88[:, :, 6], scalar1=0.5, scalar2=None, op0=ALU.mult),
              reads=[s_const], writes=[s_const])
        fw.op("dve", lambda e: e.tensor_scalar(out=scl, in0=scl, scalar1=-4.0, scalar2=None, op0=ALU.mult),
              reads=[s_const], writes=[s_const])

        arena_ap = arena_t[:, :]
        fw.stop_if("setup")

        def rmsnorm_sub(load_fn, src_ap, src_slots, gcol0, dst_ap, dst_slot, ntok, scratch, psring, sq_eng="act"):
            sq, s_sq, lnm, s_ln = scratch
            if load_fn is not None:
                load_fn()
            if sq_eng == "act":
                fw.op("act", lambda e: e.activation(out=sq, in_=src_ap, func=AF.Square),
                      reads=src_slots, writes=[s_sq], cost=1.15)
            else:
                fw.op("dve", lambda e: e.tensor_tensor(out=sq, in0=src_ap, in1=src_ap, op=ALU.mult),
                      reads=src_slots, writes=[s_sq], cost=2.4)
            ps, s_ps = psring.next()

            def f_mm(e):
                for kc in range(NKC):
                    last = e.matmul(ps[:, 0:ntok], lhsT=c_onesm, rhs=sq[:, kc, :], start=(kc == 0), stop=(kc == NKC - 1))
                return last
            fw.op("pe", f_mm, reads=[s_sq, s_const], writes=[s_ps], cost=16 * 0.085)
            fw.op("act", lambda e: e.activation(out=lnm[:, 0:ntok], in_=ps[:, 0:ntok], func=AF.Ln, bias=EPS_AP, scale=1.0),
                  reads=[s_ps], writes=[s_ln])
            fw.op("act", lambda e: e.activation(out=lnm[:, ntok:2 * ntok], in_=lnm[:, 0:ntok], func=AF.Exp, scale=-0.5),
                  reads=[s_ln], writes=[s_ln])

            def f_h(e):
                for kc in range(NKC):
                    last = e.scalar_tensor_tensor(out=dst_ap[:, kc, :], in0=src_ap[:, kc, :], scalar=g128(gcol0 + kc),
                                                  in1=lnm[:, ntok:2 * ntok], op0=ALU.mult, op1=ALU.mult)
                return last
            fw.op("dve", f_h, reads=list(src_slots) + [s_ln, s_const], writes=[dst_slot], cost=16 * 0.34)

        A = Arena(arena_ap)
        W1a = A.alloc(128, (NKC, 1408), BF16)
        s_W1a = Slot("W1a")
        off_xs = A.top
        xs_r = [(A.alloc(128, (NKC, SUB), F32), Slot()) for _ in range(2)]
        xsR = Ring(xs_r)
        sq_a = A.alloc(128, (NKC, SUB), BF16)
        lnm_a = A.alloc(128, (2 * SUB,), F32)
        scratchA = (sq_a, Slot(), lnm_a, Slot())
        off_hT0 = A.top
        hT_ring = Ring([(A.alloc(128, (NKC, TC), BF16), [Slot() for _ in range(TC // SUB)]) for _ in range(2)])
        hT0_slots_A = hT_ring.items[0][1]
        xs_slots_A = [xs_r[0][1], xs_r[1][1]]
        yrb = [(A.alloc(88, (8, TC), BF16), Slot()) for _ in range(2)]
        NSET = 2
        sets = []
        for _ in range(NSET):
            d = {}
            d["xr"] = A.alloc(88, (2, TC + 3), F32)
            d["gr"] = A.alloc(88, (2, TC), F32)
            d["xc"] = A.alloc(88, (2, TC), F32)
            d["xcb"] = A.alloc(88, (2, TC), BF16)
            for nm in ("rp", "ip", "a", "t", "s", "h"):
                d[nm] = A.alloc(88, (2, TC), F32)
            d["slots"] = {nm: [Slot(), Slot()] for nm in ("xr", "gr", "xc", "xcb", "rp", "ip", "a", "t", "s", "h")}
            sets.append(d)
        psA = Ring(list(zip([p[:, :] for p in psb[0:6]], ps_slots[0:6])))
        psA_ms = Ring(list(zip([p[:, :] for p in psb[6:8]], ps_slots[6:8])))

        for q4 in range(4):
            fw.op("pool", lambda e, q4=q4: e.dma_start(
                out=W1a[:, 4 * q4:4 * q4 + 4, :],
                in_=w1a[:, 4 * q4 * 1408:(4 * q4 + 4) * 1408].rearrange("p (k n) -> p k n", k=4)),
                writes=[s_W1a], kind="dma")

        def proj_fm(Wt, s_W, col0, M, ps, s_ps, h_ap, h_slots, ntok=TC):
            def f(e):
                for kc in range(NKC):
                    last = e.matmul(ps[0:M, 0:ntok], lhsT=Wt[:, kc, col0:col0 + M], rhs=h_ap[:, kc, :],
                                    start=(kc == 0), stop=(kc == NKC - 1))
                return last
            fw.op("pe", f, reads=[s_W] + list(h_slots), writes=[s_ps])

        def hT_chunk(G, xsrc, scratch, psring, sq_eng="act"):
            hT, s_hT = hT_ring.next()
            for sb in range(TC // SUB):
                sc = G * (TC // SUB) + sb
                xs, s_xs = xsR.next()
                fw.op("sp", lambda e, xs=xs, sc=sc: e.dma_start(out=xs.rearrange("p k t -> p (k t)"), in_=xsrc[sc]),
                      writes=[s_xs], kind="dma")
                rmsnorm_sub(None, xs, [s_xs], 0, hT[:, :, sb * SUB:(sb + 1) * SUB], s_hT[sb], SUB, scratch, psring, sq_eng)
            return hT, s_hT

        Wk = arena_at(A, off_xs, 128, (NKC, 512), BF16)
        Wq = arena_at(A, off_hT0, 128, (NKC, 512), BF16)
        s_Wq, s_Wk, s_Wv = Slot("Wq"), Slot("Wk"), Slot("Wv")
        w1b_v = w1b.rearrange("p (k n) -> p k n", k=NKC)
        for G in range(NCH):
            hT, s_hT = hT_chunk(G, xTt, scratchA, psA_ms)
            if G == NCH - 1 and not fw.stopped:
                fw.op("pool", lambda e: e.dma_start(out=Wq, in_=w1b_v[:, :, 0:512]), writes=[s_Wq] + hT0_slots_A, kind="dma", cost=12.0)
                fw.op("pool", lambda e: e.dma_start(out=Wk, in_=w1b_v[:, :, 512:1024]), writes=[s_Wk] + xs_slots_A, kind="dma", cost=12.0)
            fw.op("sp", lambda e, hT=hT, G=G: e.dma_start(out=hbuf.ap()[G], in_=hT.rearrange("p k t -> p (k t)")),
                  reads=s_hT, writes=[s_hbuf[G]], kind="dma")
            yb, s_yb = yrb[G % 2]
            for n in range(4):
                st = sets[(G * 4 + n) % NSET]
                sl = st["slots"]
                for j in range(2):
                    c8 = 2 * n + j
                    ps, s_ps = psA.next()
                    proj_fm(W1a, s_W1a, c8 * 88, 88, ps, s_ps, hT, s_hT)
                    fw.op("act", lambda e, ps=ps, st=st, j=j: e.activation(out=st["xr"][:, j, 3:TC + 3], in_=ps[0:88, :], func=AF.Copy),
                          reads=[s_ps], writes=[sl["xr"][j]])
                    ps, s_ps = psA.next()
                    proj_fm(W1a, s_W1a, 704 + c8 * 88, 88, ps, s_ps, hT, s_hT)
                    fw.op("act", lambda e, ps=ps, st=st, j=j: e.activation(out=st["gr"][:, j, :], in_=ps[0:88, :], func=AF.Copy),
                          reads=[s_ps], writes=[sl["gr"][j]])
                fw.op("dve", lambda e, st=st, n=n: e.tensor_copy(out=st["xr"][:, :, 0:3], in_=c_halo[:, 2 * n:2 * n + 2, :]),
                      reads=[s_halo[n]], writes=sl["xr"])
                fw.op("dve", lambda e, st=st, n=n: e.tensor_copy(out=c_halo[:, 2 * n:2 * n + 2, :], in_=st["xr"][:, :, TC:TC + 3]),
                      reads=sl["xr"], writes=[s_halo[n]])
                for k in range(4):
                    for j in range(2):
                        c8 = 2 * n + j
                        if k == 0:
                            fw.op("dve", lambda e, st=st, j=j, c8=c8: e.tensor_scalar(
                                out=st["xc"][:, j, :], in0=st["xr"][:, j, 0:TC], scalar1=v88[:, c8, 0:1], scalar2=v88[:, c8, 4:5],
                                op0=ALU.mult, op1=ALU.add), reads=[sl["xr"][j], s_const], writes=[sl["xc"][j]])
                        else:
                            fw.op("dve", lambda e, st=st, j=j, c8=c8, k=k: e.scalar_tensor_tensor(
                                out=st["xc"][:, j, :], in0=st["xr"][:, j, k:k + TC], scalar=v88[:, c8, k:k + 1], in1=st["xc"][:, j, :],
                                op0=ALU.mult, op1=ALU.add), reads=[sl["xr"][j], sl["xc"][j], s_const], writes=[sl["xc"][j]])
                for j in range(2):
                    fw.op("act", lambda e, st=st, j=j: e.activation(out=st["xcb"][:, j, :], in_=st["xc"][:, j, :], func=AF.Copy),
                          reads=[sl["xc"][j]], writes=[sl["xcb"][j]])
                gps = []
                for j in range(2):
                    for w in range(2):
                        ps, s_ps = psA.next()

                        def f(e, ps=ps, st=st, j=j, w=w, n=n):
                            for i in range(2):
                                last = e.matmul(ps[0:88, :], lhsT=c_wax[:, w, n, i, 88 * j:88 * j + 88], rhs=st["xcb"][:, i, :],
                                                start=(i == 0), stop=(i == 1))
                            return last
                        fw.op("pe", f, reads=sl["xcb"] + [s_const], writes=[s_ps], cost=0.9)
                        gps.append((ps, s_ps))
                for j in range(2):
                    c8 = 2 * n + j
                    ps, s_ps = gps[2 * j]
                    fw.op("act", lambda e, ps=ps, st=st, j=j, c8=c8: e.activation(
                        out=st["rp"][:, j, :], in_=ps[0:88, :], func=AF.Tanh, bias=hba[:, c8:c8 + 1], scale=0.5),
                        reads=[s_ps, s_const], writes=[sl["rp"][j]])
                    ps, s_ps = gps[2 * j + 1]
                    fw.op("act", lambda e, ps=ps, st=st, j=j, c8=c8: e.activation(
                        out=st["ip"][:, j, :], in_=ps[0:88, :], func=AF.Tanh, bias=hbx[:, c8:c8 + 1], scale=0.5),
                        reads=[s_ps, s_const], writes=[sl["ip"][j]])
                for j in range(2):
                    c8 = 2 * n + j
                    fw.op("act", lambda e, st=st, j=j, c8=c8: e.activation(
                        out=st["a"][:, j, :], in_=st["rp"][:, j, :], func=AF.Exp, bias=scl[:, c8:c8 + 1], scale=scl[:, c8:c8 + 1]),
                        reads=[sl["rp"][j], s_const], writes=[sl["a"][j]])
                for j in range(2):
                    fw.op("dve", lambda e, st=st, j=j: e.scalar_tensor_tensor(
                        out=st["t"][:, j, :], in0=st["ip"][:, j, :], scalar=1.0, in1=st["xc"][:, j, :], op0=ALU.add, op1=ALU.mult),
                        reads=[sl["ip"][j], sl["xc"][j]], writes=[sl["t"][j]])
                for j in range(2):
                    fw.op("act", lambda e, st=st, j=j: e.activation(out=st["s"][:, j, :], in_=st["a"][:, j, :], func=AF.Square),
                          reads=[sl["a"][j]], writes=[sl["s"][j]])
                for j in range(2):
                    fw.op("act", lambda e, st=st, j=j: e.activation(out=st["s"][:, j, :], in_=st["s"][:, j, :], func=AF.Sqrt,
                                                                    bias=ONE_AP[0:88, :], scale=-1.0),
                          reads=[sl["s"][j], s_const], writes=[sl["s"][j]])
                for j in range(2):
                    fw.op("dve", lambda e, st=st, j=j: e.scalar_tensor_tensor(
                        out=st["t"][:, j, :], in0=st["s"][:, j, :], scalar=0.5, in1=st["t"][:, j, :], op0=ALU.mult, op1=ALU.mult),
                        reads=[sl["s"][j], sl["t"][j]], writes=[sl["t"][j]])
                for j in range(2):
                    c8 = 2 * n + j
                    fw.op("dve", lambda e, st=st, j=j, c8=c8: e.tensor_tensor_scan(
                        out=st["h"][:, j, :], data0=st["a"][:, j, :], data1=st["t"][:, j, :], initial=c_hlast[:, c8:c8 + 1],
                        op0=ALU.mult, op1=ALU.add), reads=[sl["a"][j], sl["t"][j], s_hlast[c8]], writes=[sl["h"][j]])
                for j in range(2):
                    c8 = 2 * n + j
                    fw.op("act", lambda e, st=st, j=j, c8=c8: e.activation(out=c_hlast[:, c8:c8 + 1], in_=st["h"][:, j, TC - 1:TC], func=AF.Copy),
                          reads=[sl["h"][j]], writes=[s_hlast[c8]])
                for j in range(2):
                    fw.op("dve", lambda e, st=st, j=j: e.tensor_tensor(out=st["ip"][:, j, :], in0=st["gr"][:, j, :], in1=st["gr"][:, j, :], op=ALU.mult),
                          reads=[sl["gr"][j]], writes=[sl["ip"][j]])
                for j in range(2):
                    fw.op("dve", lambda e, st=st, j=j: e.tensor_scalar(out=st["ip"][:, j, :], in0=st["ip"][:, j, :], scalar1=0.044715, scalar2=1.0,
                                                                      op0=ALU.mult, op1=ALU.add), reads=[sl["ip"][j]], writes=[sl["ip"][j]])
                for j in range(2):
                    fw.op("dve", lambda e, st=st, j=j: e.tensor_tensor(out=st["ip"][:, j, :], in0=st["ip"][:, j, :], in1=st["gr"][:, j, :], op=ALU.mult),
                          reads=[sl["ip"][j], sl["gr"][j]], writes=[sl["ip"][j]])
                for j in range(2):
                    fw.op("act", lambda e, st=st, j=j: e.activation(out=st["rp"][:, j, :], in_=st["ip"][:, j, :], func=AF.Tanh, scale=GELU_C),
                          reads=[sl["ip"][j]], writes=[sl["rp"][j]])
                for j in range(2):
                    fw.op("dve", lambda e, st=st, j=j: e.scalar_tensor_tensor(
                        out=st["ip"][:, j, :], in0=st["rp"][:, j, :], scalar=1.0, in1=st["gr"][:, j, :], op0=ALU.add, op1=ALU.mult),
                        reads=[sl["rp"][j], sl["gr"][j]], writes=[sl["ip"][j]])
                for j in range(2):
                    c8 = 2 * n + j
                    fw.op("dve", lambda e, st=st, j=j, c8=c8, yb=yb: e.scalar_tensor_tensor(
                        out=yb[:, c8, :], in0=st["ip"][:, j, :], scalar=0.5, in1=st["h"][:, j, :], op0=ALU.mult, op1=ALU.mult),
                        reads=[sl["ip"][j], sl["h"][j]], writes=[s_yb])
            s_ci = Slot()
            fw.op("sp", lambda e, yb=yb, G=G: e.dma_start(
                out=cin_yr.ap()[G].rearrange("(c p) t -> p c t", p=88), in_=yb),
                reads=[s_yb], writes=[s_ci], kind="dma")
            fw.op("pool", lambda e, G=G: e.collective_compute("AllGather", ALU.bypass, replica_groups=RG,
                                                              ins=[cin_yr.ap()[G].opt()], outs=[cout_yr.ap()[G].opt()]),
                  reads=[s_ci], writes=[s_cyr[G]], kind="cc")
            if G == 0:
                fw.stop_if("passA1")
        fw.stop_if("passA")


        fw.stop_if("ag1")
        B = Arena(arena_ap)
        assert off_xs == 45056 and off_hT0 == 66560, (off_xs, off_hT0)
        KT = arena_at(B, 0, 128, (4, S), BF16)
        s_KT = [[Slot() for _ in range(NCH)] for _ in range(4)]
        e_r = [(arena_at(B, 32768 + 4096 * i, 128, (2, TC), F32), Slot()) for i in range(3)]
        nrm = [dict(raw=arena_at(B, 61440, 128, (TC,), F32), sq=arena_at(B, 63488, 128, (TC,), BF16),
                    ln=arena_at(B, 64512, 128, (TC,), F32), s_raw=Slot(), s_sq=Slot(), s_ln=Slot())]
        B.top = off_hT0 + 16384
        Wv = B.alloc(128, (NKC, 512), BF16)
        Vt = B.alloc(128, (S // 128, 512), BF16)
        s_V = [Slot() for _ in range(S // 128)]
        hT_ring.items = [(B.alloc(128, (NKC, TC), BF16), [Slot() for _ in range(TC // SUB)]) for _ in range(2)]
        QT_ring = Ring([(B.alloc(128, (4, TC), BF16), [Slot() for _ in range(4)]) for _ in range(2)])
        nrm.append(dict(raw=B.alloc(128, (TC,), F32), sq=B.alloc(128, (TC,), BF16), ln=B.alloc(128, (TC,), F32),
                        s_raw=Slot(), s_sq=Slot(), s_ln=Slot()))
        nrmR = Ring(nrm)
        sp_r = [(B.alloc(128, (2, TC), BF16), Slot()) for _ in range(3)]
        x_r = [(B.alloc(128, (2, TC), F32), Slot()) for _ in range(2)]
        w_r = [(B.alloc(128, (2, TC), BF16), Slot()) for _ in range(3)]
        psz2 = psb2[0][:, :].rearrange("p (a b) -> p a b", a=2)
        psc2 = psb2[1][:, :].rearrange("p (a b) -> p a b", a=2)
        obf = [(B.alloc(128, (TC,), BF16), Slot()) for _ in range(2)]
        obfR = Ring(obf)
        psz = [(psb[0][:, :], ps_slots[0]), (psb[1][:, :], ps_slots[1])]
        psc = [(psb[2][:, :], ps_slots[2]), (psb[3][:, :], ps_slots[3])]
        pso = [(psb[4][:, :], ps_slots[4]), (psb[5][:, :], ps_slots[5])]
        psM = Ring([(psb[6][:, :], ps_slots[6]), (psb[7][:, :], ps_slots[7])])

        fw.barrier(ENGS, split_cc=True)
        fw.op("pool", lambda e: e.dma_start(out=Wv, in_=w1b_v[:, :, 1024:1536]), writes=[s_Wv], kind="dma", cost=12.0)

        P = Arena(arena_ap)
        NU = 10
        wunits = [(arena_at(P, 45056 + 4096 * i, 128, (NKC, 128), BF16), Slot()) for i in range(NU)]
        wu_i = [0]

        def wload(src_ap, nk=NKC, extra_reads=(), extra_writes=()):
            u, s_u = wunits[wu_i[0] % NU]
            wu_i[0] += 1
            fw.op("pool", lambda e: e.dma_start(out=u[:, 0:nk, :].rearrange("p k n -> p (k n)"), in_=src_ap),
                  reads=list(extra_reads), writes=[s_u] + list(extra_writes), kind="dma")
            return u, s_u
        pref = {}

        def qk_norm(Wt, s_Wt, col0, gain_col, dst_ap, dst_slot, hT, s_hT):
            nb = nrmR.next()
            ps, s_ps = psM.next()
            proj_fm(Wt, s_Wt, col0, 128, ps, s_ps, hT, s_hT)
            fw.op("dve", lambda e: e.tensor_copy(out=nb["raw"], in_=ps), reads=[s_ps], writes=[nb["s_raw"]])
            fw.op("dve", lambda e: e.tensor_tensor(out=nb["sq"], in0=nb["raw"], in1=nb["raw"], op=ALU.mult),
                  reads=[nb["s_raw"]], writes=[nb["s_sq"]])
            ps2, s_ps2 = psM.next()
            fw.op("pe", lambda e: e.matmul(ps2, lhsT=c_onesh, rhs=nb["sq"], start=True, stop=True),
                  reads=[nb["s_sq"], s_const], writes=[s_ps2], cost=0.3)
            fw.op("act", lambda e: e.activation(out=nb["ln"], in_=ps2, func=AF.Ln, bias=EPS_AP, scale=1.0),
                  reads=[s_ps2], writes=[nb["s_ln"]])
            fw.op("act", lambda e: e.activation(out=nb["ln"], in_=nb["ln"], func=AF.Exp, scale=-0.5),
                  reads=[nb["s_ln"]], writes=[nb["s_ln"]])
            fw.op("dve", lambda e: e.scalar_tensor_tensor(out=dst_ap, in0=nb["raw"], scalar=g128(gain_col), in1=nb["ln"],
                                                          op0=ALU.mult, op1=ALU.mult),
                  reads=[nb["s_raw"], nb["s_ln"], s_const], writes=[dst_slot])

        import os as _os
        for G in range(int(_os.environ.get("KNB", NCH))):
            fw.stop_if("b_w")
            hT, s_hT = hT_ring.next()
            fw.op("sp", lambda e, hT=hT, G=G: e.dma_start(out=hT.rearrange("p k t -> p (k t)"), in_=hbuf.ap()[G]),
                  reads=[s_hbuf[G]], writes=s_hT, kind="dma")
            QT, s_QT = QT_ring.next()
            fw.stop_if("b_h")
            for hh in range(4):
                qk_norm(Wq, s_Wq, hh * 128, 48, QT[:, hh, :], s_QT[hh], hT, s_hT)
                qk_norm(Wk, s_Wk, hh * 128, 49, KT[:, hh, G * TC:(G + 1) * TC], s_KT[hh][G], hT, s_hT)
            fw.stop_if("b_qk")
            for tb in range(4):
                ps, s_ps = psM.next()

                def f(e, ps=ps, tb=tb, hT=hT):
                    for kc in range(NKC):
                        last = e.matmul(ps, lhsT=hT[:, kc, tb * 128:(tb + 1) * 128], rhs=Wv[:, kc, :],
                                        start=(kc == 0), stop=(kc == NKC - 1))
                    return last
                fw.op("pe", f, reads=[s_Wv] + s_hT, writes=[s_ps])
                fw.op("dve", lambda e, ps=ps, tb=tb, G=G: e.tensor_copy(out=Vt[:, 4 * G + tb, :], in_=ps),
                      reads=[s_ps], writes=[s_V[4 * G + tb]])
            fw.stop_if("b_v")
            if G == NCH - 1 and not fw.stopped:
                dead = [s_Wk, s_Wq, s_Wv, nrm[0]["s_raw"], nrm[0]["s_sq"], nrm[0]["s_ln"]]
                for cc_ in range(2):
                    kw = dict(extra_reads=[s_co[NCH - 2]], extra_writes=dead)
                    pref[cc_] = (wload(wt_g[cc_], **kw), wload(wt_g[16 + cc_], **kw), wload(wt_rnn[cc_, 0], 11, **kw),
                                 wload(wt_rnn[cc_, 1], 11, **kw), wload(wt_att[cc_], **kw))
            NT = 4 * G + 4
            s_cio = [Slot() for _ in range(4)]
            for hp in range(2):
                heads = [2 * hp, 2 * hp + 1]
                etile = {}
                sptile = {}
                wtile = {}

                def lo_of(s):
                    d = (NT - 1 - s) - 4 * G
                    return 128 * d if d > 0 else 0

                def stageZ(s):
                    kb = NT - 1 - s
                    diag = kb - 4 * G
                    lo = lo_of(s)
                    for li, hh in enumerate(heads):
                        ps, s_ps = psz[li]

                        def f(e, ps=ps, hh=hh, kb=kb, diag=diag, QT=QT, lo=lo):
                            last = e.matmul(ps[:, lo:TC], lhsT=KT[:, hh, kb * 128:(kb + 1) * 128], rhs=QT[:, hh, lo:TC],
                                            start=True, stop=(diag < 0), skip_group_check=True)
                            if diag >= 0:
                                o0 = 384 - 128 * diag
                                last = e.matmul(ps[:, lo:TC], lhsT=c_ident, rhs=c_mask[:, o0 + lo:o0 + TC], start=False, stop=True,
                                                skip_group_check=True)
                            return last
                        fw.op("pe", f, reads=[s_KT[hh][kb // 4], s_QT[hh], s_const], writes=[s_ps], cost=0.6 if diag >= 0 else 0.3)
                    eb, s_eb = e_r[s % 3]
                    etile[s] = (eb, s_eb)
                    fw.op("act", lambda e, eb=eb, lo=lo: e.activation(out=eb[:, :, lo:TC], in_=psz2[:, :, lo:TC], func=AF.Exp, scale=float(QSCALE)),
                          reads=[psz[0][1], psz[1][1]], writes=[s_eb], cost=0.9)
                    sb_, s_sb = sp_r[s % 3]
                    sptile[s] = (sb_, s_sb)
                    fw.op("act", lambda e, eb=eb, sb_=sb_, lo=lo: e.activation(out=sb_[:, :, lo:TC], in_=eb[:, :, lo:TC], func=AF.Ln, bias=ONE_AP, scale=1.0),
                          reads=[s_eb, s_const], writes=[s_sb], cost=0.9)

                def stageC(s):
                    sb_, s_sb = sptile[s]
                    lo = lo_of(s)
                    lp = lo_of(s - 1) if s > 0 else 0
                    for li, hh in enumerate(heads):
                        ps, s_ps = psc[li]
                        rd = [s_sb, s_const]
                        if s > 0:
                            sprev, s_sprev = sptile[s - 1]
                            rd += [s_sprev, s_ps]
                        else:
                            sprev = None

                        def f(e, ps=ps, sb_=sb_, sprev=sprev, s=s, li=li, lo=lo, lp=lp):
                            if s > 0:
                                e.matmul(ps[:, lp:TC], lhsT=c_SU, rhs=sprev[:, li, lp:TC], start=False, stop=False, skip_group_check=True)
                            return e.matmul(ps[:, lo:TC], lhsT=c_L, rhs=sb_[:, li, lo:TC], start=(s == 0), stop=True, skip_group_check=True)
                        fw.op("pe", f, reads=rd, writes=[s_ps], cost=0.6 if s > 0 else 0.3)
                    xb, s_xb = x_r[s % 2]
                    fw.op("act", lambda e, xb=xb, lo=lo: e.activation(out=xb[:, :, lo:TC], in_=psc2[:, :, lo:TC], func=AF.Exp, scale=-1.0),
                          reads=[psc[0][1], psc[1][1]], writes=[s_xb], cost=0.9)
                    eb, s_eb = etile[s]
                    wb, s_wb = w_r[s % 3]
                    wtile[s] = (wb, s_wb)
                    fw.op("dve", lambda e, xb=xb, eb=eb, wb=wb, lo=lo: e.tensor_tensor(out=wb[:, :, lo:TC], in0=eb[:, :, lo:TC], in1=xb[:, :, lo:TC], op=ALU.mult),
                          reads=[s_xb, s_eb], writes=[s_wb], cost=0.9)

                def stageO(s):
                    kb = NT - 1 - s
                    lo = lo_of(s)
                    for li, hh in enumerate(heads):
                        ps, s_ps = pso[li]
                        wb, s_wb = wtile[s]
                        fw.op("pe", lambda e, ps=ps, wb=wb, hh=hh, kb=kb, s=s, li=li, lo=lo: e.matmul(
                            ps[:, lo:TC], lhsT=Vt[:, kb, hh * 128:(hh + 1) * 128], rhs=wb[:, li, lo:TC], start=(s == 0), stop=(s == NT - 1),
                            skip_group_check=True),
                            reads=[s_wb, s_V[kb], s_ps] if s > 0 else [s_wb, s_V[kb]], writes=[s_ps], cost=0.3)

                for step in range(NT + 2):
                    if step < NT:
                        stageZ(step)
                    if 0 <= step - 1 < NT:
                        stageC(step - 1)
                    if 0 <= step - 2 < NT:
                        stageO(step - 2)
                for li, hh in enumerate(heads):
                    ps, s_ps = pso[li]
                    ob, s_ob = obfR.next()
                    fw.op("dve", lambda e, ps=ps, ob=ob: e.tensor_copy(out=ob, in_=ps), reads=[s_ps], writes=[s_ob])
                    fw.op("sp", lambda e, ob=ob, hh=hh, G=G: e.dma_start(
                        out=cin_o.ap()[G, hh * 128:(hh + 1) * 128, :], in_=ob), reads=[s_ob], writes=[s_cio[hh]], kind="dma")
            fw.op("pool", lambda e, G=G: e.collective_compute("AllGather", ALU.bypass, replica_groups=RG,
                                                              ins=[cin_o.ap()[G].opt()], outs=[cout_o.ap()[G].opt()]),
                  reads=s_cio, writes=[s_co[G]], kind="cc")

        fw.stop_if("passB")

        R2 = arena_at(P, 0, 128, (22, 1024), BF16)
        R1 = arena_at(P, 86016, 128, (NKC, 1024), F32)
        R1b = R1.rearrange("p k t -> p (k t)").bitcast(BF16)
        h_a = R1b[:, 0:NKC * 1024].rearrange("p (k t) -> p k t", k=NKC)
        oT_a = R1b[:, NKC * 1024:2 * NKC * 1024].rearrange("p (k t) -> p k t", k=NKC)
        R3 = arena_at(P, 151552, 128, (NKC, 1024), BF16)
        tmpf = [(arena_at(P, 184320 + 2048 * i, 128, (TC,), F32), Slot()) for i in range(4)]
        tmpR = Ring(tmpf)
        pTb = arena_at(P, 192512, 128, (2, 1024), BF16)
        s_xo = [[Slot() for _ in range(2)] for _ in range(NKC)]
        s_R1h = [Slot() for _ in range(8)]
        s_oT = Slot()
        s_R2 = Slot()
        s_R3 = [[Slot() for _ in range(2)] for _ in range(NKC)]
        s_h2 = [Slot() for _ in range(8)]
        s_pT = Slot()
        psP = Ring(list(zip([p[:, :] for p in psb], ps_slots)))
        sq_p = arena_at(P, 196608, 128, (NKC, SUB), BF16)
        lnm_p = arena_at(P, 200704, 128, (2 * SUB,), F32)
        scratchP = (sq_p, Slot(), lnm_p, Slot())

        fw.barrier(ENGS, split_cc=True)

        def acc_group(ps, s_ps, parts, tb, extra_reads):
            def f(e):
                tot = sum(p[2] for p in parts)
                i = 0
                for (u, s_u, nk, act, k0, rows) in parts:
                    for kc in range(nk):
                        last = e.matmul(ps, lhsT=u[0:rows, kc, :], rhs=act[0:rows, k0 + kc, tb * TC:(tb + 1) * TC],
                                        start=(i == 0), stop=(i == tot - 1))
                        i += 1
                return last
            fw.op("pe", f, reads=[p[1] for p in parts] + list(extra_reads), writes=[s_ps], cost=0.27 * sum(p[2] for p in parts))

        for tb in range(2):
            def f_hh(e, tb=tb):
                rank = e.partition_id() % 4
                return e.dma_start(out=h_a[:, :, tb * TC:(tb + 1) * TC],
                                   in_=hbuf.ap()[bass.ds(2 * rank + tb, 1)].rearrange("o p (k t) -> p (o k) t", k=NKC))
            fw.op("sp", f_hh, reads=s_hbuf, writes=s_R1h[4 * tb:4 * tb + 4], kind="dma")

        def decl_yr():
            for tb in range(2):
                def f_yr(e, tb=tb):
                    rank = e.partition_id() % 4
                    return e.dma_start(out=R2[:, :, tb * TC:(tb + 1) * TC],
                                       in_=cout_yr.ap()[bass.ds(2 * rank + tb, 1)].rearrange("o (k p) t -> p (o k) t", p=128))
                fw.op("sp", f_yr, reads=s_cyr, writes=[s_R2], kind="dma")

        def decl_o():
            for tb in range(2):
                def f_o(e, tb=tb):
                    rank = e.partition_id() % 4
                    return e.dma_start(out=oT_a[:, :, tb * TC:(tb + 1) * TC],
                                       in_=cout_o.ap()[bass.ds(2 * rank + tb, 1)].rearrange("o (k p) t -> p (o k) t", p=128))
                fw.op("sp", f_o, reads=s_co, writes=[s_oT], kind="dma")

        def decl_p():
            fw.op("pool", lambda e: e.dma_start(out=pTb.rearrange("p k t -> p (k t)"), in_=pTt[:, :]), writes=[s_pT], kind="dma")

        fw.stop_if("s0")
        for cc in range(NKC):
            if cc in pref:
                ugr, uga, ur0, ur1, uat = pref[cc]
                if cc == 0:
                    decl_yr()
                    decl_o()
                    decl_p()
            else:
                ugr = wload(wt_g[cc])
                uga = wload(wt_g[16 + cc])
                ur0 = wload(wt_rnn[cc, 0], 11)
                ur1 = wload(wt_rnn[cc, 1], 11)
                uat = wload(wt_att[cc])
            for tb in range(2):
                hs = s_R1h[4 * tb:4 * tb + 4]
                p_gr = psP.next()
                acc_group(p_gr[0], p_gr[1], [(ugr[0], ugr[1], NKC, h_a, 0, 128)], tb, hs)
                p_ga = psP.next()
                acc_group(p_ga[0], p_ga[1], [(uga[0], uga[1], NKC, h_a, 0, 128)], tb, hs)
                p_yr = psP.next()
                acc_group(p_yr[0], p_yr[1], [(ur0[0], ur0[1], 11, R2, 0, 128), (ur1[0], ur1[1], 11, R2, 11, 128)], tb, [s_R2])
                p_ya = psP.next()
                acc_group(p_ya[0], p_ya[1], [(uat[0], uat[1], NKC, oT_a, 0, 128)], tb, [s_oT])
                t_sr = tmpR.next()
                t_sa = tmpR.next()
                fw.op("act", lambda e, p=p_gr, t=t_sr: e.activation(out=t[0], in_=p[0], func=AF.Sigmoid), reads=[p_gr[1]], writes=[t_sr[1]])
                fw.op("act", lambda e, p=p_ga, t=t_sa: e.activation(out=t[0], in_=p[0], func=AF.Sigmoid), reads=[p_ga[1]], writes=[t_sa[1]])
                fw.op("dve", lambda e, p=p_yr, t=t_sr: e.tensor_tensor(out=t[0], in0=t[0], in1=p[0], op=ALU.mult),
                      reads=[p_yr[1], t_sr[1]], writes=[t_sr[1]])
                fw.op("dve", lambda e, p=p_ya, t=t_sa: e.tensor_tensor(out=t[0], in0=t[0], in1=p[0], op=ALU.mult),
                      reads=[p_ya[1], t_sa[1]], writes=[t_sa[1]])
                fw.op("dve", lambda e, a=t_sr, b=t_sa, cc=cc, tb=tb: e.tensor_tensor(
                    out=R3[:, cc, tb * TC:(tb + 1) * TC], in0=a[0], in1=b[0], op=ALU.add),
                    reads=[t_sr[1], t_sa[1]], writes=[s_R3[cc][tb]])

        fw.barrier(["pe", "act", "dve", "sp"])
        fw.stop_if("sa")
        mix_slots = [s_R3[c][t] for c in range(NKC) for t in range(2)]
        for cc in range(NKC):
            uo = wload(wt_o[cc])
            for tb in range(2):
                fw.op("sp", lambda e, cc=cc, tb=tb: e.dma_start(out=R1[:, cc, tb * TC:(tb + 1) * TC], in_=xo2[cc, tb]),
                      writes=[s_xo[cc][tb]], kind="dma")
                p = psP.next()
                acc_group(p[0], p[1], [(uo[0], uo[1], NKC, R3, 0, 128)], tb, [s_R3[c][tb] for c in range(NKC)])
                fw.op("dve", lambda e, p=p, cc=cc, tb=tb: e.tensor_tensor(
                    out=R1[:, cc, tb * TC:(tb + 1) * TC], in0=R1[:, cc, tb * TC:(tb + 1) * TC], in1=p[0], op=ALU.add),
                    reads=[p[1], s_xo[cc][tb]], writes=[s_xo[cc][tb]])


        def rms_resident(gcol0, dst, dst_slots):
            for sc in range(8):
                tb = sc // 4
                rmsnorm_sub(None, R1[:, :, sc * SUB:(sc + 1) * SUB], [s_xo[c][tb] for c in range(NKC)], gcol0,
                            dst[:, :, sc * SUB:(sc + 1) * SUB], dst_slots[sc], SUB, scratchP, psP)

        fw.stop_if("sb")
        rms_resident(16, R3, s_h2)
        for half in range(2):
            for m in range(22):
                mm = half * 22 + m
                ug = wload(wt_gu[mm])
                uu = wload(wt_gu[44 + mm])
                for tb in range(2):
                    hs = s_h2[4 * tb:4 * tb + 4]
                    p_g = psP.next()
                    acc_group(p_g[0], p_g[1], [(ug[0], ug[1], NKC, R3, 0, 128)], tb, hs)
                    p_u = psP.next()
                    acc_group(p_u[0], p_u[1], [(uu[0], uu[1], NKC, R3, 0, 128)], tb, hs)
                    t_s = tmpR.next()
                    fw.op("act", lambda e, p=p_g, t=t_s: e.activation(out=t[0], in_=p[0], func=AF.Silu), reads=[p_g[1]], writes=[t_s[1]])
                    fw.op("dve", lambda e, p=p_u, t=t_s, m=m, tb=tb: e.tensor_tensor(
                        out=R2[:, m, tb * TC:(tb + 1) * TC], in0=t[0], in1=p[0], op=ALU.mult),
                        reads=[p_u[1], t_s[1]], writes=[s_R2])
            for cc in range(NKC):
                ud0 = wload(wt_dn[half, cc, 0], 11)
                ud1 = wload(wt_dn[half, cc, 1], 11)
                for tb in range(2):
                    p = psP.next()
                    acc_group(p[0], p[1], [(ud0[0], ud0[1], 11, R2, 0, 128), (ud1[0], ud1[1], 11, R2, 11, 128)], tb, [s_R2])
                    fw.op("dve", lambda e, p=p, cc=cc, tb=tb: e.tensor_tensor(
                        out=R1[:, cc, tb * TC:(tb + 1) * TC], in0=R1[:, cc, tb * TC:(tb + 1) * TC], in1=p[0], op=ALU.add),
                        reads=[p[1], s_xo[cc][tb]], writes=[s_xo[cc][tb]])

        fw.stop_if("sc")
        h3 = R2[:, 0:NKC, :]
        s_h3 = [Slot() for _ in range(8)]
        rms_resident(32, h3, s_h3)
        out_ops = []
        for cc in range(NKC):
            upg = wload(wt_pg[cc])
            upp = wload(wt_pp[cc], 2)
            for tb in range(2):
                hs = s_h3[4 * tb:4 * tb + 4]
                p_g = psP.next()
                acc_group(p_g[0], p_g[1], [(upg[0], upg[1], NKC, h3, 0, 128)], tb, hs)
                p_p = psP.next()

                acc_group(p_p[0], p_p[1], [(upp[0], upp[1], 2, pTb, 0, 128)], tb, [s_pT])
                t_s = tmpR.next()
                fw.op("act", lambda e, p=p_g, t=t_s: e.activation(out=t[0], in_=p[0], func=AF.Sigmoid), reads=[p_g[1]], writes=[t_s[1]])
                fw.op("dve", lambda e, p=p_p, t=t_s: e.tensor_tensor(out=t[0], in0=t[0], in1=p[0], op=ALU.mult),
                      reads=[p_p[1], t_s[1]], writes=[t_s[1]])
                fw.op("dve", lambda e, t=t_s, cc=cc, tb=tb: e.tensor_tensor(
                    out=R1[:, cc, tb * TC:(tb + 1) * TC], in0=R1[:, cc, tb * TC:(tb + 1) * TC], in1=t[0], op=ALU.add),
                    reads=[t_s[1], s_xo[cc][tb]], writes=[s_xo[cc][tb]])
            out_ops.append(fw.op("sp", lambda e, cc=cc: e.dma_start(out=outT[cc], in_=R1[:, cc, :]),
                                 reads=[s_xo[cc][0], s_xo[cc][1]], kind="dma"))
        s_fin = Slot("fin")
        fin = fw.op("sp", lambda e: e.nop(), kind="c")
        fin.deps.update(o_ for o_ in out_ops if o_.kind != "skip")

        import os as _os2
        if _os2.environ.get("KNOSCHED", "") != "1":
            fw.schedule()
        fw.finalize()
        sems = {}
        for e in ENGS:
            sems[("eng", e)] = es.enter_context(nc.semaphore(f"s_{e}"))
            for i in range(FW.NDS):
                sems[("dma", e, i)] = es.enter_context(nc.semaphore(f"d_{e}{i}"))
        for i in range(fw.cc_count):
            sems[("cc", "pool", i)] = es.enter_context(nc.semaphore(f"cc{i}"))
        block = es.enter_context(nc.Block())

        @block.tensor
        def _(t):
            fw.emit("pe", t, sems)

        @block.scalar
        def _(a):
            fw.emit("act", a, sems)

        @block.vector
        def _(v):
            fw.emit("dve", v, sems)

        @block.gpsimd
        def _(g):
            fw.emit("pool", g, sems)

        @block.sync
        def _(sy):
            fw.emit("sp", sy, sems)
    return nc


def _tile_w(w, nk):
    K, N = w.shape
    assert K == nk * 128
    t = w.reshape(nk, 128, N // 128, 128).transpose(2, 1, 0, 3)
    return np.ascontiguousarray(t).reshape(N // 128, 128, nk * 128)


_NC_CACHE = {}


def kernel(x, p, g_mix, w_in, conv_w, conv_b, w_rg_a, b_rg_a, w_rg_x, b_rg_x, lru_lambda, q_gain, k_gain,
           w_rnn_out, w_attn_out, w_o, g_ffn, w_ffn_gu, w_ffn_down, g_ple, w_ple_gate, w_ple_proj):
    f = np.float32
    x = np.asarray(x, f)
    p = np.asarray(p, f)
    w_in0 = np.asarray(w_in, f)[0]
    if "nc" not in _NC_CACHE:
        _NC_CACHE["nc"] = build_program()
    nc = _NC_CACHE["nc"]

    vec128 = np.zeros((128, 50), f)
    vec128[:, 0:16] = np.asarray(g_mix, f)[0].reshape(16, 128).T
    vec128[:, 16:32] = np.asarray(g_ffn, f)[0].reshape(16, 128).T
    vec128[:, 32:48] = np.asarray(g_ple, f)[0].reshape(16, 128).T
    vec128[:, 48] = np.asarray(q_gain, f)[0]
    vec128[:, 49] = np.asarray(k_gain, f)[0]
    cmat = np.zeros((128, 384 + 896), f)
    cmat[:, 0:128] = np.eye(128, dtype=f)
    jj = np.arange(128)[:, None]
    ss = np.arange(128)[None, :]
    cmat[:, 128:256] = (jj >= ss).astype(f)
    cmat[:, 256:384] = (jj < ss).astype(f)
    xx = np.arange(896)[None, :]
    cmat[:, 384:] = np.where(xx <= jj + 384, NEG, 0.0).astype(f)
    wt_g = _tile_w(w_in0[:, 11776:15872], 16)
    wr = np.asarray(w_rnn_out, f)[0]
    wt_rnn = _tile_w(wr, 22).reshape(16, 128, 2, 11 * 128).transpose(0, 2, 1, 3).copy()
    wt_att = _tile_w(np.asarray(w_attn_out, f)[0], 16)
    wt_o = _tile_w(np.asarray(w_o, f)[0], 16)
    wt_gu = _tile_w(np.asarray(w_ffn_gu, f)[0], 16)
    wd = np.asarray(w_ffn_down, f)[0]
    wt_dn = np.stack([_tile_w(wd[h * 2816:(h + 1) * 2816], 22).reshape(16, 128, 2, 11 * 128).transpose(0, 2, 1, 3)
                      for h in range(2)]).copy()
    wt_pg = _tile_w(np.asarray(w_ple_gate, f)[0], 16)
    wpp = np.asarray(w_ple_proj, f)[0]
    wt_pp = _tile_w(wpp, 2)
    cw = np.asarray(conv_w, f)[0]
    cbv = np.asarray(conv_b, f)[0]
    ba = np.asarray(b_rg_a, f)[0]
    bx = np.asarray(b_rg_x, f)[0]
    lam = np.asarray(lru_lambda, f)[0]
    wa_all = np.asarray(w_rg_a, f)[0]
    wx_all = np.asarray(w_rg_x, f)[0]

    in_maps = []
    for c in range(8):
        b, g = c // 4, c % 4
        xb = x[b]
        xTt = np.ascontiguousarray(xb.reshape(32, 128, 16, 128).transpose(0, 3, 2, 1)).reshape(32, 128, 16 * 128)
        xo = xb[1024 * g:1024 * (g + 1)]
        xo2 = np.ascontiguousarray(xo.reshape(2, 512, 16, 128).transpose(2, 0, 3, 1))
        pb = p[0, b, 1024 * g:1024 * (g + 1)]
        pTt = np.ascontiguousarray(pb.reshape(1024, 2, 128).transpose(2, 1, 0)).reshape(128, 2 * 1024)
        w1a = np.concatenate([w_in0[:, 704 * g:704 * (g + 1)], w_in0[:, 2816 + 704 * g:2816 + 704 * (g + 1)]], axis=1)
        w1a = np.ascontiguousarray(w1a.reshape(16, 128, 1408).transpose(1, 0, 2)).reshape(128, 16 * 1408)
        w1b = np.concatenate([w_in0[:, 5632 + 512 * g:5632 + 512 * (g + 1)], w_in0[:, 7680 + 512 * g:7680 + 512 * (g + 1)],
                              w_in0[:, 9728 + 512 * g:9728 + 512 * (g + 1)]], axis=1)
        w1b = np.ascontiguousarray(w1b.reshape(16, 128, 1536).transpose(1, 0, 2)).reshape(128, 16 * 1536)
        ch = slice(704 * g, 704 * (g + 1))
        v88 = np.zeros((88, 8, 8), f)
        for k in range(4):
            v88[:, :, k] = cw[k, ch].reshape(8, 88).T
        v88[:, :, 4] = cbv[ch].reshape(8, 88).T
        v88[:, :, 5] = ba[ch].reshape(8, 88).T
        v88[:, :, 6] = bx[ch].reshape(8, 88).T
        v88[:, :, 7] = lam[ch].reshape(8, 88).T
        wax = np.stack([wa_all[4 * g:4 * g + 4], wx_all[4 * g:4 * g + 4]])
        wax = np.ascontiguousarray(wax.reshape(2, 4, 2, 88, 176).transpose(3, 0, 1, 2, 4)).reshape(88, 2 * 4 * 2 * 176)
        in_maps.append(dict(xTt=xTt, xo2=xo2, pTt=pTt, w1a=w1a, w1b=w1b,
                            vec88=v88.reshape(88, 64), vec128=vec128, wax=wax, cmat=cmat, wt_g=wt_g, wt_rnn=wt_rnn,
                            wt_att=wt_att, wt_o=wt_o, wt_gu=wt_gu, wt_dn=wt_dn, wt_pg=wt_pg, wt_pp=wt_pp))
    res = run_bass_kernel_spmd(nc, in_maps, core_ids=list(range(8)))
    out = np.empty((2, S, D), f)
    for c in range(8):
        b, g = c // 4, c % 4
        oT = np.asarray(res.results[c]["outT"], f).reshape(2048, 1024)
        out[b, 1024 * g:1024 * (g + 1), :] = oT.T
    return out
```

```python
import numpy as np
from contextlib import ExitStack
import concourse.bass as bass
import concourse.mybir as mybir
from concourse.bass_utils import run_bass_kernel_spmd

F32 = mybir.dt.float32
BF16 = mybir.dt.bfloat16
U8 = mybir.dt.uint8
AF = mybir.ActivationFunctionType
ALU = mybir.AluOpType

S = 4096
D = 2048
NKC = 16
TC = 512
NCH = S // TC
SUB = 128
EPS = 1e-6
QSCALE = 1.0 / np.sqrt(128.0)
NEG = -30000.0
ARENA = 198 * 1024
GELU_C = 0.7978845608028654

ENGS = ["pe", "act", "dve", "pool", "sp"]


class Slot:
    __slots__ = ("name", "w", "r", "excl")

    def __init__(self, name="", excl=False):
        self.name = name
        self.w = None
        self.r = []
        self.excl = excl


class Op:
    __slots__ = ("eng", "fn", "deps", "needs_inc", "seq", "kind", "sem", "semv", "pos", "raw", "cost", "idx",
                 "nd", "rt", "fin", "succ", "tbl")


class FW:
    NDS = 8

    def __init__(self):
        self.ops = {e: [] for e in ENGS}
        self.cur_barrier = {e: None for e in ENGS}
        self.since_barrier = []
        self.cc_count = 0
        self.stopped = False
        self.all = []

    def stop_if(self, tag):
        import os
        if os.environ.get("KSTOP", "") == tag:
            self.stopped = True

    DEFCOST = {"pe": 4.4, "act": 0.6, "dve": 0.65, "pool": 5.0, "sp": 3.0}

    def op(self, eng, fn, reads=(), writes=(), kind="c", cost=None):
        o = Op()
        o.cost = cost if cost is not None else (30.0 if kind == "cc" else self.DEFCOST[eng])
        o.tbl = None
        if eng == "act":
            names = fn.__code__.co_names
            if "Tanh" in names:
                o.tbl = "T"
            elif "Sqrt" in names:
                o.tbl = "Q"
            elif "Ln" in names:
                o.tbl = "L"
            elif "Sigmoid" in names:
                o.tbl = "S"
            elif "Silu" in names:
                o.tbl = "U"
            elif "Exp" in names:
                o.tbl = "E"
        if self.stopped:
            o.eng, o.fn, o.kind, o.deps, o.raw, o.pos = eng, fn, "skip", set(), set(), -1
            return o
        o.eng, o.fn, o.kind = eng, fn, kind
        o.needs_inc = False
        o.seq = 0
        o.sem = None
        o.semv = 0
        deps = set()
        raw = set()
        for s in reads:
            if s.w is not None:
                deps.add(s.w)
                raw.add(s.w)
            if s.excl:
                deps.update(s.r)
        for s in writes:
            if s.w is not None:
                deps.add(s.w)
            deps.update(s.r)
        for s in reads:
            s.r.append(o)
        for s in writes:
            s.w = o
            s.r = []
        if self.cur_barrier[eng] is not None:
            deps.add(self.cur_barrier[eng])
        if kind == "cc":
            o.sem = ("cc", eng, self.cc_count)
            self.cc_count += 1
            o.semv = 1
        deps.discard(o)
        o.deps = deps
        o.raw = raw
        o.pos = len(self.ops[eng])
        self.ops[eng].append(o)
        o.idx = len(self.all)
        self.all.append(o)
        self.since_barrier.append(o)
        return o

    def schedule(self, reorder=("pe", "act", "dve", "sp"), W=200):
        import bisect
        for o in self.all:
            o.succ = []
        for o in self.all:
            o.nd = len(o.deps)
            o.rt = 0.0
            for d in o.deps:
                d.succ.append(o)
        blev = {}
        for o in reversed(self.all):
            m = 0.0
            for c in o.succ:
                v = blev[id(c)]
                if v > m:
                    m = v
            blev[id(o)] = m + o.cost
        import os as _os3
        use_bl = _os3.environ.get("KPRIO", "bl") == "bl"
        for o in self.all:
            o.seq = (-blev[id(o)] if use_bl else 0.0, o.idx)
        TBL_LOAD = 1.3
        cur_tbl = [None]

        def tbl_pen(o):
            t = o.tbl
            if t is None:
                return 0.0
            if t == "E":
                return 0.0 if cur_tbl[0] in ("T", "L") else TBL_LOAD
            return 0.0 if cur_tbl[0] == t else TBL_LOAD
        avail = {e: [] for e in ENGS}
        ptr = {e: 0 for e in ENGS}
        free_at = {e: 0.0 for e in ENGS}
        order = {e: [] for e in ENGS}
        for o in self.all:
            if o.nd == 0 and o.eng in reorder:
                bisect.insort(avail[o.eng], (o.seq, o))
        remaining = len(self.all)
        while remaining:
            best = None
            for e in ENGS:
                if e in reorder:
                    lst = avail[e]
                    if not lst:
                        continue
                    fa = free_at[e]
                    cb = None
                    for (ix, o) in lst[:W]:
                        st = o.rt if o.rt > fa else fa
                        if e == "act":
                            st += tbl_pen(o)
                        if cb is None or st < cb[0] - 1e-9:
                            cb = (st, ix, o)
                        if st <= fa:
                            break
                else:
                    if ptr[e] >= len(self.ops[e]):
                        continue
                    o = self.ops[e][ptr[e]]
                    if o.nd > 0:
                        continue
                    st = max(o.rt, free_at[e])
                    cb = (st, o.seq, o)
                if best is None or (cb[0], cb[1]) < (best[0], best[1]):
                    best = cb
            assert best is not None, "scheduler stuck"
            st, ix, o = best
            e = o.eng
            if e in reorder:
                i = bisect.bisect_left(avail[e], (ix, ))
                assert avail[e][i][1] is o
                avail[e].pop(i)
            else:
                ptr[e] += 1
            if e == "act" and o.tbl is not None:
                if o.tbl == "E":
                    if cur_tbl[0] not in ("T", "L"):
                        cur_tbl[0] = "L"
                else:
                    cur_tbl[0] = o.tbl
            if o.kind in ("dma", "cc"):
                free_at[e] = st + (1.0 if e == "pool" else 0.15)
            else:
                free_at[e] = st + o.cost
            o.fin = st + o.cost
            order[e].append(o)
            remaining -= 1
            for c in o.succ:
                c.nd -= 1
                if o.fin > c.rt:
                    c.rt = o.fin
                if c.nd == 0 and c.eng in reorder:
                    bisect.insort(avail[c.eng], (c.seq, c))
        for e in ENGS:
            self.ops[e] = order[e]
            for i, o in enumerate(order[e]):
                o.pos = i
        self.makespan = max(o.fin for o in self.all)

    def barrier(self, engines, split_cc=False):
        if self.stopped:
            return
        prior = list(self.since_barrier)
        b = self.op("dve", lambda e: e.nop(), cost=0.1)
        b.deps.update(o for o in prior if not (split_cc and o.kind == "cc"))
        b.deps.discard(b)
        self.since_barrier = [b]
        for e in engines:
            self.cur_barrier[e] = b
        if split_cc and "pool" in engines:
            bp = self.op("pool", lambda e: e.nop(), cost=0.1)
            bp.deps.update(prior)
            bp.deps.add(b)
            bp.deps.discard(bp)
            self.cur_barrier["pool"] = bp

    def finalize(self):
        for e in ENGS:
            n = 0
            hist = []
            for o in self.ops[e]:
                if o.kind != "dma":
                    continue
                o.sem = ("dma", e, n % self.NDS)
                o.semv = 16 * (n // self.NDS + 1)
                if n >= self.NDS:
                    o.deps.add(hist[n - self.NDS])
                hist.append(o)
                n += 1
        for e in ENGS:
            for o in self.ops[e]:
                for d in o.deps:
                    if d.kind != "c":
                        continue
                    if d.eng != o.eng:
                        d.needs_inc = True
                    elif d in o.raw and o.pos - d.pos <= 1:
                        d.needs_inc = True
        for e in ENGS:
            n = 0
            for o in self.ops[e]:
                if o.kind == "c" and o.needs_inc:
                    n += 1
                    o.seq = n
                    o.sem = ("eng", e)
                    o.semv = n

    def emit(self, eng, handle, sems):
        waited = {}
        for o in self.ops[eng]:
            need = {}
            for d in o.deps:
                if d.kind == "c":
                    if d.eng == eng and not (d in o.raw and o.pos - d.pos <= 1):
                        continue
                if need.get(d.sem, 0) < d.semv:
                    need[d.sem] = d.semv
            for k, v in need.items():
                if waited.get(k, 0) >= v:
                    continue
                waited[k] = v
                handle.wait_ge(sems[k], v)
            inst = o.fn(handle)
            if o.kind == "dma":
                inst.then_inc(sems[o.sem], 16)
            elif o.kind == "cc":
                inst.then_inc(sems[o.sem], 1)
            elif o.needs_inc:
                inst.then_inc(sems[o.sem], 1)


class Arena:
    def __init__(self, ap):
        self.ap = ap
        self.top = 0

    def alloc(self, parts, free, dtype):
        esz = 4 if dtype == F32 else 2
        n = int(np.prod(free)) * esz
        off = (self.top + 63) // 64 * 64
        self.top = off + n
        assert self.top <= ARENA, f"arena overflow {self.top}"
        v = self.ap[0:parts, off:off + n].bitcast(dtype)
        if len(free) == 2:
            v = v.rearrange("p (a b) -> p a b", a=free[0])
        elif len(free) == 3:
            v = v.rearrange("p (a b c) -> p a b c", a=free[0], b=free[1])
        return v


def arena_at(arena, off, parts, free, dtype):
    esz = 4 if dtype == F32 else 2
    n = int(np.prod(free)) * esz
    assert off % 64 == 0 and off + n <= ARENA
    v = arena.ap[0:parts, off:off + n].bitcast(dtype)
    if len(free) == 2:
        v = v.rearrange("p (a b) -> p a b", a=free[0])
    arena.top = max(arena.top, off + n)
    return v


class Ring:
    def __init__(self, items):
        self.items = items
        self.i = 0

    def next(self):
        it = self.items[self.i % len(self.items)]
        self.i += 1
        return it


def build_program():
    nc = bass.Bass("TRN2", target_bir_lowering=False)

    def din(name, shape):
        return nc.dram_tensor(name, shape, F32, kind="ExternalInput").ap()

    xTt = din("xTt", [S // SUB, 128, NKC * SUB])
    xo2 = din("xo2", [16, 2, 128, 512])
    pTt = din("pTt", [128, 2 * 1024])
    w1a = din("w1a", [128, NKC * 1408])
    w1b = din("w1b", [128, NKC * 1536])
    vec88 = din("vec88", [88, 64])
    vec128 = din("vec128", [128, 50])
    wax = din("wax", [88, 2 * 4 * 2 * 176])
    cmat = din("cmat", [128, 384 + 896])
    wt_g = din("wt_g", [32, 128, 2048])
    wt_rnn = din("wt_rnn", [16, 2, 128, 11 * 128])
    wt_att = din("wt_att", [16, 128, 2048])
    wt_o = din("wt_o", [16, 128, 2048])
    wt_gu = din("wt_gu", [88, 128, 2048])
    wt_dn = din("wt_dn", [2, 16, 2, 128, 11 * 128])
    wt_pg = din("wt_pg", [16, 128, 2048])
    wt_pp = din("wt_pp", [16, 128, 2 * 128])
    outT = nc.dram_tensor("outT", [16, 128, 1024], F32, kind="ExternalOutput").ap()
    cin_yr = nc.dram_tensor("cin_yr", [NCH, 704, TC], BF16)
    cout_yr = nc.dram_tensor("cout_yr", [NCH, 2816, TC], BF16)
    cin_o = nc.dram_tensor("cin_o", [NCH, 512, TC], BF16)
    cout_o = nc.dram_tensor("cout_o", [NCH, 2048, TC], BF16)
    RG = [[0, 1, 2, 3], [4, 5, 6, 7]]
    hbuf = nc.dram_tensor("hbuf", [NCH, 128, NKC * TC], BF16)
    s_hbuf = [Slot() for _ in range(NCH)]
    s_cyr = [Slot() for _ in range(NCH)]
    s_co = [Slot() for _ in range(NCH)]

    fw = FW()
    es = ExitStack()
    with es:
        arena_t = es.enter_context(nc.sbuf_tensor("arena", [128, ARENA], U8))
        cst_t = es.enter_context(nc.sbuf_tensor("cst", [128, 2432], F32))
        psb2 = [es.enter_context(nc.psum_tensor(f"ps{i}", [128, 1024], F32)) for i in range(4)]
        psb = [psb2[i // 2][:, (i % 2) * 512:(i % 2 + 1) * 512] for i in range(8)]
        ps_slots = [Slot(f"ps{i}", excl=True) for i in range(8)]

        cst = cst_t
        c_vec128 = cst[:, 0:50]
        c_vec88 = cst[0:88, 64:128]
        c_der88 = cst[0:88, 128:160]
        c_num = cst[:, 160:168]
        cb = cst[:, 192:192 + 768].bitcast(BF16)
        c_ident = cb[:, 0:128]
        c_L = cb[:, 128:256]
        c_SU = cb[:, 256:384]
        c_mask = cb[:, 384:384 + 896]
        c_onesm = cb[:, 1280:1408]
        c_onesh = cb[:, 1408:1536]
        c_halo = cst[0:88, 968:968 + 24].rearrange("p (a b) -> p a b", a=8)
        c_hlast = cst[0:88, 1000:1008]
        c_wax = cst[0:88, 1016:1016 + 1408].bitcast(BF16).rearrange("p (w n i c) -> p w n i c", w=2, n=4, i=2)
        s_const = Slot("const")
        s_halo = [Slot() for _ in range(4)]
        s_hlast = [Slot() for _ in range(8)]
        EPS_AP = c_num[:, 0:1]
        ONE_AP = c_num[:, 1:2]

        def g128(col):
            return c_vec128[:, col:col + 1]

        fw.op("sp", lambda e: e.dma_start(out=c_vec128, in_=vec128[:, :]), writes=[s_const], kind="dma")
        fw.op("sp", lambda e: e.dma_start(out=c_vec88, in_=vec88[:, :]), writes=[s_const], kind="dma")
        fw.op("pool", lambda e: e.dma_start(out=cb[:, 0:1280], in_=cmat[:, :]), writes=[s_const], kind="dma")
        fw.op("pool", lambda e: e.dma_start(out=c_wax.rearrange("p w n i c -> p (w n i c)"), in_=wax[:, :]),
              writes=[s_const], kind="dma")

        def setup_consts(e):
            e.memset(c_num[:, 0:1], EPS)
            e.memset(c_num[:, 1:2], 1.0)
            e.memset(c_num[:, 2:3], 0.0)
            e.memset(c_onesm, 1.0 / 2048.0)
            e.memset(c_onesh, 1.0 / 128.0)
            e.memset(c_halo, 0.0)
            return e.memset(c_hlast, 0.0)
        fw.op("dve", setup_consts, writes=[s_const] + s_halo + s_hlast)

        v88 = c_vec88.rearrange("p (c f) -> p c f", c=8)
        hba = c_der88[:, 0:8]
        hbx = c_der88[:, 8:16]
        scl = c_der88[:, 16:24]
        tmp8 = c_der88[:, 24:32]
        fw.op("act", lambda e: e.activation(out=tmp8, in_=v88[:, :, 7], func=AF.Exp, scale=-1.0),
              reads=[s_const], writes=[s_const])
        fw.op("dve", lambda e: e.tensor_scalar(out=hba, in0=v88[:, :, 5], scalar1=0.5, scalar2=None, op0=ALU.mult),
              reads=[s_const], writes=[s_const])
        fw.op("act", lambda e: e.activation(out=scl, in_=tmp8, func=AF.Ln, bias=ONE_AP[0:88, :], scale=1.0),
              reads=[s_const], writes=[s_const])
        fw.op("dve", lambda e: e.tensor_scalar(out=hbx, in0=v88[:, :, 6], scalar1=0.5, scalar2=None, op0=ALU.mult),
              reads=[s_const], writes=[s_const])
        fw.op("dve", lambda e: e.tensor_scalar(out=scl, in0=scl, scalar1=-4.0, scalar2=None, op0=ALU.mult),
              reads=[s_const], writes=[s_const])

        arena_ap = arena_t[:, :]
        fw.stop_if("setup")

        def rmsnorm_sub(load_fn, src_ap, src_slots, gcol0, dst_ap, dst_slot, ntok, scratch, psring, sq_eng="act"):
            sq, s_sq, lnm, s_ln = scratch
            if load_fn is not None:
                load_fn()
            if sq_eng == "act":
                fw.op("act", lambda e: e.activation(out=sq, in_=src_ap, func=AF.Square),
                      reads=src_slots, writes=[s_sq], cost=1.15)
            else:
                fw.op("dve", lambda e: e.tensor_tensor(out=sq, in0=src_ap, in1=src_ap, op=ALU.mult),
                      reads=src_slots, writes=[s_sq], cost=2.4)
            ps, s_ps = psring.next()

            def f_mm(e):
                for kc in range(NKC):
                    last = e.matmul(ps[:, 0:ntok], lhsT=c_onesm, rhs=sq[:, kc, :], start=(kc == 0), stop=(kc == NKC - 1))
                return last
            fw.op("pe", f_mm, reads=[s_sq, s_const], writes=[s_ps], cost=16 * 0.085)
            fw.op("act", lambda e: e.activation(out=lnm[:, 0:ntok], in_=ps[:, 0:ntok], func=AF.Ln, bias=EPS_AP, scale=1.0),
                  reads=[s_ps], writes=[s_ln])
            fw.op("act", lambda e: e.activation(out=lnm[:, ntok:2 * ntok], in_=lnm[:, 0:ntok], func=AF.Exp, scale=-0.5),
                  reads=[s_ln], writes=[s_ln])

            def f_h(e):
                for kc in range(NKC):
                    last = e.scalar_tensor_tensor(out=dst_ap[:, kc, :], in0=src_ap[:, kc, :], scalar=g128(gcol0 + kc),
                                                  in1=lnm[:, ntok:2 * ntok], op0=ALU.mult, op1=ALU.mult)
                return last
            fw.op("dve", f_h, reads=list(src_slots) + [s_ln, s_const], writes=[dst_slot], cost=16 * 0.34)

        A = Arena(arena_ap)
        W1a = A.alloc(128, (NKC, 1408), BF16)
        s_W1a = Slot("W1a")
        off_xs = A.top
        xs_r = [(A.alloc(128, (NKC, SUB), F32), Slot()) for _ in range(2)]
        xsR = Ring(xs_r)
        sq_a = A.alloc(128, (NKC, SUB), BF16)
        lnm_a = A.alloc(128, (2 * SUB,), F32)
        scratchA = (sq_a, Slot(), lnm_a, Slot())
        off_hT0 = A.top
        hT_ring = Ring([(A.alloc(128, (NKC, TC), BF16), [Slot() for _ in range(TC // SUB)]) for _ in range(2)])
        hT0_slots_A = hT_ring.items[0][1]
        xs_slots_A = [xs_r[0][1], xs_r[1][1]]
        yrb = [(A.alloc(88, (8, TC), BF16), Slot()) for _ in range(2)]
        NSET = 2
        sets = []
        for _ in range(NSET):
            d = {}
            d["xr"] = A.alloc(88, (2, TC + 3), F32)
            d["gr"] = A.alloc(88, (2, TC), F32)
            d["xc"] = A.alloc(88, (2, TC), F32)
            d["xcb"] = A.alloc(88, (2, TC), BF16)
            for nm in ("rp", "ip", "a", "t", "s", "h"):
                d[nm] = A.alloc(88, (2, TC), F32)
            d["slots"] = {nm: [Slot(), Slot()] for nm in ("xr", "gr", "xc", "xcb", "rp", "ip", "a", "t", "s", "h")}
            sets.append(d)
        psA = Ring(list(zip([p[:, :] for p in psb[0:6]], ps_slots[0:6])))
        psA_ms = Ring(list(zip([p[:, :] for p in psb[6:8]], ps_slots[6:8])))

        for q4 in range(4):
            fw.op("pool", lambda e, q4=q4: e.dma_start(
                out=W1a[:, 4 * q4:4 * q4 + 4, :],
                in_=w1a[:, 4 * q4 * 1408:(4 * q4 + 4) * 1408].rearrange("p (k n) -> p k n", k=4)),
                writes=[s_W1a], kind="dma")

        def proj_fm(Wt, s_W, col0, M, ps, s_ps, h_ap, h_slots, ntok=TC):
            def f(e):
                for kc in range(NKC):
                    last = e.matmul(ps[0:M, 0:ntok], lhsT=Wt[:, kc, col0:col0 + M], rhs=h_ap[:, kc, :],
                                    start=(kc == 0), stop=(kc == NKC - 1))
                return last
            fw.op("pe", f, reads=[s_W] + list(h_slots), writes=[s_ps])

        def hT_chunk(G, xsrc, scratch, psring, sq_eng="act"):
            hT, s_hT = hT_ring.next()
            for sb in range(TC // SUB):
                sc = G * (TC // SUB) + sb
                xs, s_xs = xsR.next()
                fw.op("sp", lambda e, xs=xs, sc=sc: e.dma_start(out=xs.rearrange("p k t -> p (k t)"), in_=xsrc[sc]),
                      writes=[s_xs], kind="dma")
                rmsnorm_sub(None, xs, [s_xs], 0, hT[:, :, sb * SUB:(sb + 1) * SUB], s_hT[sb], SUB, scratch, psring, sq_eng)
            return hT, s_hT

        Wk = arena_at(A, off_xs, 128, (NKC, 512), BF16)
        Wq = arena_at(A, off_hT0, 128, (NKC, 512), BF16)
        s_Wq, s_Wk, s_Wv = Slot("Wq"), Slot("Wk"), Slot("Wv")
        w1b_v = w1b.rearrange("p (k n) -> p k n", k=NKC)
        for G in range(NCH):
            hT, s_hT = hT_chunk(G, xTt, scratchA, psA_ms)
            if G == NCH - 1 and not fw.stopped:
                fw.op("pool", lambda e: e.dma_start(out=Wq, in_=w1b_v[:, :, 0:512]), writes=[s_Wq] + hT0_slots_A, kind="dma", cost=12.0)
                fw.op("pool", lambda e: e.dma_start(out=Wk, in_=w1b_v[:, :, 512:1024]), writes=[s_Wk] + xs_slots_A, kind="dma", cost=12.0)
            fw.op("sp", lambda e, hT=hT, G=G: e.dma_start(out=hbuf.ap()[G], in_=hT.rearrange("p k t -> p (k t)")),
                  reads=s_hT, writes=[s_hbuf[G]], kind="dma")
            yb, s_yb = yrb[G % 2]
            for n in range(4):
                st = sets[(G * 4 + n) % NSET]
                sl = st["slots"]
                for j in range(2):
                    c8 = 2 * n + j
                    ps, s_ps = psA.next()
                    proj_fm(W1a, s_W1a, c8 * 88, 88, ps, s_ps, hT, s_hT)
                    fw.op("act", lambda e, ps=ps, st=st, j=j: e.activation(out=st["xr"][:, j, 3:TC + 3], in_=ps[0:88, :], func=AF.Copy),
                          reads=[s_ps], writes=[sl["xr"][j]])
                    ps, s_ps = psA.next()
                    proj_fm(W1a, s_W1a, 704 + c8 * 88, 88, ps, s_ps, hT, s_hT)
                    fw.op("act", lambda e, ps=ps, st=st, j=j: e.activation(out=st["gr"][:, j, :], in_=ps[0:88, :], func=AF.Copy),
                          reads=[s_ps], writes=[sl["gr"][j]])
                fw.op("dve", lambda e, st=st, n=n: e.tensor_copy(out=st["xr"][:, :, 0:3], in_=c_halo[:, 2 * n:2 * n + 2, :]),
                      reads=[s_halo[n]], writes=sl["xr"])
                fw.op("dve", lambda e, st=st, n=n: e.tensor_copy(out=c_halo[:, 2 * n:2 * n + 2, :], in_=st["xr"][:, :, TC:TC + 3]),
                      reads=sl["xr"], writes=[s_halo[n]])
                for k in range(4):
                    for j in range(2):
                        c8 = 2 * n + j
                        if k == 0:
                            fw.op("dve", lambda e, st=st, j=j, c8=c8: e.tensor_scalar(
                                out=st["xc"][:, j, :], in0=st["xr"][:, j, 0:TC], scalar1=v88[:, c8, 0:1], scalar2=v88[:, c8, 4:5],
                                op0=ALU.mult, op1=ALU.add), reads=[sl["xr"][j], s_const], writes=[sl["xc"][j]])
                        else:
                            fw.op("dve", lambda e, st=st, j=j, c8=c8, k=k: e.scalar_tensor_tensor(
                                out=st["xc"][:, j, :], in0=st["xr"][:, j, k:k + TC], scalar=v88[:, c8, k:k + 1], in1=st["xc"][:, j, :],
                                op0=ALU.mult, op1=ALU.add), reads=[sl["xr"][j], sl["xc"][j], s_const], writes=[sl["xc"][j]])
                for j in range(2):
                    fw.op("act", lambda e, st=st, j=j: e.activation(out=st["xcb"][:, j, :], in_=st["xc"][:, j, :], func=AF.Copy),
                          reads=[sl["xc"][j]], writes=[sl["xcb"][j]])
                gps = []
                for j in range(2):
                    for w in range(2):
                        ps, s_ps = psA.next()

                        def f(e, ps=ps, st=st, j=j, w=w, n=n):
                            for i in range(2):
                                last = e.matmul(ps[0:88, :], lhsT=c_wax[:, w, n, i, 88 * j:88 * j + 88], rhs=st["xcb"][:, i, :],
                                                start=(i == 0), stop=(i == 1))
                            return last
                        fw.op("pe", f, reads=sl["xcb"] + [s_const], writes=[s_ps], cost=0.9)
                        gps.append((ps, s_ps))
                for j in range(2):
                    c8 = 2 * n + j
                    ps, s_ps = gps[2 * j]
                    fw.op("act", lambda e, ps=ps, st=st, j=j, c8=c8: e.activation(
                        out=st["rp"][:, j, :], in_=ps[0:88, :], func=AF.Tanh, bias=hba[:, c8:c8 + 1], scale=0.5),
                        reads=[s_ps, s_const], writes=[sl["rp"][j]])
                    ps, s_ps = gps[2 * j + 1]
                    fw.op("act", lambda e, ps=ps, st=st, j=j, c8=c8: e.activation(
                        out=st["ip"][:, j, :], in_=ps[0:88, :], func=AF.Tanh, bias=hbx[:, c8:c8 + 1], scale=0.5),
                        reads=[s_ps, s_const], writes=[sl["ip"][j]])
                for j in range(2):
                    c8 = 2 * n + j
                    fw.op("act", lambda e, st=st, j=j, c8=c8: e.activation(
                        out=st["a"][:, j, :], in_=st["rp"][:, j, :], func=AF.Exp, bias=scl[:, c8:c8 + 1], scale=scl[:, c8:c8 + 1]),
                        reads=[sl["rp"][j], s_const], writes=[sl["a"][j]])
                for j in range(2):
                    fw.op("dve", lambda e, st=st, j=j: e.scalar_tensor_tensor(
                        out=st["t"][:, j, :], in0=st["ip"][:, j, :], scalar=1.0, in1=st["xc"][:, j, :], op0=ALU.add, op1=ALU.mult),
                        reads=[sl["ip"][j], sl["xc"][j]], writes=[sl["t"][j]])
                for j in range(2):
                    fw.op("act", lambda e, st=st, j=j: e.activation(out=st["s"][:, j, :], in_=st["a"][:, j, :], func=AF.Square),
                          reads=[sl["a"][j]], writes=[sl["s"][j]])
                for j in range(2):
                    fw.op("act", lambda e, st=st, j=j: e.activation(out=st["s"][:, j, :], in_=st["s"][:, j, :], func=AF.Sqrt,
                                                                    bias=ONE_AP[0:88, :], scale=-1.0),
                          reads=[sl["s"][j], s_const], writes=[sl["s"][j]])
                for j in range(2):
                    fw.op("dve", lambda e, st=st, j=j: e.scalar_tensor_tensor(
                        out=st["t"][:, j, :], in0=st["s"][:, j, :], scalar=0.5, in1=st["t"][:, j, :], op0=ALU.mult, op1=ALU.mult),
                        reads=[sl["s"][j], sl["t"][j]], writes=[sl["t"][j]])
                for j in range(2):
                    c8 = 2 * n + j
                    fw.op("dve", lambda e, st=st, j=j, c8=c8: e.tensor_tensor_scan(
                        out=st["h"][:, j, :], data0=st["a"][:, j, :], data1=st["t"][:, j, :], initial=c_hlast[:, c8:c8 + 1],
                        op0=ALU.mult, op1=ALU.add), reads=[sl["a"][j], sl["t"][j], s_hlast[c8]], writes=[sl["h"][j]])
                for j in range(2):
                    c8 = 2 * n + j
                    fw.op("act", lambda e, st=st, j=j, c8=c8: e.activation(out=c_hlast[:, c8:c8 + 1], in_=st["h"][:, j, TC - 1:TC], func=AF.Copy),
                          reads=[sl["h"][j]], writes=[s_hlast[c8]])
                for j in range(2):
                    fw.op("dve", lambda e, st=st, j=j: e.tensor_tensor(out=st["ip"][:, j, :], in0=st["gr"][:, j, :], in1=st["gr"][:, j, :], op=ALU.mult),
                          reads=[sl["gr"][j]], writes=[sl["ip"][j]])
                for j in range(2):
                    fw.op("dve", lambda e, st=st, j=j: e.tensor_scalar(out=st["ip"][:, j, :], in0=st["ip"][:, j, :], scalar1=0.044715, scalar2=1.0,
                                                                      op0=ALU.mult, op1=ALU.add), reads=[sl["ip"][j]], writes=[sl["ip"][j]])
                for j in range(2):
                    fw.op("dve", lambda e, st=st, j=j: e.tensor_tensor(out=st["ip"][:, j, :], in0=st["ip"][:, j, :], in1=st["gr"][:, j, :], op=ALU.mult),
                          reads=[sl["ip"][j], sl["gr"][j]], writes=[sl["ip"][j]])
                for j in range(2):
                    fw.op("act", lambda e, st=st, j=j: e.activation(out=st["rp"][:, j, :], in_=st["ip"][:, j, :], func=AF.Tanh, scale=GELU_C),
                          reads=[sl["ip"][j]], writes=[sl["rp"][j]])
                for j in range(2):
                    fw.op("dve", lambda e, st=st, j=j: e.scalar_tensor_tensor(
                        out=st["ip"][:, j, :], in0=st["rp"][:, j, :], scalar=1.0, in1=st["gr"][:, j, :], op0=ALU.add, op1=ALU.mult),
                        reads=[sl["rp"][j], sl["gr"][j]], writes=[sl["ip"][j]])
                for j in range(2):
                    c8 = 2 * n + j
                    fw.op("dve", lambda e, st=st, j=j, c8=c8, yb=yb: e.scalar_tensor_tensor(
                        out=yb[:, c8, :], in0=st["ip"][:, j, :], scalar=0.5, in1=st["h"][:, j, :], op0=ALU.mult, op1=ALU.mult),
                        reads=[sl["ip"][j], sl["h"][j]], writes=[s_yb])
            s_ci = Slot()
            fw.op("sp", lambda e, yb=yb, G=G: e.dma_start(
                out=cin_yr.ap()[G].rearrange("(c p) t -> p c t", p=88), in_=yb),
                reads=[s_yb], writes=[s_ci], kind="dma")
            fw.op("pool", lambda e, G=G: e.collective_compute("AllGather", ALU.bypass, replica_groups=RG,
                                                              ins=[cin_yr.ap()[G].opt()], outs=[cout_yr.ap()[G].opt()]),
                  reads=[s_ci], writes=[s_cyr[G]], kind="cc")
            if G == 0:
                fw.stop_if("passA1")
        fw.stop_if("passA")


        fw.stop_if("ag1")
        B = Arena(arena_ap)
        assert off_xs == 45056 and off_hT0 == 66560, (off_xs, off_hT0)
        KT = arena_at(B, 0, 128, (4, S), BF16)
        s_KT = [[Slot() for _ in range(NCH)] for _ in range(4)]
        e_r = [(arena_at(B, 32768 + 4096 * i, 128, (2, TC), F32), Slot()) for i in range(3)]
        nrm = [dict(raw=arena_at(B, 61440, 128, (TC,), F32), sq=arena_at(B, 63488, 128, (TC,), BF16),
                    ln=arena_at(B, 64512, 128, (TC,), F32), s_raw=Slot(), s_sq=Slot(), s_ln=Slot())]
        B.top = off_hT0 + 16384
        Wv = B.alloc(128, (NKC, 512), BF16)
        Vt = B.alloc(128, (S // 128, 512), BF16)
        s_V = [Slot() for _ in range(S // 128)]
        hT_ring.items = [(B.alloc(128, (NKC, TC), BF16), [Slot() for _ in range(TC // SUB)]) for _ in range(2)]
        QT_ring = Ring([(B.alloc(128, (4, TC), BF16), [Slot() for _ in range(4)]) for _ in range(2)])
        nrm.append(dict(raw=B.alloc(128, (TC,), F32), sq=B.alloc(128, (TC,), BF16), ln=B.alloc(128, (TC,), F32),
                        s_raw=Slot(), s_sq=Slot(), s_ln=Slot()))
        nrmR = Ring(nrm)
        sp_r = [(B.alloc(128, (2, TC), BF16), Slot()) for _ in range(3)]
        x_r = [(B.alloc(128, (2, TC), F32), Slot()) for _ in range(2)]
        w_r = [(B.alloc(128, (2, TC), BF16), Slot()) for _ in range(3)]
        psz2 = psb2[0][:, :].rearrange("p (a b) -> p a b", a=2)
        psc2 = psb2[1][:, :].rearrange("p (a b) -> p a b", a=2)
        obf = [(B.alloc(128, (TC,), BF16), Slot()) for _ in range(2)]
        obfR = Ring(obf)
        psz = [(psb[0][:, :], ps_slots[0]), (psb[1][:, :], ps_slots[1])]
        psc = [(psb[2][:, :], ps_slots[2]), (psb[3][:, :], ps_slots[3])]
        pso = [(psb[4][:, :], ps_slots[4]), (psb[5][:, :], ps_slots[5])]
        psM = Ring([(psb[6][:, :], ps_slots[6]), (psb[7][:, :], ps_slots[7])])

        fw.barrier(ENGS, split_cc=True)
        fw.op("pool", lambda e: e.dma_start(out=Wv, in_=w1b_v[:, :, 1024:1536]), writes=[s_Wv], kind="dma", cost=12.0)

        P = Arena(arena_ap)
        NU = 10
        wunits = [(arena_at(P, 45056 + 4096 * i, 128, (NKC, 128), BF16), Slot()) for i in range(NU)]
        wu_i = [0]

        def wload(src_ap, nk=NKC, extra_reads=(), extra_writes=()):
            u, s_u = wunits[wu_i[0] % NU]
            wu_i[0] += 1
            fw.op("pool", lambda e: e.dma_start(out=u[:, 0:nk, :].rearrange("p k n -> p (k n)"), in_=src_ap),
                  reads=list(extra_reads), writes=[s_u] + list(extra_writes), kind="dma")
            return u, s_u
        pref = {}

        def qk_norm(Wt, s_Wt, col0, gain_col, dst_ap, dst_slot, hT, s_hT):
            nb = nrmR.next()
            ps, s_ps = psM.next()
            proj_fm(Wt, s_Wt, col0, 128, ps, s_ps, hT, s_hT)
            fw.op("dve", lambda e: e.tensor_copy(out=nb["raw"], in_=ps), reads=[s_ps], writes=[nb["s_raw"]])
            fw.op("dve", lambda e: e.tensor_tensor(out=nb["sq"], in0=nb["raw"], in1=nb["raw"], op=ALU.mult),
                  reads=[nb["s_raw"]], writes=[nb["s_sq"]])
            ps2, s_ps2 = psM.next()
            fw.op("pe", lambda e: e.matmul(ps2, lhsT=c_onesh, rhs=nb["sq"], start=True, stop=True),
                  reads=[nb["s_sq"], s_const], writes=[s_ps2], cost=0.3)
            fw.op("act", lambda e: e.activation(out=nb["ln"], in_=ps2, func=AF.Ln, bias=EPS_AP, scale=1.0),
                  reads=[s_ps2], writes=[nb["s_ln"]])
            fw.op("act", lambda e: e.activation(out=nb["ln"], in_=nb["ln"], func=AF.Exp, scale=-0.5),
                  reads=[nb["s_ln"]], writes=[nb["s_ln"]])
            fw.op("dve", lambda e: e.scalar_tensor_tensor(out=dst_ap, in0=nb["raw"], scalar=g128(gain_col), in1=nb["ln"],
                                                          op0=ALU.mult, op1=ALU.mult),
                  reads=[nb["s_raw"], nb["s_ln"], s_const], writes=[dst_slot])

        import os as _os
        for G in range(int(_os.environ.get("KNB", NCH))):
            fw.stop_if("b_w")
            hT, s_hT = hT_ring.next()
            fw.op("sp", lambda e, hT=hT, G=G: e.dma_start(out=hT.rearrange("p k t -> p (k t)"), in_=hbuf.ap()[G]),
                  reads=[s_hbuf[G]], writes=s_hT, kind="dma")
            QT, s_QT = QT_ring.next()
            fw.stop_if("b_h")
            for hh in range(4):
                qk_norm(Wq, s_Wq, hh * 128, 48, QT[:, hh, :], s_QT[hh], hT, s_hT)
                qk_norm(Wk, s_Wk, hh * 128, 49, KT[:, hh, G * TC:(G + 1) * TC], s_KT[hh][G], hT, s_hT)
            fw.stop_if("b_qk")
            for tb in range(4):
                ps, s_ps = psM.next()

                def f(e, ps=ps, tb=tb, hT=hT):
                    for kc in range(NKC):
                        last = e.matmul(ps, lhsT=hT[:, kc, tb * 128:(tb + 1) * 128], rhs=Wv[:, kc, :],
                                        start=(kc == 0), stop=(kc == NKC - 1))
                    return last
                fw.op("pe", f, reads=[s_Wv] + s_hT, writes=[s_ps])
                fw.op("dve", lambda e, ps=ps, tb=tb, G=G: e.tensor_copy(out=Vt[:, 4 * G + tb, :], in_=ps),
                      reads=[s_ps], writes=[s_V[4 * G + tb]])
            fw.stop_if("b_v")
            if G == NCH - 1 and not fw.stopped:
                dead = [s_Wk, s_Wq, s_Wv, nrm[0]["s_raw"], nrm[0]["s_sq"], nrm[0]["s_ln"]]
                for cc_ in range(2):
                    kw = dict(extra_reads=[s_co[NCH - 2]], extra_writes=dead)
                    pref[cc_] = (wload(wt_g[cc_], **kw), wload(wt_g[16 + cc_], **kw), wload(wt_rnn[cc_, 0], 11, **kw),
                                 wload(wt_rnn[cc_, 1], 11, **kw), wload(wt_att[cc_], **kw))
            NT = 4 * G + 4
            s_cio = [Slot() for _ in range(4)]
            for hp in range(2):
                heads = [2 * hp, 2 * hp + 1]
                etile = {}
                sptile = {}
                wtile = {}

                def lo_of(s):
                    d = (NT - 1 - s) - 4 * G
                    return 128 * d if d > 0 else 0

                def stageZ(s):
                    kb = NT - 1 - s
                    diag = kb - 4 * G
                    lo = lo_of(s)
                    for li, hh in enumerate(heads):
                        ps, s_ps = psz[li]

                        def f(e, ps=ps, hh=hh, kb=kb, diag=diag, QT=QT, lo=lo):
                            last = e.matmul(ps[:, lo:TC], lhsT=KT[:, hh, kb * 128:(kb + 1) * 128], rhs=QT[:, hh, lo:TC],
                                            start=True, stop=(diag < 0), skip_group_check=True)
                            if diag >= 0:
                                o0 = 384 - 128 * diag
                                last = e.matmul(ps[:, lo:TC], lhsT=c_ident, rhs=c_mask[:, o0 + lo:o0 + TC], start=False, stop=True,
                                                skip_group_check=True)
                            return last
                        fw.op("pe", f, reads=[s_KT[hh][kb // 4], s_QT[hh], s_const], writes=[s_ps], cost=0.6 if diag >= 0 else 0.3)
                    eb, s_eb = e_r[s % 3]
                    etile[s] = (eb, s_eb)
                    fw.op("act", lambda e, eb=eb, lo=lo: e.activation(out=eb[:, :, lo:TC], in_=psz2[:, :, lo:TC], func=AF.Exp, scale=float(QSCALE)),
                          reads=[psz[0][1], psz[1][1]], writes=[s_eb], cost=0.9)
                    sb_, s_sb = sp_r[s % 3]
                    sptile[s] = (sb_, s_sb)
                    fw.op("act", lambda e, eb=eb, sb_=sb_, lo=lo: e.activation(out=sb_[:, :, lo:TC], in_=eb[:, :, lo:TC], func=AF.Ln, bias=ONE_AP, scale=1.0),
                          reads=[s_eb, s_const], writes=[s_sb], cost=0.9)

                def stageC(s):
                    sb_, s_sb = sptile[s]
                    lo = lo_of(s)
                    lp = lo_of(s - 1) if s > 0 else 0
                    for li, hh in enumerate(heads):
                        ps, s_ps = psc[li]
                        rd = [s_sb, s_const]
                        if s > 0:
                            sprev, s_sprev = sptile[s - 1]
                            rd += [s_sprev, s_ps]
                        else:
                            sprev = None

                        def f(e, ps=ps, sb_=sb_, sprev=sprev, s=s, li=li, lo=lo, lp=lp):
                            if s > 0:
                                e.matmul(ps[:, lp:TC], lhsT=c_SU, rhs=sprev[:, li, lp:TC], start=False, stop=False, skip_group_check=True)
                            return e.matmul(ps[:, lo:TC], lhsT=c_L, rhs=sb_[:, li, lo:TC], start=(s == 0), stop=True, skip_group_check=True)
                        fw.op("pe", f, reads=rd, writes=[s_ps], cost=0.6 if s > 0 else 0.3)
                    xb, s_xb = x_r[s % 2]
                    fw.op("act", lambda e, xb=xb, lo=lo: e.activation(out=xb[:, :, lo:TC], in_=psc2[:, :, lo:TC], func=AF.Exp, scale=-1.0),
                          reads=[psc[0][1], psc[1][1]], writes=[s_xb], cost=0.9)
                    eb, s_eb = etile[s]
                    wb, s_wb = w_r[s % 3]
                    wtile[s] = (wb, s_wb)
                    fw.op("dve", lambda e, xb=xb, eb=eb, wb=wb, lo=lo: e.tensor_tensor(out=wb[:, :, lo:TC], in0=eb[:, :, lo:TC], in1=xb[:, :, lo:TC], op=ALU.mult),
                          reads=[s_xb, s_eb], writes=[s_wb], cost=0.9)

                def stageO(s):
                    kb = NT - 1 - s
                    lo = lo_of(s)
                    for li, hh in enumerate(heads):
                        ps, s_ps = pso[li]
                        wb, s_wb = wtile[s]
                        fw.op("pe", lambda e, ps=ps, wb=wb, hh=hh, kb=kb, s=s, li=li, lo=lo: e.matmul(
                            ps[:, lo:TC], lhsT=Vt[:, kb, hh * 128:(hh + 1) * 128], rhs=wb[:, li, lo:TC], start=(s == 0), stop=(s == NT - 1),
                            skip_group_check=True),
                            reads=[s_wb, s_V[kb], s_ps] if s > 0 else [s_wb, s_V[kb]], writes=[s_ps], cost=0.3)

                for step in range(NT + 2):
                    if step < NT:
                        stageZ(step)
                    if 0 <= step - 1 < NT:
                        stageC(step - 1)
                    if 0 <= step - 2 < NT:
                        stageO(step - 2)
                for li, hh in enumerate(heads):
                    ps, s_ps = pso[li]
                    ob, s_ob = obfR.next()
                    fw.op("dve", lambda e, ps=ps, ob=ob: e.tensor_copy(out=ob, in_=ps), reads=[s_ps], writes=[s_ob])
                    fw.op("sp", lambda e, ob=ob, hh=hh, G=G: e.dma_start(
                        out=cin_o.ap()[G, hh * 128:(hh + 1) * 128, :], in_=ob), reads=[s_ob], writes=[s_cio[hh]], kind="dma")
            fw.op("pool", lambda e, G=G: e.collective_compute("AllGather", ALU.bypass, replica_groups=RG,
                                                              ins=[cin_o.ap()[G].opt()], outs=[cout_o.ap()[G].opt()]),
                  reads=s_cio, writes=[s_co[G]], kind="cc")

        fw.stop_if("passB")

        R2 = arena_at(P, 0, 128, (22, 1024), BF16)
        R1 = arena_at(P, 86016, 128, (NKC, 1024), F32)
        R1b = R1.rearrange("p k t -> p (k t)").bitcast(BF16)
        h_a = R1b[:, 0:NKC * 1024].rearrange("p (k t) -> p k t", k=NKC)
        oT_a = R1b[:, NKC * 1024:2 * NKC * 1024].rearrange("p (k t) -> p k t", k=NKC)
        R3 = arena_at(P, 151552, 128, (NKC, 1024), BF16)
        tmpf = [(arena_at(P, 184320 + 2048 * i, 128, (TC,), F32), Slot()) for i in range(4)]
        tmpR = Ring(tmpf)
        pTb = arena_at(P, 192512, 128, (2, 1024), BF16)
        s_xo = [[Slot() for _ in range(2)] for _ in range(NKC)]
        s_R1h = [Slot() for _ in range(8)]
        s_oT = Slot()
        s_R2 = Slot()
        s_R3 = [[Slot() for _ in range(2)] for _ in range(NKC)]
        s_h2 = [Slot() for _ in range(8)]
        s_pT = Slot()
        psP = Ring(list(zip([p[:, :] for p in psb], ps_slots)))
        sq_p = arena_at(P, 196608, 128, (NKC, SUB), BF16)
        lnm_p = arena_at(P, 200704, 128, (2 * SUB,), F32)
        scratchP = (sq_p, Slot(), lnm_p, Slot())

        fw.barrier(ENGS, split_cc=True)

        def acc_group(ps, s_ps, parts, tb, extra_reads):
            def f(e):
                tot = sum(p[2] for p in parts)
                i = 0
                for (u, s_u, nk, act, k0, rows) in parts:
                    for kc in range(nk):
                        last = e.matmul(ps, lhsT=u[0:rows, kc, :], rhs=act[0:rows, k0 + kc, tb * TC:(tb + 1) * TC],
                                        start=(i == 0), stop=(i == tot - 1))
                        i += 1
                return last
            fw.op("pe", f, reads=[p[1] for p in parts] + list(extra_reads), writes=[s_ps], cost=0.27 * sum(p[2] for p in parts))

        for tb in range(2):
            def f_hh(e, tb=tb):
                rank = e.partition_id() % 4
                return e.dma_start(out=h_a[:, :, tb * TC:(tb + 1) * TC],
                                   in_=hbuf.ap()[bass.ds(2 * rank + tb, 1)].rearrange("o p (k t) -> p (o k) t", k=NKC))
            fw.op("sp", f_hh, reads=s_hbuf, writes=s_R1h[4 * tb:4 * tb + 4], kind="dma")

        def decl_yr():
            for tb in range(2):
                def f_yr(e, tb=tb):
                    rank = e.partition_id() % 4
                    return e.dma_start(out=R2[:, :, tb * TC:(tb + 1) * TC],
                                       in_=cout_yr.ap()[bass.ds(2 * rank + tb, 1)].rearrange("o (k p) t -> p (o k) t", p=128))
                fw.op("sp", f_yr, reads=s_cyr, writes=[s_R2], kind="dma")

        def decl_o():
            for tb in range(2):
                def f_o(e, tb=tb):
                    rank = e.partition_id() % 4
                    return e.dma_start(out=oT_a[:, :, tb * TC:(tb + 1) * TC],
                                       in_=cout_o.ap()[bass.ds(2 * rank + tb, 1)].rearrange("o (k p) t -> p (o k) t", p=128))
                fw.op("sp", f_o, reads=s_co, writes=[s_oT], kind="dma")

        def decl_p():
            fw.op("pool", lambda e: e.dma_start(out=pTb.rearrange("p k t -> p (k t)"), in_=pTt[:, :]), writes=[s_pT], kind="dma")

        fw.stop_if("s0")
        for cc in range(NKC):
            if cc in pref:
                ugr, uga, ur0, ur1, uat = pref[cc]
                if cc == 0:
                    decl_yr()
                    decl_o()
                    decl_p()
            else:
                ugr = wload(wt_g[cc])
                uga = wload(wt_g[16 + cc])
                ur0 = wload(wt_rnn[cc, 0], 11)
                ur1 = wload(wt_rnn[cc, 1], 11)
                uat = wload(wt_att[cc])
            for tb in range(2):
                hs = s_R1h[4 * tb:4 * tb + 4]
                p_gr = psP.next()
                acc_group(p_gr[0], p_gr[1], [(ugr[0], ugr[1], NKC, h_a, 0, 128)], tb, hs)
                p_ga = psP.next()
                acc_group(p_ga[0], p_ga[1], [(uga[0], uga[1], NKC, h_a, 0, 128)], tb, hs)
                p_yr = psP.next()
                acc_group(p_yr[0], p_yr[1], [(ur0[0], ur0[1], 11, R2, 0, 128), (ur1[0], ur1[1], 11, R2, 11, 128)], tb, [s_R2])
                p_ya = psP.next()
                acc_group(p_ya[0], p_ya[1], [(uat[0], uat[1], NKC, oT_a, 0, 128)], tb, [s_oT])
                t_sr = tmpR.next()
                t_sa = tmpR.next()
                fw.op("act", lambda e, p=p_gr, t=t_sr: e.activation(out=t[0], in_=p[0], func=AF.Sigmoid), reads=[p_gr[1]], writes=[t_sr[1]])
                fw.op("act", lambda e, p=p_ga, t=t_sa: e.activation(out=t[0], in_=p[0], func=AF.Sigmoid), reads=[p_ga[1]], writes=[t_sa[1]])
                fw.op("dve", lambda e, p=p_yr, t=t_sr: e.tensor_tensor(out=t[0], in0=t[0], in1=p[0], op=ALU.mult),
                      reads=[p_yr[1], t_sr[1]], writes=[t_sr[1]])
                fw.op("dve", lambda e, p=p_ya, t=t_sa: e.tensor_tensor(out=t[0], in0=t[0], in1=p[0], op=ALU.mult),
                      reads=[p_ya[1], t_sa[1]], writes=[t_sa[1]])
                fw.op("dve", lambda e, a=t_sr, b=t_sa, cc=cc, tb=tb: e.tensor_tensor(
                    out=R3[:, cc, tb * TC:(tb + 1) * TC], in0=a[0], in1=b[0], op=ALU.add),
                    reads=[t_sr[1], t_sa[1]], writes=[s_R3[cc][tb]])

        fw.barrier(["pe", "act", "dve", "sp"])
        fw.stop_if("sa")
        mix_slots = [s_R3[c][t] for c in range(NKC) for t in range(2)]
        for cc in range(NKC):
            uo = wload(wt_o[cc])
            for tb in range(2):
                fw.op("sp", lambda e, cc=cc, tb=tb: e.dma_start(out=R1[:, cc, tb * TC:(tb + 1) * TC], in_=xo2[cc, tb]),
                      writes=[s_xo[cc][tb]], kind="dma")
                p = psP.next()
                acc_group(p[0], p[1], [(uo[0], uo[1], NKC, R3, 0, 128)], tb, [s_R3[c][tb] for c in range(NKC)])
                fw.op("dve", lambda e, p=p, cc=cc, tb=tb: e.tensor_tensor(
                    out=R1[:, cc, tb * TC:(tb + 1) * TC], in0=R1[:, cc, tb * TC:(tb + 1) * TC], in1=p[0], op=ALU.add),
                    reads=[p[1], s_xo[cc][tb]], writes=[s_xo[cc][tb]])


        def rms_resident(gcol0, dst, dst_slots):
            for sc in range(8):
                tb = sc // 4
                rmsnorm_sub(None, R1[:, :, sc * SUB:(sc + 1) * SUB], [s_xo[c][tb] for c in range(NKC)], gcol0,
                            dst[:, :, sc * SUB:(sc + 1) * SUB], dst_slots[sc], SUB, scratchP, psP)

        fw.stop_if("sb")
        rms_resident(16, R3, s_h2)
        for half in range(2):
            for m in range(22):
                mm = half * 22 + m
                ug = wload(wt_gu[mm])
                uu = wload(wt_gu[44 + mm])
                for tb in range(2):
                    hs = s_h2[4 * tb:4 * tb + 4]
                    p_g = psP.next()
                    acc_group(p_g[0], p_g[1], [(ug[0], ug[1], NKC, R3, 0, 128)], tb, hs)
                    p_u = psP.next()
                    acc_group(p_u[0], p_u[1], [(uu[0], uu[1], NKC, R3, 0, 128)], tb, hs)
                    t_s = tmpR.next()
                    fw.op("act", lambda e, p=p_g, t=t_s: e.activation(out=t[0], in_=p[0], func=AF.Silu), reads=[p_g[1]], writes=[t_s[1]])
                    fw.op("dve", lambda e, p=p_u, t=t_s, m=m, tb=tb: e.tensor_tensor(
                        out=R2[:, m, tb * TC:(tb + 1) * TC], in0=t[0], in1=p[0], op=ALU.mult),
                        reads=[p_u[1], t_s[1]], writes=[s_R2])
            for cc in range(NKC):
                ud0 = wload(wt_dn[half, cc, 0], 11)
                ud1 = wload(wt_dn[half, cc, 1], 11)
                for tb in range(2):
                    p = psP.next()
                    acc_group(p[0], p[1], [(ud0[0], ud0[1], 11, R2, 0, 128), (ud1[0], ud1[1], 11, R2, 11, 128)], tb, [s_R2])
                    fw.op("dve", lambda e, p=p, cc=cc, tb=tb: e.tensor_tensor(
                        out=R1[:, cc, tb * TC:(tb + 1) * TC], in0=R1[:, cc, tb * TC:(tb + 1) * TC], in1=p[0], op=ALU.add),
                        reads=[p[1], s_xo[cc][tb]], writes=[s_xo[cc][tb]])

        fw.stop_if("sc")
        h3 = R2[:, 0:NKC, :]
        s_h3 = [Slot() for _ in range(8)]
        rms_resident(32, h3, s_h3)
        out_ops = []
        for cc in range(NKC):
            upg = wload(wt_pg[cc])
            upp = wload(wt_pp[cc], 2)
            for tb in range(2):
                hs = s_h3[4 * tb:4 * tb + 4]
                p_g = psP.next()
                acc_group(p_g[0], p_g[1], [(upg[0], upg[1], NKC, h3, 0, 128)], tb, hs)
                p_p = psP.next()

                acc_group(p_p[0], p_p[1], [(upp[0], upp[1], 2, pTb, 0, 128)], tb, [s_pT])
                t_s = tmpR.next()
                fw.op("act", lambda e, p=p_g, t=t_s: e.activation(out=t[0], in_=p[0], func=AF.Sigmoid), reads=[p_g[1]], writes=[t_s[1]])
                fw.op("dve", lambda e, p=p_p, t=t_s: e.tensor_tensor(out=t[0], in0=t[0], in1=p[0], op=ALU.mult),
                      reads=[p_p[1], t_s[1]], writes=[t_s[1]])
                fw.op("dve", lambda e, t=t_s, cc=cc, tb=tb: e.tensor_tensor(
                    out=R1[:, cc, tb * TC:(tb + 1) * TC], in0=R1[:, cc, tb * TC:(tb + 1) * TC], in1=t[0], op=ALU.add),
                    reads=[t_s[1], s_xo[cc][tb]], writes=[s_xo[cc][tb]])
            out_ops.append(fw.op("sp", lambda e, cc=cc: e.dma_start(out=outT[cc], in_=R1[:, cc, :]),
                                 reads=[s_xo[cc][0], s_xo[cc][1]], kind="dma"))
        s_fin = Slot("fin")
        fin = fw.op("sp", lambda e: e.nop(), kind="c")
        fin.deps.update(o_ for o_ in out_ops if o_.kind != "skip")

        import os as _os2
        if _os2.environ.get("KNOSCHED", "") != "1":
            fw.schedule()
        fw.finalize()
        sems = {}
        for e in ENGS:
            sems[("eng", e)] = es.enter_context(nc.semaphore(f"s_{e}"))
            for i in range(FW.NDS):
                sems[("dma", e, i)] = es.enter_context(nc.semaphore(f"d_{e}{i}"))
        for i in range(fw.cc_count):
            sems[("cc", "pool", i)] = es.enter_context(nc.semaphore(f"cc{i}"))
        block = es.enter_context(nc.Block())

        @block.tensor
        def _(t):
            fw.emit("pe", t, sems)

        @block.scalar
        def _(a):
            fw.emit("act", a, sems)

        @block.vector
        def _(v):
            fw.emit("dve", v, sems)

        @block.gpsimd
        def _(g):
            fw.emit("pool", g, sems)

        @block.sync
        def _(sy):
            fw.emit("sp", sy, sems)
    return nc


def _tile_w(w, nk):
    K, N = w.shape
    assert K == nk * 128
    t = w.reshape(nk, 128, N // 128, 128).transpose(2, 1, 0, 3)
    return np.ascontiguousarray(t).reshape(N // 128, 128, nk * 128)


_NC_CACHE = {}


def kernel(x, p, g_mix, w_in, conv_w, conv_b, w_rg_a, b_rg_a, w_rg_x, b_rg_x, lru_lambda, q_gain, k_gain,
           w_rnn_out, w_attn_out, w_o, g_ffn, w_ffn_gu, w_ffn_down, g_ple, w_ple_gate, w_ple_proj):
    f = np.float32
    x = np.asarray(x, f)
    p = np.asarray(p, f)
    w_in0 = np.asarray(w_in, f)[0]
    if "nc" not in _NC_CACHE:
        _NC_CACHE["nc"] = build_program()
    nc = _NC_CACHE["nc"]

    vec128 = np.zeros((128, 50), f)
    vec128[:, 0:16] = np.asarray(g_mix, f)[0].reshape(16, 128).T
    vec128[:, 16:32] = np.asarray(g_ffn, f)[0].reshape(16, 128).T
    vec128[:, 32:48] = np.asarray(g_ple, f)[0].reshape(16, 128).T
    vec128[:, 48] = np.asarray(q_gain, f)[0]
    vec128[:, 49] = np.asarray(k_gain, f)[0]
    cmat = np.zeros((128, 384 + 896), f)
    cmat[:, 0:128] = np.eye(128, dtype=f)
    jj = np.arange(128)[:, None]
    ss = np.arange(128)[None, :]
    cmat[:, 128:256] = (jj >= ss).astype(f)
    cmat[:, 256:384] = (jj < ss).astype(f)
    xx = np.arange(896)[None, :]
    cmat[:, 384:] = np.where(xx <= jj + 384, NEG, 0.0).astype(f)
    wt_g = _tile_w(w_in0[:, 11776:15872], 16)
    wr = np.asarray(w_rnn_out, f)[0]
    wt_rnn = _tile_w(wr, 22).reshape(16, 128, 2, 11 * 128).transpose(0, 2, 1, 3).copy()
    wt_att = _tile_w(np.asarray(w_attn_out, f)[0], 16)
    wt_o = _tile_w(np.asarray(w_o, f)[0], 16)
    wt_gu = _tile_w(np.asarray(w_ffn_gu, f)[0], 16)
    wd = np.asarray(w_ffn_down, f)[0]
    wt_dn = np.stack([_tile_w(wd[h * 2816:(h + 1) * 2816], 22).reshape(16, 128, 2, 11 * 128).transpose(0, 2, 1, 3)
                      for h in range(2)]).copy()
    wt_pg = _tile_w(np.asarray(w_ple_gate, f)[0], 16)
    wpp = np.asarray(w_ple_proj, f)[0]
    wt_pp = _tile_w(wpp, 2)
    cw = np.asarray(conv_w, f)[0]
    cbv = np.asarray(conv_b, f)[0]
    ba = np.asarray(b_rg_a, f)[0]
    bx = np.asarray(b_rg_x, f)[0]
    lam = np.asarray(lru_lambda, f)[0]
    wa_all = np.asarray(w_rg_a, f)[0]
    wx_all = np.asarray(w_rg_x, f)[0]

    in_maps = []
    for c in range(8):
        b, g = c // 4, c % 4
        xb = x[b]
        xTt = np.ascontiguousarray(xb.reshape(32, 128, 16, 128).transpose(0, 3, 2, 1)).reshape(32, 128, 16 * 128)
        xo = xb[1024 * g:1024 * (g + 1)]
        xo2 = np.ascontiguousarray(xo.reshape(2, 512, 16, 128).transpose(2, 0, 3, 1))
        pb = p[0, b, 1024 * g:1024 * (g + 1)]
        pTt = np.ascontiguousarray(pb.reshape(1024, 2, 128).transpose(2, 1, 0)).reshape(128, 2 * 1024)
        w1a = np.concatenate([w_in0[:, 704 * g:704 * (g + 1)], w_in0[:, 2816 + 704 * g:2816 + 704 * (g + 1)]], axis=1)
        w1a = np.ascontiguousarray(w1a.reshape(16, 128, 1408).transpose(1, 0, 2)).reshape(128, 16 * 1408)
        w1b = np.concatenate([w_in0[:, 5632 + 512 * g:5632 + 512 * (g + 1)], w_in0[:, 7680 + 512 * g:7680 + 512 * (g + 1)],
                              w_in0[:, 9728 + 512 * g:9728 + 512 * (g + 1)]], axis=1)
        w1b = np.ascontiguousarray(w1b.reshape(16, 128, 1536).transpose(1, 0, 2)).reshape(128, 16 * 1536)
        ch = slice(704 * g, 704 * (g + 1))
        v88 = np.zeros((88, 8, 8), f)
        for k in range(4):
            v88[:, :, k] = cw[k, ch].reshape(8, 88).T
        v88[:, :, 4] = cbv[ch].reshape(8, 88).T
        v88[:, :, 5] = ba[ch].reshape(8, 88).T
        v88[:, :, 6] = bx[ch].reshape(8, 88).T
        v88[:, :, 7] = lam[ch].reshape(8, 88).T
        wax = np.stack([wa_all[4 * g:4 * g + 4], wx_all[4 * g:4 * g + 4]])
        wax = np.ascontiguousarray(wax.reshape(2, 4, 2, 88, 176).transpose(3, 0, 1, 2, 4)).reshape(88, 2 * 4 * 2 * 176)
        in_maps.append(dict(xTt=xTt, xo2=xo2, pTt=pTt, w1a=w1a, w1b=w1b,
                            vec88=v88.reshape(88, 64), vec128=vec128, wax=wax, cmat=cmat, wt_g=wt_g, wt_rnn=wt_rnn,
                            wt_att=wt_att, wt_o=wt_o, wt_gu=wt_gu, wt_dn=wt_dn, wt_pg=wt_pg, wt_pp=wt_pp))
    res = run_bass_kernel_spmd(nc, in_maps, core_ids=list(range(8)))
    out = np.empty((2, S, D), f)
    for c in range(8):
        b, g = c // 4, c % 4
        oT = np.asarray(res.results[c]["outT"], f).reshape(2048, 1024)
        out[b, 1024 * g:1024 * (g + 1), :] = oT.T
    return out
```
